# Optimizing a Trainium2 kernel written in Bass

```python
import math
import jax, jax.numpy as jnp
from jax import lax
import numpy as np

D_MODEL = 1024
BATCH = 4
SEQ = 4096
DEPTH = 1
DEC_BATCH = 128
DEC_SEQ = 4
PAST_LEN = 2048
PAGE_SIZE = 128

MIX_WIDTH = D_MODEL
ATTN_WIDTH = MIX_WIDTH // 2
SSM_WIDTH = MIX_WIDTH - ATTN_WIDTH
HEAD_DIM = 64
N_HEADS = ATTN_WIDTH // HEAD_DIM
SSM_GROUP = 16
N_GROUPS = SSM_WIDTH // SSM_GROUP
STATE_DIM = 64
D_FF = 4 * D_MODEL
PLE_DIM = 256
Q_BLOCK = 128
IN_WIDTH = 3 * ATTN_WIDTH + N_HEADS + SSM_WIDTH
ALPHA = (2.0 * DEPTH) ** 0.25
BETA = (8.0 * DEPTH) ** -0.25
LN_EPS = 1e-5
FGATE_BIAS = 3.0
DT_MIN = 1e-3
DT_MAX = 1e-1
NEG_INF = -1e30

kernel_name = "fox_s5_parallel_heads_deepnorm_step"


def layer_norm(x, g, b):
    xf = x.astype(jnp.float32)
    mu = jnp.mean(xf, axis=-1, keepdims=True)
    var = jnp.mean(jnp.square(xf - mu), axis=-1, keepdims=True)
    return ((xf - mu) * lax.rsqrt(var + LN_EPS) * g + b).astype(x.dtype)


def project_in(h, w_in, b_f):
    z = h @ w_in
    q, k, v, fl, u = jnp.split(z, [ATTN_WIDTH, 2 * ATTN_WIDTH, 3 * ATTN_WIDTH, 3 * ATTN_WIDTH + N_HEADS], axis=-1)
    shp = h.shape[:-1] + (N_HEADS, HEAD_DIM)
    logf = jax.nn.log_sigmoid((fl + b_f).astype(jnp.float32))
    return q.reshape(shp), k.reshape(shp), v.reshape(shp), logf, u


def fox_attend(q, k, v, cq, ck, q_pos):
    s = jnp.einsum('bqhd,bkhd->bhqk', q, k).astype(jnp.float32) * (HEAD_DIM ** -0.5)
    bias = jnp.transpose(cq, (0, 2, 1))[..., :, None] - jnp.transpose(ck, (0, 2, 1))[..., None, :]
    k_pos = jnp.arange(k.shape[1], dtype=jnp.int32)
    mask = k_pos[None, :] <= q_pos[:, None]
    s = jnp.where(mask, s + bias, NEG_INF)
    p = jax.nn.softmax(s, axis=-1).astype(v.dtype)
    return jnp.einsum('bhqk,bkhd->bqhd', p, v)


def fox_prompt(q, k, v, logf):
    b, t = q.shape[:2]
    c = jnp.cumsum(logf, axis=1)
    nb = t // Q_BLOCK
    qb = q.reshape(b, nb, Q_BLOCK, N_HEADS, HEAD_DIM).swapaxes(0, 1)
    cb = c.reshape(b, nb, Q_BLOCK, N_HEADS).swapaxes(0, 1)
    starts = jnp.arange(nb, dtype=jnp.int32) * Q_BLOCK

    def block(args):
        q_blk, c_blk, start = args
        return fox_attend(q_blk, k, v, c_blk, c, start + jnp.arange(Q_BLOCK, dtype=jnp.int32))

    o = lax.map(block, (qb, cb, starts))
    return o.swapaxes(0, 1).reshape(b, t, ATTN_WIDTH)


def fox_sample(q, k, v, logf, cache_k, cache_v, cache_logf, page_table):
    db, t = q.shape[:2]
    past_len = page_table.shape[1] * cache_k.shape[1]
    pk = cache_k[page_table].reshape(db, past_len, N_HEADS, HEAD_DIM)
    pv = cache_v[page_table].reshape(db, past_len, N_HEADS, HEAD_DIM)
    pf = cache_logf[page_table].reshape(db, past_len, N_HEADS).astype(jnp.float32)
    k_all = jnp.concatenate([pk, k.astype(pk.dtype)], axis=1)
    v_all = jnp.concatenate([pv, v.astype(pv.dtype)], axis=1)
    c = jnp.cumsum(jnp.concatenate([pf, logf], axis=1), axis=1)
    q_pos = past_len + jnp.arange(t, dtype=jnp.int32)
    o = fox_attend(q, k_all, v_all, c[:, past_len:], c, q_pos)
    return o.reshape(db, t, ATTN_WIDTH).astype(q.dtype)


def s5_discretize(lam_re, lam_im, log_dt, b_re, b_im):
    dt = jnp.exp(log_dt.astype(jnp.float32))[:, None]
    lr = lam_re.astype(jnp.float32)
    li = lam_im.astype(jnp.float32)
    mag = jnp.exp(lr * dt)
    a_re = mag * jnp.cos(li * dt)
    a_im = mag * jnp.sin(li * dt)
    nr = a_re - 1.0
    ni = a_im
    den = lr * lr + li * li
    coef_re = (nr * lr + ni * li) / den
    coef_im = (ni * lr - nr * li) / den
    br = b_re.astype(jnp.float32)
    bi = b_im.astype(jnp.float32)
    bb_re = coef_re[..., None] * br - coef_im[..., None] * bi
    bb_im = coef_re[..., None] * bi + coef_im[..., None] * br
    return a_re, a_im, bb_re, bb_im


def s5_combine(e1, e2):
    a1r, a1i, b1r, b1i = e1
    a2r, a2i, b2r, b2i = e2
    return (a2r * a1r - a2i * a1i,
            a2r * a1i + a2i * a1r,
            a2r * b1r - a2i * b1i + b2r,
            a2r * b1i + a2i * b1r + b2i)


def s5_mixer(u, h0_re, h0_im, lam_re, lam_im, log_dt, b_re, b_im, c_re, c_im, d_skip, w_glu, b_glu):
    b, t = u.shape[:2]
    ug = u.reshape(b, t, N_GROUPS, SSM_GROUP).astype(jnp.float32)
    a_re, a_im, bb_re, bb_im = s5_discretize(lam_re, lam_im, log_dt, b_re, b_im)
    bu_re = jnp.einsum('gph,btgh->btgp', bb_re, ug)
    bu_im = jnp.einsum('gph,btgh->btgp', bb_im, ug)
    ar = jnp.broadcast_to(a_re, bu_re.shape)
    ai = jnp.broadcast_to(a_im, bu_re.shape)
    acr, aci, bcr, bci = lax.associative_scan(s5_combine, (ar, ai, bu_re, bu_im), axis=1)
    h0r = h0_re.astype(jnp.float32)[:, None]
    h0i = h0_im.astype(jnp.float32)[:, None]
    hr = acr * h0r - aci * h0i + bcr
    hi = acr * h0i + aci * h0r + bci
    y = (jnp.einsum('ghp,btgp->btgh', c_re.astype(jnp.float32), hr)
         - jnp.einsum('ghp,btgp->btgh', c_im.astype(jnp.float32), hi)
         + d_skip.astype(jnp.float32) * ug)
    y = jax.nn.gelu(y.reshape(b, t, SSM_WIDTH))
    y = y * jax.nn.sigmoid(y @ w_glu.astype(jnp.float32) + b_glu.astype(jnp.float32))
    return y.astype(u.dtype), hr[:, -1], hi[:, -1]


def post_mixer(h, mix, p, w_out, ln1_g, ln1_b, w_up, w_down, w_pe, w_pg, b_pg, ln2_g, ln2_b):
    h1 = layer_norm(ALPHA * h + mix @ w_out, ln1_g, ln1_b)
    ff = jnp.square(jax.nn.relu(h1 @ w_up)) @ w_down
    e = jax.nn.sigmoid(h1 @ w_pg + b_pg) * (p @ w_pe)
    return layer_norm(ALPHA * h1 + ff + e, ln2_g, ln2_b)


def setup_inputs(seed: int = 0) -> dict:
    key = jax.random.key(seed)
    ks = iter(jax.random.split(key, 40))
    f32 = jnp.float32
    n_pages = PAST_LEN // PAGE_SIZE
    n_used = DEC_BATCH * n_pages
    n_phys = n_used + n_used // 4

    def nrm(shape, scale=1.0):
        return jax.random.normal(next(ks), shape, f32) * scale

    x_prompt = nrm((BATCH, SEQ, D_MODEL))
    x_sample = nrm((DEC_BATCH, DEC_SEQ, D_MODEL))
    cache_k = nrm((DEPTH, n_phys, PAGE_SIZE, N_HEADS, HEAD_DIM))
    cache_v = nrm((DEPTH, n_phys, PAGE_SIZE, N_HEADS, HEAD_DIM))
    cache_logf = jax.nn.log_sigmoid(FGATE_BIAS + nrm((DEPTH, n_phys, PAGE_SIZE, N_HEADS)))
    state_re = nrm((DEPTH, DEC_BATCH, N_GROUPS, STATE_DIM), 0.5)
    state_im = nrm((DEPTH, DEC_BATCH, N_GROUPS, STATE_DIM), 0.5)
    page_table = jax.random.permutation(next(ks), n_phys)[:n_used].reshape(DEC_BATCH, n_pages).astype(jnp.int32)
    p_prompt = nrm((DEPTH, BATCH, SEQ, PLE_DIM))
    p_sample = nrm((DEPTH, DEC_BATCH, DEC_SEQ, PLE_DIM))

    ln_in_g = 1.0 + nrm((D_MODEL,), 0.02)
    ln_in_b = nrm((D_MODEL,), 0.02)
    w_in = nrm((DEPTH, D_MODEL, IN_WIDTH), D_MODEL ** -0.5)
    b_f = FGATE_BIAS + nrm((DEPTH, N_HEADS), 0.1)
    n_idx = jnp.arange(STATE_DIM, dtype=f32)
    lam_re = -0.5 + nrm((DEPTH, N_GROUPS, STATE_DIM), 0.01)
    lam_im = math.pi * n_idx + nrm((DEPTH, N_GROUPS, STATE_DIM), 0.01)
    log_dt = jax.random.uniform(next(ks), (DEPTH, N_GROUPS), f32, math.log(DT_MIN), math.log(DT_MAX))
    b_re = nrm((DEPTH, N_GROUPS, STATE_DIM, SSM_GROUP), (2.0 * SSM_GROUP) ** -0.5)
    b_im = nrm((DEPTH, N_GROUPS, STATE_DIM, SSM_GROUP), (2.0 * SSM_GROUP) ** -0.5)
    c_re = nrm((DEPTH, N_GROUPS, SSM_GROUP, STATE_DIM), (2.0 * STATE_DIM) ** -0.5)
    c_im = nrm((DEPTH, N_GROUPS, SSM_GROUP, STATE_DIM), (2.0 * STATE_DIM) ** -0.5)
    d_skip = nrm((DEPTH, N_GROUPS, SSM_GROUP))
    w_glu = nrm((DEPTH, SSM_WIDTH, SSM_WIDTH), SSM_WIDTH ** -0.5)
    b_glu = nrm((DEPTH, SSM_WIDTH), 0.01)
    w_out = nrm((DEPTH, MIX_WIDTH, D_MODEL), BETA * MIX_WIDTH ** -0.5)
    ln1_g = 1.0 + nrm((DEPTH, D_MODEL), 0.02)
    ln1_b = nrm((DEPTH, D_MODEL), 0.02)
    w_up = nrm((DEPTH, D_MODEL, D_FF), D_MODEL ** -0.5)
    w_down = nrm((DEPTH, D_FF, D_MODEL), BETA * D_FF ** -0.5)
    w_pe = nrm((DEPTH, PLE_DIM, D_MODEL), BETA * PLE_DIM ** -0.5)
    w_pg = nrm((DEPTH, D_MODEL, D_MODEL), D_MODEL ** -0.5)
    b_pg = nrm((DEPTH, D_MODEL), 0.01)
    ln2_g = 1.0 + nrm((DEPTH, D_MODEL), 0.02)
    ln2_b = nrm((DEPTH, D_MODEL), 0.02)
    return {"x_prompt": x_prompt, "x_sample": x_sample, "cache_k": cache_k, "cache_v": cache_v,
            "cache_logf": cache_logf, "state_re": state_re, "state_im": state_im, "page_table": page_table,
            "p_prompt": p_prompt, "p_sample": p_sample, "ln_in_g": ln_in_g, "ln_in_b": ln_in_b,
            "w_in": w_in, "b_f": b_f, "lam_re": lam_re, "lam_im": lam_im, "log_dt": log_dt,
            "b_re": b_re, "b_im": b_im, "c_re": c_re, "c_im": c_im, "d_skip": d_skip,
            "w_glu": w_glu, "b_glu": b_glu, "w_out": w_out, "ln1_g": ln1_g, "ln1_b": ln1_b,
            "w_up": w_up, "w_down": w_down, "w_pe": w_pe, "w_pg": w_pg, "b_pg": b_pg,
            "ln2_g": ln2_g, "ln2_b": ln2_b}


def reference(x_prompt, x_sample, cache_k, cache_v, cache_logf, state_re, state_im, page_table,
              p_prompt, p_sample, ln_in_g, ln_in_b, w_in, b_f, lam_re, lam_im, log_dt,
              b_re, b_im, c_re, c_im, d_skip, w_glu, b_glu, w_out, ln1_g, ln1_b,
              w_up, w_down, w_pe, w_pg, b_pg, ln2_g, ln2_b):
    hp = layer_norm(x_prompt, ln_in_g, ln_in_b)
    hs = layer_norm(x_sample, ln_in_g, ln_in_b)
    kp_l, vp_l, fp_l, srp_l, sip_l = [], [], [], [], []
    ks_l, vs_l, fs_l, srs_l, sis_l = [], [], [], [], []
    for i in range(DEPTH):
        ssm_args = (lam_re[i], lam_im[i], log_dt[i], b_re[i], b_im[i], c_re[i], c_im[i], d_skip[i], w_glu[i], b_glu[i])
        post_args = (w_out[i], ln1_g[i], ln1_b[i], w_up[i], w_down[i], w_pe[i], w_pg[i], b_pg[i], ln2_g[i], ln2_b[i])
        q, k, v, logf, u = project_in(hp, w_in[i], b_f[i])
        att = fox_prompt(q, k, v, logf)
        h0 = jnp.zeros((hp.shape[0], N_GROUPS, STATE_DIM), jnp.float32)
        ssm, sr, si = s5_mixer(u, h0, h0, *ssm_args)
        hp = post_mixer(hp, jnp.concatenate([att, ssm], axis=-1), p_prompt[i], *post_args)
        kp_l.append(k); vp_l.append(v); fp_l.append(logf); srp_l.append(sr); sip_l.append(si)
        q, k, v, logf, u = project_in(hs, w_in[i], b_f[i])
        att = fox_sample(q, k, v, logf, cache_k[i], cache_v[i], cache_logf[i], page_table)
        ssm, sr, si = s5_mixer(u, state_re[i], state_im[i], *ssm_args)
        hs = post_mixer(hs, jnp.concatenate([att, ssm], axis=-1), p_sample[i], *post_args)
        ks_l.append(k); vs_l.append(v); fs_l.append(logf); srs_l.append(sr); sis_l.append(si)
    return (hp, hs,
            jnp.stack(kp_l), jnp.stack(vp_l), jnp.stack(fp_l), jnp.stack(srp_l), jnp.stack(sip_l),
            jnp.stack(ks_l), jnp.stack(vs_l), jnp.stack(fs_l), jnp.stack(srs_l), jnp.stack(sis_l))
```

```python
import math
import os
KSTAGE = int(os.environ.get('KSTAGE', '99'))
KSKIP = os.environ.get('KSKIP', '').split(',')
from contextlib import ExitStack
import numpy as np
import concourse.bass as bass
import concourse.mybir as mybir
from concourse.bass_utils import run_bass_kernel_spmd

F32 = mybir.dt.float32
BF16 = mybir.dt.bfloat16
I32 = mybir.dt.int32
U32 = mybir.dt.uint32
AF = mybir.ActivationFunctionType
ALU = mybir.AluOpType
AX = mybir.AxisListType

D = 1024
NH = 8
HD = 64
AW = 512
SW = 512
NG = 32
SG = 16
SP = 64
DFF = 4096
PLE = 256
INW = 2056
ALPHA = 2.0 ** 0.25
EPS = 1e-5
NEG = -60000.0


class Prog:
    STREAMS = ("sync", "scalar", "vector", "gpsimd", "tensor")

    def __init__(self, nc, es):
        self.nc = nc
        self.ops = {s: [] for s in self.STREAMS}
        self.sems = {}
        self.cnt = {}
        self.sem_stream = {}
        for name, stream in (("sync", "sync"), ("act", "scalar"), ("dve", "vector"), ("pool", "gpsimd"),
                             ("pooldma", "gpsimd"), ("pe", "tensor"), ("actdma", "scalar")):
            self.sems[name] = es.enter_context(nc.semaphore("s_" + name))
            self.cnt[name] = 0
            self.sem_stream[name] = stream
        self.waited = {}
        self.lastw = {}
        self.readers = {}
        self.es = es

    def add_sem(self, name, stream):
        self.sems[name] = self.es.enter_context(self.nc.semaphore("s_" + name))
        self.cnt[name] = 0
        self.sem_stream[name] = stream

    def op(self, sem, fn, reads=(), writes=(), inc=1):
        stream = self.sem_stream[sem]
        deps = set()
        for b in reads:
            if b in self.lastw:
                deps.add(self.lastw[b])
        for b in writes:
            if b in self.lastw:
                deps.add(self.lastw[b])
            for t in self.readers.get(b, ()):
                deps.add(t)
        waits = []
        for (ps, pc) in sorted(deps):
            if ps == "pe" and sem == "pe":
                continue
            key = (stream, ps)
            if self.waited.get(key, 0) >= pc:
                continue
            self.waited[key] = pc
            waits.append((ps, pc))
        self.cnt[sem] += inc
        tok = (sem, self.cnt[sem])
        self.ops[stream].append((waits, fn, sem, inc))
        for b in writes:
            self.lastw[b] = tok
            self.readers[b] = []
        for b in reads:
            self.readers.setdefault(b, []).append(tok)
        return tok

    def dma(self, out, in_, reads=(), writes=(), q="sync", slot=None, **kw):
        if slot is None:
            sem = {"sync": "sync", "gpsimd": "pooldma", "scalar": "actdma"}[q]
        else:
            sem = "d_" + slot
            if sem not in self.sems:
                self.add_sem(sem, q)
            assert self.sem_stream[sem] == q, (sem, q)
        return self.op(sem, lambda e: e.dma_start(out=out, in_=in_, **kw), reads, writes, inc=16)

    def pe(self, fn, reads=(), writes=()):
        return self.op("pe", fn, reads, writes)

    def act(self, fn, reads=(), writes=()):
        return self.op("act", fn, reads, writes)

    def dve(self, fn, reads=(), writes=()):
        return self.op("dve", fn, reads, writes)

    def pool(self, fn, reads=(), writes=()):
        return self.op("pool", fn, reads, writes)

    def barrier(self):
        for stream in self.STREAMS:
            waits = []
            for sname, c in self.cnt.items():
                if c > 0 and self.waited.get((stream, sname), 0) < c:
                    self.waited[(stream, sname)] = c
                    waits.append((sname, c))
            self.ops[stream].append((waits, None, None, 0))

    def emit(self, last=True):
        nc = self.nc
        final = [(s, c) for s, c in self.cnt.items() if c > 0]
        ops = self.ops
        self.ops = {s: [] for s in self.STREAMS}

        def run(e, stream):
            for waits, fn, sem, inc in ops[stream]:
                for (ps, pc) in waits:
                    e.wait_ge(self.sems[ps], pc)
                if fn is not None:
                    fn(e).then_inc(self.sems[sem], inc)
            if stream == "sync" and last:
                for (s, c) in final:
                    e.wait_ge(self.sems[s], c)

        with nc.Block() as block:
            @block.sync
            def _(e):
                run(e, "sync")

            @block.scalar
            def _(e):
                run(e, "scalar")

            @block.vector
            def _(e):
                run(e, "vector")

            @block.gpsimd
            def _(e):
                run(e, "gpsimd")

            @block.tensor
            def _(e):
                run(e, "tensor")


def build(T, NSEQ, NPHYS, dbg=False):
    nc = bass.Bass("TRN2", target_bir_lowering=False)
    NB = T // 128
    NSB = T // 512
    ST = NSEQ * 4

    def din(name, shape, dt=F32):
        return nc.dram_tensor(name, list(shape), dt, kind="ExternalInput").ap()

    def dout(name, shape, dt=F32):
        return nc.dram_tensor(name, list(shape), dt, kind="ExternalOutput").ap()

    x_p = din("x_p", [T, D])
    pp_p = din("pp_p", [T, PLE])
    w_in = din("w_in", [D, INW])
    ln_in_g = din("ln_in_g", [D])
    ln_in_b = din("ln_in_b", [D])
    b_f = din("b_f", [NH])
    lam_re = din("lam_re", [NG, SP])
    lam_im = din("lam_im", [NG, SP])
    log_dt = din("log_dt", [NG])
    b_re = din("b_re", [NG, SP, SG])
    b_im = din("b_im", [NG, SP, SG])
    c_re = din("c_re", [NG, SG, SP])
    c_im = din("c_im", [NG, SG, SP])
    d_skip = din("d_skip", [NG, SG])
    w_glu = din("w_glu", [SW, SW])
    b_glu = din("b_glu", [SW])
    w_out = din("w_out", [D, D])
    ln1_g = din("ln1_g", [D])
    ln1_b = din("ln1_b", [D])
    w_up = din("w_up", [D, DFF])
    w_down = din("w_down", [DFF, D])
    w_pe = din("w_pe", [PLE, D])
    w_pg = din("w_pg", [D, D])
    b_pg = din("b_pg", [D])
    ln2_g = din("ln2_g", [D])
    ln2_b = din("ln2_b", [D])
    x_s = din("x_s", [128, D])
    pp_s = din("pp_s", [128, PLE])
    ck = din("ck", [NPHYS * 128, AW])
    cv = din("cv", [NPHYS * 128, AW])
    clf = din("clf", [NPHYS * 128, NH])
    st_re = din("st_re", [NSEQ, NG * SP])
    st_im = din("st_im", [NSEQ, NG * SP])
    pt = din("pt", [NSEQ * 16], I32)
    y_p = dout("y_p", [T, D])
    y_s = dout("y_s", [128, D])
    k_s = dout("k_s", [ST, AW])
    v_s = dout("v_s", [ST, AW])
    lf_sd = dout("lf_sd", [ST, NH])
    ssm_re_s = dout("ssm_re_s", [NSEQ, NG * SP])
    ssm_im_s = dout("ssm_im_s", [NSEQ, NG * SP])
    if dbg:
        att_d_s = dout("att_d_s", [HD, NH, 128], BF16)
        yg_d_s = dout("yg_d_s", [4, 128, 128], BF16)
    else:
        att_d_s = nc.dram_tensor("att_d_s", [HD, NH, 128], BF16).ap()
        yg_d_s = nc.dram_tensor("yg_d_s", [4, 128, 128], BF16).ap()
    att_d = nc.dram_tensor("att_d", [HD, NH, T], BF16).ap()
    wup_d = nc.dram_tensor("wup_d", [D, DFF], BF16).ap()
    wdn_d = nc.dram_tensor("wdn_d", [DFF, D], BF16).ap()
    wout_d = nc.dram_tensor("wout_d", [D, D], BF16).ap()
    wpg_d = nc.dram_tensor("wpg_d", [D, D], BF16).ap()
    wpe_d = nc.dram_tensor("wpe_d", [PLE, D], BF16).ap()
    wglu_d = nc.dram_tensor("wglu_d", [SW, SW], BF16).ap()

    k_p = dout("k_p", [T, AW])
    v_p = dout("v_p", [T, AW])
    lf_p = dout("lf_p", [T, NH])
    ssm_re_p = dout("ssm_re_p", [NG, SP])
    ssm_im_p = dout("ssm_im_p", [NG, SP])
    if dbg:
        yg_d = dout("yg_d", [4, 128, T], BF16)
    else:
        yg_d = nc.dram_tensor("yg_d", [4, 128, T], BF16).ap()

    es = ExitStack()
    with es:
        P = Prog(nc, es)

        cur = [es]

        def sb(name, shape, dt=F32):
            return cur[0].enter_context(nc.sbuf_tensor(name, list(shape), dt))

        def ps(name, shape, dt=F32):
            return cur[0].enter_context(nc.psum_tensor(name, list(shape), dt))

        def phase_end(*stacks):
            P.barrier()
            P.emit(last=False)
            for st_ in stacks:
                st_.close()

        ident_f = sb("ident_f", [128, 128])
        ident_b = sb("ident_b", [128, 128], BF16)
        P.pool(lambda e: e.memset(ident_f[:], 1.0), writes=["ident_f"])
        P.pool(lambda e: e.affine_select(ident_f[:], ident_f[:], [[-1, 128]], ALU.is_equal, 0.0,
                                         base=0, channel_multiplier=1), reads=["ident_f"], writes=["ident_f"])
        P.pool(lambda e: e.tensor_copy(ident_b[:], ident_f[:]), reads=["ident_f"], writes=["ident_b"])
        triu = sb("triu", [128, 128])
        P.pool(lambda e: e.memset(triu[:], 1.0), writes=["triu"])
        P.pool(lambda e: e.affine_select(triu[:], triu[:], [[1, 128]], ALU.is_ge, 0.0,
                                         base=0, channel_multiplier=-1), reads=["triu"], writes=["triu"])
        ones_f = sb("ones_f", [128, 128])
        P.pool(lambda e: e.memset(ones_f[:], 1.0), writes=["ones_f"])

        g_in_col = sb("g_in_col", [128, 8])
        b_in_col = sb("b_in_col", [128, 8])
        P.dma(g_in_col[:], ln_in_g.rearrange("(c p) -> p c", p=128), writes=["g_in_col"],
              allow_slow_non_contiguous=True)
        P.dma(b_in_col[:], ln_in_b.rearrange("(c p) -> p c", p=128), writes=["b_in_col"],
              allow_slow_non_contiguous=True)
        bf_bc = sb("bf_bc", [128, NH])
        P.dma(bf_bc[:], b_f.partition_broadcast(128), writes=["bf_bc"])

        NS = NSEQ
        kT_s = sb("kT_s", [128, 4, 128], BF16)
        qT_s = sb("qT_s", [128, 4, 128], BF16)
        uT_s = sb("uT_s", [128, 4, 128], BF16)
        Vb_s = sb("Vb_s", [128, NH, HD + 2], BF16)
        lf_s = sb("lf_s", [128, NH])
        P.pool(lambda e: e.memset(Vb_s[:], 1.0), writes=["Vb_s"])
        s_u = ExitStack()
        cur[0] = s_u
        uT = sb("uT", [128, 4, T], BF16)
        s_att = ExitStack()
        cur[0] = s_att
        kT = sb("kT", [128, 4, T], BF16)
        qT = sb("qT", [128, 4, T], BF16)
        Vb = sb("Vb", [128, NB, NH, HD + 2], BF16)
        lf_all = sb("lf_all", [128, NB, NH])
        s1 = ExitStack()
        cur[0] = s1
        cst = [sb(f"cst{i}", [128, 514]) for i in range(2)]
        csb = [sb(f"csb{i}", [128, 512], BF16) for i in range(2)]
        w_in_b = sb("w_in_b", [128, 8, INW], BF16)
        ncv = 0
        for ct in range(8):
            for c4 in range(4):
                i_ = ncv % 2
                ncv += 1
                P.dma(cst[i_][:, 0:514], w_in[ct * 128:(ct + 1) * 128, c4 * 514:(c4 + 1) * 514], writes=[f"cst{i_}"], q="gpsimd", slot=f"cst{i_}")
                P.pool(lambda e, i_=i_, ct=ct, c4=c4: e.tensor_copy(w_in_b[:, ct, c4 * 514:(c4 + 1) * 514], cst[i_][:, 0:514]),
                       reads=[f"cst{i_}"], writes=[("w_in_b", ct)])

        for (nm, src, dstd, rows, cols) in (("glu", w_glu, wglu_d, SW, SW), ("out", w_out, wout_d, D, D), ("pg", w_pg, wpg_d, D, D),
                                            ("pe", w_pe, wpe_d, PLE, D), ("up", w_up, wup_d, D, DFF), ("dn", w_down, wdn_d, DFF, D)):
            cw = 512
            for r0 in range(0, rows, 128):
                for c0 in range(0, cols, cw):
                    i_ = ncv % 2
                    ncv += 1
                    P.dma(cst[i_][:, 0:cw], src[r0:r0 + 128, c0:c0 + cw], writes=[f"cst{i_}"], q="gpsimd", slot=f"cst{i_}")
                    P.pool(lambda e, i_=i_, cw=cw: e.tensor_copy(csb[i_][:, 0:cw], cst[i_][:, 0:cw]),
                           reads=[f"cst{i_}"], writes=[f"csb{i_}"])
                    P.dma(dstd[r0:r0 + 128, c0:c0 + cw], csb[i_][:, 0:cw], reads=[f"csb{i_}"], writes=[("wsc", nm)], q="gpsimd", slot=f"csb{i_}")
        xin = [sb(f"xin{i}", [128, 4, D]) for i in range(1)]
        stats = sb("stats", [128, 2, 6])
        mv = sb("mv", [128, 2])
        rstd = sb("rstd", [128, 1])
        nmr = sb("nmr", [128, 1])
        hT = sb("hT", [128, 8, 512], BF16)
        ostage = [sb(f"ostage{i}", [128, 512]) for i in range(2)]
        lstage = sb("lstage", [128, NH])
        P.pool(lambda e: e.memset(Vb[:], 1.0), writes=["Vb"])
        pst = [ps(f"pst{i}", [128, 512]) for i in range(2)]
        pmm = [ps(f"pmm{i}", [128, 512]) for i in range(4)]
        nmm = [0]
        nos = [0]

        def layer_norm_block(xt, xkey, ntt):
            for tt in range(ntt):
                for c in range(2):
                    P.dve(lambda e, tt=tt, c=c: e.bn_stats(stats[:, c, :], xt[:, tt, c * 512:(c + 1) * 512]),
                          reads=[xkey], writes=[("stats", c)])
                P.dve(lambda e: e.bn_aggr(mv[:], stats[:].rearrange("p a b -> p (a b)")),
                      reads=[("stats", 0), ("stats", 1)], writes=["mv"])
                P.dve(lambda e: e.tensor_scalar(rstd[:], mv[:, 1:2], EPS, None, ALU.add),
                      reads=["mv"], writes=["rstd"])
                P.act(lambda e: e.activation(rstd[:], rstd[:], AF.Sqrt), reads=["rstd"], writes=["rstd"])
                P.dve(lambda e: e.reciprocal(rstd[:], rstd[:]), reads=["rstd"], writes=["rstd"])
                P.dve(lambda e: e.scalar_tensor_tensor(nmr[:], mv[:, 0:1], -1.0, rstd[:], ALU.mult, ALU.mult),
                      reads=["mv", "rstd"], writes=["nmr"])
                P.act(lambda e, tt=tt: e.activation(xt[:, tt, :], xt[:, tt, :], AF.Identity,
                                                    bias=nmr[:, 0:1], scale=rstd[:, 0:1]),
                      reads=["nmr", "rstd", xkey], writes=[xkey])

        def to_feature_major(xt, xkey, ntt, dst, dkey, gcol, bcol):
            for ct in range(8):
                pt = pst[ct % 2]
                pk = f"pst{ct % 2}"
                for tt in range(ntt):
                    P.pe(lambda e, pt=pt, tt=tt, ct=ct: e.transpose(pt[:, tt * 128:(tt + 1) * 128],
                                                                  xt[:, tt, ct * 128:(ct + 1) * 128], ident_f[:]),
                         reads=[xkey, "ident_f"], writes=[pk])
                P.act(lambda e, pt=pt, ct=ct: e.activation(dst[:, ct, 0:ntt * 128], pt[:, 0:ntt * 128], AF.Identity,
                                                          bias=bcol[:, ct:ct + 1], scale=gcol[:, ct:ct + 1]),
                      reads=[pk, "g_in_col", "b_in_col"], writes=[(dkey, ct)])

        for sbk in range(NSB):
            xt = xin[0]
            xkey = "xin0"
            P.dma(xt[:], x_p[sbk * 512:(sbk + 1) * 512, :].rearrange("(t p) d -> p t d", p=128), writes=[xkey], slot="xin")
            layer_norm_block(xt, xkey, 4)
            to_feature_major(xt, xkey, 4, hT, "hT", g_in_col, b_in_col)
            hkeys = [("hT", ct) for ct in range(8)]
            wkeys = [("w_in_b", ct) for ct in range(8)]
            tok = slice(sbk * 512, (sbk + 1) * 512)
            for (dst, dkey, c0, scl) in ((kT, "kT", 512, 1.0), (qT, "qT", 0, 0.125), (uT, "uT", 1544, 1.0)):
                for ft in range(4):
                    pm = pmm[nmm[0] % 4]
                    pk = f"pmm{nmm[0] % 4}"
                    nmm[0] += 1
                    for ct in range(8):
                        P.pe(lambda e, pm=pm, ct=ct, ft=ft, c0=c0: e.matmul(
                            pm[:], w_in_b[:, ct, c0 + ft * 128:c0 + (ft + 1) * 128], hT[:, ct, :],
                            start=(ct == 0), stop=(ct == 7)), reads=hkeys + wkeys, writes=[pk])
                    P.act(lambda e, pm=pm, dst=dst, ft=ft, scl=scl, tok=tok: e.activation(dst[:, ft, tok], pm[:], AF.Copy, scale=scl),
                          reads=[pk], writes=[(dkey, sbk)])
            for tt in range(4):
                blk = sbk * 4 + tt
                for (c0, dram, isv) in ((512, k_p, False), (1024, v_p, True)):
                    pm = pmm[nmm[0] % 4]
                    pk = f"pmm{nmm[0] % 4}"
                    nmm[0] += 1
                    for ct in range(8):
                        P.pe(lambda e, pm=pm, ct=ct, tt=tt, c0=c0: e.matmul(
                            pm[:], hT[:, ct, tt * 128:(tt + 1) * 128], w_in_b[:, ct, c0:c0 + 512],
                            start=(ct == 0), stop=(ct == 7)), reads=hkeys + wkeys, writes=[pk])
                    os_ = ostage[nos[0] % 2]
                    ok = f"ostage{nos[0] % 2}"
                    nos[0] += 1
                    P.act(lambda e, pm=pm, os_=os_: e.copy(os_[:], pm[:]), reads=[pk], writes=[ok])
                    if isv and 'vb' not in KSKIP:
                        P.act(lambda e, pm=pm, blk=blk: e.copy(
                            Vb[:, blk, :, 0:HD], pm[:].rearrange("p (h d) -> p h d", h=NH)),
                            reads=[pk], writes=["Vb"])
                    if 'odma' not in KSKIP:
                        P.dma(dram[blk * 128:(blk + 1) * 128, :], os_[:], reads=[ok], slot=ok)
                if 'lf' in KSKIP:
                    continue
                pm = pmm[nmm[0] % 4]
                pk = f"pmm{nmm[0] % 4}"
                nmm[0] += 1
                for ct in range(8):
                    P.pe(lambda e, pm=pm, ct=ct, tt=tt: e.matmul(
                        pm[:, 0:NH], hT[:, ct, tt * 128:(tt + 1) * 128], w_in_b[:, ct, 1536:1544],
                        start=(ct == 0), stop=(ct == 7)), reads=hkeys + wkeys, writes=[pk])
                P.act(lambda e, pm=pm: e.copy(lstage[:], pm[:, 0:NH]), reads=[pk], writes=["lstage"])
                P.dve(lambda e: e.tensor_tensor(lstage[:], lstage[:], bf_bc[:], ALU.add),
                      reads=["lstage", "bf_bc"], writes=["lstage"])
                P.act(lambda e: e.activation(lstage[:], lstage[:], AF.Exp, scale=-1.0),
                      reads=["lstage"], writes=["lstage"])
                P.dve(lambda e: e.tensor_scalar(lstage[:], lstage[:], 1.0, None, ALU.add),
                      reads=["lstage"], writes=["lstage"])
                P.act(lambda e: e.activation(lstage[:], lstage[:], AF.Ln),
                      reads=["lstage"], writes=["lstage"])
                P.dve(lambda e, blk=blk: e.tensor_scalar(lf_all[:, blk, :], lstage[:], -1.0, None, ALU.mult),
                      reads=["lstage"], writes=[("lf_all", blk)])
        P.dma(lf_p.rearrange("(b p) h -> p b h", p=128), lf_all[:],
              reads=[("lf_all", b) for b in range(NB)])

        xt = xin[0]
        xkey = "xin0"
        P.dma(xt[:, 0, :], x_s[:, :], writes=[xkey], slot="xin")
        layer_norm_block(xt, xkey, 1)
        to_feature_major(xt, xkey, 1, hT, "hT", g_in_col, b_in_col)
        hkeys = [("hT", ct) for ct in range(8)]
        wkeys = [("w_in_b", ct) for ct in range(8)]
        for (dst, dkey, c0, scl) in ((kT_s, "kT_s", 512, 1.0), (qT_s, "qT_s", 0, 0.125), (uT_s, "uT_s", 1544, 1.0)):
            for ft in range(4):
                pm = pmm[nmm[0] % 4]
                pk = f"pmm{nmm[0] % 4}"
                nmm[0] += 1
                for ct in range(8):
                    P.pe(lambda e, pm=pm, ct=ct, ft=ft, c0=c0: e.matmul(
                        pm[:, 0:128], w_in_b[:, ct, c0 + ft * 128:c0 + (ft + 1) * 128], hT[:, ct, 0:128],
                        start=(ct == 0), stop=(ct == 7)), reads=hkeys + wkeys, writes=[pk])
                P.act(lambda e, pm=pm, dst=dst, ft=ft, scl=scl: e.activation(dst[:, ft, :], pm[:, 0:128], AF.Copy, scale=scl),
                      reads=[pk], writes=[dkey])
        for (c0, dram, isv) in ((512, k_s, False), (1024, v_s, True)):
            pm = pmm[nmm[0] % 4]
            pk = f"pmm{nmm[0] % 4}"
            nmm[0] += 1
            for ct in range(8):
                P.pe(lambda e, pm=pm, ct=ct, c0=c0: e.matmul(
                    pm[:], hT[:, ct, 0:128], w_in_b[:, ct, c0:c0 + 512],
                    start=(ct == 0), stop=(ct == 7)), reads=hkeys + wkeys, writes=[pk])
            os_ = ostage[nos[0] % 2]
            ok = f"ostage{nos[0] % 2}"
            nos[0] += 1
            P.act(lambda e, pm=pm, os_=os_: e.copy(os_[:], pm[:]), reads=[pk], writes=[ok])
            if isv:
                P.act(lambda e, pm=pm: e.copy(Vb_s[:, :, 0:HD], pm[:].rearrange("p (h d) -> p h d", h=NH)),
                      reads=[pk], writes=["Vb_s"])
            P.dma(dram[:, :], os_[0:ST, :], reads=[ok], slot=ok)
        pm = pmm[nmm[0] % 4]
        pk = f"pmm{nmm[0] % 4}"
        nmm[0] += 1
        for ct in range(8):
            P.pe(lambda e, pm=pm, ct=ct: e.matmul(
                pm[:, 0:NH], hT[:, ct, 0:128], w_in_b[:, ct, 1536:1544],
                start=(ct == 0), stop=(ct == 7)), reads=hkeys + wkeys, writes=[pk])
        P.act(lambda e, pm=pm: e.copy(lstage[:], pm[:, 0:NH]), reads=[pk], writes=["lstage"])
        P.dve(lambda e: e.tensor_tensor(lstage[:], lstage[:], bf_bc[:], ALU.add),
              reads=["lstage", "bf_bc"], writes=["lstage"])
        P.act(lambda e: e.activation(lstage[:], lstage[:], AF.Exp, scale=-1.0),
              reads=["lstage"], writes=["lstage"])
        P.dve(lambda e: e.tensor_scalar(lstage[:], lstage[:], 1.0, None, ALU.add),
              reads=["lstage"], writes=["lstage"])
        P.act(lambda e: e.activation(lstage[:], lstage[:], AF.Ln),
              reads=["lstage"], writes=["lstage"])
        P.dve(lambda e: e.tensor_scalar(lf_s[:], lstage[:], -1.0, None, ALU.mult),
              reads=["lstage"], writes=["lf_s"])
        P.dma(lf_sd[:, :], lf_s[0:ST, :], reads=["lf_s"])

        phase_end(s1)

        s3 = ExitStack()
        cur[0] = s3
        pst = [ps(f"pst{i}_3", [128, 512]) for i in range(2)]
        pmm = [ps(f"pmm{i}_3", [128, 512]) for i in range(4)]
        att_dbg = dout("att_dbg", [HD, NH, T]) if dbg else None
        dbg2 = dout("dbg2", [128, 1536]) if dbg else None
        dbg3 = dout("dbg3", [128, 2576]) if dbg else None
        dbgs = sb("dbgs", [128, 2576]) if dbg else None
        lf_keys = [("lf_all", b) for b in range(NB)]
        totb = sb("totb", [128, NB, NH])
        pre = sb("pre", [128, NB, NH])
        cc = sb("cc", [128, NB, NH])
        biasI = sb("biasI", [128, NB, NH])
        maskT = sb("maskT", [128, 128], BF16)
        maskf = sb("maskf", [128, 128])
        P.pool(lambda e: e.memset(maskf[:], 0.0), writes=["maskf"])
        P.pool(lambda e: e.affine_select(maskf[:], maskf[:], [[1, 128]], ALU.is_ge, NEG,
                                         base=0, channel_multiplier=-1), reads=["maskf"], writes=["maskf"])
        P.pool(lambda e: e.tensor_copy(maskT[:], maskf[:]), reads=["maskf"], writes=["maskT"])
        lf_flat = lf_all[:].rearrange("p b h -> p (b h)")
        pm = pmm[0]
        P.pe(lambda e: e.matmul(pm[:, 0:NB * NH], ones_f[:], lf_flat, start=True, stop=True),
             reads=lf_keys + ["ones_f"], writes=["pmm0"])
        P.act(lambda e: e.copy(totb[:].rearrange("p b h -> p (b h)"), pm[:, 0:NB * NH]), reads=["pmm0"], writes=["totb"])
        pm1 = pmm[1]
        P.pe(lambda e: e.matmul(pm1[:, 0:NB * NH], triu[:], lf_flat, start=True, stop=True),
             reads=lf_keys + ["triu"], writes=["pmm1"])
        P.act(lambda e: e.copy(cc[:].rearrange("p b h -> p (b h)"), pm1[:, 0:NB * NH]), reads=["pmm1"], writes=["cc"])
        P.dve(lambda e: e.memset(pre[:, 0, :], 0.0), writes=["pre"])
        for b in range(1, NB):
            P.dve(lambda e, b=b: e.tensor_tensor(pre[:, b, :], pre[:, b - 1, :], totb[:, b - 1, :], ALU.add),
                  reads=["pre", "totb"], writes=["pre"])
        P.dve(lambda e: e.tensor_tensor(cc[:], cc[:], pre[:], ALU.add), reads=["cc", "pre"], writes=["cc"])

        pS = [ps(f"pS{i}", [128, 512]) for i in range(2)]
        pT = [sb(f"pT{i}", [128, 512], BF16) for i in range(2)]
        osb = sb("osb", [HD + 1, 512])
        rbc = sb("rbc", [HD, 512])
        attT = sb("attT", [HD, NH, 512], BF16)
        attf = sb("attf", [HD, NH, 512]) if dbg else None
        nS = [0]
        for I in range(NSB):
            nkb = 4 * I + 4
            for kb in range(nkb):
                P.dve(lambda e, kb=kb, I=I: e.tensor_tensor(biasI[:, kb, :], pre[:, 4 * I, :], cc[:, kb, :], ALU.subtract),
                      reads=["pre", "cc"], writes=["biasI"])
            for h in range(NH):
                hp = slice((h % 2) * 64, (h % 2) * 64 + 64)
                ft = h // 2
                po = pmm[2 + (h % 2)]
                pok = f"pmm{2 + (h % 2)}"
                for kb in range(nkb):
                    c0 = 0 if kb < 4 * I else (kb - 4 * I) * 128
                    i = nS[0] % 2
                    nS[0] += 1
                    psS, pTt = pS[i], pT[i]
                    diag = kb >= 4 * I
                    P.pe(lambda e, psS=psS, kb=kb, c0=c0, diag=diag, hp=hp, ft=ft, I=I: e.matmul(
                        psS[:, c0:512], kT[hp, ft, kb * 128:(kb + 1) * 128], qT[hp, ft, I * 512 + c0:(I + 1) * 512],
                        start=True, stop=not diag), reads=[("kT", kb // 4), ("qT", I)], writes=[f"pS{i}"])
                    if diag:
                        P.pe(lambda e, psS=psS, c0=c0: e.matmul(psS[:, c0:c0 + 128], ident_b[:], maskT[:],
                                                               start=False, stop=True),
                             reads=["ident_b", "maskT"], writes=[f"pS{i}"])
                    P.act(lambda e, psS=psS, pTt=pTt, c0=c0, kb=kb, h=h: e.activation(
                        pTt[:, c0:512], psS[:, c0:512], AF.Exp, bias=biasI[:, kb, h:h + 1]),
                        reads=[f"pS{i}", "biasI"], writes=[f"pT{i}"])
                    if dbg and I == 0 and h == 0 and kb == 0:
                        P.act(lambda e, psS=psS: e.copy(dbgs[:, 0:512], psS[:]), reads=[f"pS{i}"], writes=["dbgs0"])
                        P.act(lambda e, pTt=pTt: e.copy(dbgs[:, 512:1024], pTt[:]), reads=[f"pT{i}"], writes=["dbgs1"])
                        P.act(lambda e: e.copy(dbgs[:, 1024:1536], qT[:, 0, 0:512]), reads=[("qT", 0)], writes=["dbgs2"])
                        P.act(lambda e: e.copy(dbgs[:, 1536:2048], kT[:, 0, 0:512]), reads=[("kT", 0)], writes=["dbgs3"])
                        P.act(lambda e: e.copy(dbgs[:, 2048:2048 + 66 * 8], Vb[:, 0, :, :].rearrange("p h d -> p (h d)")), reads=["Vb"], writes=["dbgs4"])
                        P.dma(dbg3[:, :], dbgs[:], reads=["dbgs0", "dbgs1", "dbgs2", "dbgs3", "dbgs4"])
                    P.pe(lambda e, po=po, pTt=pTt, c0=c0, kb=kb, h=h, nkb=nkb: e.matmul(
                        po[0:HD + 1, c0:512], Vb[:, kb, h, 0:HD + 1], pTt[:, c0:512],
                        start=(kb == 0), stop=(kb == nkb - 1)), reads=[f"pT{i}", "Vb"], writes=[pok])
                P.act(lambda e, po=po: e.copy(osb[:], po[0:HD + 1, :]), reads=[pok], writes=["osb"])
                P.dve(lambda e: e.reciprocal(osb[HD:HD + 1, :], osb[HD:HD + 1, :]), reads=["osb"], writes=["osb"])
                pb = pst[h % 2]
                pbk = f"pst{h % 2}"
                P.pe(lambda e, pb=pb: e.matmul(pb[0:HD, :], ones_f[HD:HD + 1, 0:HD], osb[HD:HD + 1, :],
                                               start=True, stop=True), reads=["osb", "ones_f"], writes=[pbk])
                P.act(lambda e, pb=pb: e.copy(rbc[:], pb[0:HD, :]), reads=[pbk], writes=["rbc"])
                P.dve(lambda e, h=h: e.tensor_tensor(attT[:, h, :], osb[0:HD, :], rbc[:], ALU.mult),
                      reads=["osb", "rbc"], writes=[("attT", h)])
                if dbg:
                    P.dve(lambda e, h=h: e.tensor_tensor(attf[:, h, :], osb[0:HD, :], rbc[:], ALU.mult),
                          reads=["osb", "rbc"], writes=[("attf", h)])
            P.dma(att_d[:, :, I * 512:(I + 1) * 512], attT[:], reads=[("attT", h) for h in range(NH)], writes=[("att_d", I)], slot="attT")
            if dbg:
                P.dma(att_dbg[:, :, I * 512:(I + 1) * 512], attf[:], reads=[("attf", h) for h in range(NH)])
                if I == 0:
                    P.dma(dbg2[:, 0:NB * NH], cc[:].rearrange("p b h -> p (b h)"), reads=["cc"])
                    P.dma(dbg2[:, 256:256 + NB * NH], biasI[:].rearrange("p b h -> p (b h)"), reads=["biasI"])
                    P.dma(dbg2[0:HD + 1, 512:1024], osb[:], reads=["osb"])
                    P.dma(dbg2[0:HD, 1024:1536], rbc[:], reads=["rbc"])

        phase_end(s3, s_att)
        s2 = ExitStack()
        cur[0] = s2
        GP = 16
        NC = T // 8
        PI = math.pi
        TWO_PI = 2.0 * math.pi
        K_ = "ssmc"

        def dv(fn):
            return P.dve(fn, reads=[K_], writes=[K_])

        def ac(fn):
            return P.act(fn, reads=[K_], writes=[K_])

        def sdma(out, in_):
            return P.dma(out, in_, writes=[K_], q="gpsimd", allow_slow_non_contiguous=True)

        lam_r = sb("lam_r", [128, GP])
        lam_i = sb("lam_i", [128, GP])
        ldt = sb("ldt", [128, GP])
        Bre = sb("Bre", [128, GP, SG])
        Bim = sb("Bim", [128, GP, SG])
        Cre = sb("Cre", [128, GP, SG])
        Cim = sb("Cim", [128, GP, SG])
        dcol = sb("dcol", [128, 4])
        sdma(lam_r[:], lam_re.rearrange("(gp g2) p -> (g2 p) gp", g2=2))
        sdma(lam_i[:], lam_im.rearrange("(gp g2) p -> (g2 p) gp", g2=2))
        for g2 in range(2):
            sdma(ldt[64 * g2:64 * g2 + 64, :], log_dt.rearrange("(gp g2) -> g2 gp", g2=2)[g2].partition_broadcast(64))
            for gp in range(GP):
                sdma(Cre[64 * g2:64 * g2 + 64, gp, :], c_re[2 * gp + g2].rearrange("h p -> p h"))
                sdma(Cim[64 * g2:64 * g2 + 64, gp, :], c_im[2 * gp + g2].rearrange("h p -> p h"))
        sdma(Bre[:], b_re.rearrange("(gp g2) p h -> (g2 p) gp h", g2=2))
        sdma(Bim[:], b_im.rearrange("(gp g2) p h -> (g2 p) gp h", g2=2))
        sdma(dcol[:], d_skip.rearrange("g h -> (g h)").rearrange("(a p) -> p a", p=128))

        def small(name):
            return sb(name, [128, GP])

        dt_, lrdt, th, mag, rho8, t1, a1, sinv, t2, cosv, phi8 = [small(n) for n in (
            "dt_", "lrdt", "th", "mag", "rho8", "t1", "a1", "sinv", "t2", "cosv", "phi8")]
        tmpa, tmpb, nr, den, cre, cim = [small(n) for n in ("tmpa", "tmpb", "nr", "den", "cre", "cim")]
        AKr = sb("AKr", [128, GP, 9])
        AKi = sb("AKi", [128, GP, 9])

        def tt(o, a, b, op):
            return dv(lambda e: e.tensor_tensor(o, a, b, op))

        def ts(o, a, s1_, s2_, op0, op1=None):
            if op1 is None:
                return dv(lambda e: e.tensor_scalar(o, a, s1_, None, op0))
            return dv(lambda e: e.tensor_scalar(o, a, s1_, s2_, op0, op1))

        def reduce_pm_pi(dst, x, ki, kf, m, emit_v):
            emit_v(lambda e: e.tensor_scalar(ki, x, 1.0 / TWO_PI, None, ALU.mult))
            emit_v(lambda e: e.tensor_copy(kf, ki))
            emit_v(lambda e: e.scalar_tensor_tensor(dst, kf, -TWO_PI, x, ALU.mult, ALU.add))
            emit_v(lambda e: e.tensor_scalar(m, dst, PI, None, ALU.is_gt))
            emit_v(lambda e: e.scalar_tensor_tensor(dst, m, -TWO_PI, dst, ALU.mult, ALU.add))
            emit_v(lambda e: e.tensor_scalar(m, dst, -PI, None, ALU.is_lt))
            emit_v(lambda e: e.scalar_tensor_tensor(dst, m, TWO_PI, dst, ALU.mult, ALU.add))

        def cos_arg(dst, r, m, emit_v):
            emit_v(lambda e: e.tensor_scalar(dst, r, PI / 2, None, ALU.add))
            emit_v(lambda e: e.tensor_scalar(m, dst, PI, None, ALU.is_gt))
            emit_v(lambda e: e.scalar_tensor_tensor(dst, m, -TWO_PI, dst, ALU.mult, ALU.add))

        ki_s = sb("ki_s", [128, GP], I32)
        kf_s = small("kf_s")
        m_s = small("m_s")
        ac(lambda e: e.activation(dt_[:], ldt[:], AF.Exp))
        tt(lrdt[:], lam_r[:], dt_[:], ALU.mult)
        tt(th[:], lam_i[:], dt_[:], ALU.mult)
        ac(lambda e: e.activation(mag[:], lrdt[:], AF.Exp))
        ac(lambda e: e.activation(rho8[:], lrdt[:], AF.Exp, scale=8.0))
        reduce_pm_pi(t1[:], th[:], ki_s[:], kf_s[:], m_s[:], dv)
        ac(lambda e: e.activation(sinv[:], t1[:], AF.Sin))
        cos_arg(t2[:], t1[:], m_s[:], dv)
        ac(lambda e: e.activation(cosv[:], t2[:], AF.Sin))
        ts(a1[:], t1[:], 8.0, None, ALU.mult)
        reduce_pm_pi(phi8[:], a1[:], ki_s[:], kf_s[:], m_s[:], dv)
        dv(lambda e: e.memset(AKr[:, :, 0], 1.0))
        dv(lambda e: e.memset(AKi[:, :, 0], 0.0))
        tt(AKr[:, :, 1], mag[:], cosv[:], ALU.mult)
        tt(AKi[:, :, 1], mag[:], sinv[:], ALU.mult)
        for k in range(2, 9):
            tt(tmpa[:], AKr[:, :, k - 1], AKr[:, :, 1], ALU.mult)
            tt(tmpb[:], AKi[:, :, k - 1], AKi[:, :, 1], ALU.mult)
            tt(AKr[:, :, k], tmpa[:], tmpb[:], ALU.subtract)
            tt(tmpa[:], AKr[:, :, k - 1], AKi[:, :, 1], ALU.mult)
            tt(tmpb[:], AKi[:, :, k - 1], AKr[:, :, 1], ALU.mult)
            tt(AKi[:, :, k], tmpa[:], tmpb[:], ALU.add)
        ts(nr[:], AKr[:, :, 1], -1.0, None, ALU.add)
        tt(tmpa[:], lam_r[:], lam_r[:], ALU.mult)
        tt(tmpb[:], lam_i[:], lam_i[:], ALU.mult)
        tt(den[:], tmpa[:], tmpb[:], ALU.add)
        dv(lambda e: e.reciprocal(den[:], den[:]))
        tt(tmpa[:], nr[:], lam_r[:], ALU.mult)
        tt(tmpb[:], AKi[:, :, 1], lam_i[:], ALU.mult)
        tt(cre[:], tmpa[:], tmpb[:], ALU.add)
        tt(cre[:], cre[:], den[:], ALU.mult)
        tt(tmpa[:], AKi[:, :, 1], lam_r[:], ALU.mult)
        tt(tmpb[:], nr[:], lam_i[:], ALU.mult)
        tt(cim[:], tmpa[:], tmpb[:], ALU.subtract)
        tt(cim[:], cim[:], den[:], ALU.mult)
        bbr = sb("bbr", [128, GP, SG])
        bbi = sb("bbi", [128, GP, SG])
        t3a = sb("t3a", [128, GP, SG])
        t3b = sb("t3b", [128, GP, SG])
        cre_b = cre[:].unsqueeze(2).broadcast_to([128, GP, SG])
        cim_b = cim[:].unsqueeze(2).broadcast_to([128, GP, SG])
        tt(t3a[:], Bre[:], cre_b, ALU.mult)
        tt(t3b[:], Bim[:], cim_b, ALU.mult)
        tt(bbr[:], t3a[:], t3b[:], ALU.subtract)
        tt(t3a[:], Bim[:], cre_b, ALU.mult)
        tt(t3b[:], Bre[:], cim_b, ALU.mult)
        tt(bbi[:], t3a[:], t3b[:], ALU.add)
        MC = sb("MC", [128, GP, 9, 2, 32], BF16)
        Wt = sb("Wt", [128, 4, 8, 2, 128], BF16)
        Kl = sb("Kl", [128, 4, 8, 32], BF16)
        pw = [ps(f"pw{i}", [128, 512]) for i in range(2)]
        s2c = ExitStack()
        cur[0] = s2c
        XAr = sb("XAr", [128, GP, 8, SG])
        XAi = sb("XAi", [128, GP, 8, SG])
        t4a = sb("t4a", [128, GP, 9, SG])
        t4b = sb("t4b", [128, GP, 9, SG])
        akr8 = AKr[:, :, 0:8].unsqueeze(3).broadcast_to([128, GP, 8, SG])
        aki8 = AKi[:, :, 0:8].unsqueeze(3).broadcast_to([128, GP, 8, SG])
        bbr8 = bbr[:].unsqueeze(2).broadcast_to([128, GP, 8, SG])
        bbi8 = bbi[:].unsqueeze(2).broadcast_to([128, GP, 8, SG])
        tt(t4a[:, :, 0:8, :], akr8, bbr8, ALU.mult)
        tt(t4b[:, :, 0:8, :], aki8, bbi8, ALU.mult)
        tt(XAr[:], t4a[:, :, 0:8, :], t4b[:, :, 0:8, :], ALU.subtract)
        tt(t4a[:, :, 0:8, :], akr8, bbi8, ALU.mult)
        tt(t4b[:, :, 0:8, :], aki8, bbr8, ALU.mult)
        tt(XAi[:], t4a[:, :, 0:8, :], t4b[:, :, 0:8, :], ALU.add)
        CAr = sb("CAr", [128, GP, 9, SG])
        CAi = sb("CAi", [128, GP, 9, SG])
        akr9 = AKr[:].unsqueeze(3).broadcast_to([128, GP, 9, SG])
        aki9 = AKi[:].unsqueeze(3).broadcast_to([128, GP, 9, SG])
        cr9 = Cre[:].unsqueeze(2).broadcast_to([128, GP, 9, SG])
        ci9 = Cim[:].unsqueeze(2).broadcast_to([128, GP, 9, SG])
        tt(t4a[:], akr9, cr9, ALU.mult)
        tt(t4b[:], aki9, ci9, ALU.mult)
        tt(CAr[:], t4a[:], t4b[:], ALU.subtract)
        tt(t4a[:], aki9, cr9, ALU.mult)
        tt(t4b[:], akr9, ci9, ALU.mult)
        tt(CAi[:], t4a[:], t4b[:], ALU.add)
        ts(CAi[:], CAi[:], -1.0, None, ALU.mult)
        XAbd = sb("XAbd", [128, 4, 8, 2, 4, 32])
        CBDr = sb("CBDr", [128, GP, 32])
        CBDni = sb("CBDni", [128, GP, 32])
        dv(lambda e: e.memset(XAbd[:], 0.0))
        dv(lambda e: e.memset(MC[:], 0.0))
        dv(lambda e: e.memset(CBDr[:], 0.0))
        dv(lambda e: e.memset(CBDni[:], 0.0))
        for g2 in range(2):
            pr = slice(64 * g2, 64 * g2 + 64)
            cs = slice(16 * g2, 16 * g2 + 16)
            for gq in range(4):
                dv(lambda e, pr=pr, cs=cs, gq=gq: e.tensor_copy(
                    XAbd[pr, gq, :, 0, :, cs], XAr[pr, 4 * gq:4 * gq + 4, :, :].rearrange("p q k h -> p k q h")))
                dv(lambda e, pr=pr, cs=cs, gq=gq: e.tensor_copy(
                    XAbd[pr, gq, :, 1, :, cs], XAi[pr, 4 * gq:4 * gq + 4, :, :].rearrange("p q k h -> p k q h")))
            dv(lambda e, pr=pr, cs=cs: e.tensor_copy(MC[pr, :, :, 0, cs], CAr[pr]))
            dv(lambda e, pr=pr, cs=cs: e.tensor_copy(MC[pr, :, :, 1, cs], CAi[pr]))
            dv(lambda e, pr=pr, cs=cs: e.tensor_copy(CBDr[pr, :, cs], Cre[pr]))
            dv(lambda e, pr=pr, cs=cs: e.tensor_scalar(CBDni[pr, :, cs], Cim[pr], -1.0, None, ALU.mult))
        npw = 0
        for gq in range(4):
            for i in range(8):
                pwt = pw[npw % 2]
                pwk = f"pw{npw % 2}"
                npw += 1
                for ri in range(2):
                    P.pe(lambda e, pwt=pwt, gq=gq, i=i, ri=ri: e.transpose(
                        pwt[:, ri * 128:(ri + 1) * 128], XAbd[:, gq, 7 - i, ri, :, :].rearrange("p q c -> p (q c)"), ident_f[:]),
                        reads=[K_, "ident_f"], writes=[pwk])
                P.act(lambda e, pwt=pwt, gq=gq, i=i: e.copy(
                    Wt[:, gq, i, :, :].rearrange("p r c -> p (r c)"), pwt[:, 0:256]), reads=[pwk], writes=["Wt"])
        for gp in range(GP):
            gq, q = gp // 4, gp % 4
            pwt = pw[npw % 2]
            pwk = f"pw{npw % 2}"
            npw += 1
            for k in range(8):
                P.pe(lambda e, pwt=pwt, gq=gq, gp=gp, k=k: e.matmul(
                    pwt[:, k * 32:(k + 1) * 32], XAbd[:, gq, k, 0, :, :].rearrange("p q c -> p (q c)"), CBDr[:, gp, :],
                    start=True, stop=False), reads=[K_], writes=[pwk])
                P.pe(lambda e, pwt=pwt, gq=gq, gp=gp, k=k: e.matmul(
                    pwt[:, k * 32:(k + 1) * 32], XAbd[:, gq, k, 1, :, :].rearrange("p q c -> p (q c)"), CBDni[:, gp, :],
                    start=False, stop=True), reads=[K_], writes=[pwk])
            P.act(lambda e, pwt=pwt, gq=gq, q=q: e.copy(
                Kl[32 * q:32 * q + 32, gq, :, :].rearrange("p k c -> p (k c)"), pwt[32 * q:32 * q + 32, 0:256]),
                reads=[pwk], writes=["Kl"])

        phase_end(s2c)
        cur[0] = s2
        cidx = sb("cidx", [128, NC])
        P.pool(lambda e: e.iota(cidx[:], [[1, NC]], base=0, channel_multiplier=0,
                                allow_small_or_imprecise_dtypes=True), writes=["cidx"])
        ang, nS, nC, sr, si, ta, tb, Gr, Gi = [sb(n, [128, NC]) for n in
                                               ("ang", "nS", "nC", "sr", "si", "ta", "tb", "Gr", "Gi")]
        ki_w = sb("ki_w", [128, NC], I32)
        Xp = [[sb(f"Xp{q}_{ri}", [128, NC], BF16) for ri in range(2)] for q in range(4)]
        fin = sb("fin", [128, GP, 2])
        ysb = sb("ysb", [128, 1024])
        yt1 = sb("yt1", [128, 1024])
        ygb = sb("ygb", [128, 1024], BF16)
        pss = [ps(f"pss{i}", [128, 512]) for i in range(2)]
        psy = ps("psy", [128, 1024])
        for q in range(4):
            for ri in range(2):
                P.pool(lambda e, q=q, ri=ri: e.memset(Xp[q][ri][:, 0:1], 0.0), writes=[("Xp", q)])
        W_ = "ssmw"

        def wv(fn, extra_r=(), extra_w=()):
            return P.dve(fn, reads=[W_] + list(extra_r), writes=[W_] + list(extra_w))

        def wa(fn, extra_r=(), extra_w=()):
            return P.act(fn, reads=[W_] + list(extra_r), writes=[W_] + list(extra_w))

        ukeys = [("uT", sbk) for sbk in range(NSB)]
        for gq in range(4):
            for q in range(4):
                gp = 4 * gq + q
                rows = slice(32 * q, 32 * q + 32)
                for ri in range(2):
                    for i in range(8):
                        P.pe(lambda e, ri=ri, i=i, rows=rows, gq=gq, q=q: e.matmul(
                            pss[ri][:, 0:NC], Wt[rows, gq, i, ri, :], uT[rows, gq, i:T:8],
                            start=(i == 0), stop=(i == 7), tile_position=(32 * q, 0)),
                            reads=["Wt"] + ukeys, writes=[f"pss{ri}"])
                wa(lambda e: e.copy(sr[:], pss[0][:, 0:NC]), extra_r=["pss0"])
                wa(lambda e: e.copy(si[:], pss[1][:, 0:NC]), extra_r=["pss1"])
                wv(lambda e, gp=gp: e.tensor_scalar(ang[:], cidx[:], phi8[:, gp:gp + 1], None, ALU.mult),
                   extra_r=["cidx", K_])
                reduce_pm_pi(ta[:], ang[:], ki_w[:], tb[:], Gr[:], wv)
                wa(lambda e: e.activation(nS[:], ta[:], AF.Sin))
                cos_arg(tb[:], ta[:], Gr[:], wv)
                wa(lambda e: e.activation(nC[:], tb[:], AF.Sin))
                wv(lambda e: e.tensor_tensor(ta[:], sr[:], nC[:], ALU.mult))
                wv(lambda e: e.tensor_tensor(tb[:], si[:], nS[:], ALU.mult))
                wv(lambda e: e.tensor_tensor(Gr[:], ta[:], tb[:], ALU.add))
                wv(lambda e: e.tensor_tensor(ta[:], si[:], nC[:], ALU.mult))
                wv(lambda e: e.tensor_tensor(tb[:], sr[:], nS[:], ALU.mult))
                wv(lambda e: e.tensor_tensor(Gi[:], ta[:], tb[:], ALU.subtract))
                rho_b = rho8[:, gp:gp + 1].broadcast_to([128, NC])
                wv(lambda e, rho_b=rho_b: e.tensor_tensor_scan(sr[:], rho_b, Gr[:], 0.0, ALU.mult, ALU.add), extra_r=[K_])
                wv(lambda e, rho_b=rho_b: e.tensor_tensor_scan(si[:], rho_b, Gi[:], 0.0, ALU.mult, ALU.add), extra_r=[K_])
                wv(lambda e: e.tensor_tensor(ta[:], sr[:], nC[:], ALU.mult))
                wv(lambda e: e.tensor_tensor(tb[:], si[:], nS[:], ALU.mult))
                wv(lambda e: e.tensor_tensor(Gr[:], ta[:], tb[:], ALU.subtract))
                wv(lambda e: e.tensor_tensor(ta[:], si[:], nC[:], ALU.mult))
                wv(lambda e: e.tensor_tensor(tb[:], sr[:], nS[:], ALU.mult))
                wv(lambda e: e.tensor_tensor(Gi[:], ta[:], tb[:], ALU.add))
                wa(lambda e, q=q: e.copy(Xp[q][0][:, 1:NC], Gr[:, 0:NC - 1]), extra_w=[("Xp", q)])
                wa(lambda e, q=q: e.copy(Xp[q][1][:, 1:NC], Gi[:, 0:NC - 1]), extra_w=[("Xp", q)])
                wa(lambda e, gp=gp: e.copy(fin[:, gp, 0:1], Gr[:, NC - 1:NC]), extra_w=["fin"])
                wa(lambda e, gp=gp: e.copy(fin[:, gp, 1:2], Gi[:, NC - 1:NC]), extra_w=["fin"])
            for cq in range(NC // 128):
                csl = slice(cq * 128, (cq + 1) * 128)
                for q in range(4):
                    gp = 4 * gq + q
                    rows = slice(32 * q, 32 * q + 32)
                    for j in range(8):
                        ops_ = [(MC[:, gp, j + 1, 0, :], Xp[q][0][:, csl], (0, 32 * q)),
                                (MC[:, gp, j + 1, 1, :], Xp[q][1][:, csl], (0, 32 * q))]
                        for i in range(j + 1):
                            ops_.append((Kl[rows, gq, j - i, :],
                                         uT[rows, gq, cq * 1024 + i:(cq + 1) * 1024:8], (32 * q, 32 * q)))
                        for n_, (l_, r_, tp_) in enumerate(ops_):
                            P.pe(lambda e, l_=l_, r_=r_, tp_=tp_, n_=n_, last=(n_ == len(ops_) - 1), rows=rows, j=j: e.matmul(
                                psy[rows, j * 128:(j + 1) * 128], l_, r_, start=(n_ == 0), stop=last, tile_position=tp_),
                                reads=[K_, "Kl", ("Xp", q)] + ukeys, writes=["psy"])
                ysb3 = ysb[:].rearrange("p (j c) -> p j c", j=8)
                uview = uT[:, gq, cq * 1024:(cq + 1) * 1024].rearrange("p (c j) -> p j c", j=8)
                P.act(lambda e: e.copy(ysb[:], psy[:]), reads=["psy"], writes=["ysb"])
                P.dve(lambda e, uview=uview, ysb3=ysb3, gq=gq: e.scalar_tensor_tensor(
                    ysb3, uview, dcol[:, gq:gq + 1], ysb3, ALU.mult, ALU.add), reads=["ysb", K_] + ukeys, writes=["ysb"])
                P.dve(lambda e: e.tensor_tensor(yt1[:], ysb[:], ysb[:], ALU.mult), reads=["ysb"], writes=["yt1"])
                P.dve(lambda e: e.tensor_scalar(yt1[:], yt1[:], 0.044715, 1.0, ALU.mult, ALU.add),
                      reads=["yt1"], writes=["yt1"])
                P.dve(lambda e: e.tensor_tensor(yt1[:], yt1[:], ysb[:], ALU.mult), reads=["yt1", "ysb"], writes=["yt1"])
                P.act(lambda e: e.activation(yt1[:], yt1[:], AF.Sigmoid, scale=2.0 * math.sqrt(2.0 / math.pi)),
                      reads=["yt1"], writes=["yt1"])
                P.dve(lambda e, ysb3=ysb3: e.tensor_tensor(
                    ygb[:].rearrange("p (c j) -> p j c", j=8), ysb3, yt1[:].rearrange("p (j c) -> p j c", j=8), ALU.mult),
                    reads=["yt1", "ysb"], writes=["ygb"])
                P.dma(yg_d[gq, :, cq * 1024:(cq + 1) * 1024], ygb[:], reads=["ygb"], slot="ygb")
        TK = 4 * NS
        h0sb = sb("h0sb", [NS, 2, NG * SP])
        P.dma(h0sb[:, 0, :], st_re[:, :], writes=["h0sb"])
        P.dma(h0sb[:, 1, :], st_im[:, :], writes=["h0sb"])
        h0f = sb("h0f", [128, 2, GP, NS])
        h0b = sb("h0b", [128, 2, GP, NS], BF16)
        for ri in range(2):
            pwt = pw[ri]
            for gp in range(GP):
                P.pe(lambda e, pwt=pwt, gp=gp, ri=ri: e.transpose(
                    pwt[:, gp * NS:(gp + 1) * NS], h0sb[:, ri, gp * 128:(gp + 1) * 128], ident_f[0:NS, 0:NS]),
                    reads=["h0sb", "ident_f"], writes=[f"pw{ri}"])
            P.act(lambda e, pwt=pwt, ri=ri: e.copy(h0f[:, ri, :, :].rearrange("p g s -> p (g s)"), pwt[:, 0:GP * NS]),
                  reads=[f"pw{ri}"], writes=["h0f"])
        P.dve(lambda e: e.tensor_copy(h0b[:].rearrange("p r g s -> p (r g s)"), h0f[:].rearrange("p r g s -> p (r g s)")),
              reads=["h0f"], writes=["h0b"])
        for gq in range(4):
            for q in range(4):
                gp = 4 * gq + q
                rows = slice(32 * q, 32 * q + 32)
                for ri in range(2):
                    for i in range(4):
                        P.pe(lambda e, ri=ri, i=i, rows=rows, gq=gq, q=q, gp=gp: e.matmul(
                            pss[ri][:, gp * NS:(gp + 1) * NS], Wt[rows, gq, i + 4, ri, :], uT_s[rows, gq, i:TK:4],
                            start=(i == 0), stop=(i == 3), tile_position=(32 * q, 0)),
                            reads=["Wt", "uT_s"], writes=[f"pss{ri}"])
        sr_s = sb("sr_s", [128, GP, NS])
        si_s = sb("si_s", [128, GP, NS])
        tA = sb("tA_s", [128, GP, NS])
        tB = sb("tB_s", [128, GP, NS])
        finr = sb("finr", [128, GP, NS])
        fini = sb("fini", [128, GP, NS])
        P.act(lambda e: e.copy(sr_s[:].rearrange("p g s -> p (g s)"), pss[0][:, 0:GP * NS]), reads=["pss0"], writes=["sr_s"])
        P.act(lambda e: e.copy(si_s[:].rearrange("p g s -> p (g s)"), pss[1][:, 0:GP * NS]), reads=["pss1"], writes=["si_s"])
        A4r = AKr[:, :, 4].unsqueeze(2).broadcast_to([128, GP, NS])
        A4i = AKi[:, :, 4].unsqueeze(2).broadcast_to([128, GP, NS])
        S_ = "ssms"

        def sv(fn, extra_r=(), extra_w=()):
            return P.dve(fn, reads=[S_, K_] + list(extra_r), writes=[S_] + list(extra_w))

        sv(lambda e: e.tensor_tensor(tA[:], h0f[:, 0, :, :], A4r, ALU.mult), extra_r=["h0f"])
        sv(lambda e: e.tensor_tensor(tB[:], h0f[:, 1, :, :], A4i, ALU.mult))
        sv(lambda e: e.tensor_tensor(tA[:], tA[:], tB[:], ALU.subtract))
        sv(lambda e: e.tensor_tensor(finr[:], tA[:], sr_s[:], ALU.add), extra_r=["sr_s"])
        sv(lambda e: e.tensor_tensor(tA[:], h0f[:, 1, :, :], A4r, ALU.mult))
        sv(lambda e: e.tensor_tensor(tB[:], h0f[:, 0, :, :], A4i, ALU.mult))
        sv(lambda e: e.tensor_tensor(tA[:], tA[:], tB[:], ALU.add))
        sv(lambda e: e.tensor_tensor(fini[:], tA[:], si_s[:], ALU.add), extra_r=["si_s"])
        fin_tm = sb("fin_tm", [NS, 2, NG * SP])
        for ri, fsrc in enumerate((finr, fini)):
            for g4 in range(4):
                pwt = pw[(ri * 4 + g4) % 2]
                pwk = f"pw{(ri * 4 + g4) % 2}"
                for gg in range(4):
                    gp = 4 * g4 + gg
                    P.pe(lambda e, pwt=pwt, gg=gg, gp=gp, fsrc=fsrc: e.transpose(
                        pwt[0:NS, gg * 128:(gg + 1) * 128], fsrc[:, gp, :], ident_f[:]),
                        reads=[S_, "ident_f"], writes=[pwk])
                P.act(lambda e, pwt=pwt, ri=ri, g4=g4: e.copy(fin_tm[:, ri, g4 * 512:(g4 + 1) * 512], pwt[0:NS, :]),
                      reads=[pwk], writes=[("fin_tm", ri, g4)])
        P.dma(ssm_re_s[:, :], fin_tm[:, 0, :], reads=[("fin_tm", 0, g4) for g4 in range(4)])
        P.dma(ssm_im_s[:, :], fin_tm[:, 1, :], reads=[("fin_tm", 1, g4) for g4 in range(4)])
        for gq in range(4):
            for q in range(4):
                gp = 4 * gq + q
                rows = slice(32 * q, 32 * q + 32)
                for j in range(4):
                    col = (gq * 4 + j) * NS
                    ops_ = [(MC[:, gp, j + 1, 0, :], h0b[:, 0, gp, :], (0, 32 * q)),
                            (MC[:, gp, j + 1, 1, :], h0b[:, 1, gp, :], (0, 32 * q))]
                    for i in range(j + 1):
                        ops_.append((Kl[rows, gq, j - i, :], uT_s[rows, gq, i:TK:4], (32 * q, 32 * q)))
                    for n_, (l_, r_, tp_) in enumerate(ops_):
                        P.pe(lambda e, l_=l_, r_=r_, tp_=tp_, n_=n_, last=(n_ == len(ops_) - 1), rows=rows, col=col: e.matmul(
                            psy[rows, col:col + NS], l_, r_, start=(n_ == 0), stop=last, tile_position=tp_),
                            reads=[K_, "Kl", "h0b", "uT_s"], writes=["psy"])
        ys_s = sb("ys_s", [128, 4, 4, NS])
        yt_s = sb("yt_s", [128, 4, 4, NS])
        ygs = sb("ygs", [128, 4, 128], BF16)
        P.pool(lambda e: e.memset(ygs[:], 0.0), writes=["ygs"])
        ysf = ys_s[:].rearrange("p g j s -> p (g j s)")
        ytf = yt_s[:].rearrange("p g j s -> p (g j s)")
        P.act(lambda e: e.copy(ysf, psy[:, 0:16 * NS]), reads=["psy"], writes=["ys_s"])
        for gq in range(4):
            P.dve(lambda e, gq=gq: e.scalar_tensor_tensor(
                ys_s[:, gq, :, :], uT_s[:, gq, 0:TK].rearrange("p (s j) -> p j s", j=4), dcol[:, gq:gq + 1],
                ys_s[:, gq, :, :], ALU.mult, ALU.add), reads=["ys_s", K_, "uT_s"], writes=["ys_s"])
        P.dve(lambda e: e.tensor_tensor(ytf, ysf, ysf, ALU.mult), reads=["ys_s"], writes=["yt_s"])
        P.dve(lambda e: e.tensor_scalar(ytf, ytf, 0.044715, 1.0, ALU.mult, ALU.add), reads=["yt_s"], writes=["yt_s"])
        P.dve(lambda e: e.tensor_tensor(ytf, ytf, ysf, ALU.mult), reads=["yt_s", "ys_s"], writes=["yt_s"])
        P.act(lambda e: e.activation(ytf, ytf, AF.Sigmoid, scale=2.0 * math.sqrt(2.0 / math.pi)),
              reads=["yt_s"], writes=["yt_s"])
        for gq in range(4):
            P.dve(lambda e, gq=gq: e.tensor_tensor(
                ygs[:, gq, 0:TK].rearrange("p (s j) -> p j s", j=4), ys_s[:, gq, :, :], yt_s[:, gq, :, :], ALU.mult),
                reads=["yt_s", "ys_s", "ygs"], writes=["ygs"])
        P.dma(yg_d_s.rearrange("g p t -> p g t"), ygs[:], reads=["ygs"])
        P.dma(ssm_re_p.rearrange("(gp g2) p -> (g2 p) gp", g2=2), fin[:, :, 0], reads=["fin"],
              allow_slow_non_contiguous=True)
        P.dma(ssm_im_p.rearrange("(gp g2) p -> (g2 p) gp", g2=2), fin[:, :, 1], reads=["fin"],
              allow_slow_non_contiguous=True)
        phase_end(s2)

        phase_end(s_u)
        s5 = ExitStack()
        cur[0] = s5
        NPG = 16
        NPAGES = NS * NPG
        TK = 4 * NS
        pKT = [ps(f"pKT{i}", [128, 512]) for i in range(2)]
        pS5 = [ps(f"pS5_{i}", [128, 512]) for i in range(2)]
        po5 = ps("po5", [128, 512])
        pon5 = ps("pon5", [128, 512])
        pb5 = ps("pb5", [128, 512])
        pt_bc = sb("pt_bc", [128, NPAGES], I32)
        P.dma(pt_bc[:], pt.partition_broadcast(128), writes=["pt_bc"])
        iota_p = sb("iota_p", [128, 1])
        P.pool(lambda e: e.iota(iota_p[:], [[0, 1]], base=0, channel_multiplier=1,
                                allow_small_or_imprecise_dtypes=True), writes=["iota_p"])
        idx_f = sb("idx_f", [128, NPAGES])
        idx_i = sb("idx_i", [128, NPAGES], I32)
        P.dve(lambda e: e.tensor_copy(idx_f[:], pt_bc[:]), reads=["pt_bc"], writes=["idx_f"])
        P.dve(lambda e: e.tensor_scalar(idx_f[:], idx_f[:], 128.0, iota_p[:, 0:1], ALU.mult, ALU.add),
              reads=["idx_f", "iota_p"], writes=["idx_f"])
        P.dve(lambda e: e.tensor_copy(idx_i[:], idx_f[:]), reads=["idx_f"], writes=["idx_i"])

        def gather(out_ap, src, c, writes, reads=(), sem="pooldma"):
            return P.op(sem, lambda e: e.indirect_dma_start(
                out=out_ap, out_offset=None, in_=src[:, :],
                in_offset=bass.IndirectOffsetOnAxis(ap=idx_i[:, c:c + 1], axis=0)),
                reads=["idx_i"] + list(reads), writes=writes, inc=16)

        pf = sb("pf", [128, NPAGES, NH])
        P.add_sem("pfg", "gpsimd")
        for i in range(6):
            P.add_sem(f"kg{i}", "gpsimd")
            P.add_sem(f"vg{i}", "gpsimd")
        for c in range(NPAGES):
            gather(pf[:, c, :], clf, c, [("pf", c)], sem="pfg")
        Ls = sb("Ls", [128, 128])
        P.pool(lambda e: e.memset(Ls[:], 1.0), writes=["Ls"])
        P.pool(lambda e: e.affine_select(Ls[:], Ls[:], [[-1, 128]], ALU.is_ge, 0.0,
                                         base=-1, channel_multiplier=1), reads=["Ls"], writes=["Ls"])
        BT = sb("BT", [128, 128])
        BT3 = BT[:].rearrange("p (s j) -> p s j", j=4)
        P.pool(lambda e: e.memset(BT[:], 1.0), writes=["BT"])
        P.pool(lambda e: e.affine_select(BT3, BT3, [[4, 32], [1, 4]], ALU.is_ge, 0.0,
                                         base=0, channel_multiplier=-1), reads=["BT"], writes=["BT"])
        P.pool(lambda e: e.affine_select(BT3, BT3, [[-4, 32], [0, 4]], ALU.is_ge, 0.0,
                                         base=0, channel_multiplier=1), reads=["BT"], writes=["BT"])
        maskN = sb("maskN", [128, 128])
        P.pool(lambda e: e.tensor_scalar(maskN[:], BT[:], -1.0, -NEG, ALU.add, ALU.mult), reads=["BT"], writes=["maskN"])
        qbd = sb("qbd", [128, 4, NS, 2, 4], BF16)
        P.pool(lambda e: e.memset(qbd[:], 0.0), writes=["qbd"])
        for h2 in range(2):
            pr = slice(64 * h2, 64 * h2 + 64)
            for hp in range(4):
                P.pool(lambda e, pr=pr, h2=h2, hp=hp: e.tensor_copy(
                    qbd[pr, hp, :, h2, :], qT_s[pr, hp, 0:TK].rearrange("p (s q) -> p s q", q=4)),
                    reads=["qT_s", "qbd"], writes=["qbd"])
        Dsb = sb("Dsb", [128, NPAGES, NH])
        tot5 = sb("tot5", [128, NPAGES, NH])
        pf_flat = pf[:].rearrange("p c h -> p (c h)")
        D_flat = Dsb[:].rearrange("p c h -> p (c h)")
        tot_flat = tot5[:].rearrange("p c h -> p (c h)")
        ncols = NPAGES * NH
        for c0 in range(0, ncols, 512):
            cw = min(512, ncols - c0)
            keys = [("pf", c) for c in range(NPAGES)]
            P.pe(lambda e, c0=c0, cw=cw: e.matmul(pS5[0][:, 0:cw], Ls[:], pf_flat[:, c0:c0 + cw], start=True, stop=True),
                 reads=keys + ["Ls"], writes=["pS5_0"])
            P.act(lambda e, c0=c0, cw=cw: e.copy(D_flat[:, c0:c0 + cw], pS5[0][:, 0:cw]), reads=["pS5_0"], writes=["Dsb"])
            P.pe(lambda e, c0=c0, cw=cw: e.matmul(pS5[1][:, 0:cw], ones_f[:], pf_flat[:, c0:c0 + cw], start=True, stop=True),
                 reads=keys + ["ones_f"], writes=["pS5_1"])
            P.act(lambda e, c0=c0, cw=cw: e.copy(tot_flat[:, c0:c0 + cw], pS5[1][:, 0:cw]), reads=["pS5_1"], writes=["tot5"])
        D4 = Dsb[:].rearrange("p (s g) h -> p s g h", g=NPG)
        T4 = tot5[:].rearrange("p (s g) h -> p s g h", g=NPG)
        lat = sb("lat", [128, NS, NH])
        P.dve(lambda e: e.memset(lat[:], 0.0), writes=["lat"])
        for pg in range(NPG - 2, -1, -1):
            P.dve(lambda e, pg=pg: e.tensor_tensor(lat[:], lat[:], T4[:, :, pg + 1, :], ALU.add),
                  reads=["lat", "tot5"], writes=["lat"])
            P.dve(lambda e, pg=pg: e.tensor_tensor(D4[:, :, pg, :], D4[:, :, pg, :], lat[:], ALU.add),
                  reads=["lat", "Dsb"], writes=["Dsb"])
        P.pe(lambda e: e.matmul(pb5[:, 0:NH], BT[:], lf_s[:], start=True, stop=True),
             reads=["BT", "lf_s"], writes=["pb5"])
        negnq = sb("negnq", [128, NH])
        P.act(lambda e: e.activation(negnq[:], pb5[:, 0:NH], AF.Copy, scale=-1.0), reads=["pb5"], writes=["negnq"])
        tmpN = sb("tmpN", [128, 4, NS, 2, 4])
        PTn = sb("PTn", [128, NH, NS, 4], BF16)
        for hp in range(4):
            P.pe(lambda e, hp=hp: e.matmul(pon5[:, hp * 8 * NS:(hp + 1) * 8 * NS], kT_s[:, hp, :],
                                           qbd[:, hp, :, :, :].rearrange("p s a q -> p (s a q)"), start=True, stop=True),
                 reads=["kT_s", "qbd"], writes=["pon5"])
        for hp in range(4):
            P.dve(lambda e, hp=hp: e.tensor_tensor(
                tmpN[:, hp, :, :, :], pon5[:, hp * 8 * NS:(hp + 1) * 8 * NS].rearrange("p (s a q) -> p s a q", a=2, q=4),
                maskN[:, 0:TK].rearrange("p (s q) -> p s q", q=4).unsqueeze(2).broadcast_to([128, NS, 2, 4]), ALU.add),
                reads=["pon5", "maskN"], writes=["tmpN"])
            P.dve(lambda e, hp=hp: e.tensor_tensor(
                tmpN[:, hp, :, :, :], tmpN[:, hp, :, :, :],
                negnq[:, 2 * hp:2 * hp + 2].unsqueeze(1).unsqueeze(3).broadcast_to([128, NS, 2, 4]), ALU.add),
                reads=["tmpN", "negnq"], writes=["tmpN"])
            P.act(lambda e, hp=hp: e.activation(
                PTn[:, 2 * hp:2 * hp + 2, :, :].rearrange("p a s q -> p s a q"), tmpN[:, hp, :, :, :], AF.Exp),
                reads=["tmpN"], writes=["PTn"])
        for h in range(NH):
            P.pe(lambda e, h=h: e.matmul(pon5[0:HD + 1, h * TK:(h + 1) * TK], Vb_s[:, h, 0:HD + 1],
                                         PTn[:, h, :, :].rearrange("p s q -> p (s q)"), start=True, stop=True),
                 reads=["Vb_s", "PTn", "tmpN"], writes=["pon5"])
        osn = sb("osn", [HD + 1, NH * TK])
        P.act(lambda e: e.copy(osn[:], pon5[0:HD + 1, 0:NH * TK]), reads=["pon5"], writes=["osn"])
        NK = 6
        kst = [sb(f"kst{i}", [128, AW]) for i in range(NK)]
        vst = [sb(f"vst{i}", [128, AW]) for i in range(NK)]
        KTb = [sb(f"KTb{i}", [128, 4, 128], BF16) for i in range(2)]
        Vp = [sb(f"Vp{i}", [128, NPG, NH, HD + 2], BF16) for i in range(2)]
        PT5 = [sb(f"PT5_{i}", [128, NPG, NH, 4], BF16) for i in range(2)]
        tmpS = sb("tmpS", [128, NPG, NH, 4])
        for i in range(2):
            P.pool(lambda e, i=i: e.memset(Vp[i][:], 1.0), writes=[f"Vp{i}"])

        def pv(s_):
            i2 = s_ % 2
            for h in range(NH):
                for pg in range(NPG):
                    P.pe(lambda e, h=h, pg=pg, i2=i2, s_=s_: e.matmul(
                        po5[0:HD + 1, h * TK + s_ * 4:h * TK + s_ * 4 + 4], Vp[i2][:, pg, h, 0:HD + 1], PT5[i2][:, pg, h, :],
                        start=(pg == 0), stop=(pg == NPG - 1)), reads=[f"Vp{i2}", f"PT5_{i2}"], writes=["po5"])

        for s_ in range(NS + 1):
            if s_ < NS:
                i2 = s_ % 2
                for pg in range(NPG):
                    c = s_ * NPG + pg
                    ik = c % NK
                    gather(kst[ik][:], ck, c, [f"kst{ik}"], sem=f"kg{ik}")
                    gather(vst[ik][:], cv, c, [f"vst{ik}"], sem=f"vg{ik}")
                    pk_ = pKT[c % 2]
                    pkk = f"pKT{c % 2}"
                    kb_ = KTb[c % 2]
                    kbk = f"KTb{c % 2}"
                    for hp in range(4):
                        P.pe(lambda e, pk_=pk_, hp=hp, ik=ik: e.transpose(
                            pk_[:, hp * 128:(hp + 1) * 128], kst[ik][:, hp * 128:(hp + 1) * 128], ident_f[:]),
                            reads=[f"kst{ik}", "ident_f"], writes=[pkk])
                    P.act(lambda e, pk_=pk_, kb_=kb_: e.copy(kb_[:].rearrange("p c k -> p (c k)"), pk_[:]),
                          reads=[pkk], writes=[kbk])
                    for hp in range(4):
                        P.pe(lambda e, kb_=kb_, hp=hp, pg=pg, i2=i2, s_=s_: e.matmul(
                            pS5[i2][:, pg * 32 + hp * 8:pg * 32 + hp * 8 + 8], kb_[:, hp, :],
                            qbd[:, hp, s_, :, :].rearrange("p a q -> p (a q)"), start=True, stop=True),
                            reads=[kbk, "qbd"], writes=[f"pS5_{i2}"])
                    P.dve(lambda e, ik=ik, pg=pg, i2=i2: e.tensor_copy(
                        Vp[i2][:, pg, :, 0:HD], vst[ik][:].rearrange("p (h d) -> p h d", h=NH)),
                        reads=[f"vst{ik}"], writes=[f"Vp{i2}"])
                P.dve(lambda e, i2=i2, s_=s_: e.tensor_tensor(
                    tmpS[:], pS5[i2][:].rearrange("p (g h q) -> p g h q", h=NH, q=4),
                    Dsb[:, s_ * NPG:(s_ + 1) * NPG, :].unsqueeze(3).broadcast_to([128, NPG, NH, 4]), ALU.add),
                    reads=[f"pS5_{i2}", "Dsb"], writes=["tmpS"])
                P.act(lambda e, i2=i2: e.activation(PT5[i2][:].rearrange("p g h q -> p (g h q)"),
                                                    tmpS[:].rearrange("p g h q -> p (g h q)"), AF.Exp),
                      reads=["tmpS"], writes=[f"PT5_{i2}"])
            if s_ >= 1:
                pv(s_ - 1)
        osb5 = sb("osb5", [HD + 1, NH * TK])
        rbc5 = sb("rbc5", [HD, NH * TK])
        attT_s = sb("attT_s", [HD, NH, 128], BF16)
        P.pool(lambda e: e.memset(attT_s[:], 0.0), writes=["attT_s"])
        P.act(lambda e: e.copy(osb5[:], po5[0:HD + 1, 0:NH * TK]), reads=["po5"], writes=["osb5"])
        P.dve(lambda e: e.tensor_tensor(osb5[:], osb5[:], osn[:], ALU.add), reads=["osb5", "osn"], writes=["osb5"])
        P.dve(lambda e: e.reciprocal(osb5[HD:HD + 1, :], osb5[HD:HD + 1, :]), reads=["osb5"], writes=["osb5"])
        P.pe(lambda e: e.matmul(pb5[0:HD, 0:NH * TK], ones_f[HD:HD + 1, 0:HD], osb5[HD:HD + 1, :], start=True, stop=True),
             reads=["osb5", "ones_f"], writes=["pb5"])
        P.act(lambda e: e.copy(rbc5[:], pb5[0:HD, 0:NH * TK]), reads=["pb5"], writes=["rbc5"])
        P.dve(lambda e: e.tensor_tensor(attT_s[:, :, 0:TK], osb5[0:HD, :].rearrange("p (h t) -> p h t", h=NH),
                                        rbc5[:].rearrange("p (h t) -> p h t", h=NH), ALU.mult),
              reads=["osb5", "rbc5", "attT_s"], writes=["attT_s"])
        P.dma(att_d_s[:, :, :], attT_s[:], reads=["attT_s"], writes=[("att_d", "s")])
        phase_end(s5)
        s4 = ExitStack()
        cur[0] = s4
        pst = [ps(f"pst{i}_4", [128, 512]) for i in range(2)]
        pmm = [ps(f"pmm{i}_4", [128, 512]) for i in range(4)]
        pS = [ps(f"pS{i}_4", [128, 512]) for i in range(2)]
        wglu = sb("wglu", [128, 4, SW], BF16)
        wout_a = sb("wout_a", [HD, NH, D], BF16)
        wout_s = sb("wout_s", [128, 4, D], BF16)
        wpg = sb("wpg", [128, 8, D], BF16)
        wpe = sb("wpe", [128, 2, D], BF16)
        P.dma(wglu[:], wglu_d.rearrange("(c p) n -> p c n", p=128), reads=[("wsc", "glu")], writes=["wglu"])
        P.dma(wout_a[:], wout_d[0:AW, :].rearrange("(h p) n -> p h n", p=HD), reads=[("wsc", "out")], writes=["wout_a"])
        P.dma(wout_s[:], wout_d[AW:D, :].rearrange("(c p) n -> p c n", p=128), reads=[("wsc", "out")], writes=["wout_s"])
        P.dma(wpg[:], wpg_d.rearrange("(c p) n -> p c n", p=128), reads=[("wsc", "pg")], writes=["wpg"])
        P.dma(wpe[:], wpe_d.rearrange("(c p) n -> p c n", p=128), reads=[("wsc", "pe")], writes=["wpe"])
        bcs = {}
        for nm, src in (("g0", ln_in_g), ("b0", ln_in_b), ("g1", ln1_g), ("b1", ln1_b), ("g2", ln2_g), ("b2", ln2_b)):
            t_ = sb("bc_" + nm, [128, D])
            P.dma(t_[:], src.partition_broadcast(128), writes=["bc_" + nm], q="gpsimd")
            bcs[nm] = t_
        bpg_row = sb("bpg_row", [1, D])
        P.dma(bpg_row[:], b_pg.rearrange("(o n) -> o n", o=1), writes=["bpg_row"])
        bglu_col = sb("bglu_col", [128, 4])
        P.dma(bglu_col[:], b_glu.rearrange("(c p) -> p c", p=128), writes=["bglu_col"], allow_slow_non_contiguous=True)

        attb = sb("attb", [HD, NH, 512], BF16)
        ygb4 = sb("ygb4", [128, 4, 512], BF16)
        gate_b = sb("gate_b", [128, 512], BF16)
        ssmT = sb("ssmT", [128, 4, 512], BF16)
        xt4 = sb("xt4", [128, 4, D])
        o2 = sb("o2", [128, 4, D])
        pt4 = sb("pt4", [128, 4, PLE])
        pT = sb("pT4", [128, 2, 512], BF16)
        h1T = sb("h1T", [128, 8, 512], BF16)
        actT = sb("actT", [128, 32, 512], BF16)
        rl = [sb(f"rl{i}", [128, 512], BF16) for i in range(2)]
        tmp1 = [sb(f"tmp1_{i}", [128, 512]) for i in range(2)]
        tmp2 = [sb(f"tmp2_{i}", [128, 512]) for i in range(2)]
        wupt = [sb(f"wupt{i}", [128, 8, 128], BF16) for i in range(3)]
        wdnt = [sb(f"wdnt{i}", [128, 512], BF16) for i in range(4)]
        stats4 = sb("stats4", [128, 2, 6])
        mv4 = sb("mv4", [128, 2])
        rstd4 = sb("rstd4", [128, 1])
        nmr4 = sb("nmr4", [128, 1])
        cnt4 = {"wup": 0, "wdn": 0, "t1": 0, "t2": 0, "rl": 0}

        def ln_tile(buf, key, tt, gname, bname):
            for c in range(2):
                P.dve(lambda e, c=c: e.bn_stats(stats4[:, c, :], buf[:, tt, c * 512:(c + 1) * 512]),
                      reads=[(key, tt)], writes=[("stats4", c)])
            P.dve(lambda e: e.bn_aggr(mv4[:], stats4[:].rearrange("p a b -> p (a b)")),
                  reads=[("stats4", 0), ("stats4", 1)], writes=["mv4"])
            P.dve(lambda e: e.tensor_scalar(rstd4[:], mv4[:, 1:2], EPS, None, ALU.add), reads=["mv4"], writes=["rstd4"])
            P.act(lambda e: e.activation(rstd4[:], rstd4[:], AF.Sqrt), reads=["rstd4"], writes=["rstd4"])
            P.dve(lambda e: e.reciprocal(rstd4[:], rstd4[:]), reads=["rstd4"], writes=["rstd4"])
            P.dve(lambda e: e.scalar_tensor_tensor(nmr4[:], mv4[:, 0:1], -1.0, rstd4[:], ALU.mult, ALU.mult),
                  reads=["mv4", "rstd4"], writes=["nmr4"])
            P.act(lambda e: e.activation(buf[:, tt, :], buf[:, tt, :], AF.Identity, bias=nmr4[:, 0:1], scale=rstd4[:, 0:1]),
                  reads=["nmr4", "rstd4", (key, tt)], writes=[(key, tt)])
            P.dve(lambda e: e.tensor_tensor(buf[:, tt, :], buf[:, tt, :], bcs[gname][:], ALU.mult),
                  reads=[(key, tt), "bc_" + gname], writes=[(key, tt)])
            P.dve(lambda e: e.tensor_tensor(buf[:, tt, :], buf[:, tt, :], bcs[bname][:], ALU.add),
                  reads=[(key, tt), "bc_" + bname], writes=[(key, tt)])

        def proj_tm(ntt, ktiles, consumer, extra=None):
            for nh in range(2):
                for ki, (lf, rf, lkeys) in enumerate(ktiles):
                    rap, rkeys = rf(nh)
                    for tt in range(ntt):
                        P.pe(lambda e, tt=tt, lf=lf, rap=rap, ki=ki: e.matmul(
                            pmm[tt][:], lf(tt), rap, start=(ki == 0), stop=(ki == len(ktiles) - 1 and extra is None)),
                            reads=list(lkeys) + list(rkeys), writes=[f"pmm{tt}"])
                if extra is not None:
                    for tt in range(ntt):
                        extra(tt, nh)
                for tt in range(ntt):
                    consumer(tt, nh, pmm[tt], f"pmm{tt}")

        def next_buf(name, bufs):
            i = cnt4[name] % len(bufs)
            cnt4[name] += 1
            return bufs[i], f"{name}{i}"

        def post_block(I, ntt, x_src, p_src, att_src, yg_src, y_dst):
            tsl = lambda tt: slice(tt * 128, (tt + 1) * 128)
            ncol = ntt * 128
            P.dma(attb[:, :, 0:ncol], att_src, reads=[("att_d", I)], writes=["attb"], slot="attb")
            P.dma(ygb4[:, :, 0:ncol], yg_src, reads=["yg_d"], writes=["ygb4"], slot="ygb4")
            P.dma(xt4[:, 0:ntt, :], x_src, writes=[("xt4", tt) for tt in range(ntt)], slot="xt4")
            P.dma(pt4[:, 0:ntt, :], p_src, writes=["pt4"], q="gpsimd", slot="pt4")
            for nt in range(4):
                psg = pS[nt % 2]
                pgk = f"pS{nt % 2}"
                for ct in range(4):
                    P.pe(lambda e, psg=psg, nt=nt, ct=ct: e.matmul(psg[:, 0:ncol], wglu[:, ct, nt * 128:(nt + 1) * 128], ygb4[:, ct, 0:ncol],
                                                                  start=(ct == 0), stop=(ct == 3)),
                         reads=["wglu", "ygb4"], writes=[pgk])
                P.act(lambda e, psg=psg, nt=nt: e.activation(gate_b[:, 0:ncol], psg[:, 0:ncol], AF.Sigmoid, bias=bglu_col[:, nt:nt + 1]),
                      reads=[pgk, "bglu_col"], writes=["gate_b"])
                P.dve(lambda e, nt=nt: e.tensor_tensor(ssmT[:, nt, 0:ncol], ygb4[:, nt, 0:ncol], gate_b[:, 0:ncol], ALU.mult),
                      reads=["gate_b", "ygb4"], writes=[("ssmT", nt)])
            for tt in range(ntt):
                ln_tile(xt4, "xt4", tt, "g0", "b0")
            for kk in range(2):
                pp_ = pst[kk]
                for tt in range(ntt):
                    P.pe(lambda e, pp_=pp_, tt=tt, kk=kk: e.transpose(pp_[:, tsl(tt)], pt4[:, tt, kk * 128:(kk + 1) * 128], ident_f[:]),
                         reads=["pt4", "ident_f"], writes=[f"pst{kk}"])
                P.act(lambda e, pp_=pp_, kk=kk: e.copy(pT[:, kk, 0:ntt * 128], pp_[:, 0:ntt * 128]),
                      reads=[f"pst{kk}"], writes=[("pT", kk)])
            kt = []
            for h in range(NH):
                kt.append((lambda tt, h=h: attb[:, h, tsl(tt)], lambda nh, h=h: (wout_a[:, h, nh * 512:(nh + 1) * 512], ["wout_a"]), ["attb"]))
            for ct in range(4):
                kt.append((lambda tt, ct=ct: ssmT[:, ct, tsl(tt)], lambda nh, ct=ct: (wout_s[:, ct, nh * 512:(nh + 1) * 512], ["wout_s"]),
                           [("ssmT", ct)]))

            def cons_out(tt, nh, pm_, pk_):
                t1_, t1k = next_buf("t1", tmp1)
                P.act(lambda e: e.copy(t1_[:], pm_[:]), reads=[pk_], writes=[t1k])
                P.dve(lambda e: e.scalar_tensor_tensor(xt4[:, tt, nh * 512:(nh + 1) * 512], xt4[:, tt, nh * 512:(nh + 1) * 512],
                                                       ALPHA, t1_[:], ALU.mult, ALU.add),
                      reads=[t1k, ("xt4", tt)], writes=[("xt4", tt)])
            proj_tm(ntt, kt, cons_out)
            for tt in range(ntt):
                ln_tile(xt4, "xt4", tt, "g1", "b1")
            for ct in range(8):
                pp_ = pst[ct % 2]
                for tt in range(ntt):
                    P.pe(lambda e, pp_=pp_, tt=tt, ct=ct: e.transpose(pp_[:, tsl(tt)], xt4[:, tt, ct * 128:(ct + 1) * 128], ident_f[:]),
                         reads=[("xt4", tt), "ident_f"], writes=[f"pst{ct % 2}"])
                P.act(lambda e, pp_=pp_, ct=ct: e.copy(h1T[:, ct, 0:ntt * 128], pp_[:, 0:ntt * 128]),
                      reads=[f"pst{ct % 2}"], writes=[("h1T", ct)])
            h1keys = [("h1T", ct) for ct in range(8)]
            for ft in range(32):
                wt_, wk_ = next_buf("wup", wupt)
                P.dma(wt_[:], wup_d[:, ft * 128:(ft + 1) * 128].rearrange("(c p) f -> p c f", p=128),
                      reads=[("wsc", "up")], writes=[wk_], q=("sync" if int(wk_[-1]) % 2 == 0 else "gpsimd"), slot=wk_)
                psu = pS[ft % 2]
                puk = f"pS{ft % 2}"
                for ct in range(8):
                    P.pe(lambda e, psu=psu, wt_=wt_, ct=ct: e.matmul(psu[:, 0:ncol], wt_[:, ct, :], h1T[:, ct, 0:ncol],
                                                                    start=(ct == 0), stop=(ct == 7)),
                         reads=[wk_] + h1keys, writes=[puk])
                r_, rk_ = next_buf("rl", rl)
                P.act(lambda e, psu=psu, r_=r_: e.activation(r_[:, 0:ncol], psu[:, 0:ncol], AF.Relu), reads=[puk], writes=[rk_])
                P.pool(lambda e, r_=r_, ft=ft: e.tensor_tensor(actT[:, ft, 0:ncol], r_[:, 0:ncol], r_[:, 0:ncol], ALU.mult),
                       reads=[rk_], writes=[("actT", ft)])
            kt = []
            for ft in range(32):
                def rf(nh, ft=ft):
                    wt_, wk_ = next_buf("wdn", wdnt)
                    P.dma(wt_[:], wdn_d[ft * 128:(ft + 1) * 128, nh * 512:(nh + 1) * 512], reads=[("wsc", "dn")], writes=[wk_],
                          q=("sync" if int(wk_[-1]) % 2 == 0 else "gpsimd"), slot=wk_)
                    return wt_[:], [wk_]
                kt.append((lambda tt, ft=ft: actT[:, ft, tsl(tt)], rf, [("actT", ft)]))

            def cons_dn(tt, nh, pm_, pk_):
                t1_, t1k = next_buf("t1", tmp1)
                P.act(lambda e: e.copy(t1_[:], pm_[:]), reads=[pk_], writes=[t1k])
                P.dve(lambda e: e.scalar_tensor_tensor(o2[:, tt, nh * 512:(nh + 1) * 512], xt4[:, tt, nh * 512:(nh + 1) * 512],
                                                       ALPHA, t1_[:], ALU.mult, ALU.add),
                      reads=[t1k, ("xt4", tt)], writes=[("o2", tt)])
            proj_tm(ntt, kt, cons_dn)
            ktg = [(lambda tt, ct=ct: h1T[:, ct, tsl(tt)], lambda nh, ct=ct: (wpg[:, ct, nh * 512:(nh + 1) * 512], ["wpg"]), [("h1T", ct)])
                   for ct in range(8)]

            def bias_mm(tt, nh):
                P.pe(lambda e: e.matmul(pmm[tt][:], ones_f[0:1, :], bpg_row[0:1, nh * 512:(nh + 1) * 512], start=False, stop=True),
                     reads=["ones_f", "bpg_row"], writes=[f"pmm{tt}"])

            kte = [(lambda tt, kk=kk: pT[:, kk, tsl(tt)], lambda nh, kk=kk: (wpe[:, kk, nh * 512:(nh + 1) * 512], ["wpe"]), [("pT", kk)])
                   for kk in range(2)]
            pe_ps = [(pS[0], "pS0"), (pS[1], "pS1"), (pst[0], "pst0"), (pst[1], "pst1")]
            for nh in range(2):
                for ki, (lf, rf, lkeys) in enumerate(ktg):
                    rap, rkeys = rf(nh)
                    for tt in range(ntt):
                        P.pe(lambda e, tt=tt, lf=lf, rap=rap, ki=ki: e.matmul(pmm[tt][:], lf(tt), rap, start=(ki == 0), stop=False),
                             reads=list(lkeys) + list(rkeys), writes=[f"pmm{tt}"])
                for tt in range(ntt):
                    bias_mm(tt, nh)
                for ki, (lf, rf, lkeys) in enumerate(kte):
                    rap, rkeys = rf(nh)
                    for tt in range(ntt):
                        P.pe(lambda e, tt=tt, lf=lf, rap=rap, ki=ki: e.matmul(pe_ps[tt][0][:], lf(tt), rap, start=(ki == 0), stop=(ki == 1)),
                             reads=list(lkeys) + list(rkeys), writes=[pe_ps[tt][1]])
                for tt in range(ntt):
                    t1_, t1k = next_buf("t1", tmp1)
                    t2_, t2k = next_buf("t2", tmp2)
                    P.act(lambda e, t1_=t1_, tt=tt: e.activation(t1_[:], pmm[tt][:], AF.Sigmoid), reads=[f"pmm{tt}"], writes=[t1k])
                    P.act(lambda e, t2_=t2_, tt=tt: e.copy(t2_[:], pe_ps[tt][0][:]), reads=[pe_ps[tt][1]], writes=[t2k])
                    P.dve(lambda e, t1_=t1_, t2_=t2_: e.tensor_tensor(t2_[:], t2_[:], t1_[:], ALU.mult), reads=[t1k, t2k], writes=[t2k])
                    P.dve(lambda e, t2_=t2_, tt=tt, nh=nh: e.tensor_tensor(o2[:, tt, nh * 512:(nh + 1) * 512],
                                                                          o2[:, tt, nh * 512:(nh + 1) * 512], t2_[:], ALU.add),
                          reads=[t2k, ("o2", tt)], writes=[("o2", tt)])
            for tt in range(ntt):
                ln_tile(o2, "o2", tt, "g2", "b2")
            P.dma(y_dst, o2[:, 0:ntt, :], reads=[("o2", tt) for tt in range(ntt)], slot="o2")

        for I in range(NSB):
            rs = slice(I * 512, (I + 1) * 512)
            post_block(I, 4,
                       x_p[rs, :].rearrange("(t p) d -> p t d", p=128),
                       pp_p[rs, :].rearrange("(t p) d -> p t d", p=128),
                       att_d[:, :, rs], yg_d[:, :, rs].rearrange("g p t -> p g t"),
                       y_p[rs, :].rearrange("(t p) d -> p t d", p=128))

        post_block("s", 1,
                   x_s.rearrange("(t p) d -> p t d", p=128),
                   pp_s.rearrange("(t p) d -> p t d", p=128),
                   att_d_s[:, :, :], yg_d_s.rearrange("g p t -> p g t"),
                   y_s.rearrange("(t p) d -> p t d", p=128))

        P.barrier()
        P.emit()
        s4.close()
    return nc


_CACHE = {}
_LAST = {}


DBG = bool(int(os.environ.get('KDBG', '0')))


def _get_nc(T, NSEQ, NPHYS):
    key = (T, NSEQ, NPHYS)
    if key not in _CACHE:
        _CACHE[key] = build(T, NSEQ, NPHYS, dbg=DBG)
    return _CACHE[key]


def run_cores(inputs, T, NSEQ, NPHYS, ncores=8):
    nc = _get_nc(T, NSEQ, NPHYS)
    f = lambda a: np.ascontiguousarray(a, dtype=np.float32)
    in_maps = []
    nb = inputs["x_prompt"].shape[0]
    ST = NSEQ * 4
    ckf = f(inputs["cache_k"][0]).reshape(NPHYS * 128, AW)
    cvf = f(inputs["cache_v"][0]).reshape(NPHYS * 128, AW)
    clff = f(inputs["cache_logf"][0]).reshape(NPHYS * 128, NH)
    shared = {
        "w_in": f(inputs["w_in"][0]),
        "ln_in_g": f(inputs["ln_in_g"]),
        "ln_in_b": f(inputs["ln_in_b"]),
        "b_f": f(inputs["b_f"][0]),
        "lam_re": f(inputs["lam_re"][0]), "lam_im": f(inputs["lam_im"][0]), "log_dt": f(inputs["log_dt"][0]),
        "b_re": f(inputs["b_re"][0]), "b_im": f(inputs["b_im"][0]),
        "c_re": f(inputs["c_re"][0]), "c_im": f(inputs["c_im"][0]), "d_skip": f(inputs["d_skip"][0]),
        "w_glu": f(inputs["w_glu"][0]), "b_glu": f(inputs["b_glu"][0]), "w_out": f(inputs["w_out"][0]),
        "ln1_g": f(inputs["ln1_g"][0]), "ln1_b": f(inputs["ln1_b"][0]), "w_up": f(inputs["w_up"][0]),
        "w_down": f(inputs["w_down"][0]), "w_pe": f(inputs["w_pe"][0]), "w_pg": f(inputs["w_pg"][0]),
        "b_pg": f(inputs["b_pg"][0]), "ln2_g": f(inputs["ln2_g"][0]), "ln2_b": f(inputs["ln2_b"][0]),
        "ck": ckf, "cv": cvf, "clf": clff,
    }
    for c in range(ncores):
        b = c % nb
        sl = slice(c * NSEQ, (c + 1) * NSEQ)
        xs = np.zeros((128, D), np.float32)
        xs[0:ST] = inputs["x_sample"][sl].reshape(ST, D)
        ps_ = np.zeros((128, PLE), np.float32)
        ps_[0:ST] = inputs["p_sample"][0, sl].reshape(ST, PLE)
        m = dict(shared)
        m.update({
            "x_p": f(inputs["x_prompt"][b]),
            "pp_p": f(inputs["p_prompt"][0, b]),
            "x_s": xs, "pp_s": ps_,
            "st_re": f(inputs["state_re"][0, sl]).reshape(NSEQ, NG * SP),
            "st_im": f(inputs["state_im"][0, sl]).reshape(NSEQ, NG * SP),
            "pt": np.ascontiguousarray(inputs["page_table"][sl], dtype=np.int32).reshape(NSEQ * 16),
        })
        in_maps.append(m)
    res = run_bass_kernel_spmd(nc, in_maps, core_ids=list(range(ncores)))
    _LAST["r"] = res.results
    return res.results


def kernel(**inputs):
    T = inputs["x_prompt"].shape[1]
    NSEQ = inputs["x_sample"].shape[0] // 8
    NPHYS = inputs["cache_k"].shape[1]
    ST = NSEQ * 4
    r = run_cores(inputs, T, NSEQ, NPHYS)
    nb = inputs["x_prompt"].shape[0]
    db = inputs["x_sample"].shape[0]
    g = lambda c, n: np.asarray(r[c][n], dtype=np.float32)
    y_prompt = np.stack([g(b, "y_p") for b in range(nb)])
    k_prompt = np.stack([g(b, "k_p").reshape(T, NH, HD) for b in range(nb)])[None]
    v_prompt = np.stack([g(b, "v_p").reshape(T, NH, HD) for b in range(nb)])[None]
    lf_prompt = np.stack([g(b, "lf_p") for b in range(nb)])[None]
    sre_p = np.stack([g(b, "ssm_re_p") for b in range(nb)])[None]
    sim_p = np.stack([g(b, "ssm_im_p") for b in range(nb)])[None]
    y_sample = np.concatenate([g(c, "y_s")[0:ST].reshape(NSEQ, 4, D) for c in range(8)])
    k_sample = np.concatenate([g(c, "k_s").reshape(NSEQ, 4, NH, HD) for c in range(8)])[None]
    v_sample = np.concatenate([g(c, "v_s").reshape(NSEQ, 4, NH, HD) for c in range(8)])[None]
    lf_sample = np.concatenate([g(c, "lf_sd").reshape(NSEQ, 4, NH) for c in range(8)])[None]
    sre_s = np.concatenate([g(c, "ssm_re_s").reshape(NSEQ, NG, SP) for c in range(8)])[None]
    sim_s = np.concatenate([g(c, "ssm_im_s").reshape(NSEQ, NG, SP) for c in range(8)])[None]
    return (y_prompt, y_sample, k_prompt, v_prompt, lf_prompt, sre_p, sim_p,
            k_sample, v_sample, lf_sample, sre_s, sim_s)
```

```python
import math
import os
KSTAGE = int(os.environ.get('KSTAGE', '99'))
KSKIP = os.environ.get('KSKIP', '').split(',')
from contextlib import ExitStack
import numpy as np
import concourse.bass as bass
import concourse.mybir as mybir
from concourse.bass_utils import run_bass_kernel_spmd

F32 = mybir.dt.float32
BF16 = mybir.dt.bfloat16
I32 = mybir.dt.int32
U32 = mybir.dt.uint32
AF = mybir.ActivationFunctionType
ALU = mybir.AluOpType
AX = mybir.AxisListType

D = 1024
NH = 8
HD = 64
AW = 512
SW = 512
NG = 32
SG = 16
SP = 64
DFF = 4096
PLE = 256
INW = 2056
ALPHA = 2.0 ** 0.25
EPS = 1e-5
NEG = -60000.0


class Prog:
    STREAMS = ("sync", "scalar", "vector", "gpsimd", "tensor")

    def __init__(self, nc, es):
        self.nc = nc
        self.ops = {s: [] for s in self.STREAMS}
        self.sems = {}
        self.cnt = {}
        self.sem_stream = {}
        for name, stream in (("sync", "sync"), ("act", "scalar"), ("dve", "vector"), ("pool", "gpsimd"),
                             ("pooldma", "gpsimd"), ("pe", "tensor"), ("actdma", "scalar")):
            self.sems[name] = es.enter_context(nc.semaphore("s_" + name))
            self.cnt[name] = 0
            self.sem_stream[name] = stream
        self.waited = {}
        self.lastw = {}
        self.readers = {}
        self.es = es

    def add_sem(self, name, stream):
        self.sems[name] = self.es.enter_context(self.nc.semaphore("s_" + name))
        self.cnt[name] = 0
        self.sem_stream[name] = stream

    def op(self, sem, fn, reads=(), writes=(), inc=1):
        stream = self.sem_stream[sem]
        deps = set()
        for b in reads:
            if b in self.lastw:
                deps.add(self.lastw[b])
        for b in writes:
            if b in self.lastw:
                deps.add(self.lastw[b])
            for t in self.readers.get(b, ()):
                deps.add(t)
        waits = []
        for (ps, pc) in sorted(deps):
            if ps == "pe" and sem == "pe":
                continue
            key = (stream, ps)
            if self.waited.get(key, 0) >= pc:
                continue
            self.waited[key] = pc
            waits.append((ps, pc))
        self.cnt[sem] += inc
        tok = (sem, self.cnt[sem])
        self.ops[stream].append((waits, fn, sem, inc))
        for b in writes:
            self.lastw[b] = tok
            self.readers[b] = []
        for b in reads:
            self.readers.setdefault(b, []).append(tok)
        return tok

    def dma(self, out, in_, reads=(), writes=(), q="sync", slot=None, **kw):
        if slot is None:
            sem = {"sync": "sync", "gpsimd": "pooldma", "scalar": "actdma"}[q]
        else:
            sem = "d_" + slot
            if sem not in self.sems:
                self.add_sem(sem, q)
            assert self.sem_stream[sem] == q, (sem, q)
        return self.op(sem, lambda e: e.dma_start(out=out, in_=in_, **kw), reads, writes, inc=16)

    def pe(self, fn, reads=(), writes=()):
        return self.op("pe", fn, reads, writes)

    def act(self, fn, reads=(), writes=()):
        return self.op("act", fn, reads, writes)

    def dve(self, fn, reads=(), writes=()):
        return self.op("dve", fn, reads, writes)

    def pool(self, fn, reads=(), writes=()):
        return self.op("pool", fn, reads, writes)

    def barrier(self):
        for stream in self.STREAMS:
            waits = []
            for sname, c in self.cnt.items():
                if c > 0 and self.waited.get((stream, sname), 0) < c:
                    self.waited[(stream, sname)] = c
                    waits.append((sname, c))
            self.ops[stream].append((waits, None, None, 0))

    def emit(self, last=True):
        nc = self.nc
        final = [(s, c) for s, c in self.cnt.items() if c > 0]
        ops = self.ops
        self.ops = {s: [] for s in self.STREAMS}

        def run(e, stream):
            for waits, fn, sem, inc in ops[stream]:
                for (ps, pc) in waits:
                    e.wait_ge(self.sems[ps], pc)
                if fn is not None:
                    fn(e).then_inc(self.sems[sem], inc)
            if stream == "sync" and last:
                for (s, c) in final:
                    e.wait_ge(self.sems[s], c)

        with nc.Block() as block:
            @block.sync
            def _(e):
                run(e, "sync")

            @block.scalar
            def _(e):
                run(e, "scalar")

            @block.vector
            def _(e):
                run(e, "vector")

            @block.gpsimd
            def _(e):
                run(e, "gpsimd")

            @block.tensor
            def _(e):
                run(e, "tensor")


def build(T, NSEQ, NPHYS, dbg=False):
    nc = bass.Bass("TRN2", target_bir_lowering=False)
    NB = T // 128
    NSB = T // 512
    ST = NSEQ * 4

    def din(name, shape, dt=F32):
        return nc.dram_tensor(name, list(shape), dt, kind="ExternalInput").ap()

    def dout(name, shape, dt=F32):
        return nc.dram_tensor(name, list(shape), dt, kind="ExternalOutput").ap()

    x_p = din("x_p", [T, D])
    pp_p = din("pp_p", [T, PLE])
    w_in = din("w_in", [D, INW])
    ln_in_g = din("ln_in_g", [D])
    ln_in_b = din("ln_in_b", [D])
    b_f = din("b_f", [NH])
    lam_re = din("lam_re", [NG, SP])
    lam_im = din("lam_im", [NG, SP])
    log_dt = din("log_dt", [NG])
    b_re = din("b_re", [NG, SP, SG])
    b_im = din("b_im", [NG, SP, SG])
    c_re = din("c_re", [NG, SG, SP])
    c_im = din("c_im", [NG, SG, SP])
    d_skip = din("d_skip", [NG, SG])
    w_glu = din("w_glu", [SW, SW])
    b_glu = din("b_glu", [SW])
    w_out = din("w_out", [D, D])
    ln1_g = din("ln1_g", [D])
    ln1_b = din("ln1_b", [D])
    w_up = din("w_up", [D, DFF])
    w_down = din("w_down", [DFF, D])
    w_pe = din("w_pe", [PLE, D])
    w_pg = din("w_pg", [D, D])
    b_pg = din("b_pg", [D])
    ln2_g = din("ln2_g", [D])
    ln2_b = din("ln2_b", [D])
    x_s = din("x_s", [128, D])
    pp_s = din("pp_s", [128, PLE])
    ck = din("ck", [NPHYS * 128, AW])
    cv = din("cv", [NPHYS * 128, AW])
    clf = din("clf", [NPHYS * 128, NH])
    st_re = din("st_re", [NSEQ, NG * SP])
    st_im = din("st_im", [NSEQ, NG * SP])
    pt = din("pt", [NSEQ * 16], I32)
    y_p = dout("y_p", [T, D])
    y_s = dout("y_s", [128, D])
    k_s = dout("k_s", [ST, AW])
    v_s = dout("v_s", [ST, AW])
    lf_sd = dout("lf_sd", [ST, NH])
    ssm_re_s = dout("ssm_re_s", [NSEQ, NG * SP])
    ssm_im_s = dout("ssm_im_s", [NSEQ, NG * SP])
    if dbg:
        att_d_s = dout("att_d_s", [HD, NH, 128], BF16)
        yg_d_s = dout("yg_d_s", [4, 128, 128], BF16)
    else:
        att_d_s = nc.dram_tensor("att_d_s", [HD, NH, 128], BF16).ap()
        yg_d_s = nc.dram_tensor("yg_d_s", [4, 128, 128], BF16).ap()
    att_d = nc.dram_tensor("att_d", [HD, NH, T], BF16).ap()
    wup_d = nc.dram_tensor("wup_d", [D, DFF], BF16).ap()
    wdn_d = nc.dram_tensor("wdn_d", [DFF, D], BF16).ap()
    wout_d = nc.dram_tensor("wout_d", [D, D], BF16).ap()
    wpg_d = nc.dram_tensor("wpg_d", [D, D], BF16).ap()
    wpe_d = nc.dram_tensor("wpe_d", [PLE, D], BF16).ap()
    wglu_d = nc.dram_tensor("wglu_d", [SW, SW], BF16).ap()

    k_p = dout("k_p", [T, AW])
    v_p = dout("v_p", [T, AW])
    lf_p = dout("lf_p", [T, NH])
    ssm_re_p = dout("ssm_re_p", [NG, SP])
    ssm_im_p = dout("ssm_im_p", [NG, SP])
    if dbg:
        yg_d = dout("yg_d", [4, 128, T], BF16)
    else:
        yg_d = nc.dram_tensor("yg_d", [4, 128, T], BF16).ap()

    es = ExitStack()
    with es:
        P = Prog(nc, es)

        cur = [es]

        def sb(name, shape, dt=F32):
            return cur[0].enter_context(nc.sbuf_tensor(name, list(shape), dt))

        def ps(name, shape, dt=F32):
            return cur[0].enter_context(nc.psum_tensor(name, list(shape), dt))

        def phase_end(*stacks):
            P.barrier()
            P.emit(last=False)
            for st_ in stacks:
                st_.close()

        ident_f = sb("ident_f", [128, 128])
        ident_b = sb("ident_b", [128, 128], BF16)
        P.pool(lambda e: e.memset(ident_f[:], 1.0), writes=["ident_f"])
        P.pool(lambda e: e.affine_select(ident_f[:], ident_f[:], [[-1, 128]], ALU.is_equal, 0.0,
                                         base=0, channel_multiplier=1), reads=["ident_f"], writes=["ident_f"])
        P.pool(lambda e: e.tensor_copy(ident_b[:], ident_f[:]), reads=["ident_f"], writes=["ident_b"])
        triu = sb("triu", [128, 128])
        P.pool(lambda e: e.memset(triu[:], 1.0), writes=["triu"])
        P.pool(lambda e: e.affine_select(triu[:], triu[:], [[1, 128]], ALU.is_ge, 0.0,
                                         base=0, channel_multiplier=-1), reads=["triu"], writes=["triu"])
        ones_f = sb("ones_f", [128, 128])
        P.pool(lambda e: e.memset(ones_f[:], 1.0), writes=["ones_f"])

        g_in_col = sb("g_in_col", [128, 8])
        b_in_col = sb("b_in_col", [128, 8])
        P.dma(g_in_col[:], ln_in_g.rearrange("(c p) -> p c", p=128), writes=["g_in_col"],
              allow_slow_non_contiguous=True)
        P.dma(b_in_col[:], ln_in_b.rearrange("(c p) -> p c", p=128), writes=["b_in_col"],
              allow_slow_non_contiguous=True)
        bf_bc = sb("bf_bc", [128, NH])
        P.dma(bf_bc[:], b_f.partition_broadcast(128), writes=["bf_bc"])

        NS = NSEQ
        kT_s = sb("kT_s", [128, 4, 128], BF16)
        qT_s = sb("qT_s", [128, 4, 128], BF16)
        uT_s = sb("uT_s", [128, 4, 128], BF16)
        Vb_s = sb("Vb_s", [128, NH, HD + 2], BF16)
        lf_s = sb("lf_s", [128, NH])
        P.pool(lambda e: e.memset(Vb_s[:], 1.0), writes=["Vb_s"])
        NCB = 3
        cst = [sb(f"cst{i}", [128, 514]) for i in range(NCB)]
        csb = [sb(f"csb{i}", [128, 512], BF16) for i in range(NCB)]
        ncv = [0]

        def convert(jobs):
            base = ncv[0]
            ncv[0] += len(jobs)

            def load(j):
                src, cw, _ = jobs[j]
                i_ = (base + j) % NCB
                P.dma(cst[i_][:, 0:cw], src, writes=[f"cst{i_}"], q="gpsimd", slot=f"cst{i_}")
            for j in range(min(NCB - 1, len(jobs))):
                load(j)
            for j in range(len(jobs)):
                if j + NCB - 1 < len(jobs):
                    load(j + NCB - 1)
                src, cw, (kind, dst, key) = jobs[j]
                i_ = (base + j) % NCB
                if kind == "sb":
                    P.pool(lambda e, i_=i_, cw=cw, dst=dst: e.tensor_copy(dst, cst[i_][:, 0:cw]),
                           reads=[f"cst{i_}"], writes=[key])
                else:
                    P.pool(lambda e, i_=i_, cw=cw: e.tensor_copy(csb[i_][:, 0:cw], cst[i_][:, 0:cw]),
                           reads=[f"cst{i_}"], writes=[f"csb{i_}"])
                    P.dma(dst, csb[i_][:, 0:cw], reads=[f"csb{i_}"], writes=[key], q="gpsimd", slot=f"csb{i_}")

        s_u = ExitStack()
        cur[0] = s_u
        uT = sb("uT", [128, 4, T], BF16)
        s_att = ExitStack()
        cur[0] = s_att
        kT = sb("kT", [128, 4, T], BF16)
        qT = sb("qT", [128, 4, T], BF16)
        Vb = sb("Vb", [128, NB, NH, HD + 2], BF16)
        lf_all = sb("lf_all", [128, NB, NH])
        s1 = ExitStack()
        cur[0] = s1
        w_in_b = sb("w_in_b", [128, 8, INW], BF16)
        jobs = []
        for ct in range(8):
            for c4 in range(4):
                jobs.append((w_in[ct * 128:(ct + 1) * 128, c4 * 514:(c4 + 1) * 514], 514,
                             ("sb", w_in_b[:, ct, c4 * 514:(c4 + 1) * 514], ("w_in_b", ct))))
        for (nm, src, dstd, rows, cols) in (("glu", w_glu, wglu_d, SW, SW), ("out", w_out, wout_d, D, D), ("pg", w_pg, wpg_d, D, D),
                                            ("pe", w_pe, wpe_d, PLE, D)):
            for r0 in range(0, rows, 128):
                for c0 in range(0, cols, 512):
                    jobs.append((src[r0:r0 + 128, c0:c0 + 512], 512, ("dram", dstd[r0:r0 + 128, c0:c0 + 512], ("wsc", nm, r0, c0))))
        convert(jobs)
        xin = [sb(f"xin{i}", [128, 4, D]) for i in range(1)]
        stats = sb("stats", [128, 2, 6])
        mv = sb("mv", [128, 2])
        rstd = sb("rstd", [128, 1])
        nmr = sb("nmr", [128, 1])
        hT = sb("hT", [128, 8, 512], BF16)
        ostage = [sb(f"ostage{i}", [128, 512]) for i in range(2)]
        lstage = sb("lstage", [128, NH])
        P.pool(lambda e: e.memset(Vb[:], 1.0), writes=["Vb"])
        pst = [ps(f"pst{i}", [128, 512]) for i in range(2)]
        pmm = [ps(f"pmm{i}", [128, 512]) for i in range(4)]
        nmm = [0]
        nos = [0]

        def layer_norm_block(xt, xkey, ntt):
            for tt in range(ntt):
                for c in range(2):
                    P.dve(lambda e, tt=tt, c=c: e.bn_stats(stats[:, c, :], xt[:, tt, c * 512:(c + 1) * 512]),
                          reads=[xkey], writes=[("stats", c)])
                P.dve(lambda e: e.bn_aggr(mv[:], stats[:].rearrange("p a b -> p (a b)")),
                      reads=[("stats", 0), ("stats", 1)], writes=["mv"])
                P.dve(lambda e: e.tensor_scalar(rstd[:], mv[:, 1:2], EPS, None, ALU.add),
                      reads=["mv"], writes=["rstd"])
                P.act(lambda e: e.activation(rstd[:], rstd[:], AF.Sqrt), reads=["rstd"], writes=["rstd"])
                P.dve(lambda e: e.reciprocal(rstd[:], rstd[:]), reads=["rstd"], writes=["rstd"])
                P.dve(lambda e: e.scalar_tensor_tensor(nmr[:], mv[:, 0:1], -1.0, rstd[:], ALU.mult, ALU.mult),
                      reads=["mv", "rstd"], writes=["nmr"])
                P.act(lambda e, tt=tt: e.activation(xt[:, tt, :], xt[:, tt, :], AF.Identity,
                                                    bias=nmr[:, 0:1], scale=rstd[:, 0:1]),
                      reads=["nmr", "rstd", xkey], writes=[xkey])

        def to_feature_major(xt, xkey, ntt, dst, dkey, gcol, bcol):
            for ct in range(8):
                pt = pst[ct % 2]
                pk = f"pst{ct % 2}"
                for tt in range(ntt):
                    P.pe(lambda e, pt=pt, tt=tt, ct=ct: e.transpose(pt[:, tt * 128:(tt + 1) * 128],
                                                                  xt[:, tt, ct * 128:(ct + 1) * 128], ident_f[:]),
                         reads=[xkey, "ident_f"], writes=[pk])
                P.act(lambda e, pt=pt, ct=ct: e.activation(dst[:, ct, 0:ntt * 128], pt[:, 0:ntt * 128], AF.Identity,
                                                          bias=bcol[:, ct:ct + 1], scale=gcol[:, ct:ct + 1]),
                      reads=[pk, "g_in_col", "b_in_col"], writes=[(dkey, ct)])

        for sbk in range(NSB):
            xt = xin[0]
            xkey = "xin0"
            P.dma(xt[:], x_p[sbk * 512:(sbk + 1) * 512, :].rearrange("(t p) d -> p t d", p=128), writes=[xkey], slot="xin")
            layer_norm_block(xt, xkey, 4)
            to_feature_major(xt, xkey, 4, hT, "hT", g_in_col, b_in_col)
            hkeys = [("hT", ct) for ct in range(8)]
            wkeys = [("w_in_b", ct) for ct in range(8)]
            tok = slice(sbk * 512, (sbk + 1) * 512)
            for (dst, dkey, c0, scl) in ((kT, "kT", 512, 1.0), (qT, "qT", 0, 0.125), (uT, "uT", 1544, 1.0)):
                for ft in range(4):
                    pm = pmm[nmm[0] % 4]
                    pk = f"pmm{nmm[0] % 4}"
                    nmm[0] += 1
                    for ct in range(8):
                        P.pe(lambda e, pm=pm, ct=ct, ft=ft, c0=c0: e.matmul(
                            pm[:], w_in_b[:, ct, c0 + ft * 128:c0 + (ft + 1) * 128], hT[:, ct, :],
                            start=(ct == 0), stop=(ct == 7)), reads=hkeys + wkeys, writes=[pk])
                    P.act(lambda e, pm=pm, dst=dst, ft=ft, scl=scl, tok=tok: e.activation(dst[:, ft, tok], pm[:], AF.Copy, scale=scl),
                          reads=[pk], writes=[(dkey, sbk)])
            for tt in range(4):
                blk = sbk * 4 + tt
                for (c0, dram, isv) in ((512, k_p, False), (1024, v_p, True)):
                    pm = pmm[nmm[0] % 4]
                    pk = f"pmm{nmm[0] % 4}"
                    nmm[0] += 1
                    for ct in range(8):
                        P.pe(lambda e, pm=pm, ct=ct, tt=tt, c0=c0: e.matmul(
                            pm[:], hT[:, ct, tt * 128:(tt + 1) * 128], w_in_b[:, ct, c0:c0 + 512],
                            start=(ct == 0), stop=(ct == 7)), reads=hkeys + wkeys, writes=[pk])
                    os_ = ostage[nos[0] % 2]
                    ok = f"ostage{nos[0] % 2}"
                    nos[0] += 1
                    P.act(lambda e, pm=pm, os_=os_: e.copy(os_[:], pm[:]), reads=[pk], writes=[ok])
                    if isv and 'vb' not in KSKIP:
                        P.act(lambda e, pm=pm, blk=blk: e.copy(
                            Vb[:, blk, :, 0:HD], pm[:].rearrange("p (h d) -> p h d", h=NH)),
                            reads=[pk], writes=["Vb"])
                    if 'odma' not in KSKIP:
                        P.dma(dram[blk * 128:(blk + 1) * 128, :], os_[:], reads=[ok], slot=ok)
                if 'lf' in KSKIP:
                    continue
                pm = pmm[nmm[0] % 4]
                pk = f"pmm{nmm[0] % 4}"
                nmm[0] += 1
                for ct in range(8):
                    P.pe(lambda e, pm=pm, ct=ct, tt=tt: e.matmul(
                        pm[:, 0:NH], hT[:, ct, tt * 128:(tt + 1) * 128], w_in_b[:, ct, 1536:1544],
                        start=(ct == 0), stop=(ct == 7)), reads=hkeys + wkeys, writes=[pk])
                P.act(lambda e, pm=pm: e.copy(lstage[:], pm[:, 0:NH]), reads=[pk], writes=["lstage"])
                P.dve(lambda e: e.tensor_tensor(lstage[:], lstage[:], bf_bc[:], ALU.add),
                      reads=["lstage", "bf_bc"], writes=["lstage"])
                P.act(lambda e: e.activation(lstage[:], lstage[:], AF.Exp, scale=-1.0),
                      reads=["lstage"], writes=["lstage"])
                P.dve(lambda e: e.tensor_scalar(lstage[:], lstage[:], 1.0, None, ALU.add),
                      reads=["lstage"], writes=["lstage"])
                P.act(lambda e: e.activation(lstage[:], lstage[:], AF.Ln),
                      reads=["lstage"], writes=["lstage"])
                P.dve(lambda e, blk=blk: e.tensor_scalar(lf_all[:, blk, :], lstage[:], -1.0, None, ALU.mult),
                      reads=["lstage"], writes=[("lf_all", blk)])
        P.dma(lf_p.rearrange("(b p) h -> p b h", p=128), lf_all[:],
              reads=[("lf_all", b) for b in range(NB)])

        xt = xin[0]
        xkey = "xin0"
        P.dma(xt[:, 0, :], x_s[:, :], writes=[xkey], slot="xin")
        layer_norm_block(xt, xkey, 1)
        to_feature_major(xt, xkey, 1, hT, "hT", g_in_col, b_in_col)
        hkeys = [("hT", ct) for ct in range(8)]
        wkeys = [("w_in_b", ct) for ct in range(8)]
        for (dst, dkey, c0, scl) in ((kT_s, "kT_s", 512, 1.0), (qT_s, "qT_s", 0, 0.125), (uT_s, "uT_s", 1544, 1.0)):
            for ft in range(4):
                pm = pmm[nmm[0] % 4]
                pk = f"pmm{nmm[0] % 4}"
                nmm[0] += 1
                for ct in range(8):
                    P.pe(lambda e, pm=pm, ct=ct, ft=ft, c0=c0: e.matmul(
                        pm[:, 0:128], w_in_b[:, ct, c0 + ft * 128:c0 + (ft + 1) * 128], hT[:, ct, 0:128],
                        start=(ct == 0), stop=(ct == 7)), reads=hkeys + wkeys, writes=[pk])
                P.act(lambda e, pm=pm, dst=dst, ft=ft, scl=scl: e.activation(dst[:, ft, :], pm[:, 0:128], AF.Copy, scale=scl),
                      reads=[pk], writes=[dkey])
        for (c0, dram, isv) in ((512, k_s, False), (1024, v_s, True)):
            pm = pmm[nmm[0] % 4]
            pk = f"pmm{nmm[0] % 4}"
            nmm[0] += 1
            for ct in range(8):
                P.pe(lambda e, pm=pm, ct=ct, c0=c0: e.matmul(
                    pm[:], hT[:, ct, 0:128], w_in_b[:, ct, c0:c0 + 512],
                    start=(ct == 0), stop=(ct == 7)), reads=hkeys + wkeys, writes=[pk])
            os_ = ostage[nos[0] % 2]
            ok = f"ostage{nos[0] % 2}"
            nos[0] += 1
            P.act(lambda e, pm=pm, os_=os_: e.copy(os_[:], pm[:]), reads=[pk], writes=[ok])
            if isv:
                P.act(lambda e, pm=pm: e.copy(Vb_s[:, :, 0:HD], pm[:].rearrange("p (h d) -> p h d", h=NH)),
                      reads=[pk], writes=["Vb_s"])
            P.dma(dram[:, :], os_[0:ST, :], reads=[ok], slot=ok)
        pm = pmm[nmm[0] % 4]
        pk = f"pmm{nmm[0] % 4}"
        nmm[0] += 1
        for ct in range(8):
            P.pe(lambda e, pm=pm, ct=ct: e.matmul(
                pm[:, 0:NH], hT[:, ct, 0:128], w_in_b[:, ct, 1536:1544],
                start=(ct == 0), stop=(ct == 7)), reads=hkeys + wkeys, writes=[pk])
        P.act(lambda e, pm=pm: e.copy(lstage[:], pm[:, 0:NH]), reads=[pk], writes=["lstage"])
        P.dve(lambda e: e.tensor_tensor(lstage[:], lstage[:], bf_bc[:], ALU.add),
              reads=["lstage", "bf_bc"], writes=["lstage"])
        P.act(lambda e: e.activation(lstage[:], lstage[:], AF.Exp, scale=-1.0),
              reads=["lstage"], writes=["lstage"])
        P.dve(lambda e: e.tensor_scalar(lstage[:], lstage[:], 1.0, None, ALU.add),
              reads=["lstage"], writes=["lstage"])
        P.act(lambda e: e.activation(lstage[:], lstage[:], AF.Ln),
              reads=["lstage"], writes=["lstage"])
        P.dve(lambda e: e.tensor_scalar(lf_s[:], lstage[:], -1.0, None, ALU.mult),
              reads=["lstage"], writes=["lf_s"])
        P.dma(lf_sd[:, :], lf_s[0:ST, :], reads=["lf_s"])

        phase_end(s1)
        if KSTAGE == 1:
            P.barrier(); P.emit(); s_att.close(); s_u.close()
            return nc

        s3 = ExitStack()
        cur[0] = s3
        pst = [ps(f"pst{i}_3", [128, 512]) for i in range(2)]
        pmm = [ps(f"pmm{i}_3", [128, 512]) for i in range(4)]
        att_dbg = dout("att_dbg", [HD, NH, T]) if dbg else None
        dbg2 = dout("dbg2", [128, 1536]) if dbg else None
        dbg3 = dout("dbg3", [128, 2576]) if dbg else None
        dbgs = sb("dbgs", [128, 2576]) if dbg else None
        lf_keys = [("lf_all", b) for b in range(NB)]
        totb = sb("totb", [128, NB, NH])
        pre = sb("pre", [128, NB, NH])
        cc = sb("cc", [128, NB, NH])
        biasI = sb("biasI", [128, NB, NH])
        maskT = sb("maskT", [128, 128], BF16)
        maskf = sb("maskf", [128, 128])
        P.pool(lambda e: e.memset(maskf[:], 0.0), writes=["maskf"])
        P.pool(lambda e: e.affine_select(maskf[:], maskf[:], [[1, 128]], ALU.is_ge, NEG,
                                         base=0, channel_multiplier=-1), reads=["maskf"], writes=["maskf"])
        P.pool(lambda e: e.tensor_copy(maskT[:], maskf[:]), reads=["maskf"], writes=["maskT"])
        jobs = []
        for (nm, src, dstd, rows, cols) in (("up", w_up, wup_d, D, DFF), ("dn", w_down, wdn_d, DFF, D)):
            for r0 in range(0, rows, 128):
                for c0 in range(0, cols, 512):
                    jobs.append((src[r0:r0 + 128, c0:c0 + 512], 512, ("dram", dstd[r0:r0 + 128, c0:c0 + 512], ("wsc", nm, r0, c0))))
        convert(jobs)
        lf_flat = lf_all[:].rearrange("p b h -> p (b h)")
        pm = pmm[0]
        P.pe(lambda e: e.matmul(pm[:, 0:NB * NH], ones_f[:], lf_flat, start=True, stop=True),
             reads=lf_keys + ["ones_f"], writes=["pmm0"])
        P.act(lambda e: e.copy(totb[:].rearrange("p b h -> p (b h)"), pm[:, 0:NB * NH]), reads=["pmm0"], writes=["totb"])
        pm1 = pmm[1]
        P.pe(lambda e: e.matmul(pm1[:, 0:NB * NH], triu[:], lf_flat, start=True, stop=True),
             reads=lf_keys + ["triu"], writes=["pmm1"])
        P.act(lambda e: e.copy(cc[:].rearrange("p b h -> p (b h)"), pm1[:, 0:NB * NH]), reads=["pmm1"], writes=["cc"])
        P.dve(lambda e: e.memset(pre[:, 0, :], 0.0), writes=["pre"])
        for b in range(1, NB):
            P.dve(lambda e, b=b: e.tensor_tensor(pre[:, b, :], pre[:, b - 1, :], totb[:, b - 1, :], ALU.add),
                  reads=["pre", "totb"], writes=["pre"])
        P.dve(lambda e: e.tensor_tensor(cc[:], cc[:], pre[:], ALU.add), reads=["cc", "pre"], writes=["cc"])

        pS = [ps(f"pS{i}", [128, 512]) for i in range(2)]
        pT = [sb(f"pT{i}", [128, 512], BF16) for i in range(2)]
        osb = sb("osb", [HD + 1, 512])
        rbc = sb("rbc", [HD, 512])
        attT = sb("attT", [HD, NH, 512], BF16)
        attf = sb("attf", [HD, NH, 512]) if dbg else None
        nS = [0]
        pend = [None]

        def flush():
            if pend[0] is not None:
                fn_ = pend[0]
                pend[0] = None
                fn_()

        def epilogue(h, po, pok):
            P.act(lambda e, po=po: e.copy(osb[:], po[0:HD + 1, :]), reads=[pok], writes=["osb"])
            P.dve(lambda e: e.reciprocal(osb[HD:HD + 1, :], osb[HD:HD + 1, :]), reads=["osb"], writes=["osb"])
            pb = pst[h % 2]
            pbk = f"pst{h % 2}"
            P.pe(lambda e, pb=pb: e.matmul(pb[0:HD, :], ones_f[HD:HD + 1, 0:HD], osb[HD:HD + 1, :],
                                           start=True, stop=True), reads=["osb", "ones_f"], writes=[pbk])
            P.act(lambda e, pb=pb: e.copy(rbc[:], pb[0:HD, :]), reads=[pbk], writes=["rbc"])
            P.dve(lambda e, h=h: e.tensor_tensor(attT[:, h, :], osb[0:HD, :], rbc[:], ALU.mult),
                  reads=["osb", "rbc"], writes=[("attT", h)])

        for I in range(NSB):
            nkb = 4 * I + 4
            for kb in range(nkb):
                P.dve(lambda e, kb=kb, I=I: e.tensor_tensor(biasI[:, kb, :], pre[:, 4 * I, :], cc[:, kb, :], ALU.subtract),
                      reads=["pre", "cc"], writes=["biasI"])
            for h in range(NH):
                hp = slice((h % 2) * 64, (h % 2) * 64 + 64)
                ft = h // 2
                po = pmm[2 + (h % 2)]
                pok = f"pmm{2 + (h % 2)}"
                for kb in range(nkb):
                    c0 = 0 if kb < 4 * I else (kb - 4 * I) * 128
                    i = nS[0] % 2
                    nS[0] += 1
                    psS, pTt = pS[i], pT[i]
                    diag = kb >= 4 * I
                    P.pe(lambda e, psS=psS, kb=kb, c0=c0, diag=diag, hp=hp, ft=ft, I=I: e.matmul(
                        psS[:, c0:512], kT[hp, ft, kb * 128:(kb + 1) * 128], qT[hp, ft, I * 512 + c0:(I + 1) * 512],
                        start=True, stop=not diag), reads=[("kT", kb // 4), ("qT", I)], writes=[f"pS{i}"])
                    if diag:
                        P.pe(lambda e, psS=psS, c0=c0: e.matmul(psS[:, c0:c0 + 128], ident_b[:], maskT[:],
                                                               start=False, stop=True),
                             reads=["ident_b", "maskT"], writes=[f"pS{i}"])
                    P.act(lambda e, psS=psS, pTt=pTt, c0=c0, kb=kb, h=h: e.activation(
                        pTt[:, c0:512], psS[:, c0:512], AF.Exp, bias=biasI[:, kb, h:h + 1]),
                        reads=[f"pS{i}", "biasI"], writes=[f"pT{i}"])

                    def later(po=po, pok=pok, pTt=pTt, c0=c0, kb=kb, h=h, nkb=nkb, i=i):
                        P.pe(lambda e: e.matmul(
                            po[0:HD + 1, c0:512], Vb[:, kb, h, 0:HD + 1], pTt[:, c0:512],
                            start=(kb == 0), stop=(kb == nkb - 1)), reads=[f"pT{i}", "Vb"], writes=[pok])
                        if kb == nkb - 1:
                            epilogue(h, po, pok)
                    flush()
                    pend[0] = later
            flush()
            P.dma(att_d[:, :, I * 512:(I + 1) * 512], attT[:], reads=[("attT", h) for h in range(NH)], writes=[("att_d", I)], slot="attT")

        phase_end(s3, s_att)
        if KSTAGE == 3:
            P.barrier(); P.emit(); s_u.close()
            return nc
        s2 = ExitStack()
        cur[0] = s2
        GP = 16
        NC = T // 8
        PI = math.pi
        TWO_PI = 2.0 * math.pi
        K_ = "ssmc"

        def dv(fn):
            return P.dve(fn, reads=[K_], writes=[K_])

        def ac(fn):
            return P.act(fn, reads=[K_], writes=[K_])

        def sdma(out, in_):
            return P.dma(out, in_, writes=[K_], q="gpsimd", allow_slow_non_contiguous=True)

        lam_r = sb("lam_r", [128, GP])
        lam_i = sb("lam_i", [128, GP])
        ldt = sb("ldt", [128, GP])
        Bre = sb("Bre", [128, GP, SG])
        Bim = sb("Bim", [128, GP, SG])
        Cre = sb("Cre", [128, GP, SG])
        Cim = sb("Cim", [128, GP, SG])
        dcol = sb("dcol", [128, 4])
        sdma(lam_r[:], lam_re.rearrange("(gp g2) p -> (g2 p) gp", g2=2))
        sdma(lam_i[:], lam_im.rearrange("(gp g2) p -> (g2 p) gp", g2=2))
        for g2 in range(2):
            sdma(ldt[64 * g2:64 * g2 + 64, :], log_dt.rearrange("(gp g2) -> g2 gp", g2=2)[g2].partition_broadcast(64))
            for gp in range(GP):
                sdma(Cre[64 * g2:64 * g2 + 64, gp, :], c_re[2 * gp + g2].rearrange("h p -> p h"))
                sdma(Cim[64 * g2:64 * g2 + 64, gp, :], c_im[2 * gp + g2].rearrange("h p -> p h"))
        sdma(Bre[:], b_re.rearrange("(gp g2) p h -> (g2 p) gp h", g2=2))
        sdma(Bim[:], b_im.rearrange("(gp g2) p h -> (g2 p) gp h", g2=2))
        sdma(dcol[:], d_skip.rearrange("g h -> (g h)").rearrange("(a p) -> p a", p=128))

        def small(name):
            return sb(name, [128, GP])

        dt_, lrdt, th, mag, rho8, t1, a1, sinv, t2, cosv, phi8 = [small(n) for n in (
            "dt_", "lrdt", "th", "mag", "rho8", "t1", "a1", "sinv", "t2", "cosv", "phi8")]
        tmpa, tmpb, nr, den, cre, cim = [small(n) for n in ("tmpa", "tmpb", "nr", "den", "cre", "cim")]
        AKr = sb("AKr", [128, GP, 9])
        AKi = sb("AKi", [128, GP, 9])

        def tt(o, a, b, op):
            return dv(lambda e: e.tensor_tensor(o, a, b, op))

        def ts(o, a, s1_, s2_, op0, op1=None):
            if op1 is None:
                return dv(lambda e: e.tensor_scalar(o, a, s1_, None, op0))
            return dv(lambda e: e.tensor_scalar(o, a, s1_, s2_, op0, op1))

        def reduce_pm_pi(dst, x, ki, kf, m, emit_v):
            emit_v(lambda e: e.tensor_scalar(ki, x, 1.0 / TWO_PI, None, ALU.mult))
            emit_v(lambda e: e.tensor_copy(kf, ki))
            emit_v(lambda e: e.scalar_tensor_tensor(dst, kf, -TWO_PI, x, ALU.mult, ALU.add))
            emit_v(lambda e: e.tensor_scalar(m, dst, PI, None, ALU.is_gt))
            emit_v(lambda e: e.scalar_tensor_tensor(dst, m, -TWO_PI, dst, ALU.mult, ALU.add))
            emit_v(lambda e: e.tensor_scalar(m, dst, -PI, None, ALU.is_lt))
            emit_v(lambda e: e.scalar_tensor_tensor(dst, m, TWO_PI, dst, ALU.mult, ALU.add))

        def cos_arg(dst, r, m, emit_v):
            emit_v(lambda e: e.tensor_scalar(dst, r, PI / 2, None, ALU.add))
            emit_v(lambda e: e.tensor_scalar(m, dst, PI, None, ALU.is_gt))
            emit_v(lambda e: e.scalar_tensor_tensor(dst, m, -TWO_PI, dst, ALU.mult, ALU.add))

        ki_s = sb("ki_s", [128, GP], I32)
        kf_s = small("kf_s")
        m_s = small("m_s")
        ac(lambda e: e.activation(dt_[:], ldt[:], AF.Exp))
        tt(lrdt[:], lam_r[:], dt_[:], ALU.mult)
        tt(th[:], lam_i[:], dt_[:], ALU.mult)
        ac(lambda e: e.activation(mag[:], lrdt[:], AF.Exp))
        ac(lambda e: e.activation(rho8[:], lrdt[:], AF.Exp, scale=8.0))
        reduce_pm_pi(t1[:], th[:], ki_s[:], kf_s[:], m_s[:], dv)
        ac(lambda e: e.activation(sinv[:], t1[:], AF.Sin))
        cos_arg(t2[:], t1[:], m_s[:], dv)
        ac(lambda e: e.activation(cosv[:], t2[:], AF.Sin))
        ts(a1[:], t1[:], 8.0, None, ALU.mult)
        reduce_pm_pi(phi8[:], a1[:], ki_s[:], kf_s[:], m_s[:], dv)
        dv(lambda e: e.memset(AKr[:, :, 0], 1.0))
        dv(lambda e: e.memset(AKi[:, :, 0], 0.0))
        tt(AKr[:, :, 1], mag[:], cosv[:], ALU.mult)
        tt(AKi[:, :, 1], mag[:], sinv[:], ALU.mult)
        for k in range(2, 9):
            tt(tmpa[:], AKr[:, :, k - 1], AKr[:, :, 1], ALU.mult)
            tt(tmpb[:], AKi[:, :, k - 1], AKi[:, :, 1], ALU.mult)
            tt(AKr[:, :, k], tmpa[:], tmpb[:], ALU.subtract)
            tt(tmpa[:], AKr[:, :, k - 1], AKi[:, :, 1], ALU.mult)
            tt(tmpb[:], AKi[:, :, k - 1], AKr[:, :, 1], ALU.mult)
            tt(AKi[:, :, k], tmpa[:], tmpb[:], ALU.add)
        ts(nr[:], AKr[:, :, 1], -1.0, None, ALU.add)
        tt(tmpa[:], lam_r[:], lam_r[:], ALU.mult)
        tt(tmpb[:], lam_i[:], lam_i[:], ALU.mult)
        tt(den[:], tmpa[:], tmpb[:], ALU.add)
        dv(lambda e: e.reciprocal(den[:], den[:]))
        tt(tmpa[:], nr[:], lam_r[:], ALU.mult)
        tt(tmpb[:], AKi[:, :, 1], lam_i[:], ALU.mult)
        tt(cre[:], tmpa[:], tmpb[:], ALU.add)
        tt(cre[:], cre[:], den[:], ALU.mult)
        tt(tmpa[:], AKi[:, :, 1], lam_r[:], ALU.mult)
        tt(tmpb[:], nr[:], lam_i[:], ALU.mult)
        tt(cim[:], tmpa[:], tmpb[:], ALU.subtract)
        tt(cim[:], cim[:], den[:], ALU.mult)
        bbr = sb("bbr", [128, GP, SG])
        bbi = sb("bbi", [128, GP, SG])
        t3a = sb("t3a", [128, GP, SG])
        t3b = sb("t3b", [128, GP, SG])
        cre_b = cre[:].unsqueeze(2).broadcast_to([128, GP, SG])
        cim_b = cim[:].unsqueeze(2).broadcast_to([128, GP, SG])
        tt(t3a[:], Bre[:], cre_b, ALU.mult)
        tt(t3b[:], Bim[:], cim_b, ALU.mult)
        tt(bbr[:], t3a[:], t3b[:], ALU.subtract)
        tt(t3a[:], Bim[:], cre_b, ALU.mult)
        tt(t3b[:], Bre[:], cim_b, ALU.mult)
        tt(bbi[:], t3a[:], t3b[:], ALU.add)
        MC = sb("MC", [128, GP, 9, 2, 32], BF16)
        Wt = sb("Wt", [128, 4, 8, 2, 128], BF16)
        Kl = sb("Kl", [128, 4, 8, 32], BF16)
        pw = [ps(f"pw{i}", [128, 512]) for i in range(2)]
        s2c = ExitStack()
        cur[0] = s2c
        XAr = sb("XAr", [128, GP, 8, SG])
        XAi = sb("XAi", [128, GP, 8, SG])
        t4a = sb("t4a", [128, GP, 9, SG])
        t4b = sb("t4b", [128, GP, 9, SG])
        akr8 = AKr[:, :, 0:8].unsqueeze(3).broadcast_to([128, GP, 8, SG])
        aki8 = AKi[:, :, 0:8].unsqueeze(3).broadcast_to([128, GP, 8, SG])
        bbr8 = bbr[:].unsqueeze(2).broadcast_to([128, GP, 8, SG])
        bbi8 = bbi[:].unsqueeze(2).broadcast_to([128, GP, 8, SG])
        tt(t4a[:, :, 0:8, :], akr8, bbr8, ALU.mult)
        tt(t4b[:, :, 0:8, :], aki8, bbi8, ALU.mult)
        tt(XAr[:], t4a[:, :, 0:8, :], t4b[:, :, 0:8, :], ALU.subtract)
        tt(t4a[:, :, 0:8, :], akr8, bbi8, ALU.mult)
        tt(t4b[:, :, 0:8, :], aki8, bbr8, ALU.mult)
        tt(XAi[:], t4a[:, :, 0:8, :], t4b[:, :, 0:8, :], ALU.add)
        CAr = sb("CAr", [128, GP, 9, SG])
        CAi = sb("CAi", [128, GP, 9, SG])
        akr9 = AKr[:].unsqueeze(3).broadcast_to([128, GP, 9, SG])
        aki9 = AKi[:].unsqueeze(3).broadcast_to([128, GP, 9, SG])
        cr9 = Cre[:].unsqueeze(2).broadcast_to([128, GP, 9, SG])
        ci9 = Cim[:].unsqueeze(2).broadcast_to([128, GP, 9, SG])
        tt(t4a[:], akr9, cr9, ALU.mult)
        tt(t4b[:], aki9, ci9, ALU.mult)
        tt(CAr[:], t4a[:], t4b[:], ALU.subtract)
        tt(t4a[:], aki9, cr9, ALU.mult)
        tt(t4b[:], akr9, ci9, ALU.mult)
        tt(CAi[:], t4a[:], t4b[:], ALU.add)
        ts(CAi[:], CAi[:], -1.0, None, ALU.mult)
        XAbd = sb("XAbd", [128, 4, 8, 2, 4, 32])
        CBDr = sb("CBDr", [128, GP, 32])
        CBDni = sb("CBDni", [128, GP, 32])
        dv(lambda e: e.memset(XAbd[:], 0.0))
        dv(lambda e: e.memset(MC[:], 0.0))
        dv(lambda e: e.memset(CBDr[:], 0.0))
        dv(lambda e: e.memset(CBDni[:], 0.0))
        for g2 in range(2):
            pr = slice(64 * g2, 64 * g2 + 64)
            cs = slice(16 * g2, 16 * g2 + 16)
            for gq in range(4):
                dv(lambda e, pr=pr, cs=cs, gq=gq: e.tensor_copy(
                    XAbd[pr, gq, :, 0, :, cs], XAr[pr, 4 * gq:4 * gq + 4, :, :].rearrange("p q k h -> p k q h")))
                dv(lambda e, pr=pr, cs=cs, gq=gq: e.tensor_copy(
                    XAbd[pr, gq, :, 1, :, cs], XAi[pr, 4 * gq:4 * gq + 4, :, :].rearrange("p q k h -> p k q h")))
            dv(lambda e, pr=pr, cs=cs: e.tensor_copy(MC[pr, :, :, 0, cs], CAr[pr]))
            dv(lambda e, pr=pr, cs=cs: e.tensor_copy(MC[pr, :, :, 1, cs], CAi[pr]))
            dv(lambda e, pr=pr, cs=cs: e.tensor_copy(CBDr[pr, :, cs], Cre[pr]))
            dv(lambda e, pr=pr, cs=cs: e.tensor_scalar(CBDni[pr, :, cs], Cim[pr], -1.0, None, ALU.mult))
        npw = 0
        for gq in range(4):
            for i in range(8):
                pwt = pw[npw % 2]
                pwk = f"pw{npw % 2}"
                npw += 1
                for ri in range(2):
                    P.pe(lambda e, pwt=pwt, gq=gq, i=i, ri=ri: e.transpose(
                        pwt[:, ri * 128:(ri + 1) * 128], XAbd[:, gq, 7 - i, ri, :, :].rearrange("p q c -> p (q c)"), ident_f[:]),
                        reads=[K_, "ident_f"], writes=[pwk])
                P.act(lambda e, pwt=pwt, gq=gq, i=i: e.copy(
                    Wt[:, gq, i, :, :].rearrange("p r c -> p (r c)"), pwt[:, 0:256]), reads=[pwk], writes=["Wt"])
        for gp in range(GP):
            gq, q = gp // 4, gp % 4
            pwt = pw[npw % 2]
            pwk = f"pw{npw % 2}"
            npw += 1
            for k in range(8):
                P.pe(lambda e, pwt=pwt, gq=gq, gp=gp, k=k: e.matmul(
                    pwt[:, k * 32:(k + 1) * 32], XAbd[:, gq, k, 0, :, :].rearrange("p q c -> p (q c)"), CBDr[:, gp, :],
                    start=True, stop=False), reads=[K_], writes=[pwk])
                P.pe(lambda e, pwt=pwt, gq=gq, gp=gp, k=k: e.matmul(
                    pwt[:, k * 32:(k + 1) * 32], XAbd[:, gq, k, 1, :, :].rearrange("p q c -> p (q c)"), CBDni[:, gp, :],
                    start=False, stop=True), reads=[K_], writes=[pwk])
            P.act(lambda e, pwt=pwt, gq=gq, q=q: e.copy(
                Kl[32 * q:32 * q + 32, gq, :, :].rearrange("p k c -> p (k c)"), pwt[32 * q:32 * q + 32, 0:256]),
                reads=[pwk], writes=["Kl"])

        phase_end(s2c)
        cur[0] = s2
        cidx = sb("cidx", [128, NC])
        P.pool(lambda e: e.iota(cidx[:], [[1, NC]], base=0, channel_multiplier=0,
                                allow_small_or_imprecise_dtypes=True), writes=["cidx"])
        ang, nS, nC, sr, si, ta, tb, Gr, Gi = [sb(n, [128, NC]) for n in
                                               ("ang", "nS", "nC", "sr", "si", "ta", "tb", "Gr", "Gi")]
        ki_w = sb("ki_w", [128, NC], I32)
        Xp = [[sb(f"Xp{q}_{ri}", [128, NC], BF16) for ri in range(2)] for q in range(4)]
        fin = sb("fin", [128, GP, 2])
        ysb = sb("ysb", [128, 1024])
        yt1 = sb("yt1", [128, 1024])
        ygb = sb("ygb", [128, 1024], BF16)
        pss = [ps(f"pss{i}", [128, 512]) for i in range(2)]
        psy = ps("psy", [128, 1024])
        for q in range(4):
            for ri in range(2):
                P.pool(lambda e, q=q, ri=ri: e.memset(Xp[q][ri][:, 0:1], 0.0), writes=[("Xp", q)])
        W_ = "ssmw"

        def wv(fn, extra_r=(), extra_w=()):
            return P.dve(fn, reads=[W_] + list(extra_r), writes=[W_] + list(extra_w))

        def wa(fn, extra_r=(), extra_w=()):
            return P.act(fn, reads=[W_] + list(extra_r), writes=[W_] + list(extra_w))

        ukeys = [("uT", sbk) for sbk in range(NSB)]
        for gq in range(4):
            for q in range(4):
                gp = 4 * gq + q
                rows = slice(32 * q, 32 * q + 32)
                for ri in range(2):
                    for i in range(8):
                        P.pe(lambda e, ri=ri, i=i, rows=rows, gq=gq, q=q: e.matmul(
                            pss[ri][:, 0:NC], Wt[rows, gq, i, ri, :], uT[rows, gq, i:T:8],
                            start=(i == 0), stop=(i == 7), tile_position=(32 * q, 0)),
                            reads=["Wt"] + ukeys, writes=[f"pss{ri}"])
                wa(lambda e: e.copy(sr[:], pss[0][:, 0:NC]), extra_r=["pss0"])
                wa(lambda e: e.copy(si[:], pss[1][:, 0:NC]), extra_r=["pss1"])
                wv(lambda e, gp=gp: e.tensor_scalar(ang[:], cidx[:], phi8[:, gp:gp + 1], None, ALU.mult),
                   extra_r=["cidx", K_])
                reduce_pm_pi(ta[:], ang[:], ki_w[:], tb[:], Gr[:], wv)
                wa(lambda e: e.activation(nS[:], ta[:], AF.Sin))
                cos_arg(tb[:], ta[:], Gr[:], wv)
                wa(lambda e: e.activation(nC[:], tb[:], AF.Sin))
                wv(lambda e: e.tensor_tensor(ta[:], sr[:], nC[:], ALU.mult))
                wv(lambda e: e.tensor_tensor(tb[:], si[:], nS[:], ALU.mult))
                wv(lambda e: e.tensor_tensor(Gr[:], ta[:], tb[:], ALU.add))
                wv(lambda e: e.tensor_tensor(ta[:], si[:], nC[:], ALU.mult))
                wv(lambda e: e.tensor_tensor(tb[:], sr[:], nS[:], ALU.mult))
                wv(lambda e: e.tensor_tensor(Gi[:], ta[:], tb[:], ALU.subtract))
                rho_b = rho8[:, gp:gp + 1].broadcast_to([128, NC])
                wv(lambda e, rho_b=rho_b: e.tensor_tensor_scan(sr[:], rho_b, Gr[:], 0.0, ALU.mult, ALU.add), extra_r=[K_])
                wv(lambda e, rho_b=rho_b: e.tensor_tensor_scan(si[:], rho_b, Gi[:], 0.0, ALU.mult, ALU.add), extra_r=[K_])
                wv(lambda e: e.tensor_tensor(ta[:], sr[:], nC[:], ALU.mult))
                wv(lambda e: e.tensor_tensor(tb[:], si[:], nS[:], ALU.mult))
                wv(lambda e: e.tensor_tensor(Gr[:], ta[:], tb[:], ALU.subtract))
                wv(lambda e: e.tensor_tensor(ta[:], si[:], nC[:], ALU.mult))
                wv(lambda e: e.tensor_tensor(tb[:], sr[:], nS[:], ALU.mult))
                wv(lambda e: e.tensor_tensor(Gi[:], ta[:], tb[:], ALU.add))
                wa(lambda e, q=q: e.copy(Xp[q][0][:, 1:NC], Gr[:, 0:NC - 1]), extra_w=[("Xp", q)])
                wa(lambda e, q=q: e.copy(Xp[q][1][:, 1:NC], Gi[:, 0:NC - 1]), extra_w=[("Xp", q)])
                wa(lambda e, gp=gp: e.copy(fin[:, gp, 0:1], Gr[:, NC - 1:NC]), extra_w=["fin"])
                wa(lambda e, gp=gp: e.copy(fin[:, gp, 1:2], Gi[:, NC - 1:NC]), extra_w=["fin"])
            for cq in range(NC // 128):
                csl = slice(cq * 128, (cq + 1) * 128)
                for q in range(4):
                    gp = 4 * gq + q
                    rows = slice(32 * q, 32 * q + 32)
                    for j in range(8):
                        ops_ = [(MC[:, gp, j + 1, 0, :], Xp[q][0][:, csl], (0, 32 * q)),
                                (MC[:, gp, j + 1, 1, :], Xp[q][1][:, csl], (0, 32 * q))]
                        for i in range(j + 1):
                            ops_.append((Kl[rows, gq, j - i, :],
                                         uT[rows, gq, cq * 1024 + i:(cq + 1) * 1024:8], (32 * q, 32 * q)))
                        for n_, (l_, r_, tp_) in enumerate(ops_):
                            P.pe(lambda e, l_=l_, r_=r_, tp_=tp_, n_=n_, last=(n_ == len(ops_) - 1), rows=rows, j=j: e.matmul(
                                psy[rows, j * 128:(j + 1) * 128], l_, r_, start=(n_ == 0), stop=last, tile_position=tp_),
                                reads=[K_, "Kl", ("Xp", q)] + ukeys, writes=["psy"])
                ysb3 = ysb[:].rearrange("p (j c) -> p j c", j=8)
                uview = uT[:, gq, cq * 1024:(cq + 1) * 1024].rearrange("p (c j) -> p j c", j=8)
                P.act(lambda e: e.copy(ysb[:], psy[:]), reads=["psy"], writes=["ysb"])
                P.dve(lambda e, uview=uview, ysb3=ysb3, gq=gq: e.scalar_tensor_tensor(
                    ysb3, uview, dcol[:, gq:gq + 1], ysb3, ALU.mult, ALU.add), reads=["ysb", K_] + ukeys, writes=["ysb"])
                P.dve(lambda e: e.tensor_tensor(yt1[:], ysb[:], ysb[:], ALU.mult), reads=["ysb"], writes=["yt1"])
                P.dve(lambda e: e.tensor_scalar(yt1[:], yt1[:], 0.044715, 1.0, ALU.mult, ALU.add),
                      reads=["yt1"], writes=["yt1"])
                P.dve(lambda e: e.tensor_tensor(yt1[:], yt1[:], ysb[:], ALU.mult), reads=["yt1", "ysb"], writes=["yt1"])
                P.act(lambda e: e.activation(yt1[:], yt1[:], AF.Sigmoid, scale=2.0 * math.sqrt(2.0 / math.pi)),
                      reads=["yt1"], writes=["yt1"])
                P.dve(lambda e, ysb3=ysb3: e.tensor_tensor(
                    ygb[:].rearrange("p (c j) -> p j c", j=8), ysb3, yt1[:].rearrange("p (j c) -> p j c", j=8), ALU.mult),
                    reads=["yt1", "ysb"], writes=["ygb"])
                P.dma(yg_d[gq, :, cq * 1024:(cq + 1) * 1024], ygb[:], reads=["ygb"], slot="ygb")
        TK = 4 * NS
        h0sb = sb("h0sb", [NS, 2, NG * SP])
        P.dma(h0sb[:, 0, :], st_re[:, :], writes=["h0sb"])
        P.dma(h0sb[:, 1, :], st_im[:, :], writes=["h0sb"])
        h0f = sb("h0f", [128, 2, GP, NS])
        h0b = sb("h0b", [128, 2, GP, NS], BF16)
        for ri in range(2):
            pwt = pw[ri]
            for gp in range(GP):
                P.pe(lambda e, pwt=pwt, gp=gp, ri=ri: e.transpose(
                    pwt[:, gp * NS:(gp + 1) * NS], h0sb[:, ri, gp * 128:(gp + 1) * 128], ident_f[0:NS, 0:NS]),
                    reads=["h0sb", "ident_f"], writes=[f"pw{ri}"])
            P.act(lambda e, pwt=pwt, ri=ri: e.copy(h0f[:, ri, :, :].rearrange("p g s -> p (g s)"), pwt[:, 0:GP * NS]),
                  reads=[f"pw{ri}"], writes=["h0f"])
        P.dve(lambda e: e.tensor_copy(h0b[:].rearrange("p r g s -> p (r g s)"), h0f[:].rearrange("p r g s -> p (r g s)")),
              reads=["h0f"], writes=["h0b"])
        for gq in range(4):
            for q in range(4):
                gp = 4 * gq + q
                rows = slice(32 * q, 32 * q + 32)
                for ri in range(2):
                    for i in range(4):
                        P.pe(lambda e, ri=ri, i=i, rows=rows, gq=gq, q=q, gp=gp: e.matmul(
                            pss[ri][:, gp * NS:(gp + 1) * NS], Wt[rows, gq, i + 4, ri, :], uT_s[rows, gq, i:TK:4],
                            start=(i == 0), stop=(i == 3), tile_position=(32 * q, 0)),
                            reads=["Wt", "uT_s"], writes=[f"pss{ri}"])
        sr_s = sb("sr_s", [128, GP, NS])
        si_s = sb("si_s", [128, GP, NS])
        tA = sb("tA_s", [128, GP, NS])
        tB = sb("tB_s", [128, GP, NS])
        finr = sb("finr", [128, GP, NS])
        fini = sb("fini", [128, GP, NS])
        P.act(lambda e: e.copy(sr_s[:].rearrange("p g s -> p (g s)"), pss[0][:, 0:GP * NS]), reads=["pss0"], writes=["sr_s"])
        P.act(lambda e: e.copy(si_s[:].rearrange("p g s -> p (g s)"), pss[1][:, 0:GP * NS]), reads=["pss1"], writes=["si_s"])
        A4r = AKr[:, :, 4].unsqueeze(2).broadcast_to([128, GP, NS])
        A4i = AKi[:, :, 4].unsqueeze(2).broadcast_to([128, GP, NS])
        S_ = "ssms"

        def sv(fn, extra_r=(), extra_w=()):
            return P.dve(fn, reads=[S_, K_] + list(extra_r), writes=[S_] + list(extra_w))

        sv(lambda e: e.tensor_tensor(tA[:], h0f[:, 0, :, :], A4r, ALU.mult), extra_r=["h0f"])
        sv(lambda e: e.tensor_tensor(tB[:], h0f[:, 1, :, :], A4i, ALU.mult))
        sv(lambda e: e.tensor_tensor(tA[:], tA[:], tB[:], ALU.subtract))
        sv(lambda e: e.tensor_tensor(finr[:], tA[:], sr_s[:], ALU.add), extra_r=["sr_s"])
        sv(lambda e: e.tensor_tensor(tA[:], h0f[:, 1, :, :], A4r, ALU.mult))
        sv(lambda e: e.tensor_tensor(tB[:], h0f[:, 0, :, :], A4i, ALU.mult))
        sv(lambda e: e.tensor_tensor(tA[:], tA[:], tB[:], ALU.add))
        sv(lambda e: e.tensor_tensor(fini[:], tA[:], si_s[:], ALU.add), extra_r=["si_s"])
        fin_tm = sb("fin_tm", [NS, 2, NG * SP])
        for ri, fsrc in enumerate((finr, fini)):
            for g4 in range(4):
                pwt = pw[(ri * 4 + g4) % 2]
                pwk = f"pw{(ri * 4 + g4) % 2}"
                for gg in range(4):
                    gp = 4 * g4 + gg
                    P.pe(lambda e, pwt=pwt, gg=gg, gp=gp, fsrc=fsrc: e.transpose(
                        pwt[0:NS, gg * 128:(gg + 1) * 128], fsrc[:, gp, :], ident_f[:]),
                        reads=[S_, "ident_f"], writes=[pwk])
                P.act(lambda e, pwt=pwt, ri=ri, g4=g4: e.copy(fin_tm[:, ri, g4 * 512:(g4 + 1) * 512], pwt[0:NS, :]),
                      reads=[pwk], writes=[("fin_tm", ri, g4)])
        P.dma(ssm_re_s[:, :], fin_tm[:, 0, :], reads=[("fin_tm", 0, g4) for g4 in range(4)])
        P.dma(ssm_im_s[:, :], fin_tm[:, 1, :], reads=[("fin_tm", 1, g4) for g4 in range(4)])
        for gq in range(4):
            for q in range(4):
                gp = 4 * gq + q
                rows = slice(32 * q, 32 * q + 32)
                for j in range(4):
                    col = (gq * 4 + j) * NS
                    ops_ = [(MC[:, gp, j + 1, 0, :], h0b[:, 0, gp, :], (0, 32 * q)),
                            (MC[:, gp, j + 1, 1, :], h0b[:, 1, gp, :], (0, 32 * q))]
                    for i in range(j + 1):
                        ops_.append((Kl[rows, gq, j - i, :], uT_s[rows, gq, i:TK:4], (32 * q, 32 * q)))
                    for n_, (l_, r_, tp_) in enumerate(ops_):
                        P.pe(lambda e, l_=l_, r_=r_, tp_=tp_, n_=n_, last=(n_ == len(ops_) - 1), rows=rows, col=col: e.matmul(
                            psy[rows, col:col + NS], l_, r_, start=(n_ == 0), stop=last, tile_position=tp_),
                            reads=[K_, "Kl", "h0b", "uT_s"], writes=["psy"])
        ys_s = sb("ys_s", [128, 4, 4, NS])
        yt_s = sb("yt_s", [128, 4, 4, NS])
        ygs = sb("ygs", [128, 4, 128], BF16)
        P.pool(lambda e: e.memset(ygs[:], 0.0), writes=["ygs"])
        ysf = ys_s[:].rearrange("p g j s -> p (g j s)")
        ytf = yt_s[:].rearrange("p g j s -> p (g j s)")
        P.act(lambda e: e.copy(ysf, psy[:, 0:16 * NS]), reads=["psy"], writes=["ys_s"])
        for gq in range(4):
            P.dve(lambda e, gq=gq: e.scalar_tensor_tensor(
                ys_s[:, gq, :, :], uT_s[:, gq, 0:TK].rearrange("p (s j) -> p j s", j=4), dcol[:, gq:gq + 1],
                ys_s[:, gq, :, :], ALU.mult, ALU.add), reads=["ys_s", K_, "uT_s"], writes=["ys_s"])
        P.dve(lambda e: e.tensor_tensor(ytf, ysf, ysf, ALU.mult), reads=["ys_s"], writes=["yt_s"])
        P.dve(lambda e: e.tensor_scalar(ytf, ytf, 0.044715, 1.0, ALU.mult, ALU.add), reads=["yt_s"], writes=["yt_s"])
        P.dve(lambda e: e.tensor_tensor(ytf, ytf, ysf, ALU.mult), reads=["yt_s", "ys_s"], writes=["yt_s"])
        P.act(lambda e: e.activation(ytf, ytf, AF.Sigmoid, scale=2.0 * math.sqrt(2.0 / math.pi)),
              reads=["yt_s"], writes=["yt_s"])
        for gq in range(4):
            P.dve(lambda e, gq=gq: e.tensor_tensor(
                ygs[:, gq, 0:TK].rearrange("p (s j) -> p j s", j=4), ys_s[:, gq, :, :], yt_s[:, gq, :, :], ALU.mult),
                reads=["yt_s", "ys_s", "ygs"], writes=["ygs"])
        P.dma(yg_d_s.rearrange("g p t -> p g t"), ygs[:], reads=["ygs"])
        P.dma(ssm_re_p.rearrange("(gp g2) p -> (g2 p) gp", g2=2), fin[:, :, 0], reads=["fin"],
              allow_slow_non_contiguous=True)
        P.dma(ssm_im_p.rearrange("(gp g2) p -> (g2 p) gp", g2=2), fin[:, :, 1], reads=["fin"],
              allow_slow_non_contiguous=True)
        phase_end(s2)

        phase_end(s_u)
        if KSTAGE == 2:
            P.barrier(); P.emit()
            return nc
        s5 = ExitStack()
        cur[0] = s5
        NPG = 16
        NPAGES = NS * NPG
        TK = 4 * NS
        pKT = [ps(f"pKT{i}", [128, 512]) for i in range(2)]
        pS5 = [ps(f"pS5_{i}", [128, 512]) for i in range(2)]
        po5 = ps("po5", [128, 512])
        pon5 = ps("pon5", [128, 512])
        pb5 = ps("pb5", [128, 512])
        pt_bc = sb("pt_bc", [128, NPAGES], I32)
        P.dma(pt_bc[:], pt.partition_broadcast(128), writes=["pt_bc"])
        iota_p = sb("iota_p", [128, 1])
        P.pool(lambda e: e.iota(iota_p[:], [[0, 1]], base=0, channel_multiplier=1,
                                allow_small_or_imprecise_dtypes=True), writes=["iota_p"])
        idx_f = sb("idx_f", [128, NPAGES])
        idx_i = sb("idx_i", [128, NPAGES], I32)
        P.dve(lambda e: e.tensor_copy(idx_f[:], pt_bc[:]), reads=["pt_bc"], writes=["idx_f"])
        P.dve(lambda e: e.tensor_scalar(idx_f[:], idx_f[:], 128.0, iota_p[:, 0:1], ALU.mult, ALU.add),
              reads=["idx_f", "iota_p"], writes=["idx_f"])
        P.dve(lambda e: e.tensor_copy(idx_i[:], idx_f[:]), reads=["idx_f"], writes=["idx_i"])

        def gather(out_ap, src, c, writes, reads=(), sem="pooldma"):
            return P.op(sem, lambda e: e.indirect_dma_start(
                out=out_ap, out_offset=None, in_=src[:, :],
                in_offset=bass.IndirectOffsetOnAxis(ap=idx_i[:, c:c + 1], axis=0)),
                reads=["idx_i"] + list(reads), writes=writes, inc=16)

        pf = sb("pf", [128, NPAGES, NH])
        P.add_sem("pfg", "gpsimd")
        for i in range(6):
            P.add_sem(f"kg{i}", "gpsimd")
            P.add_sem(f"vg{i}", "gpsimd")
        for c in range(NPAGES):
            gather(pf[:, c, :], clf, c, [("pf", c)], sem="pfg")
        Ls = sb("Ls", [128, 128])
        P.pool(lambda e: e.memset(Ls[:], 1.0), writes=["Ls"])
        P.pool(lambda e: e.affine_select(Ls[:], Ls[:], [[-1, 128]], ALU.is_ge, 0.0,
                                         base=-1, channel_multiplier=1), reads=["Ls"], writes=["Ls"])
        BT = sb("BT", [128, 128])
        BT3 = BT[:].rearrange("p (s j) -> p s j", j=4)
        P.pool(lambda e: e.memset(BT[:], 1.0), writes=["BT"])
        P.pool(lambda e: e.affine_select(BT3, BT3, [[4, 32], [1, 4]], ALU.is_ge, 0.0,
                                         base=0, channel_multiplier=-1), reads=["BT"], writes=["BT"])
        P.pool(lambda e: e.affine_select(BT3, BT3, [[-4, 32], [0, 4]], ALU.is_ge, 0.0,
                                         base=0, channel_multiplier=1), reads=["BT"], writes=["BT"])
        maskN = sb("maskN", [128, 128])
        P.pool(lambda e: e.tensor_scalar(maskN[:], BT[:], -1.0, -NEG, ALU.add, ALU.mult), reads=["BT"], writes=["maskN"])
        qbd = sb("qbd", [128, 4, NS, 2, 4], BF16)
        P.pool(lambda e: e.memset(qbd[:], 0.0), writes=["qbd"])
        for h2 in range(2):
            pr = slice(64 * h2, 64 * h2 + 64)
            for hp in range(4):
                P.pool(lambda e, pr=pr, h2=h2, hp=hp: e.tensor_copy(
                    qbd[pr, hp, :, h2, :], qT_s[pr, hp, 0:TK].rearrange("p (s q) -> p s q", q=4)),
                    reads=["qT_s", "qbd"], writes=["qbd"])
        Dsb = sb("Dsb", [128, NPAGES, NH])
        tot5 = sb("tot5", [128, NPAGES, NH])
        pf_flat = pf[:].rearrange("p c h -> p (c h)")
        D_flat = Dsb[:].rearrange("p c h -> p (c h)")
        tot_flat = tot5[:].rearrange("p c h -> p (c h)")
        ncols = NPAGES * NH
        for c0 in range(0, ncols, 512):
            cw = min(512, ncols - c0)
            keys = [("pf", c) for c in range(NPAGES)]
            P.pe(lambda e, c0=c0, cw=cw: e.matmul(pS5[0][:, 0:cw], Ls[:], pf_flat[:, c0:c0 + cw], start=True, stop=True),
                 reads=keys + ["Ls"], writes=["pS5_0"])
            P.act(lambda e, c0=c0, cw=cw: e.copy(D_flat[:, c0:c0 + cw], pS5[0][:, 0:cw]), reads=["pS5_0"], writes=["Dsb"])
            P.pe(lambda e, c0=c0, cw=cw: e.matmul(pS5[1][:, 0:cw], ones_f[:], pf_flat[:, c0:c0 + cw], start=True, stop=True),
                 reads=keys + ["ones_f"], writes=["pS5_1"])
            P.act(lambda e, c0=c0, cw=cw: e.copy(tot_flat[:, c0:c0 + cw], pS5[1][:, 0:cw]), reads=["pS5_1"], writes=["tot5"])
        D4 = Dsb[:].rearrange("p (s g) h -> p s g h", g=NPG)
        T4 = tot5[:].rearrange("p (s g) h -> p s g h", g=NPG)
        lat = sb("lat", [128, NS, NH])
        P.dve(lambda e: e.memset(lat[:], 0.0), writes=["lat"])
        for pg in range(NPG - 2, -1, -1):
            P.dve(lambda e, pg=pg: e.tensor_tensor(lat[:], lat[:], T4[:, :, pg + 1, :], ALU.add),
                  reads=["lat", "tot5"], writes=["lat"])
            P.dve(lambda e, pg=pg: e.tensor_tensor(D4[:, :, pg, :], D4[:, :, pg, :], lat[:], ALU.add),
                  reads=["lat", "Dsb"], writes=["Dsb"])
        P.pe(lambda e: e.matmul(pb5[:, 0:NH], BT[:], lf_s[:], start=True, stop=True),
             reads=["BT", "lf_s"], writes=["pb5"])
        negnq = sb("negnq", [128, NH])
        P.act(lambda e: e.activation(negnq[:], pb5[:, 0:NH], AF.Copy, scale=-1.0), reads=["pb5"], writes=["negnq"])
        tmpN = sb("tmpN", [128, 4, NS, 2, 4])
        PTn = sb("PTn", [128, NH, NS, 4], BF16)
        for hp in range(4):
            P.pe(lambda e, hp=hp: e.matmul(pon5[:, hp * 8 * NS:(hp + 1) * 8 * NS], kT_s[:, hp, :],
                                           qbd[:, hp, :, :, :].rearrange("p s a q -> p (s a q)"), start=True, stop=True),
                 reads=["kT_s", "qbd"], writes=["pon5"])
        for hp in range(4):
            P.dve(lambda e, hp=hp: e.tensor_tensor(
                tmpN[:, hp, :, :, :], pon5[:, hp * 8 * NS:(hp + 1) * 8 * NS].rearrange("p (s a q) -> p s a q", a=2, q=4),
                maskN[:, 0:TK].rearrange("p (s q) -> p s q", q=4).unsqueeze(2).broadcast_to([128, NS, 2, 4]), ALU.add),
                reads=["pon5", "maskN"], writes=["tmpN"])
            P.dve(lambda e, hp=hp: e.tensor_tensor(
                tmpN[:, hp, :, :, :], tmpN[:, hp, :, :, :],
                negnq[:, 2 * hp:2 * hp + 2].unsqueeze(1).unsqueeze(3).broadcast_to([128, NS, 2, 4]), ALU.add),
                reads=["tmpN", "negnq"], writes=["tmpN"])
            P.act(lambda e, hp=hp: e.activation(
                PTn[:, 2 * hp:2 * hp + 2, :, :].rearrange("p a s q -> p s a q"), tmpN[:, hp, :, :, :], AF.Exp),
                reads=["tmpN"], writes=["PTn"])
        for h in range(NH):
            P.pe(lambda e, h=h: e.matmul(pon5[0:HD + 1, h * TK:(h + 1) * TK], Vb_s[:, h, 0:HD + 1],
                                         PTn[:, h, :, :].rearrange("p s q -> p (s q)"), start=True, stop=True),
                 reads=["Vb_s", "PTn", "tmpN"], writes=["pon5"])
        osn = sb("osn", [HD + 1, NH * TK])
        P.act(lambda e: e.copy(osn[:], pon5[0:HD + 1, 0:NH * TK]), reads=["pon5"], writes=["osn"])
        NK = 6
        kst = [sb(f"kst{i}", [128, AW]) for i in range(NK)]
        vst = [sb(f"vst{i}", [128, AW]) for i in range(NK)]
        KTb = [sb(f"KTb{i}", [128, 4, 128], BF16) for i in range(2)]
        Vp = [sb(f"Vp{i}", [128, NPG, NH, HD + 2], BF16) for i in range(2)]
        PT5 = [sb(f"PT5_{i}", [128, NPG, NH, 4], BF16) for i in range(2)]
        tmpS = sb("tmpS", [128, NPG, NH, 4])
        for i in range(2):
            P.pool(lambda e, i=i: e.memset(Vp[i][:], 1.0), writes=[f"Vp{i}"])

        def pv(s_):
            i2 = s_ % 2
            for h in range(NH):
                for pg in range(NPG):
                    P.pe(lambda e, h=h, pg=pg, i2=i2, s_=s_: e.matmul(
                        po5[0:HD + 1, h * TK + s_ * 4:h * TK + s_ * 4 + 4], Vp[i2][:, pg, h, 0:HD + 1], PT5[i2][:, pg, h, :],
                        start=(pg == 0), stop=(pg == NPG - 1)), reads=[f"Vp{i2}", f"PT5_{i2}"], writes=["po5"])

        for s_ in range(NS + 1):
            if s_ < NS:
                i2 = s_ % 2
                for pg in range(NPG):
                    c = s_ * NPG + pg
                    ik = c % NK
                    gather(kst[ik][:], ck, c, [f"kst{ik}"], sem=f"kg{ik}")
                    gather(vst[ik][:], cv, c, [f"vst{ik}"], sem=f"vg{ik}")
                    pk_ = pKT[c % 2]
                    pkk = f"pKT{c % 2}"
                    kb_ = KTb[c % 2]
                    kbk = f"KTb{c % 2}"
                    for hp in range(4):
                        P.pe(lambda e, pk_=pk_, hp=hp, ik=ik: e.transpose(
                            pk_[:, hp * 128:(hp + 1) * 128], kst[ik][:, hp * 128:(hp + 1) * 128], ident_f[:]),
                            reads=[f"kst{ik}", "ident_f"], writes=[pkk])
                    P.act(lambda e, pk_=pk_, kb_=kb_: e.copy(kb_[:].rearrange("p c k -> p (c k)"), pk_[:]),
                          reads=[pkk], writes=[kbk])
                    for hp in range(4):
                        P.pe(lambda e, kb_=kb_, hp=hp, pg=pg, i2=i2, s_=s_: e.matmul(
                            pS5[i2][:, pg * 32 + hp * 8:pg * 32 + hp * 8 + 8], kb_[:, hp, :],
                            qbd[:, hp, s_, :, :].rearrange("p a q -> p (a q)"), start=True, stop=True),
                            reads=[kbk, "qbd"], writes=[f"pS5_{i2}"])
                    P.dve(lambda e, ik=ik, pg=pg, i2=i2: e.tensor_copy(
                        Vp[i2][:, pg, :, 0:HD], vst[ik][:].rearrange("p (h d) -> p h d", h=NH)),
                        reads=[f"vst{ik}"], writes=[f"Vp{i2}"])
                P.dve(lambda e, i2=i2, s_=s_: e.tensor_tensor(
                    tmpS[:], pS5[i2][:].rearrange("p (g h q) -> p g h q", h=NH, q=4),
                    Dsb[:, s_ * NPG:(s_ + 1) * NPG, :].unsqueeze(3).broadcast_to([128, NPG, NH, 4]), ALU.add),
                    reads=[f"pS5_{i2}", "Dsb"], writes=["tmpS"])
                P.act(lambda e, i2=i2: e.activation(PT5[i2][:].rearrange("p g h q -> p (g h q)"),
                                                    tmpS[:].rearrange("p g h q -> p (g h q)"), AF.Exp),
                      reads=["tmpS"], writes=[f"PT5_{i2}"])
            if s_ >= 1:
                pv(s_ - 1)
        osb5 = sb("osb5", [HD + 1, NH * TK])
        rbc5 = sb("rbc5", [HD, NH * TK])
        attT_s = sb("attT_s", [HD, NH, 128], BF16)
        P.pool(lambda e: e.memset(attT_s[:], 0.0), writes=["attT_s"])
        P.act(lambda e: e.copy(osb5[:], po5[0:HD + 1, 0:NH * TK]), reads=["po5"], writes=["osb5"])
        P.dve(lambda e: e.tensor_tensor(osb5[:], osb5[:], osn[:], ALU.add), reads=["osb5", "osn"], writes=["osb5"])
        P.dve(lambda e: e.reciprocal(osb5[HD:HD + 1, :], osb5[HD:HD + 1, :]), reads=["osb5"], writes=["osb5"])
        P.pe(lambda e: e.matmul(pb5[0:HD, 0:NH * TK], ones_f[HD:HD + 1, 0:HD], osb5[HD:HD + 1, :], start=True, stop=True),
             reads=["osb5", "ones_f"], writes=["pb5"])
        P.act(lambda e: e.copy(rbc5[:], pb5[0:HD, 0:NH * TK]), reads=["pb5"], writes=["rbc5"])
        P.dve(lambda e: e.tensor_tensor(attT_s[:, :, 0:TK], osb5[0:HD, :].rearrange("p (h t) -> p h t", h=NH),
                                        rbc5[:].rearrange("p (h t) -> p h t", h=NH), ALU.mult),
              reads=["osb5", "rbc5", "attT_s"], writes=["attT_s"])
        P.dma(att_d_s[:, :, :], attT_s[:], reads=["attT_s"], writes=[("att_d", "s")])
        phase_end(s5)
        if KSTAGE == 5:
            P.barrier(); P.emit()
            return nc
        s4 = ExitStack()
        cur[0] = s4
        pst = [ps(f"pst{i}_4", [128, 512]) for i in range(2)]
        pmm = [ps(f"pmm{i}_4", [128, 512]) for i in range(4)]
        pS = [ps(f"pS{i}_4", [128, 512]) for i in range(2)]
        wglu = sb("wglu", [128, 4, SW], BF16)
        wout_a = sb("wout_a", [HD, NH, D], BF16)
        wout_s = sb("wout_s", [128, 4, D], BF16)
        wpg = sb("wpg", [128, 8, D], BF16)
        wpe = sb("wpe", [128, 2, D], BF16)
        P.dma(wglu[:], wglu_d.rearrange("(c p) n -> p c n", p=128), reads=[("wsc", "glu")], writes=["wglu"])
        P.dma(wout_a[:], wout_d[0:AW, :].rearrange("(h p) n -> p h n", p=HD), reads=[("wsc", "out")], writes=["wout_a"])
        P.dma(wout_s[:], wout_d[AW:D, :].rearrange("(c p) n -> p c n", p=128), reads=[("wsc", "out")], writes=["wout_s"])
        P.dma(wpg[:], wpg_d.rearrange("(c p) n -> p c n", p=128), reads=[("wsc", "pg")], writes=["wpg"])
        P.dma(wpe[:], wpe_d.rearrange("(c p) n -> p c n", p=128), reads=[("wsc", "pe")], writes=["wpe"])
        bcs = {}
        for nm, src in (("g0", ln_in_g), ("b0", ln_in_b), ("g1", ln1_g), ("b1", ln1_b), ("g2", ln2_g), ("b2", ln2_b)):
            t_ = sb("bc_" + nm, [128, D])
            P.dma(t_[:], src.partition_broadcast(128), writes=["bc_" + nm], q="gpsimd")
            bcs[nm] = t_
        bpg_row = sb("bpg_row", [1, D])
        P.dma(bpg_row[:], b_pg.rearrange("(o n) -> o n", o=1), writes=["bpg_row"])
        bglu_col = sb("bglu_col", [128, 4])
        P.dma(bglu_col[:], b_glu.rearrange("(c p) -> p c", p=128), writes=["bglu_col"], allow_slow_non_contiguous=True)

        attb = sb("attb", [HD, NH, 512], BF16)
        ygb4 = sb("ygb4", [128, 4, 512], BF16)
        gate_b = sb("gate_b", [128, 512], BF16)
        ssmT = sb("ssmT", [128, 4, 512], BF16)
        xt4 = sb("xt4", [128, 4, D])
        o2 = sb("o2", [128, 4, D])
        pt4 = sb("pt4", [128, 4, PLE])
        pT = sb("pT4", [128, 2, 512], BF16)
        h1T = sb("h1T", [128, 8, 512], BF16)
        actT = sb("actT", [128, 32, 512], BF16)
        rl = [sb(f"rl{i}", [128, 512], BF16) for i in range(2)]
        tmp1 = [sb(f"tmp1_{i}", [128, 512]) for i in range(2)]
        tmp2 = [sb(f"tmp2_{i}", [128, 512]) for i in range(2)]
        wupt = [sb(f"wupt{i}", [128, 8, 128], BF16) for i in range(3)]
        wdnt = [sb(f"wdnt{i}", [128, 512], BF16) for i in range(4)]
        stats4 = sb("stats4", [128, 2, 6])
        mv4 = sb("mv4", [128, 2])
        rstd4 = sb("rstd4", [128, 1])
        nmr4 = sb("nmr4", [128, 1])
        cnt4 = {"wup": 0, "wdn": 0, "t1": 0, "t2": 0, "rl": 0}

        def ln_tile(buf, key, tt, gname, bname):
            for c in range(2):
                P.dve(lambda e, c=c: e.bn_stats(stats4[:, c, :], buf[:, tt, c * 512:(c + 1) * 512]),
                      reads=[(key, tt)], writes=[("stats4", c)])
            P.dve(lambda e: e.bn_aggr(mv4[:], stats4[:].rearrange("p a b -> p (a b)")),
                  reads=[("stats4", 0), ("stats4", 1)], writes=["mv4"])
            P.dve(lambda e: e.tensor_scalar(rstd4[:], mv4[:, 1:2], EPS, None, ALU.add), reads=["mv4"], writes=["rstd4"])
            P.act(lambda e: e.activation(rstd4[:], rstd4[:], AF.Sqrt), reads=["rstd4"], writes=["rstd4"])
            P.dve(lambda e: e.reciprocal(rstd4[:], rstd4[:]), reads=["rstd4"], writes=["rstd4"])
            P.dve(lambda e: e.scalar_tensor_tensor(nmr4[:], mv4[:, 0:1], -1.0, rstd4[:], ALU.mult, ALU.mult),
                  reads=["mv4", "rstd4"], writes=["nmr4"])
            P.act(lambda e: e.activation(buf[:, tt, :], buf[:, tt, :], AF.Identity, bias=nmr4[:, 0:1], scale=rstd4[:, 0:1]),
                  reads=["nmr4", "rstd4", (key, tt)], writes=[(key, tt)])
            P.dve(lambda e: e.tensor_tensor(buf[:, tt, :], buf[:, tt, :], bcs[gname][:], ALU.mult),
                  reads=[(key, tt), "bc_" + gname], writes=[(key, tt)])
            P.dve(lambda e: e.tensor_tensor(buf[:, tt, :], buf[:, tt, :], bcs[bname][:], ALU.add),
                  reads=[(key, tt), "bc_" + bname], writes=[(key, tt)])

        def proj_tm(ntt, ktiles, consumer, extra=None):
            for nh in range(2):
                for ki, (lf, rf, lkeys) in enumerate(ktiles):
                    rap, rkeys = rf(nh)
                    for tt in range(ntt):
                        P.pe(lambda e, tt=tt, lf=lf, rap=rap, ki=ki: e.matmul(
                            pmm[tt][:], lf(tt), rap, start=(ki == 0), stop=(ki == len(ktiles) - 1 and extra is None)),
                            reads=list(lkeys) + list(rkeys), writes=[f"pmm{tt}"])
                if extra is not None:
                    for tt in range(ntt):
                        extra(tt, nh)
                for tt in range(ntt):
                    consumer(tt, nh, pmm[tt], f"pmm{tt}")

        def next_buf(name, bufs):
            i = cnt4[name] % len(bufs)
            cnt4[name] += 1
            return bufs[i], f"{name}{i}"

        def post_block(I, ntt, x_src, p_src, att_src, yg_src, y_dst):
            tsl = lambda tt: slice(tt * 128, (tt + 1) * 128)
            ncol = ntt * 128
            P.dma(attb[:, :, 0:ncol], att_src, reads=[("att_d", I)], writes=["attb"], slot="attb")
            P.dma(ygb4[:, :, 0:ncol], yg_src, reads=["yg_d"], writes=["ygb4"], slot="ygb4")
            P.dma(xt4[:, 0:ntt, :], x_src, writes=[("xt4", tt) for tt in range(ntt)], slot="xt4")
            P.dma(pt4[:, 0:ntt, :], p_src, writes=["pt4"], q="gpsimd", slot="pt4")
            for nt in range(4):
                psg = pS[nt % 2]
                pgk = f"pS{nt % 2}"
                for ct in range(4):
                    P.pe(lambda e, psg=psg, nt=nt, ct=ct: e.matmul(psg[:, 0:ncol], wglu[:, ct, nt * 128:(nt + 1) * 128], ygb4[:, ct, 0:ncol],
                                                                  start=(ct == 0), stop=(ct == 3)),
                         reads=["wglu", "ygb4"], writes=[pgk])
                P.act(lambda e, psg=psg, nt=nt: e.activation(gate_b[:, 0:ncol], psg[:, 0:ncol], AF.Sigmoid, bias=bglu_col[:, nt:nt + 1]),
                      reads=[pgk, "bglu_col"], writes=["gate_b"])
                P.dve(lambda e, nt=nt: e.tensor_tensor(ssmT[:, nt, 0:ncol], ygb4[:, nt, 0:ncol], gate_b[:, 0:ncol], ALU.mult),
                      reads=["gate_b", "ygb4"], writes=[("ssmT", nt)])
            for tt in range(ntt):
                ln_tile(xt4, "xt4", tt, "g0", "b0")
            for kk in range(2):
                pp_ = pst[kk]
                for tt in range(ntt):
                    P.pe(lambda e, pp_=pp_, tt=tt, kk=kk: e.transpose(pp_[:, tsl(tt)], pt4[:, tt, kk * 128:(kk + 1) * 128], ident_f[:]),
                         reads=["pt4", "ident_f"], writes=[f"pst{kk}"])
                P.act(lambda e, pp_=pp_, kk=kk: e.copy(pT[:, kk, 0:ntt * 128], pp_[:, 0:ntt * 128]),
                      reads=[f"pst{kk}"], writes=[("pT", kk)])
            kt = []
            for h in range(NH):
                kt.append((lambda tt, h=h: attb[:, h, tsl(tt)], lambda nh, h=h: (wout_a[:, h, nh * 512:(nh + 1) * 512], ["wout_a"]), ["attb"]))
            for ct in range(4):
                kt.append((lambda tt, ct=ct: ssmT[:, ct, tsl(tt)], lambda nh, ct=ct: (wout_s[:, ct, nh * 512:(nh + 1) * 512], ["wout_s"]),
                           [("ssmT", ct)]))

            def cons_out(tt, nh, pm_, pk_):
                t1_, t1k = next_buf("t1", tmp1)
                P.act(lambda e: e.copy(t1_[:], pm_[:]), reads=[pk_], writes=[t1k])
                P.dve(lambda e: e.scalar_tensor_tensor(xt4[:, tt, nh * 512:(nh + 1) * 512], xt4[:, tt, nh * 512:(nh + 1) * 512],
                                                       ALPHA, t1_[:], ALU.mult, ALU.add),
                      reads=[t1k, ("xt4", tt)], writes=[("xt4", tt)])
            proj_tm(ntt, kt, cons_out)
            for tt in range(ntt):
                ln_tile(xt4, "xt4", tt, "g1", "b1")
            for ct in range(8):
                pp_ = pst[ct % 2]
                for tt in range(ntt):
                    P.pe(lambda e, pp_=pp_, tt=tt, ct=ct: e.transpose(pp_[:, tsl(tt)], xt4[:, tt, ct * 128:(ct + 1) * 128], ident_f[:]),
                         reads=[("xt4", tt), "ident_f"], writes=[f"pst{ct % 2}"])
                P.act(lambda e, pp_=pp_, ct=ct: e.copy(h1T[:, ct, 0:ntt * 128], pp_[:, 0:ntt * 128]),
                      reads=[f"pst{ct % 2}"], writes=[("h1T", ct)])
            h1keys = [("h1T", ct) for ct in range(8)]
            for ft in range(32):
                wt_, wk_ = next_buf("wup", wupt)
                P.dma(wt_[:], wup_d[:, ft * 128:(ft + 1) * 128].rearrange("(c p) f -> p c f", p=128),
                      reads=[("wsc", "up")], writes=[wk_], q=("sync" if int(wk_[-1]) % 2 == 0 else "gpsimd"), slot=wk_)
                psu = pS[ft % 2]
                puk = f"pS{ft % 2}"
                for ct in range(8):
                    P.pe(lambda e, psu=psu, wt_=wt_, ct=ct: e.matmul(psu[:, 0:ncol], wt_[:, ct, :], h1T[:, ct, 0:ncol],
                                                                    start=(ct == 0), stop=(ct == 7)),
                         reads=[wk_] + h1keys, writes=[puk])
                r_, rk_ = next_buf("rl", rl)
                P.act(lambda e, psu=psu, r_=r_: e.activation(r_[:, 0:ncol], psu[:, 0:ncol], AF.Relu), reads=[puk], writes=[rk_])
                P.pool(lambda e, r_=r_, ft=ft: e.tensor_tensor(actT[:, ft, 0:ncol], r_[:, 0:ncol], r_[:, 0:ncol], ALU.mult),
                       reads=[rk_], writes=[("actT", ft)])
            kt = []
            for ft in range(32):
                def rf(nh, ft=ft):
                    wt_, wk_ = next_buf("wdn", wdnt)
                    P.dma(wt_[:], wdn_d[ft * 128:(ft + 1) * 128, nh * 512:(nh + 1) * 512], reads=[("wsc", "dn")], writes=[wk_],
                          q=("sync" if int(wk_[-1]) % 2 == 0 else "gpsimd"), slot=wk_)
                    return wt_[:], [wk_]
                kt.append((lambda tt, ft=ft: actT[:, ft, tsl(tt)], rf, [("actT", ft)]))

            def cons_dn(tt, nh, pm_, pk_):
                t1_, t1k = next_buf("t1", tmp1)
                P.act(lambda e: e.copy(t1_[:], pm_[:]), reads=[pk_], writes=[t1k])
                P.dve(lambda e: e.scalar_tensor_tensor(o2[:, tt, nh * 512:(nh + 1) * 512], xt4[:, tt, nh * 512:(nh + 1) * 512],
                                                       ALPHA, t1_[:], ALU.mult, ALU.add),
                      reads=[t1k, ("xt4", tt)], writes=[("o2", tt)])
            proj_tm(ntt, kt, cons_dn)
            ktg = [(lambda tt, ct=ct: h1T[:, ct, tsl(tt)], lambda nh, ct=ct: (wpg[:, ct, nh * 512:(nh + 1) * 512], ["wpg"]), [("h1T", ct)])
                   for ct in range(8)]

            def bias_mm(tt, nh):
                P.pe(lambda e: e.matmul(pmm[tt][:], ones_f[0:1, :], bpg_row[0:1, nh * 512:(nh + 1) * 512], start=False, stop=True),
                     reads=["ones_f", "bpg_row"], writes=[f"pmm{tt}"])

            kte = [(lambda tt, kk=kk: pT[:, kk, tsl(tt)], lambda nh, kk=kk: (wpe[:, kk, nh * 512:(nh + 1) * 512], ["wpe"]), [("pT", kk)])
                   for kk in range(2)]
            pe_ps = [(pS[0], "pS0"), (pS[1], "pS1"), (pst[0], "pst0"), (pst[1], "pst1")]
            for nh in range(2):
                for ki, (lf, rf, lkeys) in enumerate(ktg):
                    rap, rkeys = rf(nh)
                    for tt in range(ntt):
                        P.pe(lambda e, tt=tt, lf=lf, rap=rap, ki=ki: e.matmul(pmm[tt][:], lf(tt), rap, start=(ki == 0), stop=False),
                             reads=list(lkeys) + list(rkeys), writes=[f"pmm{tt}"])
                for tt in range(ntt):
                    bias_mm(tt, nh)
                for ki, (lf, rf, lkeys) in enumerate(kte):
                    rap, rkeys = rf(nh)
                    for tt in range(ntt):
                        P.pe(lambda e, tt=tt, lf=lf, rap=rap, ki=ki: e.matmul(pe_ps[tt][0][:], lf(tt), rap, start=(ki == 0), stop=(ki == 1)),
                             reads=list(lkeys) + list(rkeys), writes=[pe_ps[tt][1]])
                for tt in range(ntt):
                    t1_, t1k = next_buf("t1", tmp1)
                    t2_, t2k = next_buf("t2", tmp2)
                    P.act(lambda e, t1_=t1_, tt=tt: e.activation(t1_[:], pmm[tt][:], AF.Sigmoid), reads=[f"pmm{tt}"], writes=[t1k])
                    P.act(lambda e, t2_=t2_, tt=tt: e.copy(t2_[:], pe_ps[tt][0][:]), reads=[pe_ps[tt][1]], writes=[t2k])
                    P.dve(lambda e, t1_=t1_, t2_=t2_: e.tensor_tensor(t2_[:], t2_[:], t1_[:], ALU.mult), reads=[t1k, t2k], writes=[t2k])
                    P.dve(lambda e, t2_=t2_, tt=tt, nh=nh: e.tensor_tensor(o2[:, tt, nh * 512:(nh + 1) * 512],
                                                                          o2[:, tt, nh * 512:(nh + 1) * 512], t2_[:], ALU.add),
                          reads=[t2k, ("o2", tt)], writes=[("o2", tt)])
            for tt in range(ntt):
                ln_tile(o2, "o2", tt, "g2", "b2")
            P.dma(y_dst, o2[:, 0:ntt, :], reads=[("o2", tt) for tt in range(ntt)], slot="o2")

        for I in range(NSB):
            rs = slice(I * 512, (I + 1) * 512)
            post_block(I, 4,
                       x_p[rs, :].rearrange("(t p) d -> p t d", p=128),
                       pp_p[rs, :].rearrange("(t p) d -> p t d", p=128),
                       att_d[:, :, rs], yg_d[:, :, rs].rearrange("g p t -> p g t"),
                       y_p[rs, :].rearrange("(t p) d -> p t d", p=128))

        post_block("s", 1,
                   x_s.rearrange("(t p) d -> p t d", p=128),
                   pp_s.rearrange("(t p) d -> p t d", p=128),
                   att_d_s[:, :, :], yg_d_s.rearrange("g p t -> p g t"),
                   y_s.rearrange("(t p) d -> p t d", p=128))

        P.barrier()
        P.emit()
        s4.close()
    return nc


_CACHE = {}
_LAST = {}


DBG = bool(int(os.environ.get('KDBG', '0')))


def _get_nc(T, NSEQ, NPHYS):
    key = (T, NSEQ, NPHYS)
    if key not in _CACHE:
        _CACHE[key] = build(T, NSEQ, NPHYS, dbg=DBG)
    return _CACHE[key]


def run_cores(inputs, T, NSEQ, NPHYS, ncores=8):
    nc = _get_nc(T, NSEQ, NPHYS)
    f = lambda a: np.ascontiguousarray(a, dtype=np.float32)
    in_maps = []
    nb = inputs["x_prompt"].shape[0]
    ST = NSEQ * 4
    ckf = f(inputs["cache_k"][0]).reshape(NPHYS * 128, AW)
    cvf = f(inputs["cache_v"][0]).reshape(NPHYS * 128, AW)
    clff = f(inputs["cache_logf"][0]).reshape(NPHYS * 128, NH)
    shared = {
        "w_in": f(inputs["w_in"][0]),
        "ln_in_g": f(inputs["ln_in_g"]),
        "ln_in_b": f(inputs["ln_in_b"]),
        "b_f": f(inputs["b_f"][0]),
        "lam_re": f(inputs["lam_re"][0]), "lam_im": f(inputs["lam_im"][0]), "log_dt": f(inputs["log_dt"][0]),
        "b_re": f(inputs["b_re"][0]), "b_im": f(inputs["b_im"][0]),
        "c_re": f(inputs["c_re"][0]), "c_im": f(inputs["c_im"][0]), "d_skip": f(inputs["d_skip"][0]),
        "w_glu": f(inputs["w_glu"][0]), "b_glu": f(inputs["b_glu"][0]), "w_out": f(inputs["w_out"][0]),
        "ln1_g": f(inputs["ln1_g"][0]), "ln1_b": f(inputs["ln1_b"][0]), "w_up": f(inputs["w_up"][0]),
        "w_down": f(inputs["w_down"][0]), "w_pe": f(inputs["w_pe"][0]), "w_pg": f(inputs["w_pg"][0]),
        "b_pg": f(inputs["b_pg"][0]), "ln2_g": f(inputs["ln2_g"][0]), "ln2_b": f(inputs["ln2_b"][0]),
        "ck": ckf, "cv": cvf, "clf": clff,
    }
    for c in range(ncores):
        b = c % nb
        sl = slice(c * NSEQ, (c + 1) * NSEQ)
        xs = np.zeros((128, D), np.float32)
        xs[0:ST] = inputs["x_sample"][sl].reshape(ST, D)
        ps_ = np.zeros((128, PLE), np.float32)
        ps_[0:ST] = inputs["p_sample"][0, sl].reshape(ST, PLE)
        m = dict(shared)
        m.update({
            "x_p": f(inputs["x_prompt"][b]),
            "pp_p": f(inputs["p_prompt"][0, b]),
            "x_s": xs, "pp_s": ps_,
            "st_re": f(inputs["state_re"][0, sl]).reshape(NSEQ, NG * SP),
            "st_im": f(inputs["state_im"][0, sl]).reshape(NSEQ, NG * SP),
            "pt": np.ascontiguousarray(inputs["page_table"][sl], dtype=np.int32).reshape(NSEQ * 16),
        })
        in_maps.append(m)
    res = run_bass_kernel_spmd(nc, in_maps, core_ids=list(range(ncores)))
    _LAST["r"] = res.results
    return res.results


def kernel(**inputs):
    T = inputs["x_prompt"].shape[1]
    NSEQ = inputs["x_sample"].shape[0] // 8
    NPHYS = inputs["cache_k"].shape[1]
    ST = NSEQ * 4
    r = run_cores(inputs, T, NSEQ, NPHYS)
    nb = inputs["x_prompt"].shape[0]
    db = inputs["x_sample"].shape[0]
    g = lambda c, n: np.asarray(r[c][n], dtype=np.float32)
    y_prompt = np.stack([g(b, "y_p") for b in range(nb)])
    k_prompt = np.stack([g(b, "k_p").reshape(T, NH, HD) for b in range(nb)])[None]
    v_prompt = np.stack([g(b, "v_p").reshape(T, NH, HD) for b in range(nb)])[None]
    lf_prompt = np.stack([g(b, "lf_p") for b in range(nb)])[None]
    sre_p = np.stack([g(b, "ssm_re_p") for b in range(nb)])[None]
    sim_p = np.stack([g(b, "ssm_im_p") for b in range(nb)])[None]
    y_sample = np.concatenate([g(c, "y_s")[0:ST].reshape(NSEQ, 4, D) for c in range(8)])
    k_sample = np.concatenate([g(c, "k_s").reshape(NSEQ, 4, NH, HD) for c in range(8)])[None]
    v_sample = np.concatenate([g(c, "v_s").reshape(NSEQ, 4, NH, HD) for c in range(8)])[None]
    lf_sample = np.concatenate([g(c, "lf_sd").reshape(NSEQ, 4, NH) for c in range(8)])[None]
    sre_s = np.concatenate([g(c, "ssm_re_s").reshape(NSEQ, NG, SP) for c in range(8)])[None]
    sim_s = np.concatenate([g(c, "ssm_im_s").reshape(NSEQ, NG, SP) for c in range(8)])[None]
    return (y_prompt, y_sample, k_prompt, v_prompt, lf_prompt, sre_p, sim_p,
            k_sample, v_sample, lf_sample, sre_s, sim_s)
```

```python
import math
import os
KSTAGE = int(os.environ.get('KSTAGE', '99'))
KSKIP = os.environ.get('KSKIP', '').split(',')
from contextlib import ExitStack
import numpy as np
import concourse.bass as bass
import concourse.mybir as mybir
from concourse.bass_utils import run_bass_kernel_spmd

F32 = mybir.dt.float32
BF16 = mybir.dt.bfloat16
I32 = mybir.dt.int32
U32 = mybir.dt.uint32
AF = mybir.ActivationFunctionType
ALU = mybir.AluOpType
AX = mybir.AxisListType

D = 1024
NH = 8
HD = 64
AW = 512
SW = 512
NG = 32
SG = 16
SP = 64
DFF = 4096
PLE = 256
INW = 2056
ALPHA = 2.0 ** 0.25
EPS = 1e-5
NEG = -60000.0


class Prog:
    STREAMS = ("sync", "scalar", "vector", "gpsimd", "tensor")

    def __init__(self, nc, es):
        self.nc = nc
        self.ops = {s: [] for s in self.STREAMS}
        self.sems = {}
        self.cnt = {}
        self.sem_stream = {}
        for name, stream in (("sync", "sync"), ("act", "scalar"), ("dve", "vector"), ("pool", "gpsimd"),
                             ("pooldma", "gpsimd"), ("pe", "tensor"), ("actdma", "scalar")):
            self.sems[name] = es.enter_context(nc.semaphore("s_" + name))
            self.cnt[name] = 0
            self.sem_stream[name] = stream
        self.waited = {}
        self.lastw = {}
        self.readers = {}
        self.es = es

    def add_sem(self, name, stream):
        self.sems[name] = self.es.enter_context(self.nc.semaphore("s_" + name))
        self.cnt[name] = 0
        self.sem_stream[name] = stream

    def op(self, sem, fn, reads=(), writes=(), inc=1):
        stream = self.sem_stream[sem]
        deps = set()
        for b in reads:
            if b in self.lastw:
                deps.add(self.lastw[b])
        for b in writes:
            if b in self.lastw:
                deps.add(self.lastw[b])
            for t in self.readers.get(b, ()):
                deps.add(t)
        waits = []
        for (ps, pc) in sorted(deps):
            if ps == "pe" and sem == "pe":
                continue
            key = (stream, ps)
            if self.waited.get(key, 0) >= pc:
                continue
            self.waited[key] = pc
            waits.append((ps, pc))
        self.cnt[sem] += inc
        tok = (sem, self.cnt[sem])
        self.ops[stream].append((waits, fn, sem, inc))
        for b in writes:
            self.lastw[b] = tok
            self.readers[b] = []
        for b in reads:
            self.readers.setdefault(b, []).append(tok)
        return tok

    def dma(self, out, in_, reads=(), writes=(), q="sync", slot=None, **kw):
        if slot is None:
            sem = {"sync": "sync", "gpsimd": "pooldma", "scalar": "actdma"}[q]
        else:
            sem = "d_" + slot
            if sem not in self.sems:
                self.add_sem(sem, q)
            assert self.sem_stream[sem] == q, (sem, q)
        return self.op(sem, lambda e: e.dma_start(out=out, in_=in_, **kw), reads, writes, inc=16)

    def pe(self, fn, reads=(), writes=()):
        return self.op("pe", fn, reads, writes)

    def act(self, fn, reads=(), writes=()):
        return self.op("act", fn, reads, writes)

    def dve(self, fn, reads=(), writes=()):
        return self.op("dve", fn, reads, writes)

    def pool(self, fn, reads=(), writes=()):
        return self.op("pool", fn, reads, writes)

    def barrier(self):
        for stream in self.STREAMS:
            waits = []
            for sname, c in self.cnt.items():
                if c > 0 and self.waited.get((stream, sname), 0) < c:
                    self.waited[(stream, sname)] = c
                    waits.append((sname, c))
            self.ops[stream].append((waits, None, None, 0))

    def emit(self, last=True):
        nc = self.nc
        final = [(s, c) for s, c in self.cnt.items() if c > 0]
        ops = self.ops
        self.ops = {s: [] for s in self.STREAMS}

        def run(e, stream):
            for waits, fn, sem, inc in ops[stream]:
                for (ps, pc) in waits:
                    e.wait_ge(self.sems[ps], pc)
                if fn is not None:
                    fn(e).then_inc(self.sems[sem], inc)
            if stream == "sync" and last:
                for (s, c) in final:
                    e.wait_ge(self.sems[s], c)

        with nc.Block() as block:
            @block.sync
            def _(e):
                run(e, "sync")

            @block.scalar
            def _(e):
                run(e, "scalar")

            @block.vector
            def _(e):
                run(e, "vector")

            @block.gpsimd
            def _(e):
                run(e, "gpsimd")

            @block.tensor
            def _(e):
                run(e, "tensor")


def build(T, NSEQ, NPHYS, dbg=False):
    nc = bass.Bass("TRN2", target_bir_lowering=False)
    NB = T // 128
    NSB = T // 512
    ST = NSEQ * 4

    def din(name, shape, dt=F32):
        return nc.dram_tensor(name, list(shape), dt, kind="ExternalInput").ap()

    def dout(name, shape, dt=F32):
        return nc.dram_tensor(name, list(shape), dt, kind="ExternalOutput").ap()

    x_p = din("x_p", [T, D])
    pp_p = din("pp_p", [T, PLE])
    w_in = din("w_in", [D, INW])
    ln_in_g = din("ln_in_g", [D])
    ln_in_b = din("ln_in_b", [D])
    b_f = din("b_f", [NH])
    lam_re = din("lam_re", [NG, SP])
    lam_im = din("lam_im", [NG, SP])
    log_dt = din("log_dt", [NG])
    b_re = din("b_re", [NG, SP, SG])
    b_im = din("b_im", [NG, SP, SG])
    c_re = din("c_re", [NG, SG, SP])
    c_im = din("c_im", [NG, SG, SP])
    d_skip = din("d_skip", [NG, SG])
    w_glu = din("w_glu", [SW, SW])
    b_glu = din("b_glu", [SW])
    w_out = din("w_out", [D, D])
    ln1_g = din("ln1_g", [D])
    ln1_b = din("ln1_b", [D])
    w_up = din("w_up", [D, DFF])
    w_down = din("w_down", [DFF, D])
    w_pe = din("w_pe", [PLE, D])
    w_pg = din("w_pg", [D, D])
    b_pg = din("b_pg", [D])
    ln2_g = din("ln2_g", [D])
    ln2_b = din("ln2_b", [D])
    x_s = din("x_s", [128, D])
    pp_s = din("pp_s", [128, PLE])
    ck = din("ck", [NPHYS * 128, AW])
    cv = din("cv", [NPHYS * 128, AW])
    clf = din("clf", [NPHYS * 128, NH])
    st_re = din("st_re", [NSEQ, NG * SP])
    st_im = din("st_im", [NSEQ, NG * SP])
    pt = din("pt", [NSEQ * 16], I32)
    y_p = dout("y_p", [T // 2, D])
    y_s = dout("y_s", [128, D])
    k_s = dout("k_s", [ST, AW])
    v_s = dout("v_s", [ST, AW])
    lf_sd = dout("lf_sd", [ST, NH])
    ssm_re_s = dout("ssm_re_s", [NSEQ, NG * SP])
    ssm_im_s = dout("ssm_im_s", [NSEQ, NG * SP])
    if dbg:
        att_d_s = dout("att_d_s", [HD, NH, 128], BF16)
        yg_d_s = dout("yg_d_s", [4, 128, 128], BF16)
    else:
        att_d_s = nc.dram_tensor("att_d_s", [HD, NH, 128], BF16).ap()
        yg_d_s = nc.dram_tensor("yg_d_s", [4, 128, 128], BF16).ap()
    att_d2 = nc.dram_tensor("att_d2", [NSB * HD, NH * 512], BF16).ap()
    yg_d2 = nc.dram_tensor("yg_d2", [NSB * 128, 4 * 512], BF16).ap()
    x_post = din("x_post", [T // 2, D])
    pp_post = din("pp_post", [T // 2, PLE])
    bidx_d = din("bidx", [128, NSB], I32)
    wup_d = nc.dram_tensor("wup_d", [D, DFF], BF16).ap()
    wdn_d = nc.dram_tensor("wdn_d", [DFF, D], BF16).ap()
    wout_d = nc.dram_tensor("wout_d", [D, D], BF16).ap()
    wpg_d = nc.dram_tensor("wpg_d", [D, D], BF16).ap()
    wpe_d = nc.dram_tensor("wpe_d", [PLE, D], BF16).ap()
    wglu_d = nc.dram_tensor("wglu_d", [SW, SW], BF16).ap()

    k_p = dout("k_p", [T, AW])
    v_p = dout("v_p", [T, AW])
    lf_p = dout("lf_p", [T, NH])
    ssm_re_p = dout("ssm_re_p", [NG, SP])
    ssm_im_p = dout("ssm_im_p", [NG, SP])

    es = ExitStack()
    with es:
        P = Prog(nc, es)

        cur = [es]

        def sb(name, shape, dt=F32):
            return cur[0].enter_context(nc.sbuf_tensor(name, list(shape), dt))

        def ps(name, shape, dt=F32):
            return cur[0].enter_context(nc.psum_tensor(name, list(shape), dt))

        def phase_end(*stacks):
            P.barrier()
            P.emit(last=False)
            for st_ in stacks:
                st_.close()

        ident_f = sb("ident_f", [128, 128])
        ident_b = sb("ident_b", [128, 128], BF16)
        P.pool(lambda e: e.memset(ident_f[:], 1.0), writes=["ident_f"])
        P.pool(lambda e: e.affine_select(ident_f[:], ident_f[:], [[-1, 128]], ALU.is_equal, 0.0,
                                         base=0, channel_multiplier=1), reads=["ident_f"], writes=["ident_f"])
        P.pool(lambda e: e.tensor_copy(ident_b[:], ident_f[:]), reads=["ident_f"], writes=["ident_b"])
        triu = sb("triu", [128, 128])
        P.pool(lambda e: e.memset(triu[:], 1.0), writes=["triu"])
        P.pool(lambda e: e.affine_select(triu[:], triu[:], [[1, 128]], ALU.is_ge, 0.0,
                                         base=0, channel_multiplier=-1), reads=["triu"], writes=["triu"])
        ones_f = sb("ones_f", [128, 128])
        P.pool(lambda e: e.memset(ones_f[:], 1.0), writes=["ones_f"])

        g_in_col = sb("g_in_col", [128, 8])
        b_in_col = sb("b_in_col", [128, 8])
        P.dma(g_in_col[:], ln_in_g.rearrange("(c p) -> p c", p=128), writes=["g_in_col"],
              allow_slow_non_contiguous=True)
        P.dma(b_in_col[:], ln_in_b.rearrange("(c p) -> p c", p=128), writes=["b_in_col"],
              allow_slow_non_contiguous=True)
        bf_bc = sb("bf_bc", [128, NH])
        P.dma(bf_bc[:], b_f.partition_broadcast(128), writes=["bf_bc"])

        NS = NSEQ
        kT_s = sb("kT_s", [128, 4, 128], BF16)
        qT_s = sb("qT_s", [128, 4, 128], BF16)
        uT_s = sb("uT_s", [128, 4, 128], BF16)
        Vb_s = sb("Vb_s", [128, NH, HD + 2], BF16)
        lf_s = sb("lf_s", [128, NH])
        P.pool(lambda e: e.memset(Vb_s[:], 1.0), writes=["Vb_s"])
        NCB = 3
        cst = [sb(f"cst{i}", [128, 514]) for i in range(NCB)]
        csb = [sb(f"csb{i}", [128, 512], BF16) for i in range(NCB)]
        ncv = [0]

        def convert(jobs):
            base = ncv[0]
            ncv[0] += len(jobs)

            def load(j):
                src, cw, _ = jobs[j]
                i_ = (base + j) % NCB
                P.dma(cst[i_][:, 0:cw], src, writes=[f"cst{i_}"], q="gpsimd", slot=f"cst{i_}")
            for j in range(min(NCB - 1, len(jobs))):
                load(j)
            for j in range(len(jobs)):
                if j + NCB - 1 < len(jobs):
                    load(j + NCB - 1)
                src, cw, (kind, dst, key) = jobs[j]
                i_ = (base + j) % NCB
                if kind == "sb":
                    P.pool(lambda e, i_=i_, cw=cw, dst=dst: e.tensor_copy(dst, cst[i_][:, 0:cw]),
                           reads=[f"cst{i_}"], writes=[key])
                else:
                    P.pool(lambda e, i_=i_, cw=cw: e.tensor_copy(csb[i_][:, 0:cw], cst[i_][:, 0:cw]),
                           reads=[f"cst{i_}"], writes=[f"csb{i_}"])
                    P.dma(dst, csb[i_][:, 0:cw], reads=[f"csb{i_}"], writes=[key], q="gpsimd", slot=f"csb{i_}")

        s_u = ExitStack()
        cur[0] = s_u
        uT = sb("uT", [128, 4, T], BF16)
        s_att = ExitStack()
        cur[0] = s_att
        kT = sb("kT", [128, 4, T], BF16)
        qT = sb("qT", [128, 4, T], BF16)
        Vb = sb("Vb", [128, NB, NH, HD + 2], BF16)
        lf_all = sb("lf_all", [128, NB, NH])
        s1 = ExitStack()
        cur[0] = s1
        w_in_b = sb("w_in_b", [128, 8, INW], BF16)
        jobs = []
        for ct in range(8):
            for c4 in range(4):
                jobs.append((w_in[ct * 128:(ct + 1) * 128, c4 * 514:(c4 + 1) * 514], 514,
                             ("sb", w_in_b[:, ct, c4 * 514:(c4 + 1) * 514], ("w_in_b", ct))))
        for (nm, src, dstd, rows, cols) in (("glu", w_glu, wglu_d, SW, SW), ("out", w_out, wout_d, D, D), ("pg", w_pg, wpg_d, D, D),
                                            ("pe", w_pe, wpe_d, PLE, D)):
            for r0 in range(0, rows, 128):
                for c0 in range(0, cols, 512):
                    jobs.append((src[r0:r0 + 128, c0:c0 + 512], 512, ("dram", dstd[r0:r0 + 128, c0:c0 + 512], ("wsc", nm, r0, c0))))
        convert(jobs)
        xin = [sb(f"xin{i}", [128, 4, D]) for i in range(1)]
        stats = sb("stats", [128, 2, 6])
        mv = sb("mv", [128, 2])
        rstd = sb("rstd", [128, 1])
        nmr = sb("nmr", [128, 1])
        hT = sb("hT", [128, 8, 512], BF16)
        ostage = [sb(f"ostage{i}", [128, 512]) for i in range(2)]
        lstage = sb("lstage", [128, NH])
        P.pool(lambda e: e.memset(Vb[:], 1.0), writes=["Vb"])
        pst = [ps(f"pst{i}", [128, 512]) for i in range(2)]
        pmm = [ps(f"pmm{i}", [128, 512]) for i in range(4)]
        nmm = [0]
        nos = [0]

        def layer_norm_block(xt, xkey, ntt):
            for tt in range(ntt):
                for c in range(2):
                    P.dve(lambda e, tt=tt, c=c: e.bn_stats(stats[:, c, :], xt[:, tt, c * 512:(c + 1) * 512]),
                          reads=[xkey], writes=[("stats", c)])
                P.dve(lambda e: e.bn_aggr(mv[:], stats[:].rearrange("p a b -> p (a b)")),
                      reads=[("stats", 0), ("stats", 1)], writes=["mv"])
                P.dve(lambda e: e.tensor_scalar(rstd[:], mv[:, 1:2], EPS, None, ALU.add),
                      reads=["mv"], writes=["rstd"])
                P.act(lambda e: e.activation(rstd[:], rstd[:], AF.Sqrt), reads=["rstd"], writes=["rstd"])
                P.dve(lambda e: e.reciprocal(rstd[:], rstd[:]), reads=["rstd"], writes=["rstd"])
                P.dve(lambda e: e.scalar_tensor_tensor(nmr[:], mv[:, 0:1], -1.0, rstd[:], ALU.mult, ALU.mult),
                      reads=["mv", "rstd"], writes=["nmr"])
                P.act(lambda e, tt=tt: e.activation(xt[:, tt, :], xt[:, tt, :], AF.Identity,
                                                    bias=nmr[:, 0:1], scale=rstd[:, 0:1]),
                      reads=["nmr", "rstd", xkey], writes=[xkey])

        def to_feature_major(xt, xkey, ntt, dst, dkey, gcol, bcol):
            for ct in range(8):
                pt = pst[ct % 2]
                pk = f"pst{ct % 2}"
                for tt in range(ntt):
                    P.pe(lambda e, pt=pt, tt=tt, ct=ct: e.transpose(pt[:, tt * 128:(tt + 1) * 128],
                                                                  xt[:, tt, ct * 128:(ct + 1) * 128], ident_f[:]),
                         reads=[xkey, "ident_f"], writes=[pk])
                P.act(lambda e, pt=pt, ct=ct: e.activation(dst[:, ct, 0:ntt * 128], pt[:, 0:ntt * 128], AF.Identity,
                                                          bias=bcol[:, ct:ct + 1], scale=gcol[:, ct:ct + 1]),
                      reads=[pk, "g_in_col", "b_in_col"], writes=[(dkey, ct)])

        for sbk in range(NSB):
            xt = xin[0]
            xkey = "xin0"
            P.dma(xt[:], x_p[sbk * 512:(sbk + 1) * 512, :].rearrange("(t p) d -> p t d", p=128), writes=[xkey], slot="xin")
            layer_norm_block(xt, xkey, 4)
            to_feature_major(xt, xkey, 4, hT, "hT", g_in_col, b_in_col)
            hkeys = [("hT", ct) for ct in range(8)]
            wkeys = [("w_in_b", ct) for ct in range(8)]
            tok = slice(sbk * 512, (sbk + 1) * 512)
            for (dst, dkey, c0, scl) in ((kT, "kT", 512, 1.0), (qT, "qT", 0, 0.125), (uT, "uT", 1544, 1.0)):
                for ft in range(4):
                    pm = pmm[nmm[0] % 4]
                    pk = f"pmm{nmm[0] % 4}"
                    nmm[0] += 1
                    for ct in range(8):
                        P.pe(lambda e, pm=pm, ct=ct, ft=ft, c0=c0: e.matmul(
                            pm[:], w_in_b[:, ct, c0 + ft * 128:c0 + (ft + 1) * 128], hT[:, ct, :],
                            start=(ct == 0), stop=(ct == 7)), reads=hkeys + wkeys, writes=[pk])
                    P.act(lambda e, pm=pm, dst=dst, ft=ft, scl=scl, tok=tok: e.activation(dst[:, ft, tok], pm[:], AF.Copy, scale=scl),
                          reads=[pk], writes=[(dkey, sbk)])
            for tt in range(4):
                blk = sbk * 4 + tt
                for (c0, dram, isv) in ((512, k_p, False), (1024, v_p, True)):
                    pm = pmm[nmm[0] % 4]
                    pk = f"pmm{nmm[0] % 4}"
                    nmm[0] += 1
                    for ct in range(8):
                        P.pe(lambda e, pm=pm, ct=ct, tt=tt, c0=c0: e.matmul(
                            pm[:], hT[:, ct, tt * 128:(tt + 1) * 128], w_in_b[:, ct, c0:c0 + 512],
                            start=(ct == 0), stop=(ct == 7)), reads=hkeys + wkeys, writes=[pk])
                    os_ = ostage[nos[0] % 2]
                    ok = f"ostage{nos[0] % 2}"
                    nos[0] += 1
                    P.act(lambda e, pm=pm, os_=os_: e.copy(os_[:], pm[:]), reads=[pk], writes=[ok])
                    if isv and 'vb' not in KSKIP:
                        P.act(lambda e, pm=pm, blk=blk: e.copy(
                            Vb[:, blk, :, 0:HD], pm[:].rearrange("p (h d) -> p h d", h=NH)),
                            reads=[pk], writes=["Vb"])
                    if 'odma' not in KSKIP:
                        P.dma(dram[blk * 128:(blk + 1) * 128, :], os_[:], reads=[ok], slot=ok)
                if 'lf' in KSKIP:
                    continue
                pm = pmm[nmm[0] % 4]
                pk = f"pmm{nmm[0] % 4}"
                nmm[0] += 1
                for ct in range(8):
                    P.pe(lambda e, pm=pm, ct=ct, tt=tt: e.matmul(
                        pm[:, 0:NH], hT[:, ct, tt * 128:(tt + 1) * 128], w_in_b[:, ct, 1536:1544],
                        start=(ct == 0), stop=(ct == 7)), reads=hkeys + wkeys, writes=[pk])
                P.act(lambda e, pm=pm: e.copy(lstage[:], pm[:, 0:NH]), reads=[pk], writes=["lstage"])
                P.dve(lambda e: e.tensor_tensor(lstage[:], lstage[:], bf_bc[:], ALU.add),
                      reads=["lstage", "bf_bc"], writes=["lstage"])
                P.act(lambda e: e.activation(lstage[:], lstage[:], AF.Exp, scale=-1.0),
                      reads=["lstage"], writes=["lstage"])
                P.dve(lambda e: e.tensor_scalar(lstage[:], lstage[:], 1.0, None, ALU.add),
                      reads=["lstage"], writes=["lstage"])
                P.act(lambda e: e.activation(lstage[:], lstage[:], AF.Ln),
                      reads=["lstage"], writes=["lstage"])
                P.dve(lambda e, blk=blk: e.tensor_scalar(lf_all[:, blk, :], lstage[:], -1.0, None, ALU.mult),
                      reads=["lstage"], writes=[("lf_all", blk)])
        P.dma(lf_p.rearrange("(b p) h -> p b h", p=128), lf_all[:],
              reads=[("lf_all", b) for b in range(NB)])

        xt = xin[0]
        xkey = "xin0"
        P.dma(xt[:, 0, :], x_s[:, :], writes=[xkey], slot="xin")
        layer_norm_block(xt, xkey, 1)
        to_feature_major(xt, xkey, 1, hT, "hT", g_in_col, b_in_col)
        hkeys = [("hT", ct) for ct in range(8)]
        wkeys = [("w_in_b", ct) for ct in range(8)]
        for (dst, dkey, c0, scl) in ((kT_s, "kT_s", 512, 1.0), (qT_s, "qT_s", 0, 0.125), (uT_s, "uT_s", 1544, 1.0)):
            for ft in range(4):
                pm = pmm[nmm[0] % 4]
                pk = f"pmm{nmm[0] % 4}"
                nmm[0] += 1
                for ct in range(8):
                    P.pe(lambda e, pm=pm, ct=ct, ft=ft, c0=c0: e.matmul(
                        pm[:, 0:128], w_in_b[:, ct, c0 + ft * 128:c0 + (ft + 1) * 128], hT[:, ct, 0:128],
                        start=(ct == 0), stop=(ct == 7)), reads=hkeys + wkeys, writes=[pk])
                P.act(lambda e, pm=pm, dst=dst, ft=ft, scl=scl: e.activation(dst[:, ft, :], pm[:, 0:128], AF.Copy, scale=scl),
                      reads=[pk], writes=[dkey])
        for (c0, dram, isv) in ((512, k_s, False), (1024, v_s, True)):
            pm = pmm[nmm[0] % 4]
            pk = f"pmm{nmm[0] % 4}"
            nmm[0] += 1
            for ct in range(8):
                P.pe(lambda e, pm=pm, ct=ct, c0=c0: e.matmul(
                    pm[:], hT[:, ct, 0:128], w_in_b[:, ct, c0:c0 + 512],
                    start=(ct == 0), stop=(ct == 7)), reads=hkeys + wkeys, writes=[pk])
            os_ = ostage[nos[0] % 2]
            ok = f"ostage{nos[0] % 2}"
            nos[0] += 1
            P.act(lambda e, pm=pm, os_=os_: e.copy(os_[:], pm[:]), reads=[pk], writes=[ok])
            if isv:
                P.act(lambda e, pm=pm: e.copy(Vb_s[:, :, 0:HD], pm[:].rearrange("p (h d) -> p h d", h=NH)),
                      reads=[pk], writes=["Vb_s"])
            P.dma(dram[:, :], os_[0:ST, :], reads=[ok], slot=ok)
        pm = pmm[nmm[0] % 4]
        pk = f"pmm{nmm[0] % 4}"
        nmm[0] += 1
        for ct in range(8):
            P.pe(lambda e, pm=pm, ct=ct: e.matmul(
                pm[:, 0:NH], hT[:, ct, 0:128], w_in_b[:, ct, 1536:1544],
                start=(ct == 0), stop=(ct == 7)), reads=hkeys + wkeys, writes=[pk])
        P.act(lambda e, pm=pm: e.copy(lstage[:], pm[:, 0:NH]), reads=[pk], writes=["lstage"])
        P.dve(lambda e: e.tensor_tensor(lstage[:], lstage[:], bf_bc[:], ALU.add),
              reads=["lstage", "bf_bc"], writes=["lstage"])
        P.act(lambda e: e.activation(lstage[:], lstage[:], AF.Exp, scale=-1.0),
              reads=["lstage"], writes=["lstage"])
        P.dve(lambda e: e.tensor_scalar(lstage[:], lstage[:], 1.0, None, ALU.add),
              reads=["lstage"], writes=["lstage"])
        P.act(lambda e: e.activation(lstage[:], lstage[:], AF.Ln),
              reads=["lstage"], writes=["lstage"])
        P.dve(lambda e: e.tensor_scalar(lf_s[:], lstage[:], -1.0, None, ALU.mult),
              reads=["lstage"], writes=["lf_s"])
        P.dma(lf_sd[:, :], lf_s[0:ST, :], reads=["lf_s"])

        phase_end(s1)
        if KSTAGE == 1:
            P.barrier(); P.emit(); s_att.close(); s_u.close()
            return nc

        s3 = ExitStack()
        cur[0] = s3
        pst = [ps(f"pst{i}_3", [128, 512]) for i in range(2)]
        pmm = [ps(f"pmm{i}_3", [128, 512]) for i in range(4)]
        att_dbg = dout("att_dbg", [HD, NH, T]) if dbg else None
        dbg2 = dout("dbg2", [128, 1536]) if dbg else None
        dbg3 = dout("dbg3", [128, 2576]) if dbg else None
        dbgs = sb("dbgs", [128, 2576]) if dbg else None
        lf_keys = [("lf_all", b) for b in range(NB)]
        totb = sb("totb", [128, NB, NH])
        pre = sb("pre", [128, NB, NH])
        cc = sb("cc", [128, NB, NH])
        biasI = sb("biasI", [128, NB, NH])
        maskT = sb("maskT", [128, 128], BF16)
        maskf = sb("maskf", [128, 128])
        P.pool(lambda e: e.memset(maskf[:], 0.0), writes=["maskf"])
        P.pool(lambda e: e.affine_select(maskf[:], maskf[:], [[1, 128]], ALU.is_ge, NEG,
                                         base=0, channel_multiplier=-1), reads=["maskf"], writes=["maskf"])
        P.pool(lambda e: e.tensor_copy(maskT[:], maskf[:]), reads=["maskf"], writes=["maskT"])
        jobs = []
        for (nm, src, dstd, rows, cols) in (("up", w_up, wup_d, D, DFF), ("dn", w_down, wdn_d, DFF, D)):
            for r0 in range(0, rows, 128):
                for c0 in range(0, cols, 512):
                    jobs.append((src[r0:r0 + 128, c0:c0 + 512], 512, ("dram", dstd[r0:r0 + 128, c0:c0 + 512], ("wsc", nm, r0, c0))))
        convert(jobs)
        lf_flat = lf_all[:].rearrange("p b h -> p (b h)")
        pm = pmm[0]
        P.pe(lambda e: e.matmul(pm[:, 0:NB * NH], ones_f[:], lf_flat, start=True, stop=True),
             reads=lf_keys + ["ones_f"], writes=["pmm0"])
        P.act(lambda e: e.copy(totb[:].rearrange("p b h -> p (b h)"), pm[:, 0:NB * NH]), reads=["pmm0"], writes=["totb"])
        pm1 = pmm[1]
        P.pe(lambda e: e.matmul(pm1[:, 0:NB * NH], triu[:], lf_flat, start=True, stop=True),
             reads=lf_keys + ["triu"], writes=["pmm1"])
        P.act(lambda e: e.copy(cc[:].rearrange("p b h -> p (b h)"), pm1[:, 0:NB * NH]), reads=["pmm1"], writes=["cc"])
        P.dve(lambda e: e.memset(pre[:, 0, :], 0.0), writes=["pre"])
        for b in range(1, NB):
            P.dve(lambda e, b=b: e.tensor_tensor(pre[:, b, :], pre[:, b - 1, :], totb[:, b - 1, :], ALU.add),
                  reads=["pre", "totb"], writes=["pre"])
        P.dve(lambda e: e.tensor_tensor(cc[:], cc[:], pre[:], ALU.add), reads=["cc", "pre"], writes=["cc"])

        pS = [ps(f"pS{i}", [128, 512]) for i in range(2)]
        pT = [sb(f"pT{i}", [128, 512], BF16) for i in range(2)]
        osb = sb("osb", [HD + 1, 512])
        rbc = sb("rbc", [HD, 512])
        attT = sb("attT", [HD, NH, 512], BF16)
        attf = sb("attf", [HD, NH, 512]) if dbg else None
        nS = [0]
        pend = [None]

        def flush():
            if pend[0] is not None:
                fn_ = pend[0]
                pend[0] = None
                fn_()

        def epilogue(h, po, pok):
            P.act(lambda e, po=po: e.copy(osb[:], po[0:HD + 1, :]), reads=[pok], writes=["osb"])
            P.dve(lambda e: e.reciprocal(osb[HD:HD + 1, :], osb[HD:HD + 1, :]), reads=["osb"], writes=["osb"])
            pb = pst[h % 2]
            pbk = f"pst{h % 2}"
            P.pe(lambda e, pb=pb: e.matmul(pb[0:HD, :], ones_f[HD:HD + 1, 0:HD], osb[HD:HD + 1, :],
                                           start=True, stop=True), reads=["osb", "ones_f"], writes=[pbk])
            P.act(lambda e, pb=pb: e.copy(rbc[:], pb[0:HD, :]), reads=[pbk], writes=["rbc"])
            P.dve(lambda e, h=h: e.tensor_tensor(attT[:, h, :], osb[0:HD, :], rbc[:], ALU.mult),
                  reads=["osb", "rbc"], writes=[("attT", h)])

        for I in range(NSB):
            nkb = 4 * I + 4
            for kb in range(nkb):
                P.dve(lambda e, kb=kb, I=I: e.tensor_tensor(biasI[:, kb, :], pre[:, 4 * I, :], cc[:, kb, :], ALU.subtract),
                      reads=["pre", "cc"], writes=["biasI"])
            for h in range(NH):
                hp = slice((h % 2) * 64, (h % 2) * 64 + 64)
                ft = h // 2
                po = pmm[2 + (h % 2)]
                pok = f"pmm{2 + (h % 2)}"
                for kb in range(nkb):
                    c0 = 0 if kb < 4 * I else (kb - 4 * I) * 128
                    i = nS[0] % 2
                    nS[0] += 1
                    psS, pTt = pS[i], pT[i]
                    diag = kb >= 4 * I
                    P.pe(lambda e, psS=psS, kb=kb, c0=c0, diag=diag, hp=hp, ft=ft, I=I: e.matmul(
                        psS[:, c0:512], kT[hp, ft, kb * 128:(kb + 1) * 128], qT[hp, ft, I * 512 + c0:(I + 1) * 512],
                        start=True, stop=not diag), reads=[("kT", kb // 4), ("qT", I)], writes=[f"pS{i}"])
                    if diag:
                        P.pe(lambda e, psS=psS, c0=c0: e.matmul(psS[:, c0:c0 + 128], ident_b[:], maskT[:],
                                                               start=False, stop=True),
                             reads=["ident_b", "maskT"], writes=[f"pS{i}"])
                    P.act(lambda e, psS=psS, pTt=pTt, c0=c0, kb=kb, h=h: e.activation(
                        pTt[:, c0:512], psS[:, c0:512], AF.Exp, bias=biasI[:, kb, h:h + 1]),
                        reads=[f"pS{i}", "biasI"], writes=[f"pT{i}"])

                    def later(po=po, pok=pok, pTt=pTt, c0=c0, kb=kb, h=h, nkb=nkb, i=i):
                        P.pe(lambda e: e.matmul(
                            po[0:HD + 1, c0:512], Vb[:, kb, h, 0:HD + 1], pTt[:, c0:512],
                            start=(kb == 0), stop=(kb == nkb - 1)), reads=[f"pT{i}", "Vb"], writes=[pok])
                        if kb == nkb - 1:
                            epilogue(h, po, pok)
                    flush()
                    pend[0] = later
            flush()
            P.dma(att_d2[I * HD:(I + 1) * HD, :], attT[:].rearrange("p h t -> p (h t)"), reads=[("attT", h) for h in range(NH)],
                  writes=[("att_d", I)], slot="attT")

        phase_end(s3, s_att)
        if KSTAGE == 3:
            P.barrier(); P.emit(); s_u.close()
            return nc
        s2 = ExitStack()
        cur[0] = s2
        GP = 16
        NC = T // 8
        PI = math.pi
        TWO_PI = 2.0 * math.pi
        K_ = "ssmc"

        def dv(fn):
            return P.dve(fn, reads=[K_], writes=[K_])

        def ac(fn):
            return P.act(fn, reads=[K_], writes=[K_])

        def sdma(out, in_):
            return P.dma(out, in_, writes=[K_], q="gpsimd", allow_slow_non_contiguous=True)

        lam_r = sb("lam_r", [128, GP])
        lam_i = sb("lam_i", [128, GP])
        ldt = sb("ldt", [128, GP])
        Bre = sb("Bre", [128, GP, SG])
        Bim = sb("Bim", [128, GP, SG])
        Cre = sb("Cre", [128, GP, SG])
        Cim = sb("Cim", [128, GP, SG])
        dcol = sb("dcol", [128, 4])
        sdma(lam_r[:], lam_re.rearrange("(gp g2) p -> (g2 p) gp", g2=2))
        sdma(lam_i[:], lam_im.rearrange("(gp g2) p -> (g2 p) gp", g2=2))
        for g2 in range(2):
            sdma(ldt[64 * g2:64 * g2 + 64, :], log_dt.rearrange("(gp g2) -> g2 gp", g2=2)[g2].partition_broadcast(64))
            for gp in range(GP):
                sdma(Cre[64 * g2:64 * g2 + 64, gp, :], c_re[2 * gp + g2].rearrange("h p -> p h"))
                sdma(Cim[64 * g2:64 * g2 + 64, gp, :], c_im[2 * gp + g2].rearrange("h p -> p h"))
        sdma(Bre[:], b_re.rearrange("(gp g2) p h -> (g2 p) gp h", g2=2))
        sdma(Bim[:], b_im.rearrange("(gp g2) p h -> (g2 p) gp h", g2=2))
        sdma(dcol[:], d_skip.rearrange("g h -> (g h)").rearrange("(a p) -> p a", p=128))

        def small(name):
            return sb(name, [128, GP])

        dt_, lrdt, th, mag, rho8, t1, a1, sinv, t2, cosv, phi8 = [small(n) for n in (
            "dt_", "lrdt", "th", "mag", "rho8", "t1", "a1", "sinv", "t2", "cosv", "phi8")]
        tmpa, tmpb, nr, den, cre, cim = [small(n) for n in ("tmpa", "tmpb", "nr", "den", "cre", "cim")]
        AKr = sb("AKr", [128, GP, 9])
        AKi = sb("AKi", [128, GP, 9])

        def tt(o, a, b, op):
            return dv(lambda e: e.tensor_tensor(o, a, b, op))

        def ts(o, a, s1_, s2_, op0, op1=None):
            if op1 is None:
                return dv(lambda e: e.tensor_scalar(o, a, s1_, None, op0))
            return dv(lambda e: e.tensor_scalar(o, a, s1_, s2_, op0, op1))

        def reduce_pm_pi(dst, x, ki, kf, m, emit_v):
            emit_v(lambda e: e.tensor_scalar(ki, x, 1.0 / TWO_PI, None, ALU.mult))
            emit_v(lambda e: e.tensor_copy(kf, ki))
            emit_v(lambda e: e.scalar_tensor_tensor(dst, kf, -TWO_PI, x, ALU.mult, ALU.add))
            emit_v(lambda e: e.tensor_scalar(m, dst, PI, None, ALU.is_gt))
            emit_v(lambda e: e.scalar_tensor_tensor(dst, m, -TWO_PI, dst, ALU.mult, ALU.add))
            emit_v(lambda e: e.tensor_scalar(m, dst, -PI, None, ALU.is_lt))
            emit_v(lambda e: e.scalar_tensor_tensor(dst, m, TWO_PI, dst, ALU.mult, ALU.add))

        def cos_arg(dst, r, m, emit_v):
            emit_v(lambda e: e.tensor_scalar(dst, r, PI / 2, None, ALU.add))
            emit_v(lambda e: e.tensor_scalar(m, dst, PI, None, ALU.is_gt))
            emit_v(lambda e: e.scalar_tensor_tensor(dst, m, -TWO_PI, dst, ALU.mult, ALU.add))

        ki_s = sb("ki_s", [128, GP], I32)
        kf_s = small("kf_s")
        m_s = small("m_s")
        ac(lambda e: e.activation(dt_[:], ldt[:], AF.Exp))
        tt(lrdt[:], lam_r[:], dt_[:], ALU.mult)
        tt(th[:], lam_i[:], dt_[:], ALU.mult)
        ac(lambda e: e.activation(mag[:], lrdt[:], AF.Exp))
        ac(lambda e: e.activation(rho8[:], lrdt[:], AF.Exp, scale=8.0))
        reduce_pm_pi(t1[:], th[:], ki_s[:], kf_s[:], m_s[:], dv)
        ac(lambda e: e.activation(sinv[:], t1[:], AF.Sin))
        cos_arg(t2[:], t1[:], m_s[:], dv)
        ac(lambda e: e.activation(cosv[:], t2[:], AF.Sin))
        ts(a1[:], t1[:], 8.0, None, ALU.mult)
        reduce_pm_pi(phi8[:], a1[:], ki_s[:], kf_s[:], m_s[:], dv)
        dv(lambda e: e.memset(AKr[:, :, 0], 1.0))
        dv(lambda e: e.memset(AKi[:, :, 0], 0.0))
        tt(AKr[:, :, 1], mag[:], cosv[:], ALU.mult)
        tt(AKi[:, :, 1], mag[:], sinv[:], ALU.mult)
        for k in range(2, 9):
            tt(tmpa[:], AKr[:, :, k - 1], AKr[:, :, 1], ALU.mult)
            tt(tmpb[:], AKi[:, :, k - 1], AKi[:, :, 1], ALU.mult)
            tt(AKr[:, :, k], tmpa[:], tmpb[:], ALU.subtract)
            tt(tmpa[:], AKr[:, :, k - 1], AKi[:, :, 1], ALU.mult)
            tt(tmpb[:], AKi[:, :, k - 1], AKr[:, :, 1], ALU.mult)
            tt(AKi[:, :, k], tmpa[:], tmpb[:], ALU.add)
        ts(nr[:], AKr[:, :, 1], -1.0, None, ALU.add)
        tt(tmpa[:], lam_r[:], lam_r[:], ALU.mult)
        tt(tmpb[:], lam_i[:], lam_i[:], ALU.mult)
        tt(den[:], tmpa[:], tmpb[:], ALU.add)
        dv(lambda e: e.reciprocal(den[:], den[:]))
        tt(tmpa[:], nr[:], lam_r[:], ALU.mult)
        tt(tmpb[:], AKi[:, :, 1], lam_i[:], ALU.mult)
        tt(cre[:], tmpa[:], tmpb[:], ALU.add)
        tt(cre[:], cre[:], den[:], ALU.mult)
        tt(tmpa[:], AKi[:, :, 1], lam_r[:], ALU.mult)
        tt(tmpb[:], nr[:], lam_i[:], ALU.mult)
        tt(cim[:], tmpa[:], tmpb[:], ALU.subtract)
        tt(cim[:], cim[:], den[:], ALU.mult)
        bbr = sb("bbr", [128, GP, SG])
        bbi = sb("bbi", [128, GP, SG])
        t3a = sb("t3a", [128, GP, SG])
        t3b = sb("t3b", [128, GP, SG])
        cre_b = cre[:].unsqueeze(2).broadcast_to([128, GP, SG])
        cim_b = cim[:].unsqueeze(2).broadcast_to([128, GP, SG])
        tt(t3a[:], Bre[:], cre_b, ALU.mult)
        tt(t3b[:], Bim[:], cim_b, ALU.mult)
        tt(bbr[:], t3a[:], t3b[:], ALU.subtract)
        tt(t3a[:], Bim[:], cre_b, ALU.mult)
        tt(t3b[:], Bre[:], cim_b, ALU.mult)
        tt(bbi[:], t3a[:], t3b[:], ALU.add)
        MC = sb("MC", [128, GP, 9, 2, 32], BF16)
        Wt = sb("Wt", [128, 4, 8, 2, 128], BF16)
        Kl = sb("Kl", [128, 4, 8, 32], BF16)
        pw = [ps(f"pw{i}", [128, 512]) for i in range(2)]
        s2c = ExitStack()
        cur[0] = s2c
        XAr = sb("XAr", [128, GP, 8, SG])
        XAi = sb("XAi", [128, GP, 8, SG])
        t4a = sb("t4a", [128, GP, 9, SG])
        t4b = sb("t4b", [128, GP, 9, SG])
        akr8 = AKr[:, :, 0:8].unsqueeze(3).broadcast_to([128, GP, 8, SG])
        aki8 = AKi[:, :, 0:8].unsqueeze(3).broadcast_to([128, GP, 8, SG])
        bbr8 = bbr[:].unsqueeze(2).broadcast_to([128, GP, 8, SG])
        bbi8 = bbi[:].unsqueeze(2).broadcast_to([128, GP, 8, SG])
        tt(t4a[:, :, 0:8, :], akr8, bbr8, ALU.mult)
        tt(t4b[:, :, 0:8, :], aki8, bbi8, ALU.mult)
        tt(XAr[:], t4a[:, :, 0:8, :], t4b[:, :, 0:8, :], ALU.subtract)
        tt(t4a[:, :, 0:8, :], akr8, bbi8, ALU.mult)
        tt(t4b[:, :, 0:8, :], aki8, bbr8, ALU.mult)
        tt(XAi[:], t4a[:, :, 0:8, :], t4b[:, :, 0:8, :], ALU.add)
        CAr = sb("CAr", [128, GP, 9, SG])
        CAi = sb("CAi", [128, GP, 9, SG])
        akr9 = AKr[:].unsqueeze(3).broadcast_to([128, GP, 9, SG])
        aki9 = AKi[:].unsqueeze(3).broadcast_to([128, GP, 9, SG])
        cr9 = Cre[:].unsqueeze(2).broadcast_to([128, GP, 9, SG])
        ci9 = Cim[:].unsqueeze(2).broadcast_to([128, GP, 9, SG])
        tt(t4a[:], akr9, cr9, ALU.mult)
        tt(t4b[:], aki9, ci9, ALU.mult)
        tt(CAr[:], t4a[:], t4b[:], ALU.subtract)
        tt(t4a[:], aki9, cr9, ALU.mult)
        tt(t4b[:], akr9, ci9, ALU.mult)
        tt(CAi[:], t4a[:], t4b[:], ALU.add)
        ts(CAi[:], CAi[:], -1.0, None, ALU.mult)
        XAbd = sb("XAbd", [128, 4, 8, 2, 4, 32])
        CBDr = sb("CBDr", [128, GP, 32])
        CBDni = sb("CBDni", [128, GP, 32])
        dv(lambda e: e.memset(XAbd[:], 0.0))
        dv(lambda e: e.memset(MC[:], 0.0))
        dv(lambda e: e.memset(CBDr[:], 0.0))
        dv(lambda e: e.memset(CBDni[:], 0.0))
        for g2 in range(2):
            pr = slice(64 * g2, 64 * g2 + 64)
            cs = slice(16 * g2, 16 * g2 + 16)
            for gq in range(4):
                dv(lambda e, pr=pr, cs=cs, gq=gq: e.tensor_copy(
                    XAbd[pr, gq, :, 0, :, cs], XAr[pr, 4 * gq:4 * gq + 4, :, :].rearrange("p q k h -> p k q h")))
                dv(lambda e, pr=pr, cs=cs, gq=gq: e.tensor_copy(
                    XAbd[pr, gq, :, 1, :, cs], XAi[pr, 4 * gq:4 * gq + 4, :, :].rearrange("p q k h -> p k q h")))
            dv(lambda e, pr=pr, cs=cs: e.tensor_copy(MC[pr, :, :, 0, cs], CAr[pr]))
            dv(lambda e, pr=pr, cs=cs: e.tensor_copy(MC[pr, :, :, 1, cs], CAi[pr]))
            dv(lambda e, pr=pr, cs=cs: e.tensor_copy(CBDr[pr, :, cs], Cre[pr]))
            dv(lambda e, pr=pr, cs=cs: e.tensor_scalar(CBDni[pr, :, cs], Cim[pr], -1.0, None, ALU.mult))
        npw = 0
        for gq in range(4):
            for i in range(8):
                pwt = pw[npw % 2]
                pwk = f"pw{npw % 2}"
                npw += 1
                for ri in range(2):
                    P.pe(lambda e, pwt=pwt, gq=gq, i=i, ri=ri: e.transpose(
                        pwt[:, ri * 128:(ri + 1) * 128], XAbd[:, gq, 7 - i, ri, :, :].rearrange("p q c -> p (q c)"), ident_f[:]),
                        reads=[K_, "ident_f"], writes=[pwk])
                P.act(lambda e, pwt=pwt, gq=gq, i=i: e.copy(
                    Wt[:, gq, i, :, :].rearrange("p r c -> p (r c)"), pwt[:, 0:256]), reads=[pwk], writes=["Wt"])
        for gp in range(GP):
            gq, q = gp // 4, gp % 4
            pwt = pw[npw % 2]
            pwk = f"pw{npw % 2}"
            npw += 1
            for k in range(8):
                P.pe(lambda e, pwt=pwt, gq=gq, gp=gp, k=k: e.matmul(
                    pwt[:, k * 32:(k + 1) * 32], XAbd[:, gq, k, 0, :, :].rearrange("p q c -> p (q c)"), CBDr[:, gp, :],
                    start=True, stop=False), reads=[K_], writes=[pwk])
                P.pe(lambda e, pwt=pwt, gq=gq, gp=gp, k=k: e.matmul(
                    pwt[:, k * 32:(k + 1) * 32], XAbd[:, gq, k, 1, :, :].rearrange("p q c -> p (q c)"), CBDni[:, gp, :],
                    start=False, stop=True), reads=[K_], writes=[pwk])
            P.act(lambda e, pwt=pwt, gq=gq, q=q: e.copy(
                Kl[32 * q:32 * q + 32, gq, :, :].rearrange("p k c -> p (k c)"), pwt[32 * q:32 * q + 32, 0:256]),
                reads=[pwk], writes=["Kl"])

        phase_end(s2c)
        cur[0] = s2
        cidx = sb("cidx", [128, NC])
        P.pool(lambda e: e.iota(cidx[:], [[1, NC]], base=0, channel_multiplier=0,
                                allow_small_or_imprecise_dtypes=True), writes=["cidx"])
        ang, nS, nC, sr, si, ta, tb, Gr, Gi = [sb(n, [128, NC]) for n in
                                               ("ang", "nS", "nC", "sr", "si", "ta", "tb", "Gr", "Gi")]
        ki_w = sb("ki_w", [128, NC], I32)
        Xp = [[sb(f"Xp{q}_{ri}", [128, NC], BF16) for ri in range(2)] for q in range(4)]
        fin = sb("fin", [128, GP, 2])
        ysb = sb("ysb", [128, 1024])
        yt1 = sb("yt1", [128, 1024])
        ygb = sb("ygb", [128, 1024], BF16)
        pss = [ps(f"pss{i}", [128, 512]) for i in range(2)]
        psy = ps("psy", [128, 1024])
        for q in range(4):
            for ri in range(2):
                P.pool(lambda e, q=q, ri=ri: e.memset(Xp[q][ri][:, 0:1], 0.0), writes=[("Xp", q)])
        W_ = "ssmw"

        def wv(fn, extra_r=(), extra_w=()):
            return P.dve(fn, reads=[W_] + list(extra_r), writes=[W_] + list(extra_w))

        def wa(fn, extra_r=(), extra_w=()):
            return P.act(fn, reads=[W_] + list(extra_r), writes=[W_] + list(extra_w))

        ukeys = [("uT", sbk) for sbk in range(NSB)]
        for gq in range(4):
            for q in range(4):
                gp = 4 * gq + q
                rows = slice(32 * q, 32 * q + 32)
                for ri in range(2):
                    for i in range(8):
                        P.pe(lambda e, ri=ri, i=i, rows=rows, gq=gq, q=q: e.matmul(
                            pss[ri][:, 0:NC], Wt[rows, gq, i, ri, :], uT[rows, gq, i:T:8],
                            start=(i == 0), stop=(i == 7), tile_position=(32 * q, 0)),
                            reads=["Wt"] + ukeys, writes=[f"pss{ri}"])
                wa(lambda e: e.copy(sr[:], pss[0][:, 0:NC]), extra_r=["pss0"])
                wa(lambda e: e.copy(si[:], pss[1][:, 0:NC]), extra_r=["pss1"])
                wv(lambda e, gp=gp: e.tensor_scalar(ang[:], cidx[:], phi8[:, gp:gp + 1], None, ALU.mult),
                   extra_r=["cidx", K_])
                reduce_pm_pi(ta[:], ang[:], ki_w[:], tb[:], Gr[:], wv)
                wa(lambda e: e.activation(nS[:], ta[:], AF.Sin))
                cos_arg(tb[:], ta[:], Gr[:], wv)
                wa(lambda e: e.activation(nC[:], tb[:], AF.Sin))
                wv(lambda e: e.tensor_tensor(ta[:], sr[:], nC[:], ALU.mult))
                wv(lambda e: e.tensor_tensor(tb[:], si[:], nS[:], ALU.mult))
                wv(lambda e: e.tensor_tensor(Gr[:], ta[:], tb[:], ALU.add))
                wv(lambda e: e.tensor_tensor(ta[:], si[:], nC[:], ALU.mult))
                wv(lambda e: e.tensor_tensor(tb[:], sr[:], nS[:], ALU.mult))
                wv(lambda e: e.tensor_tensor(Gi[:], ta[:], tb[:], ALU.subtract))
                rho_b = rho8[:, gp:gp + 1].broadcast_to([128, NC])
                wv(lambda e, rho_b=rho_b: e.tensor_tensor_scan(sr[:], rho_b, Gr[:], 0.0, ALU.mult, ALU.add), extra_r=[K_])
                wv(lambda e, rho_b=rho_b: e.tensor_tensor_scan(si[:], rho_b, Gi[:], 0.0, ALU.mult, ALU.add), extra_r=[K_])
                wv(lambda e: e.tensor_tensor(ta[:], sr[:], nC[:], ALU.mult))
                wv(lambda e: e.tensor_tensor(tb[:], si[:], nS[:], ALU.mult))
                wv(lambda e: e.tensor_tensor(Gr[:], ta[:], tb[:], ALU.subtract))
                wv(lambda e: e.tensor_tensor(ta[:], si[:], nC[:], ALU.mult))
                wv(lambda e: e.tensor_tensor(tb[:], sr[:], nS[:], ALU.mult))
                wv(lambda e: e.tensor_tensor(Gi[:], ta[:], tb[:], ALU.add))
                wa(lambda e, q=q: e.copy(Xp[q][0][:, 1:NC], Gr[:, 0:NC - 1]), extra_w=[("Xp", q)])
                wa(lambda e, q=q: e.copy(Xp[q][1][:, 1:NC], Gi[:, 0:NC - 1]), extra_w=[("Xp", q)])
                wa(lambda e, gp=gp: e.copy(fin[:, gp, 0:1], Gr[:, NC - 1:NC]), extra_w=["fin"])
                wa(lambda e, gp=gp: e.copy(fin[:, gp, 1:2], Gi[:, NC - 1:NC]), extra_w=["fin"])
            for cq in range(NC // 128):
                csl = slice(cq * 128, (cq + 1) * 128)
                for q in range(4):
                    gp = 4 * gq + q
                    rows = slice(32 * q, 32 * q + 32)
                    for j in range(8):
                        ops_ = [(MC[:, gp, j + 1, 0, :], Xp[q][0][:, csl], (0, 32 * q)),
                                (MC[:, gp, j + 1, 1, :], Xp[q][1][:, csl], (0, 32 * q))]
                        for i in range(j + 1):
                            ops_.append((Kl[rows, gq, j - i, :],
                                         uT[rows, gq, cq * 1024 + i:(cq + 1) * 1024:8], (32 * q, 32 * q)))
                        for n_, (l_, r_, tp_) in enumerate(ops_):
                            P.pe(lambda e, l_=l_, r_=r_, tp_=tp_, n_=n_, last=(n_ == len(ops_) - 1), rows=rows, j=j: e.matmul(
                                psy[rows, j * 128:(j + 1) * 128], l_, r_, start=(n_ == 0), stop=last, tile_position=tp_),
                                reads=[K_, "Kl", ("Xp", q)] + ukeys, writes=["psy"])
                ysb3 = ysb[:].rearrange("p (j c) -> p j c", j=8)
                uview = uT[:, gq, cq * 1024:(cq + 1) * 1024].rearrange("p (c j) -> p j c", j=8)
                P.act(lambda e: e.copy(ysb[:], psy[:]), reads=["psy"], writes=["ysb"])
                P.dve(lambda e, uview=uview, ysb3=ysb3, gq=gq: e.scalar_tensor_tensor(
                    ysb3, uview, dcol[:, gq:gq + 1], ysb3, ALU.mult, ALU.add), reads=["ysb", K_] + ukeys, writes=["ysb"])
                P.dve(lambda e: e.tensor_tensor(yt1[:], ysb[:], ysb[:], ALU.mult), reads=["ysb"], writes=["yt1"])
                P.dve(lambda e: e.tensor_scalar(yt1[:], yt1[:], 0.044715, 1.0, ALU.mult, ALU.add),
                      reads=["yt1"], writes=["yt1"])
                P.dve(lambda e: e.tensor_tensor(yt1[:], yt1[:], ysb[:], ALU.mult), reads=["yt1", "ysb"], writes=["yt1"])
                P.act(lambda e: e.activation(yt1[:], yt1[:], AF.Sigmoid, scale=2.0 * math.sqrt(2.0 / math.pi)),
                      reads=["yt1"], writes=["yt1"])
                P.dve(lambda e, ysb3=ysb3: e.tensor_tensor(
                    ygb[:].rearrange("p (c j) -> p j c", j=8), ysb3, yt1[:].rearrange("p (j c) -> p j c", j=8), ALU.mult),
                    reads=["yt1", "ysb"], writes=["ygb"])
                for i2 in range(2):
                    I_ = 2 * cq + i2
                    P.dma(yg_d2[I_ * 128:(I_ + 1) * 128, gq * 512:(gq + 1) * 512], ygb[:, i2 * 512:(i2 + 1) * 512],
                          reads=["ygb"], slot="ygb")
        TK = 4 * NS
        h0sb = sb("h0sb", [NS, 2, NG * SP])
        P.dma(h0sb[:, 0, :], st_re[:, :], writes=["h0sb"])
        P.dma(h0sb[:, 1, :], st_im[:, :], writes=["h0sb"])
        h0f = sb("h0f", [128, 2, GP, NS])
        h0b = sb("h0b", [128, 2, GP, NS], BF16)
        for ri in range(2):
            pwt = pw[ri]
            for gp in range(GP):
                P.pe(lambda e, pwt=pwt, gp=gp, ri=ri: e.transpose(
                    pwt[:, gp * NS:(gp + 1) * NS], h0sb[:, ri, gp * 128:(gp + 1) * 128], ident_f[0:NS, 0:NS]),
                    reads=["h0sb", "ident_f"], writes=[f"pw{ri}"])
            P.act(lambda e, pwt=pwt, ri=ri: e.copy(h0f[:, ri, :, :].rearrange("p g s -> p (g s)"), pwt[:, 0:GP * NS]),
                  reads=[f"pw{ri}"], writes=["h0f"])
        P.dve(lambda e: e.tensor_copy(h0b[:].rearrange("p r g s -> p (r g s)"), h0f[:].rearrange("p r g s -> p (r g s)")),
              reads=["h0f"], writes=["h0b"])
        for gq in range(4):
            for q in range(4):
                gp = 4 * gq + q
                rows = slice(32 * q, 32 * q + 32)
                for ri in range(2):
                    for i in range(4):
                        P.pe(lambda e, ri=ri, i=i, rows=rows, gq=gq, q=q, gp=gp: e.matmul(
                            pss[ri][:, gp * NS:(gp + 1) * NS], Wt[rows, gq, i + 4, ri, :], uT_s[rows, gq, i:TK:4],
                            start=(i == 0), stop=(i == 3), tile_position=(32 * q, 0)),
                            reads=["Wt", "uT_s"], writes=[f"pss{ri}"])
        sr_s = sb("sr_s", [128, GP, NS])
        si_s = sb("si_s", [128, GP, NS])
        tA = sb("tA_s", [128, GP, NS])
        tB = sb("tB_s", [128, GP, NS])
        finr = sb("finr", [128, GP, NS])
        fini = sb("fini", [128, GP, NS])
        P.act(lambda e: e.copy(sr_s[:].rearrange("p g s -> p (g s)"), pss[0][:, 0:GP * NS]), reads=["pss0"], writes=["sr_s"])
        P.act(lambda e: e.copy(si_s[:].rearrange("p g s -> p (g s)"), pss[1][:, 0:GP * NS]), reads=["pss1"], writes=["si_s"])
        A4r = AKr[:, :, 4].unsqueeze(2).broadcast_to([128, GP, NS])
        A4i = AKi[:, :, 4].unsqueeze(2).broadcast_to([128, GP, NS])
        S_ = "ssms"

        def sv(fn, extra_r=(), extra_w=()):
            return P.dve(fn, reads=[S_, K_] + list(extra_r), writes=[S_] + list(extra_w))

        sv(lambda e: e.tensor_tensor(tA[:], h0f[:, 0, :, :], A4r, ALU.mult), extra_r=["h0f"])
        sv(lambda e: e.tensor_tensor(tB[:], h0f[:, 1, :, :], A4i, ALU.mult))
        sv(lambda e: e.tensor_tensor(tA[:], tA[:], tB[:], ALU.subtract))
        sv(lambda e: e.tensor_tensor(finr[:], tA[:], sr_s[:], ALU.add), extra_r=["sr_s"])
        sv(lambda e: e.tensor_tensor(tA[:], h0f[:, 1, :, :], A4r, ALU.mult))
        sv(lambda e: e.tensor_tensor(tB[:], h0f[:, 0, :, :], A4i, ALU.mult))
        sv(lambda e: e.tensor_tensor(tA[:], tA[:], tB[:], ALU.add))
        sv(lambda e: e.tensor_tensor(fini[:], tA[:], si_s[:], ALU.add), extra_r=["si_s"])
        fin_tm = sb("fin_tm", [NS, 2, NG * SP])
        for ri, fsrc in enumerate((finr, fini)):
            for g4 in range(4):
                pwt = pw[(ri * 4 + g4) % 2]
                pwk = f"pw{(ri * 4 + g4) % 2}"
                for gg in range(4):
                    gp = 4 * g4 + gg
                    P.pe(lambda e, pwt=pwt, gg=gg, gp=gp, fsrc=fsrc: e.transpose(
                        pwt[0:NS, gg * 128:(gg + 1) * 128], fsrc[:, gp, :], ident_f[:]),
                        reads=[S_, "ident_f"], writes=[pwk])
                P.act(lambda e, pwt=pwt, ri=ri, g4=g4: e.copy(fin_tm[:, ri, g4 * 512:(g4 + 1) * 512], pwt[0:NS, :]),
                      reads=[pwk], writes=[("fin_tm", ri, g4)])
        P.dma(ssm_re_s[:, :], fin_tm[:, 0, :], reads=[("fin_tm", 0, g4) for g4 in range(4)])
        P.dma(ssm_im_s[:, :], fin_tm[:, 1, :], reads=[("fin_tm", 1, g4) for g4 in range(4)])
        for gq in range(4):
            for q in range(4):
                gp = 4 * gq + q
                rows = slice(32 * q, 32 * q + 32)
                for j in range(4):
                    col = (gq * 4 + j) * NS
                    ops_ = [(MC[:, gp, j + 1, 0, :], h0b[:, 0, gp, :], (0, 32 * q)),
                            (MC[:, gp, j + 1, 1, :], h0b[:, 1, gp, :], (0, 32 * q))]
                    for i in range(j + 1):
                        ops_.append((Kl[rows, gq, j - i, :], uT_s[rows, gq, i:TK:4], (32 * q, 32 * q)))
                    for n_, (l_, r_, tp_) in enumerate(ops_):
                        P.pe(lambda e, l_=l_, r_=r_, tp_=tp_, n_=n_, last=(n_ == len(ops_) - 1), rows=rows, col=col: e.matmul(
                            psy[rows, col:col + NS], l_, r_, start=(n_ == 0), stop=last, tile_position=tp_),
                            reads=[K_, "Kl", "h0b", "uT_s"], writes=["psy"])
        ys_s = sb("ys_s", [128, 4, 4, NS])
        yt_s = sb("yt_s", [128, 4, 4, NS])
        ygs = sb("ygs", [128, 4, 128], BF16)
        P.pool(lambda e: e.memset(ygs[:], 0.0), writes=["ygs"])
        ysf = ys_s[:].rearrange("p g j s -> p (g j s)")
        ytf = yt_s[:].rearrange("p g j s -> p (g j s)")
        P.act(lambda e: e.copy(ysf, psy[:, 0:16 * NS]), reads=["psy"], writes=["ys_s"])
        for gq in range(4):
            P.dve(lambda e, gq=gq: e.scalar_tensor_tensor(
                ys_s[:, gq, :, :], uT_s[:, gq, 0:TK].rearrange("p (s j) -> p j s", j=4), dcol[:, gq:gq + 1],
                ys_s[:, gq, :, :], ALU.mult, ALU.add), reads=["ys_s", K_, "uT_s"], writes=["ys_s"])
        P.dve(lambda e: e.tensor_tensor(ytf, ysf, ysf, ALU.mult), reads=["ys_s"], writes=["yt_s"])
        P.dve(lambda e: e.tensor_scalar(ytf, ytf, 0.044715, 1.0, ALU.mult, ALU.add), reads=["yt_s"], writes=["yt_s"])
        P.dve(lambda e: e.tensor_tensor(ytf, ytf, ysf, ALU.mult), reads=["yt_s", "ys_s"], writes=["yt_s"])
        P.act(lambda e: e.activation(ytf, ytf, AF.Sigmoid, scale=2.0 * math.sqrt(2.0 / math.pi)),
              reads=["yt_s"], writes=["yt_s"])
        for gq in range(4):
            P.dve(lambda e, gq=gq: e.tensor_tensor(
                ygs[:, gq, 0:TK].rearrange("p (s j) -> p j s", j=4), ys_s[:, gq, :, :], yt_s[:, gq, :, :], ALU.mult),
                reads=["yt_s", "ys_s", "ygs"], writes=["ygs"])
        P.dma(yg_d_s.rearrange("g p t -> p g t"), ygs[:], reads=["ygs"])
        P.dma(ssm_re_p.rearrange("(gp g2) p -> (g2 p) gp", g2=2), fin[:, :, 0], reads=["fin"],
              allow_slow_non_contiguous=True)
        P.dma(ssm_im_p.rearrange("(gp g2) p -> (g2 p) gp", g2=2), fin[:, :, 1], reads=["fin"],
              allow_slow_non_contiguous=True)
        phase_end(s2)

        phase_end(s_u)
        if KSTAGE == 2:
            P.barrier(); P.emit()
            return nc
        s5 = ExitStack()
        cur[0] = s5
        NPG = 16
        NPAGES = NS * NPG
        TK = 4 * NS
        pKT = [ps(f"pKT{i}", [128, 512]) for i in range(2)]
        pS5 = [ps(f"pS5_{i}", [128, 512]) for i in range(2)]
        po5 = ps("po5", [128, 512])
        pon5 = ps("pon5", [128, 512])
        pb5 = ps("pb5", [128, 512])
        pt_bc = sb("pt_bc", [128, NPAGES], I32)
        P.dma(pt_bc[:], pt.partition_broadcast(128), writes=["pt_bc"])
        iota_p = sb("iota_p", [128, 1])
        P.pool(lambda e: e.iota(iota_p[:], [[0, 1]], base=0, channel_multiplier=1,
                                allow_small_or_imprecise_dtypes=True), writes=["iota_p"])
        idx_f = sb("idx_f", [128, NPAGES])
        idx_i = sb("idx_i", [128, NPAGES], I32)
        P.dve(lambda e: e.tensor_copy(idx_f[:], pt_bc[:]), reads=["pt_bc"], writes=["idx_f"])
        P.dve(lambda e: e.tensor_scalar(idx_f[:], idx_f[:], 128.0, iota_p[:, 0:1], ALU.mult, ALU.add),
              reads=["idx_f", "iota_p"], writes=["idx_f"])
        P.dve(lambda e: e.tensor_copy(idx_i[:], idx_f[:]), reads=["idx_f"], writes=["idx_i"])

        def gather(out_ap, src, c, writes, reads=(), sem="pooldma"):
            return P.op(sem, lambda e: e.indirect_dma_start(
                out=out_ap, out_offset=None, in_=src[:, :],
                in_offset=bass.IndirectOffsetOnAxis(ap=idx_i[:, c:c + 1], axis=0)),
                reads=["idx_i"] + list(reads), writes=writes, inc=16)

        pf = sb("pf", [128, NPAGES, NH])
        P.add_sem("pfg", "gpsimd")
        for i in range(6):
            P.add_sem(f"kg{i}", "gpsimd")
            P.add_sem(f"vg{i}", "gpsimd")
        for c in range(NPAGES):
            gather(pf[:, c, :], clf, c, [("pf", c)], sem="pfg")
        Ls = sb("Ls", [128, 128])
        P.pool(lambda e: e.memset(Ls[:], 1.0), writes=["Ls"])
        P.pool(lambda e: e.affine_select(Ls[:], Ls[:], [[-1, 128]], ALU.is_ge, 0.0,
                                         base=-1, channel_multiplier=1), reads=["Ls"], writes=["Ls"])
        BT = sb("BT", [128, 128])
        BT3 = BT[:].rearrange("p (s j) -> p s j", j=4)
        P.pool(lambda e: e.memset(BT[:], 1.0), writes=["BT"])
        P.pool(lambda e: e.affine_select(BT3, BT3, [[4, 32], [1, 4]], ALU.is_ge, 0.0,
                                         base=0, channel_multiplier=-1), reads=["BT"], writes=["BT"])
        P.pool(lambda e: e.affine_select(BT3, BT3, [[-4, 32], [0, 4]], ALU.is_ge, 0.0,
                                         base=0, channel_multiplier=1), reads=["BT"], writes=["BT"])
        maskN = sb("maskN", [128, 128])
        P.pool(lambda e: e.tensor_scalar(maskN[:], BT[:], -1.0, -NEG, ALU.add, ALU.mult), reads=["BT"], writes=["maskN"])
        qbd = sb("qbd", [128, 4, NS, 2, 4], BF16)
        P.pool(lambda e: e.memset(qbd[:], 0.0), writes=["qbd"])
        for h2 in range(2):
            pr = slice(64 * h2, 64 * h2 + 64)
            for hp in range(4):
                P.pool(lambda e, pr=pr, h2=h2, hp=hp: e.tensor_copy(
                    qbd[pr, hp, :, h2, :], qT_s[pr, hp, 0:TK].rearrange("p (s q) -> p s q", q=4)),
                    reads=["qT_s", "qbd"], writes=["qbd"])
        Dsb = sb("Dsb", [128, NPAGES, NH])
        tot5 = sb("tot5", [128, NPAGES, NH])
        pf_flat = pf[:].rearrange("p c h -> p (c h)")
        D_flat = Dsb[:].rearrange("p c h -> p (c h)")
        tot_flat = tot5[:].rearrange("p c h -> p (c h)")
        ncols = NPAGES * NH
        for c0 in range(0, ncols, 512):
            cw = min(512, ncols - c0)
            keys = [("pf", c) for c in range(NPAGES)]
            P.pe(lambda e, c0=c0, cw=cw: e.matmul(pS5[0][:, 0:cw], Ls[:], pf_flat[:, c0:c0 + cw], start=True, stop=True),
                 reads=keys + ["Ls"], writes=["pS5_0"])
            P.act(lambda e, c0=c0, cw=cw: e.copy(D_flat[:, c0:c0 + cw], pS5[0][:, 0:cw]), reads=["pS5_0"], writes=["Dsb"])
            P.pe(lambda e, c0=c0, cw=cw: e.matmul(pS5[1][:, 0:cw], ones_f[:], pf_flat[:, c0:c0 + cw], start=True, stop=True),
                 reads=keys + ["ones_f"], writes=["pS5_1"])
            P.act(lambda e, c0=c0, cw=cw: e.copy(tot_flat[:, c0:c0 + cw], pS5[1][:, 0:cw]), reads=["pS5_1"], writes=["tot5"])
        D4 = Dsb[:].rearrange("p (s g) h -> p s g h", g=NPG)
        T4 = tot5[:].rearrange("p (s g) h -> p s g h", g=NPG)
        lat = sb("lat", [128, NS, NH])
        P.dve(lambda e: e.memset(lat[:], 0.0), writes=["lat"])
        for pg in range(NPG - 2, -1, -1):
            P.dve(lambda e, pg=pg: e.tensor_tensor(lat[:], lat[:], T4[:, :, pg + 1, :], ALU.add),
                  reads=["lat", "tot5"], writes=["lat"])
            P.dve(lambda e, pg=pg: e.tensor_tensor(D4[:, :, pg, :], D4[:, :, pg, :], lat[:], ALU.add),
                  reads=["lat", "Dsb"], writes=["Dsb"])
        P.pe(lambda e: e.matmul(pb5[:, 0:NH], BT[:], lf_s[:], start=True, stop=True),
             reads=["BT", "lf_s"], writes=["pb5"])
        negnq = sb("negnq", [128, NH])
        P.act(lambda e: e.activation(negnq[:], pb5[:, 0:NH], AF.Copy, scale=-1.0), reads=["pb5"], writes=["negnq"])
        tmpN = sb("tmpN", [128, 4, NS, 2, 4])
        PTn = sb("PTn", [128, NH, NS, 4], BF16)
        for hp in range(4):
            P.pe(lambda e, hp=hp: e.matmul(pon5[:, hp * 8 * NS:(hp + 1) * 8 * NS], kT_s[:, hp, :],
                                           qbd[:, hp, :, :, :].rearrange("p s a q -> p (s a q)"), start=True, stop=True),
                 reads=["kT_s", "qbd"], writes=["pon5"])
        for hp in range(4):
            P.dve(lambda e, hp=hp: e.tensor_tensor(
                tmpN[:, hp, :, :, :], pon5[:, hp * 8 * NS:(hp + 1) * 8 * NS].rearrange("p (s a q) -> p s a q", a=2, q=4),
                maskN[:, 0:TK].rearrange("p (s q) -> p s q", q=4).unsqueeze(2).broadcast_to([128, NS, 2, 4]), ALU.add),
                reads=["pon5", "maskN"], writes=["tmpN"])
            P.dve(lambda e, hp=hp: e.tensor_tensor(
                tmpN[:, hp, :, :, :], tmpN[:, hp, :, :, :],
                negnq[:, 2 * hp:2 * hp + 2].unsqueeze(1).unsqueeze(3).broadcast_to([128, NS, 2, 4]), ALU.add),
                reads=["tmpN", "negnq"], writes=["tmpN"])
            P.act(lambda e, hp=hp: e.activation(
                PTn[:, 2 * hp:2 * hp + 2, :, :].rearrange("p a s q -> p s a q"), tmpN[:, hp, :, :, :], AF.Exp),
                reads=["tmpN"], writes=["PTn"])
        for h in range(NH):
            P.pe(lambda e, h=h: e.matmul(pon5[0:HD + 1, h * TK:(h + 1) * TK], Vb_s[:, h, 0:HD + 1],
                                         PTn[:, h, :, :].rearrange("p s q -> p (s q)"), start=True, stop=True),
                 reads=["Vb_s", "PTn", "tmpN"], writes=["pon5"])
        osn = sb("osn", [HD + 1, NH * TK])
        P.act(lambda e: e.copy(osn[:], pon5[0:HD + 1, 0:NH * TK]), reads=["pon5"], writes=["osn"])
        NK = 6
        kst = [sb(f"kst{i}", [128, AW]) for i in range(NK)]
        vst = [sb(f"vst{i}", [128, AW]) for i in range(NK)]
        KTb = [sb(f"KTb{i}", [128, 4, 128], BF16) for i in range(2)]
        Vp = [sb(f"Vp{i}", [128, NPG, NH, HD + 2], BF16) for i in range(2)]
        PT5 = [sb(f"PT5_{i}", [128, NPG, NH, 4], BF16) for i in range(2)]
        tmpS = sb("tmpS", [128, NPG, NH, 4])
        for i in range(2):
            P.pool(lambda e, i=i: e.memset(Vp[i][:], 1.0), writes=[f"Vp{i}"])

        def pv(s_):
            i2 = s_ % 2
            for h in range(NH):
                for pg in range(NPG):
                    P.pe(lambda e, h=h, pg=pg, i2=i2, s_=s_: e.matmul(
                        po5[0:HD + 1, h * TK + s_ * 4:h * TK + s_ * 4 + 4], Vp[i2][:, pg, h, 0:HD + 1], PT5[i2][:, pg, h, :],
                        start=(pg == 0), stop=(pg == NPG - 1)), reads=[f"Vp{i2}", f"PT5_{i2}"], writes=["po5"])

        for s_ in range(NS + 1):
            if s_ < NS:
                i2 = s_ % 2
                for pg in range(NPG):
                    c = s_ * NPG + pg
                    ik = c % NK
                    gather(kst[ik][:], ck, c, [f"kst{ik}"], sem=f"kg{ik}")
                    gather(vst[ik][:], cv, c, [f"vst{ik}"], sem=f"vg{ik}")
                    pk_ = pKT[c % 2]
                    pkk = f"pKT{c % 2}"
                    kb_ = KTb[c % 2]
                    kbk = f"KTb{c % 2}"
                    for hp in range(4):
                        P.pe(lambda e, pk_=pk_, hp=hp, ik=ik: e.transpose(
                            pk_[:, hp * 128:(hp + 1) * 128], kst[ik][:, hp * 128:(hp + 1) * 128], ident_f[:]),
                            reads=[f"kst{ik}", "ident_f"], writes=[pkk])
                    P.act(lambda e, pk_=pk_, kb_=kb_: e.copy(kb_[:].rearrange("p c k -> p (c k)"), pk_[:]),
                          reads=[pkk], writes=[kbk])
                    for hp in range(4):
                        P.pe(lambda e, kb_=kb_, hp=hp, pg=pg, i2=i2, s_=s_: e.matmul(
                            pS5[i2][:, pg * 32 + hp * 8:pg * 32 + hp * 8 + 8], kb_[:, hp, :],
                            qbd[:, hp, s_, :, :].rearrange("p a q -> p (a q)"), start=True, stop=True),
                            reads=[kbk, "qbd"], writes=[f"pS5_{i2}"])
                    P.dve(lambda e, ik=ik, pg=pg, i2=i2: e.tensor_copy(
                        Vp[i2][:, pg, :, 0:HD], vst[ik][:].rearrange("p (h d) -> p h d", h=NH)),
                        reads=[f"vst{ik}"], writes=[f"Vp{i2}"])
                P.dve(lambda e, i2=i2, s_=s_: e.tensor_tensor(
                    tmpS[:], pS5[i2][:].rearrange("p (g h q) -> p g h q", h=NH, q=4),
                    Dsb[:, s_ * NPG:(s_ + 1) * NPG, :].unsqueeze(3).broadcast_to([128, NPG, NH, 4]), ALU.add),
                    reads=[f"pS5_{i2}", "Dsb"], writes=["tmpS"])
                P.act(lambda e, i2=i2: e.activation(PT5[i2][:].rearrange("p g h q -> p (g h q)"),
                                                    tmpS[:].rearrange("p g h q -> p (g h q)"), AF.Exp),
                      reads=["tmpS"], writes=[f"PT5_{i2}"])
            if s_ >= 1:
                pv(s_ - 1)
        osb5 = sb("osb5", [HD + 1, NH * TK])
        rbc5 = sb("rbc5", [HD, NH * TK])
        attT_s = sb("attT_s", [HD, NH, 128], BF16)
        P.pool(lambda e: e.memset(attT_s[:], 0.0), writes=["attT_s"])
        P.act(lambda e: e.copy(osb5[:], po5[0:HD + 1, 0:NH * TK]), reads=["po5"], writes=["osb5"])
        P.dve(lambda e: e.tensor_tensor(osb5[:], osb5[:], osn[:], ALU.add), reads=["osb5", "osn"], writes=["osb5"])
        P.dve(lambda e: e.reciprocal(osb5[HD:HD + 1, :], osb5[HD:HD + 1, :]), reads=["osb5"], writes=["osb5"])
        P.pe(lambda e: e.matmul(pb5[0:HD, 0:NH * TK], ones_f[HD:HD + 1, 0:HD], osb5[HD:HD + 1, :], start=True, stop=True),
             reads=["osb5", "ones_f"], writes=["pb5"])
        P.act(lambda e: e.copy(rbc5[:], pb5[0:HD, 0:NH * TK]), reads=["pb5"], writes=["rbc5"])
        P.dve(lambda e: e.tensor_tensor(attT_s[:, :, 0:TK], osb5[0:HD, :].rearrange("p (h t) -> p h t", h=NH),
                                        rbc5[:].rearrange("p (h t) -> p h t", h=NH), ALU.mult),
              reads=["osb5", "rbc5", "attT_s"], writes=["attT_s"])
        P.dma(att_d_s[:, :, :], attT_s[:], reads=["attT_s"], writes=[("att_d", "s")])
        phase_end(s5)
        if KSTAGE == 5:
            P.barrier(); P.emit()
            return nc
        s4 = ExitStack()
        cur[0] = s4
        pst = [ps(f"pst{i}_4", [128, 512]) for i in range(2)]
        pmm = [ps(f"pmm{i}_4", [128, 512]) for i in range(4)]
        pS = [ps(f"pS{i}_4", [128, 512]) for i in range(2)]
        wglu = sb("wglu", [128, 4, SW], BF16)
        wout_a = sb("wout_a", [HD, NH, D], BF16)
        wout_s = sb("wout_s", [128, 4, D], BF16)
        wpg = sb("wpg", [128, 8, D], BF16)
        wpe = sb("wpe", [128, 2, D], BF16)
        P.dma(wglu[:], wglu_d.rearrange("(c p) n -> p c n", p=128), reads=[("wsc", "glu")], writes=["wglu"])
        P.dma(wout_a[:], wout_d[0:AW, :].rearrange("(h p) n -> p h n", p=HD), reads=[("wsc", "out")], writes=["wout_a"])
        P.dma(wout_s[:], wout_d[AW:D, :].rearrange("(c p) n -> p c n", p=128), reads=[("wsc", "out")], writes=["wout_s"])
        P.dma(wpg[:], wpg_d.rearrange("(c p) n -> p c n", p=128), reads=[("wsc", "pg")], writes=["wpg"])
        P.dma(wpe[:], wpe_d.rearrange("(c p) n -> p c n", p=128), reads=[("wsc", "pe")], writes=["wpe"])
        bcs = {}
        for nm, src in (("g0", ln_in_g), ("b0", ln_in_b), ("g1", ln1_g), ("b1", ln1_b), ("g2", ln2_g), ("b2", ln2_b)):
            t_ = sb("bc_" + nm, [128, D])
            P.dma(t_[:], src.partition_broadcast(128), writes=["bc_" + nm], q="gpsimd")
            bcs[nm] = t_
        bpg_row = sb("bpg_row", [1, D])
        P.dma(bpg_row[:], b_pg.rearrange("(o n) -> o n", o=1), writes=["bpg_row"])
        bglu_col = sb("bglu_col", [128, 4])
        P.dma(bglu_col[:], b_glu.rearrange("(c p) -> p c", p=128), writes=["bglu_col"], allow_slow_non_contiguous=True)

        attb = sb("attb", [HD, NH, 512], BF16)
        ygb4 = sb("ygb4", [128, 4, 512], BF16)
        gate_b = sb("gate_b", [128, 512], BF16)
        ssmT = sb("ssmT", [128, 4, 512], BF16)
        xt4 = sb("xt4", [128, 4, D])
        o2 = sb("o2", [128, 4, D])
        pt4 = sb("pt4", [128, 4, PLE])
        pT = sb("pT4", [128, 2, 512], BF16)
        h1T = sb("h1T", [128, 8, 512], BF16)
        actT = sb("actT", [128, 32, 512], BF16)
        rl = [sb(f"rl{i}", [128, 512], BF16) for i in range(2)]
        tmp1 = [sb(f"tmp1_{i}", [128, 512]) for i in range(2)]
        tmp2 = [sb(f"tmp2_{i}", [128, 512]) for i in range(2)]
        wupt = [sb(f"wupt{i}", [128, 8, 128], BF16) for i in range(3)]
        wdnt = [sb(f"wdnt{i}", [128, 512], BF16) for i in range(4)]
        stats4 = sb("stats4", [128, 2, 6])
        mv4 = sb("mv4", [128, 2])
        rstd4 = sb("rstd4", [128, 1])
        nmr4 = sb("nmr4", [128, 1])
        cnt4 = {"wup": 0, "wdn": 0, "t1": 0, "t2": 0, "rl": 0}

        def ln_tile(buf, key, tt, gname, bname):
            for c in range(2):
                P.dve(lambda e, c=c: e.bn_stats(stats4[:, c, :], buf[:, tt, c * 512:(c + 1) * 512]),
                      reads=[(key, tt)], writes=[("stats4", c)])
            P.dve(lambda e: e.bn_aggr(mv4[:], stats4[:].rearrange("p a b -> p (a b)")),
                  reads=[("stats4", 0), ("stats4", 1)], writes=["mv4"])
            P.dve(lambda e: e.tensor_scalar(rstd4[:], mv4[:, 1:2], EPS, None, ALU.add), reads=["mv4"], writes=["rstd4"])
            P.act(lambda e: e.activation(rstd4[:], rstd4[:], AF.Sqrt), reads=["rstd4"], writes=["rstd4"])
            P.dve(lambda e: e.reciprocal(rstd4[:], rstd4[:]), reads=["rstd4"], writes=["rstd4"])
            P.dve(lambda e: e.scalar_tensor_tensor(nmr4[:], mv4[:, 0:1], -1.0, rstd4[:], ALU.mult, ALU.mult),
                  reads=["mv4", "rstd4"], writes=["nmr4"])
            P.act(lambda e: e.activation(buf[:, tt, :], buf[:, tt, :], AF.Identity, bias=nmr4[:, 0:1], scale=rstd4[:, 0:1]),
                  reads=["nmr4", "rstd4", (key, tt)], writes=[(key, tt)])
            P.dve(lambda e: e.tensor_tensor(buf[:, tt, :], buf[:, tt, :], bcs[gname][:], ALU.mult),
                  reads=[(key, tt), "bc_" + gname], writes=[(key, tt)])
            P.dve(lambda e: e.tensor_tensor(buf[:, tt, :], buf[:, tt, :], bcs[bname][:], ALU.add),
                  reads=[(key, tt), "bc_" + bname], writes=[(key, tt)])

        def proj_tm(ntt, ktiles, consumer, extra=None):
            for nh in range(2):
                for ki, (lf, rf, lkeys) in enumerate(ktiles):
                    rap, rkeys = rf(nh)
                    for tt in range(ntt):
                        P.pe(lambda e, tt=tt, lf=lf, rap=rap, ki=ki: e.matmul(
                            pmm[tt][:], lf(tt), rap, start=(ki == 0), stop=(ki == len(ktiles) - 1 and extra is None)),
                            reads=list(lkeys) + list(rkeys), writes=[f"pmm{tt}"])
                if extra is not None:
                    for tt in range(ntt):
                        extra(tt, nh)
                for tt in range(ntt):
                    consumer(tt, nh, pmm[tt], f"pmm{tt}")

        def next_buf(name, bufs):
            i = cnt4[name] % len(bufs)
            cnt4[name] += 1
            return bufs[i], f"{name}{i}"

        def post_block(I, ntt, x_src, p_src, att_src, yg_src, y_dst, gcol=None):
            tsl = lambda tt: slice(tt * 128, (tt + 1) * 128)
            ncol = ntt * 128
            if gcol is None:
                P.dma(attb[:, :, 0:ncol], att_src, reads=[("att_d", I)], writes=["attb"], slot="attb")
                P.dma(ygb4[:, :, 0:ncol], yg_src, reads=["yg_d"], writes=["ygb4"], slot="ygb4")
            else:
                P.op("g_attb", lambda e: e.indirect_dma_start(
                    out=attb[:].rearrange("p h t -> p (h t)"), out_offset=None, in_=att_d2[:, :],
                    in_offset=bass.IndirectOffsetOnAxis(ap=bidx[0:HD, gcol:gcol + 1], axis=0)),
                    reads=["bidx"], writes=["attb"], inc=16)
                P.op("g_ygb4", lambda e: e.indirect_dma_start(
                    out=ygb4[:].rearrange("p g t -> p (g t)"), out_offset=None, in_=yg_d2[:, :],
                    in_offset=bass.IndirectOffsetOnAxis(ap=bidx[:, NSB // 2 + gcol:NSB // 2 + gcol + 1], axis=0)),
                    reads=["bidx"], writes=["ygb4"], inc=16)
            P.dma(xt4[:, 0:ntt, :], x_src, writes=[("xt4", tt) for tt in range(ntt)], slot="xt4")
            P.dma(pt4[:, 0:ntt, :], p_src, writes=["pt4"], q="gpsimd", slot="pt4")
            for nt in range(4):
                psg = pS[nt % 2]
                pgk = f"pS{nt % 2}"
                for ct in range(4):
                    P.pe(lambda e, psg=psg, nt=nt, ct=ct: e.matmul(psg[:, 0:ncol], wglu[:, ct, nt * 128:(nt + 1) * 128], ygb4[:, ct, 0:ncol],
                                                                  start=(ct == 0), stop=(ct == 3)),
                         reads=["wglu", "ygb4"], writes=[pgk])
                P.act(lambda e, psg=psg, nt=nt: e.activation(gate_b[:, 0:ncol], psg[:, 0:ncol], AF.Sigmoid, bias=bglu_col[:, nt:nt + 1]),
                      reads=[pgk, "bglu_col"], writes=["gate_b"])
                P.dve(lambda e, nt=nt: e.tensor_tensor(ssmT[:, nt, 0:ncol], ygb4[:, nt, 0:ncol], gate_b[:, 0:ncol], ALU.mult),
                      reads=["gate_b", "ygb4"], writes=[("ssmT", nt)])
            for tt in range(ntt):
                ln_tile(xt4, "xt4", tt, "g0", "b0")
            for kk in range(2):
                pp_ = pst[kk]
                for tt in range(ntt):
                    P.pe(lambda e, pp_=pp_, tt=tt, kk=kk: e.transpose(pp_[:, tsl(tt)], pt4[:, tt, kk * 128:(kk + 1) * 128], ident_f[:]),
                         reads=["pt4", "ident_f"], writes=[f"pst{kk}"])
                P.act(lambda e, pp_=pp_, kk=kk: e.copy(pT[:, kk, 0:ntt * 128], pp_[:, 0:ntt * 128]),
                      reads=[f"pst{kk}"], writes=[("pT", kk)])
            kt = []
            for h in range(NH):
                kt.append((lambda tt, h=h: attb[:, h, tsl(tt)], lambda nh, h=h: (wout_a[:, h, nh * 512:(nh + 1) * 512], ["wout_a"]), ["attb"]))
            for ct in range(4):
                kt.append((lambda tt, ct=ct: ssmT[:, ct, tsl(tt)], lambda nh, ct=ct: (wout_s[:, ct, nh * 512:(nh + 1) * 512], ["wout_s"]),
                           [("ssmT", ct)]))

            def cons_out(tt, nh, pm_, pk_):
                t1_, t1k = next_buf("t1", tmp1)
                P.act(lambda e: e.copy(t1_[:], pm_[:]), reads=[pk_], writes=[t1k])
                P.dve(lambda e: e.scalar_tensor_tensor(xt4[:, tt, nh * 512:(nh + 1) * 512], xt4[:, tt, nh * 512:(nh + 1) * 512],
                                                       ALPHA, t1_[:], ALU.mult, ALU.add),
                      reads=[t1k, ("xt4", tt)], writes=[("xt4", tt)])
            proj_tm(ntt, kt, cons_out)
            for tt in range(ntt):
                ln_tile(xt4, "xt4", tt, "g1", "b1")
            for ct in range(8):
                pp_ = pst[ct % 2]
                for tt in range(ntt):
                    P.pe(lambda e, pp_=pp_, tt=tt, ct=ct: e.transpose(pp_[:, tsl(tt)], xt4[:, tt, ct * 128:(ct + 1) * 128], ident_f[:]),
                         reads=[("xt4", tt), "ident_f"], writes=[f"pst{ct % 2}"])
                P.act(lambda e, pp_=pp_, ct=ct: e.copy(h1T[:, ct, 0:ntt * 128], pp_[:, 0:ntt * 128]),
                      reads=[f"pst{ct % 2}"], writes=[("h1T", ct)])
            h1keys = [("h1T", ct) for ct in range(8)]
            for ft in range(32):
                wt_, wk_ = next_buf("wup", wupt)
                P.dma(wt_[:], wup_d[:, ft * 128:(ft + 1) * 128].rearrange("(c p) f -> p c f", p=128),
                      reads=[("wsc", "up")], writes=[wk_], q=("sync" if int(wk_[-1]) % 2 == 0 else "gpsimd"), slot=wk_)
                psu = pS[ft % 2]
                puk = f"pS{ft % 2}"
                for ct in range(8):
                    P.pe(lambda e, psu=psu, wt_=wt_, ct=ct: e.matmul(psu[:, 0:ncol], wt_[:, ct, :], h1T[:, ct, 0:ncol],
                                                                    start=(ct == 0), stop=(ct == 7)),
                         reads=[wk_] + h1keys, writes=[puk])
                r_, rk_ = next_buf("rl", rl)
                P.act(lambda e, psu=psu, r_=r_: e.activation(r_[:, 0:ncol], psu[:, 0:ncol], AF.Relu), reads=[puk], writes=[rk_])
                P.pool(lambda e, r_=r_, ft=ft: e.tensor_tensor(actT[:, ft, 0:ncol], r_[:, 0:ncol], r_[:, 0:ncol], ALU.mult),
                       reads=[rk_], writes=[("actT", ft)])
            kt = []
            for ft in range(32):
                def rf(nh, ft=ft):
                    wt_, wk_ = next_buf("wdn", wdnt)
                    P.dma(wt_[:], wdn_d[ft * 128:(ft + 1) * 128, nh * 512:(nh + 1) * 512], reads=[("wsc", "dn")], writes=[wk_],
                          q=("sync" if int(wk_[-1]) % 2 == 0 else "gpsimd"), slot=wk_)
                    return wt_[:], [wk_]
                kt.append((lambda tt, ft=ft: actT[:, ft, tsl(tt)], rf, [("actT", ft)]))

            def cons_dn(tt, nh, pm_, pk_):
                t1_, t1k = next_buf("t1", tmp1)
                P.act(lambda e: e.copy(t1_[:], pm_[:]), reads=[pk_], writes=[t1k])
                P.dve(lambda e: e.scalar_tensor_tensor(o2[:, tt, nh * 512:(nh + 1) * 512], xt4[:, tt, nh * 512:(nh + 1) * 512],
                                                       ALPHA, t1_[:], ALU.mult, ALU.add),
                      reads=[t1k, ("xt4", tt)], writes=[("o2", tt)])
            proj_tm(ntt, kt, cons_dn)
            ktg = [(lambda tt, ct=ct: h1T[:, ct, tsl(tt)], lambda nh, ct=ct: (wpg[:, ct, nh * 512:(nh + 1) * 512], ["wpg"]), [("h1T", ct)])
                   for ct in range(8)]

            def bias_mm(tt, nh):
                P.pe(lambda e: e.matmul(pmm[tt][:], ones_f[0:1, :], bpg_row[0:1, nh * 512:(nh + 1) * 512], start=False, stop=True),
                     reads=["ones_f", "bpg_row"], writes=[f"pmm{tt}"])

            kte = [(lambda tt, kk=kk: pT[:, kk, tsl(tt)], lambda nh, kk=kk: (wpe[:, kk, nh * 512:(nh + 1) * 512], ["wpe"]), [("pT", kk)])
                   for kk in range(2)]
            pe_ps = [(pS[0], "pS0"), (pS[1], "pS1"), (pst[0], "pst0"), (pst[1], "pst1")]
            for nh in range(2):
                for ki, (lf, rf, lkeys) in enumerate(ktg):
                    rap, rkeys = rf(nh)
                    for tt in range(ntt):
                        P.pe(lambda e, tt=tt, lf=lf, rap=rap, ki=ki: e.matmul(pmm[tt][:], lf(tt), rap, start=(ki == 0), stop=False),
                             reads=list(lkeys) + list(rkeys), writes=[f"pmm{tt}"])
                for tt in range(ntt):
                    bias_mm(tt, nh)
                for ki, (lf, rf, lkeys) in enumerate(kte):
                    rap, rkeys = rf(nh)
                    for tt in range(ntt):
                        P.pe(lambda e, tt=tt, lf=lf, rap=rap, ki=ki: e.matmul(pe_ps[tt][0][:], lf(tt), rap, start=(ki == 0), stop=(ki == 1)),
                             reads=list(lkeys) + list(rkeys), writes=[pe_ps[tt][1]])
                for tt in range(ntt):
                    t1_, t1k = next_buf("t1", tmp1)
                    t2_, t2k = next_buf("t2", tmp2)
                    P.act(lambda e, t1_=t1_, tt=tt: e.activation(t1_[:], pmm[tt][:], AF.Sigmoid), reads=[f"pmm{tt}"], writes=[t1k])
                    P.act(lambda e, t2_=t2_, tt=tt: e.copy(t2_[:], pe_ps[tt][0][:]), reads=[pe_ps[tt][1]], writes=[t2k])
                    P.dve(lambda e, t1_=t1_, t2_=t2_: e.tensor_tensor(t2_[:], t2_[:], t1_[:], ALU.mult), reads=[t1k, t2k], writes=[t2k])
                    P.dve(lambda e, t2_=t2_, tt=tt, nh=nh: e.tensor_tensor(o2[:, tt, nh * 512:(nh + 1) * 512],
                                                                          o2[:, tt, nh * 512:(nh + 1) * 512], t2_[:], ALU.add),
                          reads=[t2k, ("o2", tt)], writes=[("o2", tt)])
            for tt in range(ntt):
                ln_tile(o2, "o2", tt, "g2", "b2")
            P.dma(y_dst, o2[:, 0:ntt, :], reads=[("o2", tt) for tt in range(ntt)], slot="o2")

        bidx = sb("bidx_sb", [128, NSB], I32)
        P.dma(bidx[:], bidx_d[:, :], writes=["bidx"])
        P.add_sem("g_attb", "gpsimd")
        P.add_sem("g_ygb4", "gpsimd")
        for I in range(NSB // 2):
            rs = slice(I * 512, (I + 1) * 512)
            post_block(I, 4,
                       x_post[rs, :].rearrange("(t p) d -> p t d", p=128),
                       pp_post[rs, :].rearrange("(t p) d -> p t d", p=128),
                       None, None,
                       y_p[rs, :].rearrange("(t p) d -> p t d", p=128), gcol=I)

        post_block("s", 1,
                   x_s.rearrange("(t p) d -> p t d", p=128),
                   pp_s.rearrange("(t p) d -> p t d", p=128),
                   att_d_s[:, :, :], yg_d_s.rearrange("g p t -> p g t"),
                   y_s.rearrange("(t p) d -> p t d", p=128))

        P.barrier()
        P.emit()
        s4.close()
    return nc


_CACHE = {}
_LAST = {}


DBG = bool(int(os.environ.get('KDBG', '0')))


def _get_nc(T, NSEQ, NPHYS):
    key = (T, NSEQ, NPHYS)
    if key not in _CACHE:
        _CACHE[key] = build(T, NSEQ, NPHYS, dbg=DBG)
    return _CACHE[key]


def run_cores(inputs, T, NSEQ, NPHYS, ncores=8):
    nc = _get_nc(T, NSEQ, NPHYS)
    f = lambda a: np.ascontiguousarray(a, dtype=np.float32)
    in_maps = []
    nb = inputs["x_prompt"].shape[0]
    ST = NSEQ * 4
    ckf = f(inputs["cache_k"][0]).reshape(NPHYS * 128, AW)
    cvf = f(inputs["cache_v"][0]).reshape(NPHYS * 128, AW)
    clff = f(inputs["cache_logf"][0]).reshape(NPHYS * 128, NH)
    shared = {
        "w_in": f(inputs["w_in"][0]),
        "ln_in_g": f(inputs["ln_in_g"]),
        "ln_in_b": f(inputs["ln_in_b"]),
        "b_f": f(inputs["b_f"][0]),
        "lam_re": f(inputs["lam_re"][0]), "lam_im": f(inputs["lam_im"][0]), "log_dt": f(inputs["log_dt"][0]),
        "b_re": f(inputs["b_re"][0]), "b_im": f(inputs["b_im"][0]),
        "c_re": f(inputs["c_re"][0]), "c_im": f(inputs["c_im"][0]), "d_skip": f(inputs["d_skip"][0]),
        "w_glu": f(inputs["w_glu"][0]), "b_glu": f(inputs["b_glu"][0]), "w_out": f(inputs["w_out"][0]),
        "ln1_g": f(inputs["ln1_g"][0]), "ln1_b": f(inputs["ln1_b"][0]), "w_up": f(inputs["w_up"][0]),
        "w_down": f(inputs["w_down"][0]), "w_pe": f(inputs["w_pe"][0]), "w_pg": f(inputs["w_pg"][0]),
        "b_pg": f(inputs["b_pg"][0]), "ln2_g": f(inputs["ln2_g"][0]), "ln2_b": f(inputs["ln2_b"][0]),
        "ck": ckf, "cv": cvf, "clf": clff,
    }
    nh_ = T // 1024
    for c in range(ncores):
        b = c % nb
        sl = slice(c * NSEQ, (c + 1) * NSEQ)
        xs = np.zeros((128, D), np.float32)
        xs[0:ST] = inputs["x_sample"][sl].reshape(ST, D)
        ps_ = np.zeros((128, PLE), np.float32)
        ps_[0:ST] = inputs["p_sample"][0, sl].reshape(ST, PLE)
        m = dict(shared)
        m.update({
            "x_p": f(inputs["x_prompt"][b]),
            "pp_p": f(inputs["p_prompt"][0, b]),
            "x_post": f(inputs["x_prompt"][b, (c // nb) * (T // 2):(c // nb + 1) * (T // 2)]),
            "pp_post": f(inputs["p_prompt"][0, b, (c // nb) * (T // 2):(c // nb + 1) * (T // 2)]),
            "bidx": np.concatenate(
                [((c // nb) * nh_ + np.arange(nh_, dtype=np.int32))[None, :] * HD + (np.arange(128, dtype=np.int32) % HD)[:, None],
                 ((c // nb) * nh_ + np.arange(nh_, dtype=np.int32))[None, :] * 128 + np.arange(128, dtype=np.int32)[:, None]],
                axis=1).astype(np.int32),
            "x_s": xs, "pp_s": ps_,
            "st_re": f(inputs["state_re"][0, sl]).reshape(NSEQ, NG * SP),
            "st_im": f(inputs["state_im"][0, sl]).reshape(NSEQ, NG * SP),
            "pt": np.ascontiguousarray(inputs["page_table"][sl], dtype=np.int32).reshape(NSEQ * 16),
        })
        in_maps.append(m)
    res = run_bass_kernel_spmd(nc, in_maps, core_ids=list(range(ncores)))
    _LAST["r"] = res.results
    return res.results


def kernel(**inputs):
    T = inputs["x_prompt"].shape[1]
    NSEQ = inputs["x_sample"].shape[0] // 8
    NPHYS = inputs["cache_k"].shape[1]
    ST = NSEQ * 4
    r = run_cores(inputs, T, NSEQ, NPHYS)
    nb = inputs["x_prompt"].shape[0]
    db = inputs["x_sample"].shape[0]
    g = lambda c, n: np.asarray(r[c][n], dtype=np.float32)
    y_prompt = np.stack([np.concatenate([g(b, "y_p"), g(b + nb, "y_p")]) for b in range(nb)])
    k_prompt = np.stack([g(b, "k_p").reshape(T, NH, HD) for b in range(nb)])[None]
    v_prompt = np.stack([g(b, "v_p").reshape(T, NH, HD) for b in range(nb)])[None]
    lf_prompt = np.stack([g(b, "lf_p") for b in range(nb)])[None]
    sre_p = np.stack([g(b, "ssm_re_p") for b in range(nb)])[None]
    sim_p = np.stack([g(b, "ssm_im_p") for b in range(nb)])[None]
    y_sample = np.concatenate([g(c, "y_s")[0:ST].reshape(NSEQ, 4, D) for c in range(8)])
    k_sample = np.concatenate([g(c, "k_s").reshape(NSEQ, 4, NH, HD) for c in range(8)])[None]
    v_sample = np.concatenate([g(c, "v_s").reshape(NSEQ, 4, NH, HD) for c in range(8)])[None]
    lf_sample = np.concatenate([g(c, "lf_sd").reshape(NSEQ, 4, NH) for c in range(8)])[None]
    sre_s = np.concatenate([g(c, "ssm_re_s").reshape(NSEQ, NG, SP) for c in range(8)])[None]
    sim_s = np.concatenate([g(c, "ssm_im_s").reshape(NSEQ, NG, SP) for c in range(8)])[None]
    return (y_prompt, y_sample, k_prompt, v_prompt, lf_prompt, sre_p, sim_p,
            k_sample, v_sample, lf_sample, sre_s, sim_s)
```

```python
import math
import os
KSTAGE = int(os.environ.get('KSTAGE', '99'))
KSKIP = os.environ.get('KSKIP', '').split(',')
from contextlib import ExitStack
import numpy as np
import concourse.bass as bass
import concourse.mybir as mybir
from concourse.bass_utils import run_bass_kernel_spmd

F32 = mybir.dt.float32
BF16 = mybir.dt.bfloat16
I32 = mybir.dt.int32
U32 = mybir.dt.uint32
AF = mybir.ActivationFunctionType
ALU = mybir.AluOpType
AX = mybir.AxisListType

D = 1024
NH = 8
HD = 64
AW = 512
SW = 512
NG = 32
SG = 16
SP = 64
DFF = 4096
PLE = 256
INW = 2056
ALPHA = 2.0 ** 0.25
EPS = 1e-5
NEG = -60000.0


class Prog:
    STREAMS = ("sync", "scalar", "vector", "gpsimd", "tensor")

    def __init__(self, nc, es):
        self.nc = nc
        self.ops = {s: [] for s in self.STREAMS}
        self.sems = {}
        self.cnt = {}
        self.sem_stream = {}
        for name, stream in (("sync", "sync"), ("act", "scalar"), ("dve", "vector"), ("pool", "gpsimd"),
                             ("pooldma", "gpsimd"), ("pe", "tensor"), ("actdma", "scalar")):
            self.sems[name] = es.enter_context(nc.semaphore("s_" + name))
            self.cnt[name] = 0
            self.sem_stream[name] = stream
        self.waited = {}
        self.lastw = {}
        self.readers = {}
        self.es = es

    def add_sem(self, name, stream):
        self.sems[name] = self.es.enter_context(self.nc.semaphore("s_" + name))
        self.cnt[name] = 0
        self.sem_stream[name] = stream

    def op(self, sem, fn, reads=(), writes=(), inc=1):
        stream = self.sem_stream[sem]
        deps = set()
        for b in reads:
            if b in self.lastw:
                deps.add(self.lastw[b])
        for b in writes:
            if b in self.lastw:
                deps.add(self.lastw[b])
            for t in self.readers.get(b, ()):
                deps.add(t)
        waits = []
        for (ps, pc) in sorted(deps):
            if ps == "pe" and sem == "pe":
                continue
            key = (stream, ps)
            if self.waited.get(key, 0) >= pc:
                continue
            self.waited[key] = pc
            waits.append((ps, pc))
        self.cnt[sem] += inc
        tok = (sem, self.cnt[sem])
        self.ops[stream].append((waits, fn, sem, inc))
        for b in writes:
            self.lastw[b] = tok
            self.readers[b] = []
        for b in reads:
            self.readers.setdefault(b, []).append(tok)
        return tok

    def dma(self, out, in_, reads=(), writes=(), q="sync", slot=None, **kw):
        if slot is None:
            sem = {"sync": "sync", "gpsimd": "pooldma", "scalar": "actdma"}[q]
        else:
            sem = "d_" + slot
            if sem not in self.sems:
                self.add_sem(sem, q)
            assert self.sem_stream[sem] == q, (sem, q)
        return self.op(sem, lambda e: e.dma_start(out=out, in_=in_, **kw), reads, writes, inc=16)

    def pe(self, fn, reads=(), writes=()):
        return self.op("pe", fn, reads, writes)

    def act(self, fn, reads=(), writes=()):
        return self.op("act", fn, reads, writes)

    def dve(self, fn, reads=(), writes=()):
        return self.op("dve", fn, reads, writes)

    def pool(self, fn, reads=(), writes=()):
        return self.op("pool", fn, reads, writes)

    def barrier(self):
        for stream in self.STREAMS:
            waits = []
            for sname, c in self.cnt.items():
                if c > 0 and self.waited.get((stream, sname), 0) < c:
                    self.waited[(stream, sname)] = c
                    waits.append((sname, c))
            self.ops[stream].append((waits, None, None, 0))

    def emit(self, last=True):
        nc = self.nc
        final = [(s, c) for s, c in self.cnt.items() if c > 0]
        ops = self.ops
        self.ops = {s: [] for s in self.STREAMS}

        def run(e, stream):
            for waits, fn, sem, inc in ops[stream]:
                for (ps, pc) in waits:
                    e.wait_ge(self.sems[ps], pc)
                if fn is not None:
                    fn(e).then_inc(self.sems[sem], inc)
            if stream == "sync" and last:
                for (s, c) in final:
                    e.wait_ge(self.sems[s], c)

        with nc.Block() as block:
            @block.sync
            def _(e):
                run(e, "sync")

            @block.scalar
            def _(e):
                run(e, "scalar")

            @block.vector
            def _(e):
                run(e, "vector")

            @block.gpsimd
            def _(e):
                run(e, "gpsimd")

            @block.tensor
            def _(e):
                run(e, "tensor")


def build(T, NSEQ, NPHYS, dbg=False):
    nc = bass.Bass("TRN2", target_bir_lowering=False)
    NB = T // 128
    NSB = T // 512
    ST = NSEQ * 4

    def din(name, shape, dt=F32):
        return nc.dram_tensor(name, list(shape), dt, kind="ExternalInput").ap()

    def dout(name, shape, dt=F32):
        return nc.dram_tensor(name, list(shape), dt, kind="ExternalOutput").ap()

    x_p = din("x_p", [T, D])
    pp_p = din("pp_p", [T, PLE])
    w_in = din("w_in", [D, INW])
    ln_in_g = din("ln_in_g", [D])
    ln_in_b = din("ln_in_b", [D])
    b_f = din("b_f", [NH])
    lam_re = din("lam_re", [NG, SP])
    lam_im = din("lam_im", [NG, SP])
    log_dt = din("log_dt", [NG])
    b_re = din("b_re", [NG, SP, SG])
    b_im = din("b_im", [NG, SP, SG])
    c_re = din("c_re", [NG, SG, SP])
    c_im = din("c_im", [NG, SG, SP])
    d_skip = din("d_skip", [NG, SG])
    w_glu = din("w_glu", [SW, SW])
    b_glu = din("b_glu", [SW])
    w_out = din("w_out", [D, D])
    ln1_g = din("ln1_g", [D])
    ln1_b = din("ln1_b", [D])
    w_up = din("w_up", [D, DFF])
    w_down = din("w_down", [DFF, D])
    w_pe = din("w_pe", [PLE, D])
    w_pg = din("w_pg", [D, D])
    b_pg = din("b_pg", [D])
    ln2_g = din("ln2_g", [D])
    ln2_b = din("ln2_b", [D])
    x_s = din("x_s", [128, D])
    pp_s = din("pp_s", [128, PLE])
    ck = din("ck", [NPHYS * 128, AW])
    cv = din("cv", [NPHYS * 128, AW])
    clf = din("clf", [NPHYS * 128, NH])
    st_re = din("st_re", [NSEQ, NG * SP])
    st_im = din("st_im", [NSEQ, NG * SP])
    pt = din("pt", [NSEQ * 16], I32)
    y_p = dout("y_p", [T // 2, D])
    y_s = dout("y_s", [128, D])
    k_s = dout("k_s", [ST, AW])
    v_s = dout("v_s", [ST, AW])
    lf_sd = dout("lf_sd", [ST, NH])
    ssm_re_s = dout("ssm_re_s", [NSEQ, NG * SP])
    ssm_im_s = dout("ssm_im_s", [NSEQ, NG * SP])
    if dbg:
        att_d_s = dout("att_d_s", [HD, NH, 128], BF16)
        yg_d_s = dout("yg_d_s", [4, 128, 128], BF16)
    else:
        att_d_s = nc.dram_tensor("att_d_s", [HD, NH, 128], BF16).ap()
        yg_d_s = nc.dram_tensor("yg_d_s", [4, 128, 128], BF16).ap()
    att_d2 = nc.dram_tensor("att_d2", [NSB * HD, NH * 512], BF16).ap()
    yg_d2 = nc.dram_tensor("yg_d2", [NSB * 128, 4 * 512], BF16).ap()
    x_post = din("x_post", [T // 2, D])
    pp_post = din("pp_post", [T // 2, PLE])
    bidx_d = din("bidx", [128, NSB], I32)
    wup_d = nc.dram_tensor("wup_d", [D, DFF], BF16).ap()
    wdn_d = nc.dram_tensor("wdn_d", [DFF, D], BF16).ap()
    wout_d = nc.dram_tensor("wout_d", [D, D], BF16).ap()
    wpg_d = nc.dram_tensor("wpg_d", [D, D], BF16).ap()
    wpe_d = nc.dram_tensor("wpe_d", [PLE, D], BF16).ap()
    wglu_d = nc.dram_tensor("wglu_d", [SW, SW], BF16).ap()

    k_p = dout("k_p", [T, AW])
    v_p = dout("v_p", [T, AW])
    lf_p = dout("lf_p", [T, NH])
    ssm_re_p = dout("ssm_re_p", [NG, SP])
    ssm_im_p = dout("ssm_im_p", [NG, SP])

    es = ExitStack()
    with es:
        P = Prog(nc, es)

        cur = [es]

        def sb(name, shape, dt=F32):
            return cur[0].enter_context(nc.sbuf_tensor(name, list(shape), dt))

        def ps(name, shape, dt=F32):
            return cur[0].enter_context(nc.psum_tensor(name, list(shape), dt))

        def phase_end(*stacks):
            P.barrier()
            P.emit(last=False)
            for st_ in stacks:
                st_.close()

        ident_f = sb("ident_f", [128, 128])
        ident_b = sb("ident_b", [128, 128], BF16)
        P.pool(lambda e: e.memset(ident_f[:], 1.0), writes=["ident_f"])
        P.pool(lambda e: e.affine_select(ident_f[:], ident_f[:], [[-1, 128]], ALU.is_equal, 0.0,
                                         base=0, channel_multiplier=1), reads=["ident_f"], writes=["ident_f"])
        P.pool(lambda e: e.tensor_copy(ident_b[:], ident_f[:]), reads=["ident_f"], writes=["ident_b"])
        triu = sb("triu", [128, 128])
        P.pool(lambda e: e.memset(triu[:], 1.0), writes=["triu"])
        P.pool(lambda e: e.affine_select(triu[:], triu[:], [[1, 128]], ALU.is_ge, 0.0,
                                         base=0, channel_multiplier=-1), reads=["triu"], writes=["triu"])
        ones_f = sb("ones_f", [128, 128])
        P.pool(lambda e: e.memset(ones_f[:], 1.0), writes=["ones_f"])

        g_in_col = sb("g_in_col", [128, 8])
        b_in_col = sb("b_in_col", [128, 8])
        P.dma(g_in_col[:], ln_in_g.rearrange("(c p) -> p c", p=128), writes=["g_in_col"],
              allow_slow_non_contiguous=True)
        P.dma(b_in_col[:], ln_in_b.rearrange("(c p) -> p c", p=128), writes=["b_in_col"],
              allow_slow_non_contiguous=True)
        bf_bc = sb("bf_bc", [128, NH])
        P.dma(bf_bc[:], b_f.partition_broadcast(128), writes=["bf_bc"])

        NS = NSEQ
        kT_s = sb("kT_s", [128, 4, 128], BF16)
        qT_s = sb("qT_s", [128, 4, 128], BF16)
        uT_s = sb("uT_s", [128, 4, 128], BF16)
        Vb_s = sb("Vb_s", [128, NH, HD + 2], BF16)
        lf_s = sb("lf_s", [128, NH])
        P.pool(lambda e: e.memset(Vb_s[:], 1.0), writes=["Vb_s"])
        NCB = 3
        cst = [sb(f"cst{i}", [128, 514]) for i in range(NCB)]
        csb = [sb(f"csb{i}", [128, 512], BF16) for i in range(NCB)]
        ncv = [0]

        def convert(jobs):
            base = ncv[0]
            ncv[0] += len(jobs)

            def load(j):
                src, cw, _ = jobs[j]
                i_ = (base + j) % NCB
                P.dma(cst[i_][:, 0:cw], src, writes=[f"cst{i_}"], q="gpsimd", slot=f"cst{i_}")
            for j in range(min(NCB - 1, len(jobs))):
                load(j)
            for j in range(len(jobs)):
                if j + NCB - 1 < len(jobs):
                    load(j + NCB - 1)
                src, cw, (kind, dst, key) = jobs[j]
                i_ = (base + j) % NCB
                if kind == "sb":
                    P.pool(lambda e, i_=i_, cw=cw, dst=dst: e.tensor_copy(dst, cst[i_][:, 0:cw]),
                           reads=[f"cst{i_}"], writes=[key])
                else:
                    P.pool(lambda e, i_=i_, cw=cw: e.tensor_copy(csb[i_][:, 0:cw], cst[i_][:, 0:cw]),
                           reads=[f"cst{i_}"], writes=[f"csb{i_}"])
                    P.dma(dst, csb[i_][:, 0:cw], reads=[f"csb{i_}"], writes=[key], q="gpsimd", slot=f"csb{i_}")

        s_u = ExitStack()
        cur[0] = s_u
        uT = sb("uT", [128, 4, T], BF16)
        s_att = ExitStack()
        cur[0] = s_att
        kT = sb("kT", [128, 4, T], BF16)
        qT = sb("qT", [128, 4, T], BF16)
        Vb = sb("Vb", [128, NB, NH, HD + 2], BF16)
        lf_all = sb("lf_all", [128, NB, NH])
        s1 = ExitStack()
        cur[0] = s1
        w_in_b = sb("w_in_b", [128, 8, INW], BF16)
        jobs = []
        for ct in range(8):
            for c4 in range(4):
                jobs.append((w_in[ct * 128:(ct + 1) * 128, c4 * 514:(c4 + 1) * 514], 514,
                             ("sb", w_in_b[:, ct, c4 * 514:(c4 + 1) * 514], ("w_in_b", ct))))
        for (nm, src, dstd, rows, cols) in (("glu", w_glu, wglu_d, SW, SW), ("out", w_out, wout_d, D, D), ("pg", w_pg, wpg_d, D, D),
                                            ("pe", w_pe, wpe_d, PLE, D)):
            for r0 in range(0, rows, 128):
                for c0 in range(0, cols, 512):
                    jobs.append((src[r0:r0 + 128, c0:c0 + 512], 512, ("dram", dstd[r0:r0 + 128, c0:c0 + 512], ("wsc", nm, r0, c0))))
        convert(jobs)
        xin = [sb(f"xin{i}", [128, 4, D]) for i in range(1)]
        stats = sb("stats", [128, 2, 6])
        mv = sb("mv", [128, 2])
        rstd = sb("rstd", [128, 1])
        nmr = sb("nmr", [128, 1])
        hT = sb("hT", [128, 8, 512], BF16)
        ostage = [sb(f"ostage{i}", [128, 512]) for i in range(2)]
        lstage = sb("lstage", [128, NH])
        P.pool(lambda e: e.memset(Vb[:], 1.0), writes=["Vb"])
        pst = [ps(f"pst{i}", [128, 512]) for i in range(2)]
        pmm = [ps(f"pmm{i}", [128, 512]) for i in range(4)]
        nmm = [0]
        nos = [0]

        def layer_norm_block(xt, xkey, ntt):
            for tt in range(ntt):
                for c in range(2):
                    P.dve(lambda e, tt=tt, c=c: e.bn_stats(stats[:, c, :], xt[:, tt, c * 512:(c + 1) * 512]),
                          reads=[xkey], writes=[("stats", c)])
                P.dve(lambda e: e.bn_aggr(mv[:], stats[:].rearrange("p a b -> p (a b)")),
                      reads=[("stats", 0), ("stats", 1)], writes=["mv"])
                P.dve(lambda e: e.tensor_scalar(rstd[:], mv[:, 1:2], EPS, None, ALU.add),
                      reads=["mv"], writes=["rstd"])
                P.act(lambda e: e.activation(rstd[:], rstd[:], AF.Sqrt), reads=["rstd"], writes=["rstd"])
                P.dve(lambda e: e.reciprocal(rstd[:], rstd[:]), reads=["rstd"], writes=["rstd"])
                P.dve(lambda e: e.scalar_tensor_tensor(nmr[:], mv[:, 0:1], -1.0, rstd[:], ALU.mult, ALU.mult),
                      reads=["mv", "rstd"], writes=["nmr"])
                P.act(lambda e, tt=tt: e.activation(xt[:, tt, :], xt[:, tt, :], AF.Identity,
                                                    bias=nmr[:, 0:1], scale=rstd[:, 0:1]),
                      reads=["nmr", "rstd", xkey], writes=[xkey])

        def to_feature_major(xt, xkey, ntt, dst, dkey, gcol, bcol):
            for ct in range(8):
                pt = pst[ct % 2]
                pk = f"pst{ct % 2}"
                for tt in range(ntt):
                    P.pe(lambda e, pt=pt, tt=tt, ct=ct: e.transpose(pt[:, tt * 128:(tt + 1) * 128],
                                                                  xt[:, tt, ct * 128:(ct + 1) * 128], ident_f[:]),
                         reads=[xkey, "ident_f"], writes=[pk])
                P.act(lambda e, pt=pt, ct=ct: e.activation(dst[:, ct, 0:ntt * 128], pt[:, 0:ntt * 128], AF.Identity,
                                                          bias=bcol[:, ct:ct + 1], scale=gcol[:, ct:ct + 1]),
                      reads=[pk, "g_in_col", "b_in_col"], writes=[(dkey, ct)])

        for sbk in range(NSB):
            xt = xin[0]
            xkey = "xin0"
            P.dma(xt[:], x_p[sbk * 512:(sbk + 1) * 512, :].rearrange("(t p) d -> p t d", p=128), writes=[xkey], slot="xin")
            layer_norm_block(xt, xkey, 4)
            to_feature_major(xt, xkey, 4, hT, "hT", g_in_col, b_in_col)
            hkeys = [("hT", ct) for ct in range(8)]
            wkeys = [("w_in_b", ct) for ct in range(8)]
            tok = slice(sbk * 512, (sbk + 1) * 512)
            for (dst, dkey, c0, scl) in ((kT, "kT", 512, 1.0), (qT, "qT", 0, 0.125), (uT, "uT", 1544, 1.0)):
                for ft in range(4):
                    pm = pmm[nmm[0] % 4]
                    pk = f"pmm{nmm[0] % 4}"
                    nmm[0] += 1
                    for ct in range(8):
                        P.pe(lambda e, pm=pm, ct=ct, ft=ft, c0=c0: e.matmul(
                            pm[:], w_in_b[:, ct, c0 + ft * 128:c0 + (ft + 1) * 128], hT[:, ct, :],
                            start=(ct == 0), stop=(ct == 7)), reads=hkeys + wkeys, writes=[pk])
                    P.act(lambda e, pm=pm, dst=dst, ft=ft, scl=scl, tok=tok: e.activation(dst[:, ft, tok], pm[:], AF.Copy, scale=scl),
                          reads=[pk], writes=[(dkey, sbk)])
            for tt in range(4):
                blk = sbk * 4 + tt
                for (c0, dram, isv) in ((512, k_p, False), (1024, v_p, True)):
                    pm = pmm[nmm[0] % 4]
                    pk = f"pmm{nmm[0] % 4}"
                    nmm[0] += 1
                    for ct in range(8):
                        P.pe(lambda e, pm=pm, ct=ct, tt=tt, c0=c0: e.matmul(
                            pm[:], hT[:, ct, tt * 128:(tt + 1) * 128], w_in_b[:, ct, c0:c0 + 512],
                            start=(ct == 0), stop=(ct == 7)), reads=hkeys + wkeys, writes=[pk])
                    os_ = ostage[nos[0] % 2]
                    ok = f"ostage{nos[0] % 2}"
                    nos[0] += 1
                    P.act(lambda e, pm=pm, os_=os_: e.copy(os_[:], pm[:]), reads=[pk], writes=[ok])
                    if isv and 'vb' not in KSKIP:
                        P.act(lambda e, pm=pm, blk=blk: e.copy(
                            Vb[:, blk, :, 0:HD], pm[:].rearrange("p (h d) -> p h d", h=NH)),
                            reads=[pk], writes=["Vb"])
                    if 'odma' not in KSKIP:
                        P.dma(dram[blk * 128:(blk + 1) * 128, :], os_[:], reads=[ok], slot=ok)
                if 'lf' in KSKIP:
                    continue
                pm = pmm[nmm[0] % 4]
                pk = f"pmm{nmm[0] % 4}"
                nmm[0] += 1
                for ct in range(8):
                    P.pe(lambda e, pm=pm, ct=ct, tt=tt: e.matmul(
                        pm[:, 0:NH], hT[:, ct, tt * 128:(tt + 1) * 128], w_in_b[:, ct, 1536:1544],
                        start=(ct == 0), stop=(ct == 7)), reads=hkeys + wkeys, writes=[pk])
                P.act(lambda e, pm=pm: e.copy(lstage[:], pm[:, 0:NH]), reads=[pk], writes=["lstage"])
                P.dve(lambda e: e.tensor_tensor(lstage[:], lstage[:], bf_bc[:], ALU.add),
                      reads=["lstage", "bf_bc"], writes=["lstage"])
                P.act(lambda e: e.activation(lstage[:], lstage[:], AF.Exp, scale=-1.0),
                      reads=["lstage"], writes=["lstage"])
                P.dve(lambda e: e.tensor_scalar(lstage[:], lstage[:], 1.0, None, ALU.add),
                      reads=["lstage"], writes=["lstage"])
                P.act(lambda e: e.activation(lstage[:], lstage[:], AF.Ln),
                      reads=["lstage"], writes=["lstage"])
                P.dve(lambda e, blk=blk: e.tensor_scalar(lf_all[:, blk, :], lstage[:], -1.0, None, ALU.mult),
                      reads=["lstage"], writes=[("lf_all", blk)])
        P.dma(lf_p.rearrange("(b p) h -> p b h", p=128), lf_all[:],
              reads=[("lf_all", b) for b in range(NB)])

        xt = xin[0]
        xkey = "xin0"
        P.dma(xt[:, 0, :], x_s[:, :], writes=[xkey], slot="xin")
        layer_norm_block(xt, xkey, 1)
        to_feature_major(xt, xkey, 1, hT, "hT", g_in_col, b_in_col)
        hkeys = [("hT", ct) for ct in range(8)]
        wkeys = [("w_in_b", ct) for ct in range(8)]
        for (dst, dkey, c0, scl) in ((kT_s, "kT_s", 512, 1.0), (qT_s, "qT_s", 0, 0.125), (uT_s, "uT_s", 1544, 1.0)):
            for ft in range(4):
                pm = pmm[nmm[0] % 4]
                pk = f"pmm{nmm[0] % 4}"
                nmm[0] += 1
                for ct in range(8):
                    P.pe(lambda e, pm=pm, ct=ct, ft=ft, c0=c0: e.matmul(
                        pm[:, 0:128], w_in_b[:, ct, c0 + ft * 128:c0 + (ft + 1) * 128], hT[:, ct, 0:128],
                        start=(ct == 0), stop=(ct == 7)), reads=hkeys + wkeys, writes=[pk])
                P.act(lambda e, pm=pm, dst=dst, ft=ft, scl=scl: e.activation(dst[:, ft, :], pm[:, 0:128], AF.Copy, scale=scl),
                      reads=[pk], writes=[dkey])
        for (c0, dram, isv) in ((512, k_s, False), (1024, v_s, True)):
            pm = pmm[nmm[0] % 4]
            pk = f"pmm{nmm[0] % 4}"
            nmm[0] += 1
            for ct in range(8):
                P.pe(lambda e, pm=pm, ct=ct, c0=c0: e.matmul(
                    pm[:], hT[:, ct, 0:128], w_in_b[:, ct, c0:c0 + 512],
                    start=(ct == 0), stop=(ct == 7)), reads=hkeys + wkeys, writes=[pk])
            os_ = ostage[nos[0] % 2]
            ok = f"ostage{nos[0] % 2}"
            nos[0] += 1
            P.act(lambda e, pm=pm, os_=os_: e.copy(os_[:], pm[:]), reads=[pk], writes=[ok])
            if isv:
                P.act(lambda e, pm=pm: e.copy(Vb_s[:, :, 0:HD], pm[:].rearrange("p (h d) -> p h d", h=NH)),
                      reads=[pk], writes=["Vb_s"])
            P.dma(dram[:, :], os_[0:ST, :], reads=[ok], slot=ok)
        pm = pmm[nmm[0] % 4]
        pk = f"pmm{nmm[0] % 4}"
        nmm[0] += 1
        for ct in range(8):
            P.pe(lambda e, pm=pm, ct=ct: e.matmul(
                pm[:, 0:NH], hT[:, ct, 0:128], w_in_b[:, ct, 1536:1544],
                start=(ct == 0), stop=(ct == 7)), reads=hkeys + wkeys, writes=[pk])
        P.act(lambda e, pm=pm: e.copy(lstage[:], pm[:, 0:NH]), reads=[pk], writes=["lstage"])
        P.dve(lambda e: e.tensor_tensor(lstage[:], lstage[:], bf_bc[:], ALU.add),
              reads=["lstage", "bf_bc"], writes=["lstage"])
        P.act(lambda e: e.activation(lstage[:], lstage[:], AF.Exp, scale=-1.0),
              reads=["lstage"], writes=["lstage"])
        P.dve(lambda e: e.tensor_scalar(lstage[:], lstage[:], 1.0, None, ALU.add),
              reads=["lstage"], writes=["lstage"])
        P.act(lambda e: e.activation(lstage[:], lstage[:], AF.Ln),
              reads=["lstage"], writes=["lstage"])
        P.dve(lambda e: e.tensor_scalar(lf_s[:], lstage[:], -1.0, None, ALU.mult),
              reads=["lstage"], writes=["lf_s"])
        P.dma(lf_sd[:, :], lf_s[0:ST, :], reads=["lf_s"])

        phase_end(s1)
        if KSTAGE == 1:
            P.barrier(); P.emit(); s_att.close(); s_u.close()
            return nc

        s3 = ExitStack()
        cur[0] = s3
        pst = [ps(f"pst{i}_3", [128, 512]) for i in range(2)]
        pmm = [ps(f"pmm{i}_3", [128, 512]) for i in range(4)]
        att_dbg = dout("att_dbg", [HD, NH, T]) if dbg else None
        dbg2 = dout("dbg2", [128, 1536]) if dbg else None
        dbg3 = dout("dbg3", [128, 2576]) if dbg else None
        dbgs = sb("dbgs", [128, 2576]) if dbg else None
        lf_keys = [("lf_all", b) for b in range(NB)]
        totb = sb("totb", [128, NB, NH])
        pre = sb("pre", [128, NB, NH])
        cc = sb("cc", [128, NB, NH])
        biasI = sb("biasI", [128, NB, NH])
        maskT = sb("maskT", [128, 128], BF16)
        maskf = sb("maskf", [128, 128])
        P.pool(lambda e: e.memset(maskf[:], 0.0), writes=["maskf"])
        P.pool(lambda e: e.affine_select(maskf[:], maskf[:], [[1, 128]], ALU.is_ge, NEG,
                                         base=0, channel_multiplier=-1), reads=["maskf"], writes=["maskf"])
        P.pool(lambda e: e.tensor_copy(maskT[:], maskf[:]), reads=["maskf"], writes=["maskT"])
        jobs = []
        for (nm, src, dstd, rows, cols) in (("up", w_up, wup_d, D, DFF), ("dn", w_down, wdn_d, DFF, D)):
            for r0 in range(0, rows, 128):
                for c0 in range(0, cols, 512):
                    jobs.append((src[r0:r0 + 128, c0:c0 + 512], 512, ("dram", dstd[r0:r0 + 128, c0:c0 + 512], ("wsc", nm, r0, c0))))
        convert(jobs)
        lf_flat = lf_all[:].rearrange("p b h -> p (b h)")
        pm = pmm[0]
        P.pe(lambda e: e.matmul(pm[:, 0:NB * NH], ones_f[:], lf_flat, start=True, stop=True),
             reads=lf_keys + ["ones_f"], writes=["pmm0"])
        P.act(lambda e: e.copy(totb[:].rearrange("p b h -> p (b h)"), pm[:, 0:NB * NH]), reads=["pmm0"], writes=["totb"])
        pm1 = pmm[1]
        P.pe(lambda e: e.matmul(pm1[:, 0:NB * NH], triu[:], lf_flat, start=True, stop=True),
             reads=lf_keys + ["triu"], writes=["pmm1"])
        P.act(lambda e: e.copy(cc[:].rearrange("p b h -> p (b h)"), pm1[:, 0:NB * NH]), reads=["pmm1"], writes=["cc"])
        P.dve(lambda e: e.memset(pre[:, 0, :], 0.0), writes=["pre"])
        for b in range(1, NB):
            P.dve(lambda e, b=b: e.tensor_tensor(pre[:, b, :], pre[:, b - 1, :], totb[:, b - 1, :], ALU.add),
                  reads=["pre", "totb"], writes=["pre"])
        P.dve(lambda e: e.tensor_tensor(cc[:], cc[:], pre[:], ALU.add), reads=["cc", "pre"], writes=["cc"])

        pS = [ps(f"pS{i}", [128, 512]) for i in range(2)]
        pT = [sb(f"pT{i}", [128, 512], BF16) for i in range(2)]
        osb = sb("osb", [HD + 1, 512])
        rbc = sb("rbc", [HD, 512])
        attT = sb("attT", [HD, NH, 512], BF16)
        attf = sb("attf", [HD, NH, 512]) if dbg else None
        nS = [0]
        pend = [None]

        def flush():
            if pend[0] is not None:
                fn_ = pend[0]
                pend[0] = None
                fn_()

        def epilogue(h, po, pok):
            P.act(lambda e, po=po: e.copy(osb[:], po[0:HD + 1, :]), reads=[pok], writes=["osb"])
            P.dve(lambda e: e.reciprocal(osb[HD:HD + 1, :], osb[HD:HD + 1, :]), reads=["osb"], writes=["osb"])
            pb = pst[h % 2]
            pbk = f"pst{h % 2}"
            P.pe(lambda e, pb=pb: e.matmul(pb[0:HD, :], ones_f[HD:HD + 1, 0:HD], osb[HD:HD + 1, :],
                                           start=True, stop=True), reads=["osb", "ones_f"], writes=[pbk])
            P.act(lambda e, pb=pb: e.copy(rbc[:], pb[0:HD, :]), reads=[pbk], writes=["rbc"])
            P.dve(lambda e, h=h: e.tensor_tensor(attT[:, h, :], osb[0:HD, :], rbc[:], ALU.mult),
                  reads=["osb", "rbc"], writes=[("attT", h)])

        for I in range(NSB):
            nkb = 4 * I + 4
            for kb in range(nkb):
                P.dve(lambda e, kb=kb, I=I: e.tensor_tensor(biasI[:, kb, :], pre[:, 4 * I, :], cc[:, kb, :], ALU.subtract),
                      reads=["pre", "cc"], writes=["biasI"])
            for h in range(NH):
                hp = slice((h % 2) * 64, (h % 2) * 64 + 64)
                ft = h // 2
                po = pmm[2 + (h % 2)]
                pok = f"pmm{2 + (h % 2)}"
                for kb in range(nkb):
                    c0 = 0 if kb < 4 * I else (kb - 4 * I) * 128
                    i = nS[0] % 2
                    nS[0] += 1
                    psS, pTt = pS[i], pT[i]
                    diag = kb >= 4 * I
                    P.pe(lambda e, psS=psS, kb=kb, c0=c0, diag=diag, hp=hp, ft=ft, I=I: e.matmul(
                        psS[:, c0:512], kT[hp, ft, kb * 128:(kb + 1) * 128], qT[hp, ft, I * 512 + c0:(I + 1) * 512],
                        start=True, stop=not diag), reads=[("kT", kb // 4), ("qT", I)], writes=[f"pS{i}"])
                    if diag:
                        P.pe(lambda e, psS=psS, c0=c0: e.matmul(psS[:, c0:c0 + 128], ident_b[:], maskT[:],
                                                               start=False, stop=True),
                             reads=["ident_b", "maskT"], writes=[f"pS{i}"])
                    P.act(lambda e, psS=psS, pTt=pTt, c0=c0, kb=kb, h=h: e.activation(
                        pTt[:, c0:512], psS[:, c0:512], AF.Exp, bias=biasI[:, kb, h:h + 1]),
                        reads=[f"pS{i}", "biasI"], writes=[f"pT{i}"])

                    def later(po=po, pok=pok, pTt=pTt, c0=c0, kb=kb, h=h, nkb=nkb, i=i):
                        P.pe(lambda e: e.matmul(
                            po[0:HD + 1, c0:512], Vb[:, kb, h, 0:HD + 1], pTt[:, c0:512],
                            start=(kb == 0), stop=(kb == nkb - 1)), reads=[f"pT{i}", "Vb"], writes=[pok])
                        if kb == nkb - 1:
                            epilogue(h, po, pok)
                    flush()
                    pend[0] = later
            flush()
            P.dma(att_d2[I * HD:(I + 1) * HD, :], attT[:].rearrange("p h t -> p (h t)"), reads=[("attT", h) for h in range(NH)],
                  writes=[("att_d", I)], slot="attT")

        phase_end(s3, s_att)
        if KSTAGE == 3:
            P.barrier(); P.emit(); s_u.close()
            return nc
        s2 = ExitStack()
        cur[0] = s2
        GP = 16
        NC = T // 8
        PI = math.pi
        TWO_PI = 2.0 * math.pi
        K_ = "ssmc"

        def dv(fn):
            return P.dve(fn, reads=[K_], writes=[K_])

        def ac(fn):
            return P.act(fn, reads=[K_], writes=[K_])

        def sdma(out, in_):
            return P.dma(out, in_, writes=[K_], q="gpsimd", allow_slow_non_contiguous=True)

        lam_r = sb("lam_r", [128, GP])
        lam_i = sb("lam_i", [128, GP])
        ldt = sb("ldt", [128, GP])
        Bre = sb("Bre", [128, GP, SG])
        Bim = sb("Bim", [128, GP, SG])
        Cre = sb("Cre", [128, GP, SG])
        Cim = sb("Cim", [128, GP, SG])
        dcol = sb("dcol", [128, 4])
        sdma(lam_r[:], lam_re.rearrange("(gp g2) p -> (g2 p) gp", g2=2))
        sdma(lam_i[:], lam_im.rearrange("(gp g2) p -> (g2 p) gp", g2=2))
        for g2 in range(2):
            sdma(ldt[64 * g2:64 * g2 + 64, :], log_dt.rearrange("(gp g2) -> g2 gp", g2=2)[g2].partition_broadcast(64))
            for gp in range(GP):
                sdma(Cre[64 * g2:64 * g2 + 64, gp, :], c_re[2 * gp + g2].rearrange("h p -> p h"))
                sdma(Cim[64 * g2:64 * g2 + 64, gp, :], c_im[2 * gp + g2].rearrange("h p -> p h"))
        sdma(Bre[:], b_re.rearrange("(gp g2) p h -> (g2 p) gp h", g2=2))
        sdma(Bim[:], b_im.rearrange("(gp g2) p h -> (g2 p) gp h", g2=2))
        sdma(dcol[:], d_skip.rearrange("g h -> (g h)").rearrange("(a p) -> p a", p=128))

        def small(name):
            return sb(name, [128, GP])

        dt_, lrdt, th, mag, rho8, t1, a1, sinv, t2, cosv, phi8 = [small(n) for n in (
            "dt_", "lrdt", "th", "mag", "rho8", "t1", "a1", "sinv", "t2", "cosv", "phi8")]
        tmpa, tmpb, nr, den, cre, cim = [small(n) for n in ("tmpa", "tmpb", "nr", "den", "cre", "cim")]
        AKr = sb("AKr", [128, GP, 9])
        AKi = sb("AKi", [128, GP, 9])

        def tt(o, a, b, op):
            return dv(lambda e: e.tensor_tensor(o, a, b, op))

        def ts(o, a, s1_, s2_, op0, op1=None):
            if op1 is None:
                return dv(lambda e: e.tensor_scalar(o, a, s1_, None, op0))
            return dv(lambda e: e.tensor_scalar(o, a, s1_, s2_, op0, op1))

        def reduce_pm_pi(dst, x, ki, kf, m, emit_v):
            emit_v(lambda e: e.tensor_scalar(ki, x, 1.0 / TWO_PI, None, ALU.mult))
            emit_v(lambda e: e.tensor_copy(kf, ki))
            emit_v(lambda e: e.scalar_tensor_tensor(dst, kf, -TWO_PI, x, ALU.mult, ALU.add))
            emit_v(lambda e: e.tensor_scalar(m, dst, PI, None, ALU.is_gt))
            emit_v(lambda e: e.scalar_tensor_tensor(dst, m, -TWO_PI, dst, ALU.mult, ALU.add))
            emit_v(lambda e: e.tensor_scalar(m, dst, -PI, None, ALU.is_lt))
            emit_v(lambda e: e.scalar_tensor_tensor(dst, m, TWO_PI, dst, ALU.mult, ALU.add))

        def cos_arg(dst, r, m, emit_v):
            emit_v(lambda e: e.tensor_scalar(dst, r, PI / 2, None, ALU.add))
            emit_v(lambda e: e.tensor_scalar(m, dst, PI, None, ALU.is_gt))
            emit_v(lambda e: e.scalar_tensor_tensor(dst, m, -TWO_PI, dst, ALU.mult, ALU.add))

        ki_s = sb("ki_s", [128, GP], I32)
        kf_s = small("kf_s")
        m_s = small("m_s")
        ac(lambda e: e.activation(dt_[:], ldt[:], AF.Exp))
        tt(lrdt[:], lam_r[:], dt_[:], ALU.mult)
        tt(th[:], lam_i[:], dt_[:], ALU.mult)
        ac(lambda e: e.activation(mag[:], lrdt[:], AF.Exp))
        ac(lambda e: e.activation(rho8[:], lrdt[:], AF.Exp, scale=8.0))
        reduce_pm_pi(t1[:], th[:], ki_s[:], kf_s[:], m_s[:], dv)
        ac(lambda e: e.activation(sinv[:], t1[:], AF.Sin))
        cos_arg(t2[:], t1[:], m_s[:], dv)
        ac(lambda e: e.activation(cosv[:], t2[:], AF.Sin))
        ts(a1[:], t1[:], 8.0, None, ALU.mult)
        reduce_pm_pi(phi8[:], a1[:], ki_s[:], kf_s[:], m_s[:], dv)
        dv(lambda e: e.memset(AKr[:, :, 0], 1.0))
        dv(lambda e: e.memset(AKi[:, :, 0], 0.0))
        tt(AKr[:, :, 1], mag[:], cosv[:], ALU.mult)
        tt(AKi[:, :, 1], mag[:], sinv[:], ALU.mult)
        for k in range(2, 9):
            tt(tmpa[:], AKr[:, :, k - 1], AKr[:, :, 1], ALU.mult)
            tt(tmpb[:], AKi[:, :, k - 1], AKi[:, :, 1], ALU.mult)
            tt(AKr[:, :, k], tmpa[:], tmpb[:], ALU.subtract)
            tt(tmpa[:], AKr[:, :, k - 1], AKi[:, :, 1], ALU.mult)
            tt(tmpb[:], AKi[:, :, k - 1], AKr[:, :, 1], ALU.mult)
            tt(AKi[:, :, k], tmpa[:], tmpb[:], ALU.add)
        ts(nr[:], AKr[:, :, 1], -1.0, None, ALU.add)
        tt(tmpa[:], lam_r[:], lam_r[:], ALU.mult)
        tt(tmpb[:], lam_i[:], lam_i[:], ALU.mult)
        tt(den[:], tmpa[:], tmpb[:], ALU.add)
        dv(lambda e: e.reciprocal(den[:], den[:]))
        tt(tmpa[:], nr[:], lam_r[:], ALU.mult)
        tt(tmpb[:], AKi[:, :, 1], lam_i[:], ALU.mult)
        tt(cre[:], tmpa[:], tmpb[:], ALU.add)
        tt(cre[:], cre[:], den[:], ALU.mult)
        tt(tmpa[:], AKi[:, :, 1], lam_r[:], ALU.mult)
        tt(tmpb[:], nr[:], lam_i[:], ALU.mult)
        tt(cim[:], tmpa[:], tmpb[:], ALU.subtract)
        tt(cim[:], cim[:], den[:], ALU.mult)
        bbr = sb("bbr", [128, GP, SG])
        bbi = sb("bbi", [128, GP, SG])
        t3a = sb("t3a", [128, GP, SG])
        t3b = sb("t3b", [128, GP, SG])
        cre_b = cre[:].unsqueeze(2).broadcast_to([128, GP, SG])
        cim_b = cim[:].unsqueeze(2).broadcast_to([128, GP, SG])
        tt(t3a[:], Bre[:], cre_b, ALU.mult)
        tt(t3b[:], Bim[:], cim_b, ALU.mult)
        tt(bbr[:], t3a[:], t3b[:], ALU.subtract)
        tt(t3a[:], Bim[:], cre_b, ALU.mult)
        tt(t3b[:], Bre[:], cim_b, ALU.mult)
        tt(bbi[:], t3a[:], t3b[:], ALU.add)
        MC = sb("MC", [128, GP, 9, 2, 32], BF16)
        Wt = sb("Wt", [128, 4, 8, 2, 128], BF16)
        Kl = sb("Kl", [128, 4, 8, 32], BF16)
        pw = [ps(f"pw{i}", [128, 512]) for i in range(2)]
        s2c = ExitStack()
        cur[0] = s2c
        XAr = sb("XAr", [128, GP, 8, SG])
        XAi = sb("XAi", [128, GP, 8, SG])
        t4a = sb("t4a", [128, GP, 9, SG])
        t4b = sb("t4b", [128, GP, 9, SG])
        akr8 = AKr[:, :, 0:8].unsqueeze(3).broadcast_to([128, GP, 8, SG])
        aki8 = AKi[:, :, 0:8].unsqueeze(3).broadcast_to([128, GP, 8, SG])
        bbr8 = bbr[:].unsqueeze(2).broadcast_to([128, GP, 8, SG])
        bbi8 = bbi[:].unsqueeze(2).broadcast_to([128, GP, 8, SG])
        tt(t4a[:, :, 0:8, :], akr8, bbr8, ALU.mult)
        tt(t4b[:, :, 0:8, :], aki8, bbi8, ALU.mult)
        tt(XAr[:], t4a[:, :, 0:8, :], t4b[:, :, 0:8, :], ALU.subtract)
        tt(t4a[:, :, 0:8, :], akr8, bbi8, ALU.mult)
        tt(t4b[:, :, 0:8, :], aki8, bbr8, ALU.mult)
        tt(XAi[:], t4a[:, :, 0:8, :], t4b[:, :, 0:8, :], ALU.add)
        CAr = sb("CAr", [128, GP, 9, SG])
        CAi = sb("CAi", [128, GP, 9, SG])
        akr9 = AKr[:].unsqueeze(3).broadcast_to([128, GP, 9, SG])
        aki9 = AKi[:].unsqueeze(3).broadcast_to([128, GP, 9, SG])
        cr9 = Cre[:].unsqueeze(2).broadcast_to([128, GP, 9, SG])
        ci9 = Cim[:].unsqueeze(2).broadcast_to([128, GP, 9, SG])
        tt(t4a[:], akr9, cr9, ALU.mult)
        tt(t4b[:], aki9, ci9, ALU.mult)
        tt(CAr[:], t4a[:], t4b[:], ALU.subtract)
        tt(t4a[:], aki9, cr9, ALU.mult)
        tt(t4b[:], akr9, ci9, ALU.mult)
        tt(CAi[:], t4a[:], t4b[:], ALU.add)
        ts(CAi[:], CAi[:], -1.0, None, ALU.mult)
        XAbd = sb("XAbd", [128, 4, 8, 2, 4, 32])
        CBDr = sb("CBDr", [128, GP, 32])
        CBDni = sb("CBDni", [128, GP, 32])
        dv(lambda e: e.memset(XAbd[:], 0.0))
        dv(lambda e: e.memset(MC[:], 0.0))
        dv(lambda e: e.memset(CBDr[:], 0.0))
        dv(lambda e: e.memset(CBDni[:], 0.0))
        for g2 in range(2):
            pr = slice(64 * g2, 64 * g2 + 64)
            cs = slice(16 * g2, 16 * g2 + 16)
            for gq in range(4):
                dv(lambda e, pr=pr, cs=cs, gq=gq: e.tensor_copy(
                    XAbd[pr, gq, :, 0, :, cs], XAr[pr, 4 * gq:4 * gq + 4, :, :].rearrange("p q k h -> p k q h")))
                dv(lambda e, pr=pr, cs=cs, gq=gq: e.tensor_copy(
                    XAbd[pr, gq, :, 1, :, cs], XAi[pr, 4 * gq:4 * gq + 4, :, :].rearrange("p q k h -> p k q h")))
            dv(lambda e, pr=pr, cs=cs: e.tensor_copy(MC[pr, :, :, 0, cs], CAr[pr]))
            dv(lambda e, pr=pr, cs=cs: e.tensor_copy(MC[pr, :, :, 1, cs], CAi[pr]))
            dv(lambda e, pr=pr, cs=cs: e.tensor_copy(CBDr[pr, :, cs], Cre[pr]))
            dv(lambda e, pr=pr, cs=cs: e.tensor_scalar(CBDni[pr, :, cs], Cim[pr], -1.0, None, ALU.mult))
        npw = 0
        for gq in range(4):
            for i in range(8):
                pwt = pw[npw % 2]
                pwk = f"pw{npw % 2}"
                npw += 1
                for ri in range(2):
                    P.pe(lambda e, pwt=pwt, gq=gq, i=i, ri=ri: e.transpose(
                        pwt[:, ri * 128:(ri + 1) * 128], XAbd[:, gq, 7 - i, ri, :, :].rearrange("p q c -> p (q c)"), ident_f[:]),
                        reads=[K_, "ident_f"], writes=[pwk])
                P.act(lambda e, pwt=pwt, gq=gq, i=i: e.copy(
                    Wt[:, gq, i, :, :].rearrange("p r c -> p (r c)"), pwt[:, 0:256]), reads=[pwk], writes=["Wt"])
        for gp in range(GP):
            gq, q = gp // 4, gp % 4
            pwt = pw[npw % 2]
            pwk = f"pw{npw % 2}"
            npw += 1
            for k in range(8):
                P.pe(lambda e, pwt=pwt, gq=gq, gp=gp, k=k: e.matmul(
                    pwt[:, k * 32:(k + 1) * 32], XAbd[:, gq, k, 0, :, :].rearrange("p q c -> p (q c)"), CBDr[:, gp, :],
                    start=True, stop=False), reads=[K_], writes=[pwk])
                P.pe(lambda e, pwt=pwt, gq=gq, gp=gp, k=k: e.matmul(
                    pwt[:, k * 32:(k + 1) * 32], XAbd[:, gq, k, 1, :, :].rearrange("p q c -> p (q c)"), CBDni[:, gp, :],
                    start=False, stop=True), reads=[K_], writes=[pwk])
            P.act(lambda e, pwt=pwt, gq=gq, q=q: e.copy(
                Kl[32 * q:32 * q + 32, gq, :, :].rearrange("p k c -> p (k c)"), pwt[32 * q:32 * q + 32, 0:256]),
                reads=[pwk], writes=["Kl"])

        phase_end(s2c)
        cur[0] = s2
        cidx = sb("cidx", [128, NC])
        P.pool(lambda e: e.iota(cidx[:], [[1, NC]], base=0, channel_multiplier=0,
                                allow_small_or_imprecise_dtypes=True), writes=["cidx"])
        ang, nS, nC, sr, si, ta, tb, Gr, Gi = [sb(n, [128, NC]) for n in
                                               ("ang", "nS", "nC", "sr", "si", "ta", "tb", "Gr", "Gi")]
        ki_w = sb("ki_w", [128, NC], I32)
        Xp = [[sb(f"Xp{q}_{ri}", [128, NC], BF16) for ri in range(2)] for q in range(4)]
        fin = sb("fin", [128, GP, 2])
        ysb = sb("ysb", [128, 1024])
        yt1 = sb("yt1", [128, 1024])
        ygb = sb("ygb", [128, 1024], BF16)
        pss = [ps(f"pss{i}", [128, 512]) for i in range(2)]
        psy = ps("psy", [128, 1024])
        for q in range(4):
            for ri in range(2):
                P.pool(lambda e, q=q, ri=ri: e.memset(Xp[q][ri][:, 0:1], 0.0), writes=[("Xp", q)])
        W_ = "ssmw"

        def wv(fn, extra_r=(), extra_w=()):
            return P.dve(fn, reads=[W_] + list(extra_r), writes=[W_] + list(extra_w))

        def wa(fn, extra_r=(), extra_w=()):
            return P.act(fn, reads=[W_] + list(extra_r), writes=[W_] + list(extra_w))

        ukeys = [("uT", sbk) for sbk in range(NSB)]
        for gq in range(4):
            for q in range(4):
                gp = 4 * gq + q
                rows = slice(32 * q, 32 * q + 32)
                for ri in range(2):
                    for i in range(8):
                        P.pe(lambda e, ri=ri, i=i, rows=rows, gq=gq, q=q: e.matmul(
                            pss[ri][:, 0:NC], Wt[rows, gq, i, ri, :], uT[rows, gq, i:T:8],
                            start=(i == 0), stop=(i == 7), tile_position=(32 * q, 0)),
                            reads=["Wt"] + ukeys, writes=[f"pss{ri}"])
                wa(lambda e: e.copy(sr[:], pss[0][:, 0:NC]), extra_r=["pss0"])
                wa(lambda e: e.copy(si[:], pss[1][:, 0:NC]), extra_r=["pss1"])
                wv(lambda e, gp=gp: e.tensor_scalar(ang[:], cidx[:], phi8[:, gp:gp + 1], None, ALU.mult),
                   extra_r=["cidx", K_])
                reduce_pm_pi(ta[:], ang[:], ki_w[:], tb[:], Gr[:], wv)
                wa(lambda e: e.activation(nS[:], ta[:], AF.Sin))
                cos_arg(tb[:], ta[:], Gr[:], wv)
                wa(lambda e: e.activation(nC[:], tb[:], AF.Sin))
                wv(lambda e: e.tensor_tensor(ta[:], sr[:], nC[:], ALU.mult))
                wv(lambda e: e.tensor_tensor(tb[:], si[:], nS[:], ALU.mult))
                wv(lambda e: e.tensor_tensor(Gr[:], ta[:], tb[:], ALU.add))
                wv(lambda e: e.tensor_tensor(ta[:], si[:], nC[:], ALU.mult))
                wv(lambda e: e.tensor_tensor(tb[:], sr[:], nS[:], ALU.mult))
                wv(lambda e: e.tensor_tensor(Gi[:], ta[:], tb[:], ALU.subtract))
                rho_b = rho8[:, gp:gp + 1].broadcast_to([128, NC])
                wv(lambda e, rho_b=rho_b: e.tensor_tensor_scan(sr[:], rho_b, Gr[:], 0.0, ALU.mult, ALU.add), extra_r=[K_])
                wv(lambda e, rho_b=rho_b: e.tensor_tensor_scan(si[:], rho_b, Gi[:], 0.0, ALU.mult, ALU.add), extra_r=[K_])
                wv(lambda e: e.tensor_tensor(ta[:], sr[:], nC[:], ALU.mult))
                wv(lambda e: e.tensor_tensor(tb[:], si[:], nS[:], ALU.mult))
                wv(lambda e: e.tensor_tensor(Gr[:], ta[:], tb[:], ALU.subtract))
                wv(lambda e: e.tensor_tensor(ta[:], si[:], nC[:], ALU.mult))
                wv(lambda e: e.tensor_tensor(tb[:], sr[:], nS[:], ALU.mult))
                wv(lambda e: e.tensor_tensor(Gi[:], ta[:], tb[:], ALU.add))
                wa(lambda e, q=q: e.copy(Xp[q][0][:, 1:NC], Gr[:, 0:NC - 1]), extra_w=[("Xp", q)])
                wa(lambda e, q=q: e.copy(Xp[q][1][:, 1:NC], Gi[:, 0:NC - 1]), extra_w=[("Xp", q)])
                wa(lambda e, gp=gp: e.copy(fin[:, gp, 0:1], Gr[:, NC - 1:NC]), extra_w=["fin"])
                wa(lambda e, gp=gp: e.copy(fin[:, gp, 1:2], Gi[:, NC - 1:NC]), extra_w=["fin"])
            for cq in range(NC // 128):
                csl = slice(cq * 128, (cq + 1) * 128)
                for q in range(4):
                    gp = 4 * gq + q
                    rows = slice(32 * q, 32 * q + 32)
                    for j in range(8):
                        ops_ = [(MC[:, gp, j + 1, 0, :], Xp[q][0][:, csl], (0, 32 * q)),
                                (MC[:, gp, j + 1, 1, :], Xp[q][1][:, csl], (0, 32 * q))]
                        for i in range(j + 1):
                            ops_.append((Kl[rows, gq, j - i, :],
                                         uT[rows, gq, cq * 1024 + i:(cq + 1) * 1024:8], (32 * q, 32 * q)))
                        for n_, (l_, r_, tp_) in enumerate(ops_):
                            P.pe(lambda e, l_=l_, r_=r_, tp_=tp_, n_=n_, last=(n_ == len(ops_) - 1), rows=rows, j=j: e.matmul(
                                psy[rows, j * 128:(j + 1) * 128], l_, r_, start=(n_ == 0), stop=last, tile_position=tp_),
                                reads=[K_, "Kl", ("Xp", q)] + ukeys, writes=["psy"])
                ysb3 = ysb[:].rearrange("p (j c) -> p j c", j=8)
                uview = uT[:, gq, cq * 1024:(cq + 1) * 1024].rearrange("p (c j) -> p j c", j=8)
                P.act(lambda e: e.copy(ysb[:], psy[:]), reads=["psy"], writes=["ysb"])
                P.dve(lambda e, uview=uview, ysb3=ysb3, gq=gq: e.scalar_tensor_tensor(
                    ysb3, uview, dcol[:, gq:gq + 1], ysb3, ALU.mult, ALU.add), reads=["ysb", K_] + ukeys, writes=["ysb"])
                P.dve(lambda e: e.tensor_tensor(yt1[:], ysb[:], ysb[:], ALU.mult), reads=["ysb"], writes=["yt1"])
                P.dve(lambda e: e.tensor_scalar(yt1[:], yt1[:], 0.044715, 1.0, ALU.mult, ALU.add),
                      reads=["yt1"], writes=["yt1"])
                P.dve(lambda e: e.tensor_tensor(yt1[:], yt1[:], ysb[:], ALU.mult), reads=["yt1", "ysb"], writes=["yt1"])
                P.act(lambda e: e.activation(yt1[:], yt1[:], AF.Sigmoid, scale=2.0 * math.sqrt(2.0 / math.pi)),
                      reads=["yt1"], writes=["yt1"])
                P.dve(lambda e, ysb3=ysb3: e.tensor_tensor(
                    ygb[:].rearrange("p (c j) -> p j c", j=8), ysb3, yt1[:].rearrange("p (j c) -> p j c", j=8), ALU.mult),
                    reads=["yt1", "ysb"], writes=["ygb"])
                for i2 in range(2):
                    I_ = 2 * cq + i2
                    P.dma(yg_d2[I_ * 128:(I_ + 1) * 128, gq * 512:(gq + 1) * 512], ygb[:, i2 * 512:(i2 + 1) * 512],
                          reads=["ygb"], slot="ygb")
        TK = 4 * NS
        h0sb = sb("h0sb", [NS, 2, NG * SP])
        P.dma(h0sb[:, 0, :], st_re[:, :], writes=["h0sb"])
        P.dma(h0sb[:, 1, :], st_im[:, :], writes=["h0sb"])
        h0f = sb("h0f", [128, 2, GP, NS])
        h0b = sb("h0b", [128, 2, GP, NS], BF16)
        for ri in range(2):
            pwt = pw[ri]
            for gp in range(GP):
                P.pe(lambda e, pwt=pwt, gp=gp, ri=ri: e.transpose(
                    pwt[:, gp * NS:(gp + 1) * NS], h0sb[:, ri, gp * 128:(gp + 1) * 128], ident_f[0:NS, 0:NS]),
                    reads=["h0sb", "ident_f"], writes=[f"pw{ri}"])
            P.act(lambda e, pwt=pwt, ri=ri: e.copy(h0f[:, ri, :, :].rearrange("p g s -> p (g s)"), pwt[:, 0:GP * NS]),
                  reads=[f"pw{ri}"], writes=["h0f"])
        P.dve(lambda e: e.tensor_copy(h0b[:].rearrange("p r g s -> p (r g s)"), h0f[:].rearrange("p r g s -> p (r g s)")),
              reads=["h0f"], writes=["h0b"])
        for gq in range(4):
            for q in range(4):
                gp = 4 * gq + q
                rows = slice(32 * q, 32 * q + 32)
                for ri in range(2):
                    for i in range(4):
                        P.pe(lambda e, ri=ri, i=i, rows=rows, gq=gq, q=q, gp=gp: e.matmul(
                            pss[ri][:, gp * NS:(gp + 1) * NS], Wt[rows, gq, i + 4, ri, :], uT_s[rows, gq, i:TK:4],
                            start=(i == 0), stop=(i == 3), tile_position=(32 * q, 0)),
                            reads=["Wt", "uT_s"], writes=[f"pss{ri}"])
        sr_s = sb("sr_s", [128, GP, NS])
        si_s = sb("si_s", [128, GP, NS])
        tA = sb("tA_s", [128, GP, NS])
        tB = sb("tB_s", [128, GP, NS])
        finr = sb("finr", [128, GP, NS])
        fini = sb("fini", [128, GP, NS])
        P.act(lambda e: e.copy(sr_s[:].rearrange("p g s -> p (g s)"), pss[0][:, 0:GP * NS]), reads=["pss0"], writes=["sr_s"])
        P.act(lambda e: e.copy(si_s[:].rearrange("p g s -> p (g s)"), pss[1][:, 0:GP * NS]), reads=["pss1"], writes=["si_s"])
        A4r = AKr[:, :, 4].unsqueeze(2).broadcast_to([128, GP, NS])
        A4i = AKi[:, :, 4].unsqueeze(2).broadcast_to([128, GP, NS])
        S_ = "ssms"

        def sv(fn, extra_r=(), extra_w=()):
            return P.dve(fn, reads=[S_, K_] + list(extra_r), writes=[S_] + list(extra_w))

        sv(lambda e: e.tensor_tensor(tA[:], h0f[:, 0, :, :], A4r, ALU.mult), extra_r=["h0f"])
        sv(lambda e: e.tensor_tensor(tB[:], h0f[:, 1, :, :], A4i, ALU.mult))
        sv(lambda e: e.tensor_tensor(tA[:], tA[:], tB[:], ALU.subtract))
        sv(lambda e: e.tensor_tensor(finr[:], tA[:], sr_s[:], ALU.add), extra_r=["sr_s"])
        sv(lambda e: e.tensor_tensor(tA[:], h0f[:, 1, :, :], A4r, ALU.mult))
        sv(lambda e: e.tensor_tensor(tB[:], h0f[:, 0, :, :], A4i, ALU.mult))
        sv(lambda e: e.tensor_tensor(tA[:], tA[:], tB[:], ALU.add))
        sv(lambda e: e.tensor_tensor(fini[:], tA[:], si_s[:], ALU.add), extra_r=["si_s"])
        fin_tm = sb("fin_tm", [NS, 2, NG * SP])
        for ri, fsrc in enumerate((finr, fini)):
            for g4 in range(4):
                pwt = pw[(ri * 4 + g4) % 2]
                pwk = f"pw{(ri * 4 + g4) % 2}"
                for gg in range(4):
                    gp = 4 * g4 + gg
                    P.pe(lambda e, pwt=pwt, gg=gg, gp=gp, fsrc=fsrc: e.transpose(
                        pwt[0:NS, gg * 128:(gg + 1) * 128], fsrc[:, gp, :], ident_f[:]),
                        reads=[S_, "ident_f"], writes=[pwk])
                P.act(lambda e, pwt=pwt, ri=ri, g4=g4: e.copy(fin_tm[:, ri, g4 * 512:(g4 + 1) * 512], pwt[0:NS, :]),
                      reads=[pwk], writes=[("fin_tm", ri, g4)])
        P.dma(ssm_re_s[:, :], fin_tm[:, 0, :], reads=[("fin_tm", 0, g4) for g4 in range(4)])
        P.dma(ssm_im_s[:, :], fin_tm[:, 1, :], reads=[("fin_tm", 1, g4) for g4 in range(4)])
        for gq in range(4):
            for q in range(4):
                gp = 4 * gq + q
                rows = slice(32 * q, 32 * q + 32)
                for j in range(4):
                    col = (gq * 4 + j) * NS
                    ops_ = [(MC[:, gp, j + 1, 0, :], h0b[:, 0, gp, :], (0, 32 * q)),
                            (MC[:, gp, j + 1, 1, :], h0b[:, 1, gp, :], (0, 32 * q))]
                    for i in range(j + 1):
                        ops_.append((Kl[rows, gq, j - i, :], uT_s[rows, gq, i:TK:4], (32 * q, 32 * q)))
                    for n_, (l_, r_, tp_) in enumerate(ops_):
                        P.pe(lambda e, l_=l_, r_=r_, tp_=tp_, n_=n_, last=(n_ == len(ops_) - 1), rows=rows, col=col: e.matmul(
                            psy[rows, col:col + NS], l_, r_, start=(n_ == 0), stop=last, tile_position=tp_),
                            reads=[K_, "Kl", "h0b", "uT_s"], writes=["psy"])
        ys_s = sb("ys_s", [128, 4, 4, NS])
        yt_s = sb("yt_s", [128, 4, 4, NS])
        ygs = sb("ygs", [128, 4, 128], BF16)
        P.pool(lambda e: e.memset(ygs[:], 0.0), writes=["ygs"])
        ysf = ys_s[:].rearrange("p g j s -> p (g j s)")
        ytf = yt_s[:].rearrange("p g j s -> p (g j s)")
        P.act(lambda e: e.copy(ysf, psy[:, 0:16 * NS]), reads=["psy"], writes=["ys_s"])
        for gq in range(4):
            P.dve(lambda e, gq=gq: e.scalar_tensor_tensor(
                ys_s[:, gq, :, :], uT_s[:, gq, 0:TK].rearrange("p (s j) -> p j s", j=4), dcol[:, gq:gq + 1],
                ys_s[:, gq, :, :], ALU.mult, ALU.add), reads=["ys_s", K_, "uT_s"], writes=["ys_s"])
        P.dve(lambda e: e.tensor_tensor(ytf, ysf, ysf, ALU.mult), reads=["ys_s"], writes=["yt_s"])
        P.dve(lambda e: e.tensor_scalar(ytf, ytf, 0.044715, 1.0, ALU.mult, ALU.add), reads=["yt_s"], writes=["yt_s"])
        P.dve(lambda e: e.tensor_tensor(ytf, ytf, ysf, ALU.mult), reads=["yt_s", "ys_s"], writes=["yt_s"])
        P.act(lambda e: e.activation(ytf, ytf, AF.Sigmoid, scale=2.0 * math.sqrt(2.0 / math.pi)),
              reads=["yt_s"], writes=["yt_s"])
        for gq in range(4):
            P.dve(lambda e, gq=gq: e.tensor_tensor(
                ygs[:, gq, 0:TK].rearrange("p (s j) -> p j s", j=4), ys_s[:, gq, :, :], yt_s[:, gq, :, :], ALU.mult),
                reads=["yt_s", "ys_s", "ygs"], writes=["ygs"])
        P.dma(yg_d_s.rearrange("g p t -> p g t"), ygs[:], reads=["ygs"])
        P.dma(ssm_re_p.rearrange("(gp g2) p -> (g2 p) gp", g2=2), fin[:, :, 0], reads=["fin"],
              allow_slow_non_contiguous=True)
        P.dma(ssm_im_p.rearrange("(gp g2) p -> (g2 p) gp", g2=2), fin[:, :, 1], reads=["fin"],
              allow_slow_non_contiguous=True)
        phase_end(s2)

        phase_end(s_u)
        if KSTAGE == 2:
            P.barrier(); P.emit()
            return nc
        s5 = ExitStack()
        cur[0] = s5
        NPG = 16
        NPAGES = NS * NPG
        TK = 4 * NS
        pKT = [ps(f"pKT{i}", [128, 512]) for i in range(2)]
        pS5 = [ps(f"pS5_{i}", [128, 512]) for i in range(2)]
        po5 = ps("po5", [128, 512])
        pon5 = ps("pon5", [128, 512])
        pb5 = ps("pb5", [128, 512])
        pt_bc = sb("pt_bc", [128, NPAGES], I32)
        P.dma(pt_bc[:], pt.partition_broadcast(128), writes=["pt_bc"])
        iota_p = sb("iota_p", [128, 1])
        P.pool(lambda e: e.iota(iota_p[:], [[0, 1]], base=0, channel_multiplier=1,
                                allow_small_or_imprecise_dtypes=True), writes=["iota_p"])
        idx_f = sb("idx_f", [128, NPAGES])
        idx_i = sb("idx_i", [128, NPAGES], I32)
        P.dve(lambda e: e.tensor_copy(idx_f[:], pt_bc[:]), reads=["pt_bc"], writes=["idx_f"])
        P.dve(lambda e: e.tensor_scalar(idx_f[:], idx_f[:], 128.0, iota_p[:, 0:1], ALU.mult, ALU.add),
              reads=["idx_f", "iota_p"], writes=["idx_f"])
        P.dve(lambda e: e.tensor_copy(idx_i[:], idx_f[:]), reads=["idx_f"], writes=["idx_i"])

        def gather(out_ap, src, c, writes, reads=(), sem="pooldma"):
            return P.op(sem, lambda e: e.indirect_dma_start(
                out=out_ap, out_offset=None, in_=src[:, :],
                in_offset=bass.IndirectOffsetOnAxis(ap=idx_i[:, c:c + 1], axis=0)),
                reads=["idx_i"] + list(reads), writes=writes, inc=16)

        pf = sb("pf", [128, NPAGES, NH])
        P.add_sem("pfg", "gpsimd")
        for i in range(6):
            P.add_sem(f"kg{i}", "gpsimd")
            P.add_sem(f"vg{i}", "gpsimd")
        NPH = (NPAGES + 127) // 128
        PP = min(128, NPAGES)
        pt_col = sb("pt_col", [128, NPH], I32)
        P.dma(pt_col[0:PP, :], pt.rearrange("(a p) -> p a", p=PP), writes=["pt_col"], allow_slow_non_contiguous=True)
        pfP = sb("pfP", [128, NPH, 128 * NH])
        clf_pages = clf.rearrange("(n k) h -> n (k h)", k=128)
        for a in range(NPH):
            P.op("pfg", lambda e, a=a: e.indirect_dma_start(
                out=pfP[0:PP, a, :], out_offset=None, in_=clf_pages,
                in_offset=bass.IndirectOffsetOnAxis(ap=pt_col[0:PP, a:a + 1], axis=0)),
                reads=["pt_col"], writes=[("pfP", a)], inc=16)
        ntp = 0
        for a in range(NPH):
            for h4 in range(0, NH, 4):
                pX = pKT[ntp % 2]
                pXk = f"pKT{ntp % 2}"
                ntp += 1
                for hh in range(4):
                    P.pe(lambda e, pX=pX, a=a, h=h4 + hh, hh=hh: e.transpose(
                        pX[:, hh * 128:hh * 128 + PP], pfP[0:PP, a, h:128 * NH:NH], ident_f[0:PP, 0:PP]),
                        reads=[("pfP", a), "ident_f"], writes=[pXk])
                P.act(lambda e, pX=pX, a=a, h4=h4: e.copy(
                    pf[:, a * 128:a * 128 + PP, h4:h4 + 4].rearrange("k c h -> k h c"),
                    pX[:].rearrange("k (h c) -> k h c", h=4)[:, :, 0:PP]), reads=[pXk], writes=["pf"])
        Ls = sb("Ls", [128, 128])
        P.pool(lambda e: e.memset(Ls[:], 1.0), writes=["Ls"])
        P.pool(lambda e: e.affine_select(Ls[:], Ls[:], [[-1, 128]], ALU.is_ge, 0.0,
                                         base=-1, channel_multiplier=1), reads=["Ls"], writes=["Ls"])
        BT = sb("BT", [128, 128])
        BT3 = BT[:].rearrange("p (s j) -> p s j", j=4)
        P.pool(lambda e: e.memset(BT[:], 1.0), writes=["BT"])
        P.pool(lambda e: e.affine_select(BT3, BT3, [[4, 32], [1, 4]], ALU.is_ge, 0.0,
                                         base=0, channel_multiplier=-1), reads=["BT"], writes=["BT"])
        P.pool(lambda e: e.affine_select(BT3, BT3, [[-4, 32], [0, 4]], ALU.is_ge, 0.0,
                                         base=0, channel_multiplier=1), reads=["BT"], writes=["BT"])
        maskN = sb("maskN", [128, 128])
        P.pool(lambda e: e.tensor_scalar(maskN[:], BT[:], -1.0, -NEG, ALU.add, ALU.mult), reads=["BT"], writes=["maskN"])
        qbd = sb("qbd", [128, 4, NS, 2, 4], BF16)
        P.pool(lambda e: e.memset(qbd[:], 0.0), writes=["qbd"])
        for h2 in range(2):
            pr = slice(64 * h2, 64 * h2 + 64)
            for hp in range(4):
                P.pool(lambda e, pr=pr, h2=h2, hp=hp: e.tensor_copy(
                    qbd[pr, hp, :, h2, :], qT_s[pr, hp, 0:TK].rearrange("p (s q) -> p s q", q=4)),
                    reads=["qT_s", "qbd"], writes=["qbd"])
        Dsb = sb("Dsb", [128, NPAGES, NH])
        tot5 = sb("tot5", [128, NPAGES, NH])
        pf_flat = pf[:].rearrange("p c h -> p (c h)")
        D_flat = Dsb[:].rearrange("p c h -> p (c h)")
        tot_flat = tot5[:].rearrange("p c h -> p (c h)")
        ncols = NPAGES * NH
        for c0 in range(0, ncols, 512):
            cw = min(512, ncols - c0)
            keys = ["pf"]
            P.pe(lambda e, c0=c0, cw=cw: e.matmul(pS5[0][:, 0:cw], Ls[:], pf_flat[:, c0:c0 + cw], start=True, stop=True),
                 reads=keys + ["Ls"], writes=["pS5_0"])
            P.act(lambda e, c0=c0, cw=cw: e.copy(D_flat[:, c0:c0 + cw], pS5[0][:, 0:cw]), reads=["pS5_0"], writes=["Dsb"])
            P.pe(lambda e, c0=c0, cw=cw: e.matmul(pS5[1][:, 0:cw], ones_f[:], pf_flat[:, c0:c0 + cw], start=True, stop=True),
                 reads=keys + ["ones_f"], writes=["pS5_1"])
            P.act(lambda e, c0=c0, cw=cw: e.copy(tot_flat[:, c0:c0 + cw], pS5[1][:, 0:cw]), reads=["pS5_1"], writes=["tot5"])
        D4 = Dsb[:].rearrange("p (s g) h -> p s g h", g=NPG)
        T4 = tot5[:].rearrange("p (s g) h -> p s g h", g=NPG)
        lat = sb("lat", [128, NS, NH])
        P.dve(lambda e: e.memset(lat[:], 0.0), writes=["lat"])
        for pg in range(NPG - 2, -1, -1):
            P.dve(lambda e, pg=pg: e.tensor_tensor(lat[:], lat[:], T4[:, :, pg + 1, :], ALU.add),
                  reads=["lat", "tot5"], writes=["lat"])
            P.dve(lambda e, pg=pg: e.tensor_tensor(D4[:, :, pg, :], D4[:, :, pg, :], lat[:], ALU.add),
                  reads=["lat", "Dsb"], writes=["Dsb"])
        P.pe(lambda e: e.matmul(pb5[:, 0:NH], BT[:], lf_s[:], start=True, stop=True),
             reads=["BT", "lf_s"], writes=["pb5"])
        negnq = sb("negnq", [128, NH])
        P.act(lambda e: e.activation(negnq[:], pb5[:, 0:NH], AF.Copy, scale=-1.0), reads=["pb5"], writes=["negnq"])
        tmpN = sb("tmpN", [128, 4, NS, 2, 4])
        PTn = sb("PTn", [128, NH, NS, 4], BF16)
        for hp in range(4):
            P.pe(lambda e, hp=hp: e.matmul(pon5[:, hp * 8 * NS:(hp + 1) * 8 * NS], kT_s[:, hp, :],
                                           qbd[:, hp, :, :, :].rearrange("p s a q -> p (s a q)"), start=True, stop=True),
                 reads=["kT_s", "qbd"], writes=["pon5"])
        for hp in range(4):
            P.dve(lambda e, hp=hp: e.tensor_tensor(
                tmpN[:, hp, :, :, :], pon5[:, hp * 8 * NS:(hp + 1) * 8 * NS].rearrange("p (s a q) -> p s a q", a=2, q=4),
                maskN[:, 0:TK].rearrange("p (s q) -> p s q", q=4).unsqueeze(2).broadcast_to([128, NS, 2, 4]), ALU.add),
                reads=["pon5", "maskN"], writes=["tmpN"])
            P.dve(lambda e, hp=hp: e.tensor_tensor(
                tmpN[:, hp, :, :, :], tmpN[:, hp, :, :, :],
                negnq[:, 2 * hp:2 * hp + 2].unsqueeze(1).unsqueeze(3).broadcast_to([128, NS, 2, 4]), ALU.add),
                reads=["tmpN", "negnq"], writes=["tmpN"])
            P.act(lambda e, hp=hp: e.activation(
                PTn[:, 2 * hp:2 * hp + 2, :, :].rearrange("p a s q -> p s a q"), tmpN[:, hp, :, :, :], AF.Exp),
                reads=["tmpN"], writes=["PTn"])
        for h in range(NH):
            P.pe(lambda e, h=h: e.matmul(pon5[0:HD + 1, h * TK:(h + 1) * TK], Vb_s[:, h, 0:HD + 1],
                                         PTn[:, h, :, :].rearrange("p s q -> p (s q)"), start=True, stop=True),
                 reads=["Vb_s", "PTn", "tmpN"], writes=["pon5"])
        osn = sb("osn", [HD + 1, NH * TK])
        P.act(lambda e: e.copy(osn[:], pon5[0:HD + 1, 0:NH * TK]), reads=["pon5"], writes=["osn"])
        NK = 6
        kst = [sb(f"kst{i}", [128, AW]) for i in range(NK)]
        vst = [sb(f"vst{i}", [128, AW]) for i in range(NK)]
        KTb = [sb(f"KTb{i}", [128, 4, 128], BF16) for i in range(2)]
        Vp = [sb(f"Vp{i}", [128, NPG, NH, HD + 2], BF16) for i in range(2)]
        PT5 = [sb(f"PT5_{i}", [128, NPG, NH, 4], BF16) for i in range(2)]
        tmpS = sb("tmpS", [128, NPG, NH, 4])
        for i in range(2):
            P.pool(lambda e, i=i: e.memset(Vp[i][:], 1.0), writes=[f"Vp{i}"])

        def pv(s_):
            i2 = s_ % 2
            for h in range(NH):
                for pg in range(NPG):
                    P.pe(lambda e, h=h, pg=pg, i2=i2, s_=s_: e.matmul(
                        po5[0:HD + 1, h * TK + s_ * 4:h * TK + s_ * 4 + 4], Vp[i2][:, pg, h, 0:HD + 1], PT5[i2][:, pg, h, :],
                        start=(pg == 0), stop=(pg == NPG - 1)), reads=[f"Vp{i2}", f"PT5_{i2}"], writes=["po5"])

        for s_ in range(NS + 1):
            if s_ < NS:
                i2 = s_ % 2
                for pg in range(NPG):
                    c = s_ * NPG + pg
                    ik = c % NK
                    gather(kst[ik][:], ck, c, [f"kst{ik}"], sem=f"kg{ik}")
                    gather(vst[ik][:], cv, c, [f"vst{ik}"], sem=f"vg{ik}")
                    pk_ = pKT[c % 2]
                    pkk = f"pKT{c % 2}"
                    kb_ = KTb[c % 2]
                    kbk = f"KTb{c % 2}"
                    for hp in range(4):
                        P.pe(lambda e, pk_=pk_, hp=hp, ik=ik: e.transpose(
                            pk_[:, hp * 128:(hp + 1) * 128], kst[ik][:, hp * 128:(hp + 1) * 128], ident_f[:]),
                            reads=[f"kst{ik}", "ident_f"], writes=[pkk])
                    P.act(lambda e, pk_=pk_, kb_=kb_: e.copy(kb_[:].rearrange("p c k -> p (c k)"), pk_[:]),
                          reads=[pkk], writes=[kbk])
                    for hp in range(4):
                        P.pe(lambda e, kb_=kb_, hp=hp, pg=pg, i2=i2, s_=s_: e.matmul(
                            pS5[i2][:, pg * 32 + hp * 8:pg * 32 + hp * 8 + 8], kb_[:, hp, :],
                            qbd[:, hp, s_, :, :].rearrange("p a q -> p (a q)"), start=True, stop=True),
                            reads=[kbk, "qbd"], writes=[f"pS5_{i2}"])
                    P.dve(lambda e, ik=ik, pg=pg, i2=i2: e.tensor_copy(
                        Vp[i2][:, pg, :, 0:HD], vst[ik][:].rearrange("p (h d) -> p h d", h=NH)),
                        reads=[f"vst{ik}"], writes=[f"Vp{i2}"])
                P.dve(lambda e, i2=i2, s_=s_: e.tensor_tensor(
                    tmpS[:], pS5[i2][:].rearrange("p (g h q) -> p g h q", h=NH, q=4),
                    Dsb[:, s_ * NPG:(s_ + 1) * NPG, :].unsqueeze(3).broadcast_to([128, NPG, NH, 4]), ALU.add),
                    reads=[f"pS5_{i2}", "Dsb"], writes=["tmpS"])
                P.act(lambda e, i2=i2: e.activation(PT5[i2][:].rearrange("p g h q -> p (g h q)"),
                                                    tmpS[:].rearrange("p g h q -> p (g h q)"), AF.Exp),
                      reads=["tmpS"], writes=[f"PT5_{i2}"])
            if s_ >= 1:
                pv(s_ - 1)
        osb5 = sb("osb5", [HD + 1, NH * TK])
        rbc5 = sb("rbc5", [HD, NH * TK])
        attT_s = sb("attT_s", [HD, NH, 128], BF16)
        P.pool(lambda e: e.memset(attT_s[:], 0.0), writes=["attT_s"])
        P.act(lambda e: e.copy(osb5[:], po5[0:HD + 1, 0:NH * TK]), reads=["po5"], writes=["osb5"])
        P.dve(lambda e: e.tensor_tensor(osb5[:], osb5[:], osn[:], ALU.add), reads=["osb5", "osn"], writes=["osb5"])
        P.dve(lambda e: e.reciprocal(osb5[HD:HD + 1, :], osb5[HD:HD + 1, :]), reads=["osb5"], writes=["osb5"])
        P.pe(lambda e: e.matmul(pb5[0:HD, 0:NH * TK], ones_f[HD:HD + 1, 0:HD], osb5[HD:HD + 1, :], start=True, stop=True),
             reads=["osb5", "ones_f"], writes=["pb5"])
        P.act(lambda e: e.copy(rbc5[:], pb5[0:HD, 0:NH * TK]), reads=["pb5"], writes=["rbc5"])
        P.dve(lambda e: e.tensor_tensor(attT_s[:, :, 0:TK], osb5[0:HD, :].rearrange("p (h t) -> p h t", h=NH),
                                        rbc5[:].rearrange("p (h t) -> p h t", h=NH), ALU.mult),
              reads=["osb5", "rbc5", "attT_s"], writes=["attT_s"])
        P.dma(att_d_s[:, :, :], attT_s[:], reads=["attT_s"], writes=[("att_d", "s")])
        phase_end(s5)
        if KSTAGE == 5:
            P.barrier(); P.emit()
            return nc
        s4 = ExitStack()
        cur[0] = s4
        pst = [ps(f"pst{i}_4", [128, 512]) for i in range(2)]
        pmm = [ps(f"pmm{i}_4", [128, 512]) for i in range(4)]
        pS = [ps(f"pS{i}_4", [128, 512]) for i in range(2)]
        wglu = sb("wglu", [128, 4, SW], BF16)
        wout_a = sb("wout_a", [HD, NH, D], BF16)
        wout_s = sb("wout_s", [128, 4, D], BF16)
        wpg = sb("wpg", [128, 8, D], BF16)
        wpe = sb("wpe", [128, 2, D], BF16)
        P.dma(wglu[:], wglu_d.rearrange("(c p) n -> p c n", p=128), reads=[("wsc", "glu")], writes=["wglu"])
        P.dma(wout_a[:], wout_d[0:AW, :].rearrange("(h p) n -> p h n", p=HD), reads=[("wsc", "out")], writes=["wout_a"])
        P.dma(wout_s[:], wout_d[AW:D, :].rearrange("(c p) n -> p c n", p=128), reads=[("wsc", "out")], writes=["wout_s"])
        P.dma(wpg[:], wpg_d.rearrange("(c p) n -> p c n", p=128), reads=[("wsc", "pg")], writes=["wpg"])
        P.dma(wpe[:], wpe_d.rearrange("(c p) n -> p c n", p=128), reads=[("wsc", "pe")], writes=["wpe"])
        bcs = {}
        for nm, src in (("g0", ln_in_g), ("b0", ln_in_b), ("g1", ln1_g), ("b1", ln1_b), ("g2", ln2_g), ("b2", ln2_b)):
            t_ = sb("bc_" + nm, [128, D])
            P.dma(t_[:], src.partition_broadcast(128), writes=["bc_" + nm], q="gpsimd")
            bcs[nm] = t_
        bpg_row = sb("bpg_row", [1, D])
        P.dma(bpg_row[:], b_pg.rearrange("(o n) -> o n", o=1), writes=["bpg_row"])
        bglu_col = sb("bglu_col", [128, 4])
        P.dma(bglu_col[:], b_glu.rearrange("(c p) -> p c", p=128), writes=["bglu_col"], allow_slow_non_contiguous=True)

        attb = sb("attb", [HD, NH, 512], BF16)
        ygb4 = sb("ygb4", [128, 4, 512], BF16)
        gate_b = sb("gate_b", [128, 512], BF16)
        ssmT = sb("ssmT", [128, 4, 512], BF16)
        xt4 = sb("xt4", [128, 4, D])
        o2 = sb("o2", [128, 4, D])
        pt4 = sb("pt4", [128, 4, PLE])
        pT = sb("pT4", [128, 2, 512], BF16)
        h1T = sb("h1T", [128, 8, 512], BF16)
        actT = sb("actT", [128, 32, 512], BF16)
        rl = [sb(f"rl{i}", [128, 512], BF16) for i in range(2)]
        tmp1 = [sb(f"tmp1_{i}", [128, 512]) for i in range(2)]
        tmp2 = [sb(f"tmp2_{i}", [128, 512]) for i in range(2)]
        wupt = [sb(f"wupt{i}", [128, 8, 128], BF16) for i in range(3)]
        wdnt = [sb(f"wdnt{i}", [128, 512], BF16) for i in range(4)]
        stats4 = sb("stats4", [128, 2, 6])
        mv4 = sb("mv4", [128, 2])
        rstd4 = sb("rstd4", [128, 1])
        nmr4 = sb("nmr4", [128, 1])
        cnt4 = {"wup": 0, "wdn": 0, "t1": 0, "t2": 0, "rl": 0}

        def ln_tile(buf, key, tt, gname, bname):
            for c in range(2):
                P.dve(lambda e, c=c: e.bn_stats(stats4[:, c, :], buf[:, tt, c * 512:(c + 1) * 512]),
                      reads=[(key, tt)], writes=[("stats4", c)])
            P.dve(lambda e: e.bn_aggr(mv4[:], stats4[:].rearrange("p a b -> p (a b)")),
                  reads=[("stats4", 0), ("stats4", 1)], writes=["mv4"])
            P.dve(lambda e: e.tensor_scalar(rstd4[:], mv4[:, 1:2], EPS, None, ALU.add), reads=["mv4"], writes=["rstd4"])
            P.act(lambda e: e.activation(rstd4[:], rstd4[:], AF.Sqrt), reads=["rstd4"], writes=["rstd4"])
            P.dve(lambda e: e.reciprocal(rstd4[:], rstd4[:]), reads=["rstd4"], writes=["rstd4"])
            P.dve(lambda e: e.scalar_tensor_tensor(nmr4[:], mv4[:, 0:1], -1.0, rstd4[:], ALU.mult, ALU.mult),
                  reads=["mv4", "rstd4"], writes=["nmr4"])
            P.act(lambda e: e.activation(buf[:, tt, :], buf[:, tt, :], AF.Identity, bias=nmr4[:, 0:1], scale=rstd4[:, 0:1]),
                  reads=["nmr4", "rstd4", (key, tt)], writes=[(key, tt)])
            P.dve(lambda e: e.tensor_tensor(buf[:, tt, :], buf[:, tt, :], bcs[gname][:], ALU.mult),
                  reads=[(key, tt), "bc_" + gname], writes=[(key, tt)])
            P.dve(lambda e: e.tensor_tensor(buf[:, tt, :], buf[:, tt, :], bcs[bname][:], ALU.add),
                  reads=[(key, tt), "bc_" + bname], writes=[(key, tt)])

        def proj_tm(ntt, ktiles, consumer, extra=None):
            for nh in range(2):
                for ki, (lf, rf, lkeys) in enumerate(ktiles):
                    rap, rkeys = rf(nh)
                    for tt in range(ntt):
                        P.pe(lambda e, tt=tt, lf=lf, rap=rap, ki=ki: e.matmul(
                            pmm[tt][:], lf(tt), rap, start=(ki == 0), stop=(ki == len(ktiles) - 1 and extra is None)),
                            reads=list(lkeys) + list(rkeys), writes=[f"pmm{tt}"])
                if extra is not None:
                    for tt in range(ntt):
                        extra(tt, nh)
                for tt in range(ntt):
                    consumer(tt, nh, pmm[tt], f"pmm{tt}")

        def next_buf(name, bufs):
            i = cnt4[name] % len(bufs)
            cnt4[name] += 1
            return bufs[i], f"{name}{i}"

        def post_block(I, ntt, x_src, p_src, att_src, yg_src, y_dst, gcol=None):
            tsl = lambda tt: slice(tt * 128, (tt + 1) * 128)
            ncol = ntt * 128
            if gcol is None:
                P.dma(attb[:, :, 0:ncol], att_src, reads=[("att_d", I)], writes=["attb"], slot="attb")
                P.dma(ygb4[:, :, 0:ncol], yg_src, reads=["yg_d"], writes=["ygb4"], slot="ygb4")
            else:
                P.op("g_attb", lambda e: e.indirect_dma_start(
                    out=attb[:].rearrange("p h t -> p (h t)"), out_offset=None, in_=att_d2[:, :],
                    in_offset=bass.IndirectOffsetOnAxis(ap=bidx[0:HD, gcol:gcol + 1], axis=0)),
                    reads=["bidx"], writes=["attb"], inc=16)
                P.op("g_ygb4", lambda e: e.indirect_dma_start(
                    out=ygb4[:].rearrange("p g t -> p (g t)"), out_offset=None, in_=yg_d2[:, :],
                    in_offset=bass.IndirectOffsetOnAxis(ap=bidx[:, NSB // 2 + gcol:NSB // 2 + gcol + 1], axis=0)),
                    reads=["bidx"], writes=["ygb4"], inc=16)
            P.dma(xt4[:, 0:ntt, :], x_src, writes=[("xt4", tt) for tt in range(ntt)], slot="xt4")
            P.dma(pt4[:, 0:ntt, :], p_src, writes=["pt4"], q="gpsimd", slot="pt4")
            for nt in range(4):
                psg = pS[nt % 2]
                pgk = f"pS{nt % 2}"
                for ct in range(4):
                    P.pe(lambda e, psg=psg, nt=nt, ct=ct: e.matmul(psg[:, 0:ncol], wglu[:, ct, nt * 128:(nt + 1) * 128], ygb4[:, ct, 0:ncol],
                                                                  start=(ct == 0), stop=(ct == 3)),
                         reads=["wglu", "ygb4"], writes=[pgk])
                P.act(lambda e, psg=psg, nt=nt: e.activation(gate_b[:, 0:ncol], psg[:, 0:ncol], AF.Sigmoid, bias=bglu_col[:, nt:nt + 1]),
                      reads=[pgk, "bglu_col"], writes=["gate_b"])
                P.dve(lambda e, nt=nt: e.tensor_tensor(ssmT[:, nt, 0:ncol], ygb4[:, nt, 0:ncol], gate_b[:, 0:ncol], ALU.mult),
                      reads=["gate_b", "ygb4"], writes=[("ssmT", nt)])
            for tt in range(ntt):
                ln_tile(xt4, "xt4", tt, "g0", "b0")
            for kk in range(2):
                pp_ = pst[kk]
                for tt in range(ntt):
                    P.pe(lambda e, pp_=pp_, tt=tt, kk=kk: e.transpose(pp_[:, tsl(tt)], pt4[:, tt, kk * 128:(kk + 1) * 128], ident_f[:]),
                         reads=["pt4", "ident_f"], writes=[f"pst{kk}"])
                P.act(lambda e, pp_=pp_, kk=kk: e.copy(pT[:, kk, 0:ntt * 128], pp_[:, 0:ntt * 128]),
                      reads=[f"pst{kk}"], writes=[("pT", kk)])
            kt = []
            for h in range(NH):
                kt.append((lambda tt, h=h: attb[:, h, tsl(tt)], lambda nh, h=h: (wout_a[:, h, nh * 512:(nh + 1) * 512], ["wout_a"]), ["attb"]))
            for ct in range(4):
                kt.append((lambda tt, ct=ct: ssmT[:, ct, tsl(tt)], lambda nh, ct=ct: (wout_s[:, ct, nh * 512:(nh + 1) * 512], ["wout_s"]),
                           [("ssmT", ct)]))

            def cons_out(tt, nh, pm_, pk_):
                t1_, t1k = next_buf("t1", tmp1)
                P.act(lambda e: e.copy(t1_[:], pm_[:]), reads=[pk_], writes=[t1k])
                P.dve(lambda e: e.scalar_tensor_tensor(xt4[:, tt, nh * 512:(nh + 1) * 512], xt4[:, tt, nh * 512:(nh + 1) * 512],
                                                       ALPHA, t1_[:], ALU.mult, ALU.add),
                      reads=[t1k, ("xt4", tt)], writes=[("xt4", tt)])
            proj_tm(ntt, kt, cons_out)
            for tt in range(ntt):
                ln_tile(xt4, "xt4", tt, "g1", "b1")
            for ct in range(8):
                pp_ = pst[ct % 2]
                for tt in range(ntt):
                    P.pe(lambda e, pp_=pp_, tt=tt, ct=ct: e.transpose(pp_[:, tsl(tt)], xt4[:, tt, ct * 128:(ct + 1) * 128], ident_f[:]),
                         reads=[("xt4", tt), "ident_f"], writes=[f"pst{ct % 2}"])
                P.act(lambda e, pp_=pp_, ct=ct: e.copy(h1T[:, ct, 0:ntt * 128], pp_[:, 0:ntt * 128]),
                      reads=[f"pst{ct % 2}"], writes=[("h1T", ct)])
            h1keys = [("h1T", ct) for ct in range(8)]
            for ft in range(32):
                wt_, wk_ = next_buf("wup", wupt)
                P.dma(wt_[:], wup_d[:, ft * 128:(ft + 1) * 128].rearrange("(c p) f -> p c f", p=128),
                      reads=[("wsc", "up")], writes=[wk_], q=("sync" if int(wk_[-1]) % 2 == 0 else "gpsimd"), slot=wk_)
                psu = pS[ft % 2]
                puk = f"pS{ft % 2}"
                for ct in range(8):
                    P.pe(lambda e, psu=psu, wt_=wt_, ct=ct: e.matmul(psu[:, 0:ncol], wt_[:, ct, :], h1T[:, ct, 0:ncol],
                                                                    start=(ct == 0), stop=(ct == 7)),
                         reads=[wk_] + h1keys, writes=[puk])
                r_, rk_ = next_buf("rl", rl)
                P.act(lambda e, psu=psu, r_=r_: e.activation(r_[:, 0:ncol], psu[:, 0:ncol], AF.Relu), reads=[puk], writes=[rk_])
                P.pool(lambda e, r_=r_, ft=ft: e.tensor_tensor(actT[:, ft, 0:ncol], r_[:, 0:ncol], r_[:, 0:ncol], ALU.mult),
                       reads=[rk_], writes=[("actT", ft)])
            kt = []
            for ft in range(32):
                def rf(nh, ft=ft):
                    wt_, wk_ = next_buf("wdn", wdnt)
                    P.dma(wt_[:], wdn_d[ft * 128:(ft + 1) * 128, nh * 512:(nh + 1) * 512], reads=[("wsc", "dn")], writes=[wk_],
                          q=("sync" if int(wk_[-1]) % 2 == 0 else "gpsimd"), slot=wk_)
                    return wt_[:], [wk_]
                kt.append((lambda tt, ft=ft: actT[:, ft, tsl(tt)], rf, [("actT", ft)]))

            def cons_dn(tt, nh, pm_, pk_):
                t1_, t1k = next_buf("t1", tmp1)
                P.act(lambda e: e.copy(t1_[:], pm_[:]), reads=[pk_], writes=[t1k])
                P.dve(lambda e: e.scalar_tensor_tensor(o2[:, tt, nh * 512:(nh + 1) * 512], xt4[:, tt, nh * 512:(nh + 1) * 512],
                                                       ALPHA, t1_[:], ALU.mult, ALU.add),
                      reads=[t1k, ("xt4", tt)], writes=[("o2", tt)])
            proj_tm(ntt, kt, cons_dn)
            ktg = [(lambda tt, ct=ct: h1T[:, ct, tsl(tt)], lambda nh, ct=ct: (wpg[:, ct, nh * 512:(nh + 1) * 512], ["wpg"]), [("h1T", ct)])
                   for ct in range(8)]

            def bias_mm(tt, nh):
                P.pe(lambda e: e.matmul(pmm[tt][:], ones_f[0:1, :], bpg_row[0:1, nh * 512:(nh + 1) * 512], start=False, stop=True),
                     reads=["ones_f", "bpg_row"], writes=[f"pmm{tt}"])

            kte = [(lambda tt, kk=kk: pT[:, kk, tsl(tt)], lambda nh, kk=kk: (wpe[:, kk, nh * 512:(nh + 1) * 512], ["wpe"]), [("pT", kk)])
                   for kk in range(2)]
            pe_ps = [(pS[0], "pS0"), (pS[1], "pS1"), (pst[0], "pst0"), (pst[1], "pst1")]
            for nh in range(2):
                for ki, (lf, rf, lkeys) in enumerate(ktg):
                    rap, rkeys = rf(nh)
                    for tt in range(ntt):
                        P.pe(lambda e, tt=tt, lf=lf, rap=rap, ki=ki: e.matmul(pmm[tt][:], lf(tt), rap, start=(ki == 0), stop=False),
                             reads=list(lkeys) + list(rkeys), writes=[f"pmm{tt}"])
                for tt in range(ntt):
                    bias_mm(tt, nh)
                for ki, (lf, rf, lkeys) in enumerate(kte):
                    rap, rkeys = rf(nh)
                    for tt in range(ntt):
                        P.pe(lambda e, tt=tt, lf=lf, rap=rap, ki=ki: e.matmul(pe_ps[tt][0][:], lf(tt), rap, start=(ki == 0), stop=(ki == 1)),
                             reads=list(lkeys) + list(rkeys), writes=[pe_ps[tt][1]])
                for tt in range(ntt):
                    t1_, t1k = next_buf("t1", tmp1)
                    t2_, t2k = next_buf("t2", tmp2)
                    P.act(lambda e, t1_=t1_, tt=tt: e.activation(t1_[:], pmm[tt][:], AF.Sigmoid), reads=[f"pmm{tt}"], writes=[t1k])
                    P.act(lambda e, t2_=t2_, tt=tt: e.copy(t2_[:], pe_ps[tt][0][:]), reads=[pe_ps[tt][1]], writes=[t2k])
                    P.dve(lambda e, t1_=t1_, t2_=t2_: e.tensor_tensor(t2_[:], t2_[:], t1_[:], ALU.mult), reads=[t1k, t2k], writes=[t2k])
                    P.dve(lambda e, t2_=t2_, tt=tt, nh=nh: e.tensor_tensor(o2[:, tt, nh * 512:(nh + 1) * 512],
                                                                          o2[:, tt, nh * 512:(nh + 1) * 512], t2_[:], ALU.add),
                          reads=[t2k, ("o2", tt)], writes=[("o2", tt)])
            for tt in range(ntt):
                ln_tile(o2, "o2", tt, "g2", "b2")
            P.dma(y_dst, o2[:, 0:ntt, :], reads=[("o2", tt) for tt in range(ntt)], slot="o2")

        bidx = sb("bidx_sb", [128, NSB], I32)
        P.dma(bidx[:], bidx_d[:, :], writes=["bidx"])
        P.add_sem("g_attb", "gpsimd")
        P.add_sem("g_ygb4", "gpsimd")
        for I in range(NSB // 2):
            rs = slice(I * 512, (I + 1) * 512)
            post_block(I, 4,
                       x_post[rs, :].rearrange("(t p) d -> p t d", p=128),
                       pp_post[rs, :].rearrange("(t p) d -> p t d", p=128),
                       None, None,
                       y_p[rs, :].rearrange("(t p) d -> p t d", p=128), gcol=I)

        post_block("s", 1,
                   x_s.rearrange("(t p) d -> p t d", p=128),
                   pp_s.rearrange("(t p) d -> p t d", p=128),
                   att_d_s[:, :, :], yg_d_s.rearrange("g p t -> p g t"),
                   y_s.rearrange("(t p) d -> p t d", p=128))

        P.barrier()
        P.emit()
        s4.close()
    return nc


_CACHE = {}
_LAST = {}


DBG = bool(int(os.environ.get('KDBG', '0')))


def _get_nc(T, NSEQ, NPHYS):
    key = (T, NSEQ, NPHYS)
    if key not in _CACHE:
        _CACHE[key] = build(T, NSEQ, NPHYS, dbg=DBG)
    return _CACHE[key]


def run_cores(inputs, T, NSEQ, NPHYS, ncores=8):
    nc = _get_nc(T, NSEQ, NPHYS)
    f = lambda a: np.ascontiguousarray(a, dtype=np.float32)
    in_maps = []
    nb = inputs["x_prompt"].shape[0]
    ST = NSEQ * 4
    ckf = f(inputs["cache_k"][0]).reshape(NPHYS * 128, AW)
    cvf = f(inputs["cache_v"][0]).reshape(NPHYS * 128, AW)
    clff = f(inputs["cache_logf"][0]).reshape(NPHYS * 128, NH)
    shared = {
        "w_in": f(inputs["w_in"][0]),
        "ln_in_g": f(inputs["ln_in_g"]),
        "ln_in_b": f(inputs["ln_in_b"]),
        "b_f": f(inputs["b_f"][0]),
        "lam_re": f(inputs["lam_re"][0]), "lam_im": f(inputs["lam_im"][0]), "log_dt": f(inputs["log_dt"][0]),
        "b_re": f(inputs["b_re"][0]), "b_im": f(inputs["b_im"][0]),
        "c_re": f(inputs["c_re"][0]), "c_im": f(inputs["c_im"][0]), "d_skip": f(inputs["d_skip"][0]),
        "w_glu": f(inputs["w_glu"][0]), "b_glu": f(inputs["b_glu"][0]), "w_out": f(inputs["w_out"][0]),
        "ln1_g": f(inputs["ln1_g"][0]), "ln1_b": f(inputs["ln1_b"][0]), "w_up": f(inputs["w_up"][0]),
        "w_down": f(inputs["w_down"][0]), "w_pe": f(inputs["w_pe"][0]), "w_pg": f(inputs["w_pg"][0]),
        "b_pg": f(inputs["b_pg"][0]), "ln2_g": f(inputs["ln2_g"][0]), "ln2_b": f(inputs["ln2_b"][0]),
        "ck": ckf, "cv": cvf, "clf": clff,
    }
    nh_ = T // 1024
    for c in range(ncores):
        b = c % nb
        sl = slice(c * NSEQ, (c + 1) * NSEQ)
        xs = np.zeros((128, D), np.float32)
        xs[0:ST] = inputs["x_sample"][sl].reshape(ST, D)
        ps_ = np.zeros((128, PLE), np.float32)
        ps_[0:ST] = inputs["p_sample"][0, sl].reshape(ST, PLE)
        m = dict(shared)
        m.update({
            "x_p": f(inputs["x_prompt"][b]),
            "pp_p": f(inputs["p_prompt"][0, b]),
            "x_post": f(inputs["x_prompt"][b, (c // nb) * (T // 2):(c // nb + 1) * (T // 2)]),
            "pp_post": f(inputs["p_prompt"][0, b, (c // nb) * (T // 2):(c // nb + 1) * (T // 2)]),
            "bidx": np.concatenate(
                [((c // nb) * nh_ + np.arange(nh_, dtype=np.int32))[None, :] * HD + (np.arange(128, dtype=np.int32) % HD)[:, None],
                 ((c // nb) * nh_ + np.arange(nh_, dtype=np.int32))[None, :] * 128 + np.arange(128, dtype=np.int32)[:, None]],
                axis=1).astype(np.int32),
            "x_s": xs, "pp_s": ps_,
            "st_re": f(inputs["state_re"][0, sl]).reshape(NSEQ, NG * SP),
            "st_im": f(inputs["state_im"][0, sl]).reshape(NSEQ, NG * SP),
            "pt": np.ascontiguousarray(inputs["page_table"][sl], dtype=np.int32).reshape(NSEQ * 16),
        })
        in_maps.append(m)
    res = run_bass_kernel_spmd(nc, in_maps, core_ids=list(range(ncores)))
    _LAST["r"] = res.results
    return res.results


def kernel(**inputs):
    T = inputs["x_prompt"].shape[1]
    NSEQ = inputs["x_sample"].shape[0] // 8
    NPHYS = inputs["cache_k"].shape[1]
    ST = NSEQ * 4
    r = run_cores(inputs, T, NSEQ, NPHYS)
    nb = inputs["x_prompt"].shape[0]
    db = inputs["x_sample"].shape[0]
    g = lambda c, n: np.asarray(r[c][n], dtype=np.float32)
    y_prompt = np.stack([np.concatenate([g(b, "y_p"), g(b + nb, "y_p")]) for b in range(nb)])
    k_prompt = np.stack([g(b, "k_p").reshape(T, NH, HD) for b in range(nb)])[None]
    v_prompt = np.stack([g(b, "v_p").reshape(T, NH, HD) for b in range(nb)])[None]
    lf_prompt = np.stack([g(b, "lf_p") for b in range(nb)])[None]
    sre_p = np.stack([g(b, "ssm_re_p") for b in range(nb)])[None]
    sim_p = np.stack([g(b, "ssm_im_p") for b in range(nb)])[None]
    y_sample = np.concatenate([g(c, "y_s")[0:ST].reshape(NSEQ, 4, D) for c in range(8)])
    k_sample = np.concatenate([g(c, "k_s").reshape(NSEQ, 4, NH, HD) for c in range(8)])[None]
    v_sample = np.concatenate([g(c, "v_s").reshape(NSEQ, 4, NH, HD) for c in range(8)])[None]
    lf_sample = np.concatenate([g(c, "lf_sd").reshape(NSEQ, 4, NH) for c in range(8)])[None]
    sre_s = np.concatenate([g(c, "ssm_re_s").reshape(NSEQ, NG, SP) for c in range(8)])[None]
    sim_s = np.concatenate([g(c, "ssm_im_s").reshape(NSEQ, NG, SP) for c in range(8)])[None]
    return (y_prompt, y_sample, k_prompt, v_prompt, lf_prompt, sre_p, sim_p,
            k_sample, v_sample, lf_sample, sre_s, sim_s)
```
